# Optimizing a Trainium2 kernel written in Bass

```python
import math
import jax
import jax.numpy as jnp
from jax import lax
import numpy as np

D_MODEL = 1024
BATCH = 16
SEQ = 4096
DEPTH = 2
DEC_BATCH = 4
DEC_SEQ = 4096
PAST_LEN = 128

HEAD_DIM = 64
ATTN_PATTERNS = ((128, 1), (512, 4), (2048, 16))
N_PATTERNS = len(ATTN_PATTERNS)
ATTN_HEADS_PER_GROUP = 4
N_ATTN_HEADS = N_PATTERNS * ATTN_HEADS_PER_GROUP
ATTN_WIDTH = N_ATTN_HEADS * HEAD_DIM
ATTN_OUT = ATTN_HEADS_PER_GROUP * HEAD_DIM
ALIBI_MAX_EXP = 8.0
HYENA_WIDTH = D_MODEL // 4
HYENA_BANDS = 16
HYENA_EMB = 2 * HYENA_BANDS + 1
HYENA_FFN = 64
HYENA_TARGET = 1e-2
HYENA_FAST_DECAY = 0.3
HYENA_SLOW_DECAY = 1.5
IN_PROJ_WIDTH = 3 * ATTN_WIDTH + 3 * HYENA_WIDTH
MIX_OUT_WIDTH = ATTN_OUT + HYENA_WIDTH
RWKV_HEAD = 64
RWKV_HEADS = D_MODEL // RWKV_HEAD
DECAY_LORA = max(32, int(round(1.8 * D_MODEL ** 0.5 / 32)) * 32)
AAA_LORA = max(32, int(round(1.8 * D_MODEL ** 0.5 / 32)) * 32)
GATE_LORA = max(32, int(round(0.6 * D_MODEL ** 0.8 / 32)) * 32)
D_FF = ((8 * D_MODEL) // 3 + 127) // 128 * 128
RMS_EPS = 1e-6
GN_EPS = 64e-5
NEG_INF = -1e30

kernel_name = "hybrid_dilated_hyena_rwkv7_encoder"


def _rmsnorm(x, gain):
    x32 = x.astype(jnp.float32)
    y = x32 * lax.rsqrt(jnp.mean(x32 * x32, axis=-1, keepdims=True) + RMS_EPS)
    return (y * gain.astype(jnp.float32)).astype(x.dtype)


def _adaln_params(c, w, b):
    m = jax.nn.silu(c) @ w + b
    return [t[:, None, :] for t in jnp.split(m, 6, axis=-1)]


def _centred_dwconv3(x, w, b):
    xp = jnp.pad(x, ((0, 0), (1, 1), (0, 0)))
    return xp[:, :-2] * w[0] + xp[:, 1:-1] * w[1] + xp[:, 2:] * w[2] + b


def _alibi_slopes():
    return jnp.exp2(-ALIBI_MAX_EXP * (jnp.arange(N_ATTN_HEADS, dtype=jnp.float32) + 1.0) / N_ATTN_HEADS)


def _dilated_window_attention(q, k, v, slopes, window, dilation):
    B, L, H, Dh = q.shape
    radius = window // (2 * dilation)
    blk = radius
    n = L // dilation
    nb = -(-n // blk)
    n_pad = nb * blk
    bd = B * dilation

    def to_classes(t):
        return t.reshape(B, n, dilation, H, Dh).transpose(0, 2, 1, 3, 4).reshape(bd, n, H, Dh)

    def key_windows(t):
        tp = jnp.pad(to_classes(t), ((0, 0), (radius, n_pad - n + radius), (0, 0), (0, 0)))
        tp = tp.reshape(bd, nb + 2, blk, H, Dh)
        return jnp.concatenate([tp[:, :-2], tp[:, 1:-1], tp[:, 2:]], axis=2)

    qc = jnp.pad(to_classes(q), ((0, 0), (0, n_pad - n), (0, 0), (0, 0))).reshape(bd, nb, blk, H, Dh)
    kw = key_windows(k)
    vw = key_windows(v)
    s = jnp.einsum('bnqhd,bnkhd->bnhqk', qc, kw, preferred_element_type=jnp.float32) * (Dh ** -0.5)
    qi = jnp.arange(blk)
    kj = jnp.arange(3 * blk)
    rel = kj[None, :] - radius - qi[:, None]
    kpos = jnp.arange(nb)[:, None] * blk - radius + kj[None, :]
    valid = (jnp.abs(rel)[None] <= radius) & ((kpos >= 0) & (kpos < n))[:, None, :]
    alibi = -slopes[:, None, None] * (jnp.abs(rel) * dilation).astype(jnp.float32)[None]
    s = jnp.where(valid[None, :, None], s + alibi[None, None], NEG_INF)
    m = jnp.max(s, axis=-1, keepdims=True)
    p = jnp.exp(s - m)
    den = jnp.sum(p, axis=-1, keepdims=True)
    o = jnp.einsum('bnhqk,bnkhd->bnqhd', p / den, vw)
    lse = jnp.transpose((m + jnp.log(den))[..., 0], (0, 1, 3, 2))

    def from_classes(t):
        t = t.reshape((bd, n_pad) + t.shape[3:])[:, :n]
        t = t.reshape((B, dilation, n) + t.shape[2:])
        t = jnp.swapaxes(t, 1, 2)
        return t.reshape((B, L) + t.shape[3:])

    return from_classes(o), from_classes(lse)


def _hyena_positional_features(L):
    t = jnp.linspace(0.0, 1.0, L, dtype=jnp.float32)[:, None]
    w = 2.0 * math.pi * jnp.arange(L, dtype=jnp.float32)[:, None] / L
    f = jnp.linspace(1e-4, HYENA_BANDS - 1, HYENA_BANDS, dtype=jnp.float32)[None, :]
    z = f * w
    return jnp.concatenate([t, jnp.cos(z), -jnp.sin(z)], axis=-1)


def _hyena_two_sided_filter(L, w1, b1, w2, b2, w3, b3, w4, freq):
    z = _hyena_positional_features(L)
    h = jnp.sin(freq * (z @ w1 + b1))
    h = jnp.sin(freq * (h @ w2 + b2))
    h = jnp.sin(freq * (h @ w3 + b3))
    h = (h @ w4).astype(jnp.float32).reshape(L, 2, HYENA_WIDTH)
    t = jnp.linspace(0.0, 1.0, L, dtype=jnp.float32)[:, None]
    max_decay = math.log(HYENA_TARGET) / HYENA_FAST_DECAY
    min_decay = math.log(HYENA_TARGET) / HYENA_SLOW_DECAY
    deltas = jnp.linspace(min_decay, max_decay, HYENA_WIDTH, dtype=jnp.float32)[None, :]
    window = jnp.exp(-t * jnp.abs(deltas))
    h_fwd = h[:, 0] * window
    h_bwd = h[:, 1] * window
    two_sided = jnp.concatenate([h_fwd, jnp.zeros((1, HYENA_WIDTH), jnp.float32), h_bwd[:0:-1]], axis=0)
    return two_sided / jnp.sum(jnp.abs(two_sided), axis=0, keepdims=True)


def _hyena(u, short_w, short_b, w1, b1, w2, b2, w3, b3, w4, freq, filt_bias):
    B, L, _ = u.shape
    u = _centred_dwconv3(u, short_w, short_b)
    x0, x1, v = jnp.split(u, 3, axis=-1)
    z = (v * x1).astype(jnp.float32)
    filt = _hyena_two_sided_filter(L, w1, b1, w2, b2, w3, b3, w4, freq)
    n_fft = 2 * L
    zf = jnp.fft.rfft(z, n=n_fft, axis=1)
    ff = jnp.fft.rfft(filt, n=n_fft, axis=0)
    conv = jnp.fft.irfft(zf * ff[None], n=n_fft, axis=1)[:, :L]
    z = conv + z * filt_bias.astype(jnp.float32)
    return z * x0.astype(jnp.float32)


def _attn_hyena_mixer(h, p):
    B, L, _ = h.shape
    proj = h @ p['w_in']
    qkv = proj[..., :3 * ATTN_WIDTH].reshape(B, L, 3, N_PATTERNS, ATTN_HEADS_PER_GROUP, HEAD_DIM)
    hy = proj[..., 3 * ATTN_WIDTH:]
    slopes = _alibi_slopes().reshape(N_PATTERNS, ATTN_HEADS_PER_GROUP)
    outs = []
    lses = []
    for g, (window, dilation) in enumerate(ATTN_PATTERNS):
        o, lse = _dilated_window_attention(qkv[:, :, 0, g], qkv[:, :, 1, g], qkv[:, :, 2, g],
                                           slopes[g], window, dilation)
        outs.append(o)
        lses.append(lse)
    alpha = jax.nn.softmax(jnp.stack(lses, axis=0), axis=0)
    attn = jnp.sum(alpha[..., None] * jnp.stack(outs, axis=0), axis=0).reshape(B, L, ATTN_OUT)
    hy_out = _hyena(hy, p['short_w'], p['short_b'], p['filt_w1'], p['filt_b1'], p['filt_w2'], p['filt_b2'],
                    p['filt_w3'], p['filt_b3'], p['filt_w4'], p['filt_freq'], p['filt_bias'])
    mixed = jnp.concatenate([attn.astype(h.dtype), hy_out.astype(h.dtype)], axis=-1)
    return mixed @ p['w_out']


def _wkv_scan(r, decay, k, v, kk, a, reverse):
    B, L, H, N = r.shape

    def step(S, inp):
        r_t, w_t, k_t, v_t, kk_t, a_t = inp
        sa = jnp.einsum('bhvk,bhk->bhv', S, -kk_t)
        S = (S * w_t[:, :, None, :] + sa[..., None] * (kk_t * a_t)[:, :, None, :]
             + v_t[..., None] * k_t[:, :, None, :])
        return S, jnp.einsum('bhvk,bhk->bhv', S, r_t)

    xs = tuple(jnp.swapaxes(t, 0, 1) for t in (r, decay, k, v, kk, a))
    S0 = jnp.zeros((B, H, N, N), jnp.float32)
    _, ys = lax.scan(step, S0, xs, reverse=reverse)
    return jnp.swapaxes(ys, 0, 1)


def _rwkv7_bidir(h, p):
    B, L, D = h.shape
    H, N = RWKV_HEADS, RWKV_HEAD

    def heads(t):
        return t.astype(jnp.float32).reshape(B, L, H, N)

    hp = jnp.pad(h, ((0, 0), (1, 1), (0, 0)))
    xx = 0.5 * (hp[:, :-2] + hp[:, 2:]) - h
    xr, xw, xk, xv, xa, xg = [h + xx * p['mu'][i] for i in range(6)]
    r = xr @ p['w_r']
    k = xk @ p['w_k']
    v = xv @ p['w_v']
    g = jax.nn.sigmoid(xg @ p['g1']) @ p['g2']
    kk = heads(k * p['k_k'])
    kk = kk / jnp.maximum(jnp.linalg.norm(kk, axis=-1, keepdims=True), 1e-12)
    ys = []
    ks = []
    for direction in range(2):
        w_log = -jax.nn.softplus(-(p['w0'][direction] + jnp.tanh(xw @ p['w1'][direction]) @ p['w2'][direction])) - 0.5
        decay = jnp.exp(-jnp.exp(w_log.astype(jnp.float32)))
        a = jax.nn.sigmoid(p['a0'][direction] + (xa @ p['a1'][direction]) @ p['a2'][direction])
        k_dir = k * (1.0 + (a - 1.0) * p['k_a'])
        ys.append(_wkv_scan(heads(r), heads(decay), heads(k_dir), heads(v), kk, heads(a),
                            reverse=(direction == 1)))
        ks.append(heads(k_dir))
    y = ys[0] + ys[1]
    mu = jnp.mean(y, axis=-1, keepdims=True)
    var = jnp.mean(jnp.square(y - mu), axis=-1, keepdims=True)
    yn = ((y - mu) * lax.rsqrt(var + GN_EPS)).reshape(B, L, D) * p['ln_w'] + p['ln_b']
    k_mean = 0.5 * (ks[0] + ks[1])
    bonus = jnp.sum(heads(r) * k_mean * p['r_k'], axis=-1, keepdims=True) * heads(v)
    out = (yn + bonus.reshape(B, L, D)) * g
    return out.astype(h.dtype) @ p['w_o']


def _conv_ffn(h, p):
    a, gate = jnp.split(h @ p['ffn_up'], 2, axis=-1)
    a = _centred_dwconv3(a, p['ffn_conv_w'], p['ffn_conv_b'])
    return (jax.nn.gelu(a) * gate) @ p['ffn_down']


def _trunk(x, c, layers, final_norm):
    for i in range(DEPTH):
        p = layers[i]
        sh1, sc1, g1, sh2, sc2, g2 = _adaln_params(c, p['ada_w'], p['ada_b'])
        h = _rmsnorm(x, p['norm1']) * (1.0 + sc1) + sh1
        mix = _attn_hyena_mixer(h, p) if i % 2 == 0 else _rwkv7_bidir(h, p)
        x = x + g1 * mix
        h = _rmsnorm(x, p['norm2']) * (1.0 + sc2) + sh2
        x = x + g2 * _conv_ffn(h, p)
    return _rmsnorm(x, final_norm)


def setup_inputs(seed: int = 0) -> dict:
    key = jax.random.key(seed)
    keys = iter(jax.random.split(key, 96))
    D = D_MODEL

    def nrm(shape, scale):
        return scale * jax.random.normal(next(keys), shape, jnp.float32)

    def gain(shape):
        return 1.0 + nrm(shape, 0.02)

    inp = {}
    inp['x_prompt'] = nrm((BATCH, SEQ, D), 1.0)
    inp['x_sample'] = nrm((DEC_BATCH, DEC_SEQ, D), 1.0)
    inp['c_prompt'] = nrm((BATCH, D), 1.0)
    inp['c_sample'] = nrm((DEC_BATCH, D), 1.0)
    inp['l0_ada_w'] = nrm((D, 6 * D), 0.5 * D ** -0.5)
    inp['l0_ada_b'] = nrm((6 * D,), 0.01)
    inp['l0_norm1'] = gain((D,))
    inp['l0_norm2'] = gain((D,))
    inp['l0_w_in'] = nrm((D, IN_PROJ_WIDTH), D ** -0.5)
    inp['l0_short_w'] = nrm((3, 3 * HYENA_WIDTH), 3 ** -0.5)
    inp['l0_short_b'] = nrm((3 * HYENA_WIDTH,), 0.01)
    inp['l0_filt_w1'] = nrm((HYENA_EMB, HYENA_FFN), HYENA_EMB ** -0.5)
    inp['l0_filt_b1'] = nrm((HYENA_FFN,), 0.1)
    inp['l0_filt_w2'] = nrm((HYENA_FFN, HYENA_FFN), HYENA_FFN ** -0.5)
    inp['l0_filt_b2'] = nrm((HYENA_FFN,), 0.1)
    inp['l0_filt_w3'] = nrm((HYENA_FFN, HYENA_FFN), HYENA_FFN ** -0.5)
    inp['l0_filt_b3'] = nrm((HYENA_FFN,), 0.1)
    inp['l0_filt_w4'] = nrm((HYENA_FFN, 2 * HYENA_WIDTH), HYENA_FFN ** -0.5)
    inp['l0_filt_freq'] = 1.0 + nrm((HYENA_FFN,), 0.1)
    inp['l0_filt_bias'] = nrm((HYENA_WIDTH,), 1.0)
    inp['l0_w_out'] = nrm((MIX_OUT_WIDTH, D), MIX_OUT_WIDTH ** -0.5)
    inp['l0_ffn_up'] = nrm((D, 2 * D_FF), D ** -0.5)
    inp['l0_ffn_conv_w'] = nrm((3, D_FF), 3 ** -0.5)
    inp['l0_ffn_conv_b'] = nrm((D_FF,), 0.01)
    inp['l0_ffn_down'] = nrm((D_FF, D), D_FF ** -0.5)
    inp['l1_ada_w'] = nrm((D, 6 * D), 0.5 * D ** -0.5)
    inp['l1_ada_b'] = nrm((6 * D,), 0.01)
    inp['l1_norm1'] = gain((D,))
    inp['l1_norm2'] = gain((D,))
    inp['l1_mu'] = jax.random.uniform(next(keys), (6, D), jnp.float32)
    inp['l1_w_r'] = nrm((D, D), D ** -0.5)
    inp['l1_w_k'] = nrm((D, D), D ** -0.5)
    inp['l1_w_v'] = nrm((D, D), D ** -0.5)
    inp['l1_w_o'] = nrm((D, D), D ** -0.5)
    inp['l1_w0'] = jax.random.uniform(next(keys), (2, D), jnp.float32, minval=-5.0, maxval=1.0)
    inp['l1_w1'] = nrm((2, D, DECAY_LORA), D ** -0.5)
    inp['l1_w2'] = nrm((2, DECAY_LORA, D), 0.1 * DECAY_LORA ** -0.5)
    inp['l1_a0'] = nrm((2, D), 0.01)
    inp['l1_a1'] = nrm((2, D, AAA_LORA), D ** -0.5)
    inp['l1_a2'] = nrm((2, AAA_LORA, D), 0.1 * AAA_LORA ** -0.5)
    inp['l1_g1'] = nrm((D, GATE_LORA), D ** -0.5)
    inp['l1_g2'] = nrm((GATE_LORA, D), GATE_LORA ** -0.5)
    inp['l1_k_k'] = 0.85 + nrm((D,), 0.02)
    inp['l1_k_a'] = 1.0 + nrm((D,), 0.02)
    inp['l1_r_k'] = nrm((RWKV_HEADS, RWKV_HEAD), 0.1)
    inp['l1_ln_w'] = gain((D,))
    inp['l1_ln_b'] = nrm((D,), 0.01)
    inp['l1_ffn_up'] = nrm((D, 2 * D_FF), D ** -0.5)
    inp['l1_ffn_conv_w'] = nrm((3, D_FF), 3 ** -0.5)
    inp['l1_ffn_conv_b'] = nrm((D_FF,), 0.01)
    inp['l1_ffn_down'] = nrm((D_FF, D), D_FF ** -0.5)
    inp['final_norm'] = gain((D,))
    return inp


def reference(x_prompt, x_sample, c_prompt, c_sample,
              l0_ada_w, l0_ada_b, l0_norm1, l0_norm2, l0_w_in, l0_short_w, l0_short_b,
              l0_filt_w1, l0_filt_b1, l0_filt_w2, l0_filt_b2, l0_filt_w3, l0_filt_b3, l0_filt_w4,
              l0_filt_freq, l0_filt_bias, l0_w_out, l0_ffn_up, l0_ffn_conv_w, l0_ffn_conv_b, l0_ffn_down,
              l1_ada_w, l1_ada_b, l1_norm1, l1_norm2, l1_mu, l1_w_r, l1_w_k, l1_w_v, l1_w_o,
              l1_w0, l1_w1, l1_w2, l1_a0, l1_a1, l1_a2, l1_g1, l1_g2, l1_k_k, l1_k_a, l1_r_k,
              l1_ln_w, l1_ln_b, l1_ffn_up, l1_ffn_conv_w, l1_ffn_conv_b, l1_ffn_down, final_norm):
    layer0 = dict(ada_w=l0_ada_w, ada_b=l0_ada_b, norm1=l0_norm1, norm2=l0_norm2, w_in=l0_w_in,
                  short_w=l0_short_w, short_b=l0_short_b, filt_w1=l0_filt_w1, filt_b1=l0_filt_b1,
                  filt_w2=l0_filt_w2, filt_b2=l0_filt_b2, filt_w3=l0_filt_w3, filt_b3=l0_filt_b3,
                  filt_w4=l0_filt_w4, filt_freq=l0_filt_freq, filt_bias=l0_filt_bias, w_out=l0_w_out,
                  ffn_up=l0_ffn_up, ffn_conv_w=l0_ffn_conv_w, ffn_conv_b=l0_ffn_conv_b, ffn_down=l0_ffn_down)
    layer1 = dict(ada_w=l1_ada_w, ada_b=l1_ada_b, norm1=l1_norm1, norm2=l1_norm2, mu=l1_mu,
                  w_r=l1_w_r, w_k=l1_w_k, w_v=l1_w_v, w_o=l1_w_o, w0=l1_w0, w1=l1_w1, w2=l1_w2,
                  a0=l1_a0, a1=l1_a1, a2=l1_a2, g1=l1_g1, g2=l1_g2, k_k=l1_k_k, k_a=l1_k_a, r_k=l1_r_k,
                  ln_w=l1_ln_w, ln_b=l1_ln_b, ffn_up=l1_ffn_up, ffn_conv_w=l1_ffn_conv_w,
                  ffn_conv_b=l1_ffn_conv_b, ffn_down=l1_ffn_down)
    layers = [layer0, layer1]
    y_prompt = _trunk(x_prompt, c_prompt, layers, final_norm)
    y_sample = _trunk(x_sample, c_sample, layers, final_norm)
    return (y_prompt, y_sample)
```

```python
import contextlib
import math
import numpy as np
import concourse.bass as bass
import concourse.mybir as mybir
from concourse.bass_utils import run_bass_kernel_spmd

F32 = mybir.dt.float32
BF16 = mybir.dt.bfloat16
AF = mybir.ActivationFunctionType
ALU = mybir.AluOpType

D = 1024
L = 4096
NK = 8
BLK = 512
NBLK = L // BLK
NCORES = 8
NS = 3
DFF = 2816
NFC = DFF // 128
RMS_EPS = 1e-6

ENGS = ("pe", "act", "dve", "pool", "sp")
NDMASEM = 6


class Buf:
    __slots__ = ("t", "name", "w", "r", "a")

    def __init__(self, t, name):
        self.t = t
        self.name = name
        self.w = {}
        self.r = {}
        self.a = {}

    def __getitem__(self, idx):
        return self.t[idx]


class Prog:
    def __init__(self, nc):
        self.nc = nc
        self.base = contextlib.ExitStack()
        self.es = self.base
        self.streams = {e: [] for e in ENGS}
        self.cnt = {e: 0 for e in ENGS}
        self.seen = {e: {} for e in ENGS}
        self.sem = {}
        self.dtot = {}
        self.rr = {e: 0 for e in ENGS}
        for e in ENGS:
            self.sem[e] = self.base.enter_context(nc.semaphore("c_" + e))
        for q in ("sp", "act", "pool"):
            for i in range(NDMASEM):
                k = "d_%s%d" % (q, i)
                self.sem[k] = self.base.enter_context(nc.semaphore(k))
                self.dtot[k] = 0
        self.nbuf = 0
        self.ninst = 0

    @contextlib.contextmanager
    def scope(self):
        old = self.es
        es = contextlib.ExitStack()
        self.es = es
        try:
            yield
            self.barrier()
            self.emit()
        finally:
            es.close()
            self.es = old

    def sbuf(self, shape, dt, name=None):
        self.nbuf += 1
        name = (name or "sb") + "_%d" % self.nbuf
        t = self.es.enter_context(self.nc.sbuf_tensor(name, list(shape), dt))
        return Buf(t, name)

    def psum(self, shape, dt=F32, name=None):
        self.nbuf += 1
        name = (name or "ps") + "_%d" % self.nbuf
        t = self.es.enter_context(self.nc.psum_tensor(name, list(shape), dt))
        return Buf(t, name)

    def dram(self, name, shape, dt, kind="Internal"):
        t = self.nc.dram_tensor(name, list(shape), dt, kind=kind)
        return Buf(t.ap(), name)

    def _deps(self, eng, reads, writes, accs=()):
        need = {}
        seen = self.seen[eng]

        def add(k, v):
            if k == eng and eng == "pe":
                return
            if seen.get(k, 0) >= v:
                return
            if need.get(k, 0) < v:
                need[k] = v

        for b in reads:
            for k, v in b.w.items():
                add(k, v)
            for k, v in b.a.items():
                add(k, v)
        for b in writes:
            for k, v in b.w.items():
                add(k, v)
            for k, v in b.a.items():
                add(k, v)
            for k, v in b.r.items():
                add(k, v)
        for b in accs:
            for k, v in b.w.items():
                add(k, v)
            for k, v in b.r.items():
                add(k, v)
        for k, v in need.items():
            seen[k] = v
        return list(need.items())

    @staticmethod
    def _commit(ev, reads, writes, accs):
        k, v = ev
        for b in reads:
            if b.r.get(k, 0) < v:
                b.r[k] = v
        for b in writes:
            b.w.clear()
            b.w[k] = v
            b.r.clear()
            b.a.clear()
        for b in accs:
            if b.a.get(k, 0) < v:
                b.a[k] = v

    def op(self, eng, fn, reads=(), writes=(), accs=()):
        waits = self._deps(eng, reads, writes, accs)
        self.cnt[eng] += 1
        ev = (eng, self.cnt[eng])
        self.streams[eng].append((waits, fn, eng, 1))
        self._commit(ev, reads, writes, accs)
        self.ninst += 1 + len(waits)

    def dma(self, q, out, in_, reads=(), writes=(), accs=(), **kw):
        i = self.rr[q]
        self.rr[q] = (i + 1) % NDMASEM
        k = "d_%s%d" % (q, i)
        waits = self._deps(q, reads, writes, accs)
        prev = self.dtot[k]
        if prev > 0 and self.seen[q].get(k, 0) < prev:
            waits.append((k, prev))
            self.seen[q][k] = prev
        self.dtot[k] = prev + 16
        ev = (k, prev + 16)

        def fn(e, out=out, in_=in_, kw=kw):
            return e.dma_start(out=out, in_=in_, **kw)

        self.streams[q].append((waits, fn, k, 16))
        self._commit(ev, reads, writes, accs)
        self.ninst += 1 + len(waits)

    def barrier(self):
        tot = dict(self.dtot)
        for e in ENGS:
            tot[e] = self.cnt[e]
        for e in ENGS:
            waits = []
            for k, v in tot.items():
                if v > 0 and k != e and self.seen[e].get(k, 0) < v:
                    waits.append((k, v))
                    self.seen[e][k] = v
            if waits:
                self.streams[e].append((waits, None, None, 0))
                self.ninst += len(waits)

    def wait_all(self, eng, bufs):
        waits = self._deps(eng, bufs, ())
        self.streams[eng].append((waits, None, None, 0))

    def emit(self):
        if not any(self.streams.values()):
            return
        nc = self.nc
        engobj = {"pe": "tensor", "act": "scalar", "dve": "vector", "pool": "gpsimd", "sp": "sync"}
        with nc.Block() as block:
            for e in ENGS:
                stream = self.streams[e]

                def body(eng, stream=stream):
                    for waits, fn, sk, inc in stream:
                        for k, v in waits:
                            eng.wait_ge(self.sem[k], v)
                        if fn is not None:
                            fn(eng).then_inc(self.sem[sk], inc)

                getattr(block, engobj[e])(body)
        self.streams = {e: [] for e in ENGS}

    def close(self):
        self.emit()
        self.base.close()


class Rot:
    def __init__(self, bufs):
        self.bufs = bufs
        self.i = 0

    def next(self):
        b = self.bufs[self.i % len(self.bufs)]
        self.i += 1
        return b


class Ctx:
    pass


def mm(P, out_buf, out_ap, lhsT_buf, lhsT_ap, rhs_buf, rhs_ap, start, stop):
    P.op("pe", lambda e: e.matmul(out_ap, lhsT=lhsT_ap, rhs=rhs_ap, start=start, stop=stop),
         reads=[lhsT_buf, rhs_buf], writes=[out_buf])


def load_weight_bf16(P, C, dram_buf, nk, ncols, name):
    wb = P.sbuf([128, nk, ncols], BF16, name)
    grp = max(32, (C.wstage_n // nk) // 32 * 32)
    i = 0
    for c0 in range(0, ncols, grp):
        cw = min(grp, ncols - c0)
        st = C.wstage.next()
        P.dma("sp", st[:, 0:nk * cw].rearrange("p (k c) -> p k c", c=cw),
              dram_buf[:, :, c0:c0 + cw], reads=[dram_buf], writes=[st])
        if i % 2 == 0:
            P.op("dve", lambda e, st=st, c0=c0, cw=cw: e.tensor_copy(
                out=wb[:, :, c0:c0 + cw], in_=st[:, 0:nk * cw].rearrange("p (k c) -> p k c", c=cw)),
                reads=[st], accs=[wb])
        else:
            P.op("act", lambda e, st=st, c0=c0, cw=cw: e.copy(
                out=wb[:, :, c0:c0 + cw], in_=st[:, 0:nk * cw].rearrange("p (k c) -> p k c", c=cw)),
                reads=[st], accs=[wb])
        i += 1
    return wb


def norm_mod(P, C, xT, W, gain, shift, s, out_buf, out_dt_is_bf16=True):
    ssp = C.ps_ss.next()
    for k in range(NK):
        sq = C.nm_sq.next()
        P.op("act", lambda e, sq=sq, k=k: e.activation(out=sq[:, 0:W], in_=xT[:, k, 0:W], func=AF.Square),
             reads=[xT], writes=[sq])
        mm(P, ssp, ssp[:, 0:W], C.ones32, C.ones32[:], sq, sq[:, 0:W], k == 0, k == NK - 1)
    rstd = C.nm_rstd.next()
    P.op("act", lambda e: e.activation(out=rstd[:, 0:W], in_=ssp[:, 0:W], func=AF.Sqrt,
                                       scale=1.0 / D, bias=C.eps_col[:, 0:1]),
         reads=[ssp, C.eps_col], writes=[rstd])
    P.op("dve", lambda e: e.reciprocal(out=rstd[:, 0:W], in_=rstd[:, 0:W]), reads=[rstd], writes=[rstd])
    for k in range(NK):
        if shift is None:
            P.op("dve", lambda e, k=k: e.scalar_tensor_tensor(
                out=out_buf[:, k, 0:W], in0=xT[:, k, 0:W], scalar=gain[:, k, s:s + 1], in1=rstd[:, 0:W],
                op0=ALU.mult, op1=ALU.mult), reads=[xT, gain, rstd], accs=[out_buf])
        else:
            tmp = C.nm_tmp.next()
            P.op("dve", lambda e, k=k, tmp=tmp: e.scalar_tensor_tensor(
                out=tmp[:, 0:W], in0=xT[:, k, 0:W], scalar=gain[:, k, s:s + 1], in1=rstd[:, 0:W],
                op0=ALU.mult, op1=ALU.mult), reads=[xT, gain, rstd], writes=[tmp])
            P.op("act", lambda e, k=k, tmp=tmp: e.activation(
                out=out_buf[:, k, 0:W], in_=tmp[:, 0:W], func=AF.Identity, bias=shift[:, k, s:s + 1], scale=1.0),
                reads=[tmp, shift], accs=[out_buf])


def alloc_norm_scratch(P, C, W=BLK):
    C.nm_sq = Rot([P.sbuf([128, W], F32, "nmsq") for _ in range(2)])
    C.nm_rstd = Rot([P.sbuf([128, W], F32, "nmrs") for _ in range(2)])
    C.nm_tmp = Rot([P.sbuf([128, W], F32, "nmtmp") for _ in range(2)])
    C.ps_ss = Rot([P.psum([128, W], F32, "psss")])


def fence(P, buf):
    return buf


def stage_adaln(P, C):
    with P.scope():
        cT = P.sbuf([128, NK, NS], F32, "cT")
        P.dma("sp", cT[:], C.d_cT[:], reads=[C.d_cT], writes=[cT])
        sc = P.sbuf([128, NK, NS], F32, "silu_c")
        P.op("act", lambda e: e.activation(out=sc[:], in_=cT[:], func=AF.Silu), reads=[cT], writes=[sc])
        wts = Rot([P.sbuf([128, NK, 1024], F32, "adaw") for _ in range(2)])
        pss = Rot([P.psum([128, 512], F32, "adaps") for _ in range(2)])
        adab = P.sbuf([128, 2, 48], F32, "adab")
        P.dma("act", adab[:], C.d_adab[:], reads=[C.d_adab], writes=[adab])
        for l in range(2):
            for j in range(6):
                wt = wts.next()
                P.dma("sp" if j % 2 == 0 else "act", wt[:], C.d_adaw[l][:, :, j * 1024:(j + 1) * 1024],
                      reads=[C.d_adaw[l]], writes=[wt])
                psb = pss.next()
                ps = psb[:, 0:8 * NS].rearrange("p (c s) -> p c s", s=NS)
                for cc in range(8):
                    for k in range(NK):
                        mm(P, psb, ps[:, cc, :], wt, wt[:, k, cc * 128:(cc + 1) * 128], sc, sc[:, k, :],
                           k == 0, k == NK - 1)
                P.op("dve", lambda e, l=l, j=j, ps=ps: e.tensor_tensor(
                    out=C.modT[l][:, j * 8:(j + 1) * 8, :], in0=ps,
                    in1=adab[:, l, j * 8:(j + 1) * 8].unsqueeze(2).to_broadcast([128, 8, NS]), op=ALU.add),
                    reads=[psb, adab], accs=[C.modT[l]])
        nw = P.sbuf([128, 5, NK], F32, "normw")
        P.dma("act", nw[:], C.d_normw[:], reads=[C.d_normw], writes=[nw])
        for l in range(2):
            for which in range(2):
                jsc = 1 + 3 * which
                g = C.gain[l][which]
                P.op("dve", lambda e, l=l, which=which, jsc=jsc, g=g: e.scalar_tensor_tensor(
                    out=g[:], in0=C.modT[l][:, jsc * 8:(jsc + 1) * 8, :], scalar=1.0,
                    in1=nw[:, 2 * l + which, :].unsqueeze(2).to_broadcast([128, NK, NS]),
                    op0=ALU.add, op1=ALU.mult), reads=[C.modT[l], nw], writes=[g])
        P.op("dve", lambda e: e.tensor_copy(
            out=C.gain_fin[:], in_=nw[:, 4, :].unsqueeze(2).to_broadcast([128, NK, NS])),
            reads=[nw], writes=[C.gain_fin])


def mod_vec(C, l, j):
    return C.modT[l]


def stage_l0_inproj(P, C):
    with P.scope():
        C.wstage_n = 2048
        C.wstage = Rot([P.sbuf([128, 2048], F32, "wst") for _ in range(2)])
        w_in = load_weight_bf16(P, C, C.d_w_in, NK, 3072, "w_in")
        alloc_norm_scratch(P, C)
        xins = Rot([P.sbuf([128, 4, D], F32, "xin") for _ in range(2)])
        xTs = Rot([P.sbuf([128, NK, BLK], F32, "xT") for _ in range(2)])
        hTs = Rot([P.sbuf([128, NK, BLK], BF16, "hT") for _ in range(2)])
        pts = Rot([P.psum([128, BLK], F32, "pt") for _ in range(2)])
        pps = Rot([P.psum([128, BLK], F32, "pp") for _ in range(2)])
        pvs = Rot([P.psum([128, 1024], F32, "pv") for _ in range(1)])
        qks = Rot([P.sbuf([128, 12, BLK], BF16, "qk") for _ in range(1)])
        hys = Rot([P.sbuf([128, 6, BLK], F32, "hy") for _ in range(1)])
        vas = []
        for _ in range(2):
            va = P.sbuf([128, 4, 12, 65], BF16, "vaug")
            P.op("pool", lambda e, va=va: e.memset(va[:], 1.0), writes=[va])
            vas.append(va)
        vas = Rot(vas)
        mod = C.modT[0]
        ev = 0
        for s in range(NS):
            for blk in range(NBLK):
                t0 = blk * BLK
                xin = xins.next()
                P.dma("sp", xin[:], C.d_x[s, t0:t0 + BLK, :].rearrange("(j p) f -> p j f", p=128),
                      reads=[C.d_x], writes=[xin])
                xT = xTs.next()
                for k in range(NK):
                    pt = pts.next()
                    for j in range(4):
                        P.op("pe", lambda e, pt=pt, j=j, k=k, xin=xin: e.transpose(
                            pt[:, j * 128:(j + 1) * 128], xin[:, j, k * 128:(k + 1) * 128], C.ident32[:]),
                            reads=[xin, C.ident32], writes=[pt])
                    if k % 2 == 0:
                        P.op("act", lambda e, pt=pt, k=k, xT=xT: e.copy(out=xT[:, k, :], in_=pt[:]),
                             reads=[pt], accs=[xT])
                    else:
                        P.op("dve", lambda e, pt=pt, k=k, xT=xT: e.tensor_copy(out=xT[:, k, :], in_=pt[:]),
                             reads=[pt], accs=[xT])
                P.dma("sp", C.d_xT[0][s][:, :, t0:t0 + BLK], xT[:], reads=[xT], accs=[C.d_xT[0][s]])
                hT = hTs.next()
                norm_mod(P, C, xT, BLK, C.gain[0][0], mod_shift(C, 0, 0), s, hT)
                qk = qks.next()
                for c in range(12):
                    pp = pps.next()
                    for k in range(NK):
                        mm(P, pp, pp[:], w_in, w_in[:, k, c * 128:(c + 1) * 128], hT, hT[:, k, :], k == 0, k == NK - 1)
                    if c % 2 == 0:
                        P.op("act", lambda e, pp=pp, c=c, qk=qk: e.copy(out=qk[:, c, :], in_=pp[:]),
                             reads=[pp], accs=[qk])
                    else:
                        P.op("dve", lambda e, pp=pp, c=c, qk=qk: e.tensor_copy(out=qk[:, c, :], in_=pp[:]),
                             reads=[pp], accs=[qk])
                P.dma("sp", C.d_qkT[s][:, :, t0:t0 + BLK], qk[:], reads=[qk], accs=[C.d_qkT[s]])
                va = vas.next()
                for j in range(4):
                    pv = pvs.next()
                    for k in range(NK):
                        mm(P, pv, pv[:, 0:512], hT, hT[:, k, j * 128:(j + 1) * 128], w_in, w_in[:, k, 1536:2048],
                           k == 0, k == NK - 1)
                    for k in range(NK):
                        mm(P, pv, pv[:, 512:768], hT, hT[:, k, j * 128:(j + 1) * 128], w_in, w_in[:, k, 2048:2304],
                           k == 0, k == NK - 1)
                    P.op("act" if j % 2 == 0 else "dve",
                         (lambda e, pv=pv, j=j, va=va: e.copy(
                             out=va[:, j, :, 0:64], in_=pv[:, 0:768].rearrange("p (h d) -> p h d", d=64)))
                         if j % 2 == 0 else
                         (lambda e, pv=pv, j=j, va=va: e.tensor_copy(
                             out=va[:, j, :, 0:64], in_=pv[:, 0:768].rearrange("p (h d) -> p h d", d=64))),
                         reads=[pv], accs=[va])
                P.dma("sp", C.d_vaug[s][t0:t0 + BLK].rearrange("(j p) h d -> p j h d", p=128), va[:],
                      reads=[va], accs=[C.d_vaug[s]])
                hy = hys.next()
                for c in range(6):
                    pp = pps.next()
                    for k in range(NK):
                        mm(P, pp, pp[:], w_in, w_in[:, k, 2304 + c * 128:2304 + (c + 1) * 128], hT, hT[:, k, :],
                           k == 0, k == NK - 1)
                    if c % 2 == 0:
                        P.op("act", lambda e, pp=pp, c=c, hy=hy: e.copy(out=hy[:, c, :], in_=pp[:]),
                             reads=[pp], accs=[hy])
                    else:
                        P.op("dve", lambda e, pp=pp, c=c, hy=hy: e.tensor_copy(out=hy[:, c, :], in_=pp[:]),
                             reads=[pp], accs=[hy])
                P.dma("sp", C.d_hyT[s][:, :, t0:t0 + BLK], hy[:], reads=[hy], accs=[C.d_hyT[s]])


DILS = (1, 4, 16)
FINAL_SRC = 4
import os
ATT_STOP = int(os.environ.get("ATT_STOP", "9"))
ATT_GROUPS = tuple(int(x) for x in os.environ.get("ATT_GROUPS", "0,1,2").split(","))


def stage_attention(P, C):
    with P.scope():
        tabs = P.sbuf([128, 9, 4, 128], F32, "abias")
        P.dma("sp", tabs[:], C.d_abias[:], reads=[C.d_abias], writes=[tabs])
        qTs = Rot([P.sbuf([128, 2, L], BF16, "qT") for _ in range(2)])
        kTs = Rot([P.sbuf([128, 2, L], BF16, "kT") for _ in range(2)])
        vts = Rot([P.sbuf([128, 4, 65], BF16, "vt") for _ in range(4)])
        pss = Rot([P.psum([128, 512], F32, "pss") for _ in range(4)])
        pos = Rot([P.psum([128, 512], F32, "pso") for _ in range(2)])
        sbs = Rot([P.sbuf([128, 4, 128], F32, "ssb") for _ in range(2)])
        pTs = Rot([P.sbuf([128, 4, 128], BF16, "pT") for _ in range(4)])
        osb = Rot([P.sbuf([128, 4, 65], F32, "osb") for _ in range(3)])
        cnt = 0
        for s in range(NS):
            for g, d in enumerate(DILS):
                if g not in ATT_GROUPS:
                    continue
                qT = qTs.next()
                kT = kTs.next()
                P.dma("sp", qT[:], C.d_qkT[s][:, 2 * g:2 * g + 2, :], reads=[C.d_qkT[s]], writes=[qT])
                P.dma("act", kT[:], C.d_qkT[s][:, 6 + 2 * g:6 + 2 * g + 2, :], reads=[C.d_qkT[s]], writes=[kT])
                n = L // d
                ntile = n // 128
                for r in range(d):
                    def load_v(m):
                        vt = vts.next()
                        if m == 0:
                            ks, nk = 0, 64
                        elif m == ntile:
                            ks, nk = n - 64, 64
                        else:
                            ks, nk = 128 * m - 64, 128
                        t0 = ks * d + r
                        P.dma("sp", vt[0:nk], C.d_vaug[s][t0:t0 + (nk - 1) * d + 1:d, 4 * g:4 * g + 4, :],
                              reads=[C.d_vaug[s]], writes=[vt])
                        return vt, ks, nk
                    vcur = load_v(0)
                    for qt in range(ntile):
                        vnext = load_v(qt + 1)
                        i0 = qt * 128
                        q0 = i0 * d + r
                        qsl = slice(q0, q0 + 127 * d + 1, d)
                        pTl = []
                        if ATT_STOP <= 1:
                            vcur = vnext
                            continue
                        for bi, (vt, ks, nk) in enumerate((vcur, vnext)):
                            if bi == 0:
                                ti = 3 * g + (2 if qt == 0 else 0)
                            else:
                                ti = 3 * g + 1
                            k0 = ks * d + r
                            ksl = slice(k0, k0 + (nk - 1) * d + 1, d)
                            sb = sbs.next()
                            pT = pTs.next()
                            for hh in range(2):
                                ps = pss.next()
                                psv = ps[:, 0:256].rearrange("p (a q) -> p a q", q=128)
                                for hp in range(2):
                                    mm(P, ps, psv[0:nk, hp, :], kT, kT[hh * 64:(hh + 1) * 64, hp, ksl],
                                       qT, qT[hh * 64:(hh + 1) * 64, hp, qsl], True, True)
                                P.op("dve", lambda e, ps=ps, psv=psv, sb=sb, nk=nk, ti=ti, hh=hh: e.scalar_tensor_tensor(
                                    out=sb[0:nk, hh::2, :], in0=psv[0:nk], scalar=0.125, in1=tabs[0:nk, ti, hh::2, :],
                                    op0=ALU.mult, op1=ALU.add), reads=[ps, tabs], accs=[sb])
                            P.op("act", lambda e, sb=sb, pT=pT, nk=nk: e.activation(
                                out=pT[0:nk], in_=sb[0:nk], func=AF.Exp), reads=[sb], writes=[pT])
                            pTl.append((pT, vt, nk))
                        if ATT_STOP <= 2:
                            vcur = vnext
                            continue
                        po = pos.next()
                        pov = po[:, 0:260].rearrange("p (h d) -> p h d", d=65)
                        for h in range(4):
                            for bi, (pT, vt, nk) in enumerate(pTl):
                                mm(P, po, pov[:, h, :], pT, pT[0:nk, h, :], vt, vt[0:nk, h, :], bi == 0, bi == 1)
                        ob = osb.next()
                        if cnt % 2 == 0:
                            P.op("act", lambda e, ob=ob, pov=pov: e.copy(out=ob[:], in_=pov), reads=[po], writes=[ob])
                        else:
                            P.op("dve", lambda e, ob=ob, pov=pov: e.tensor_copy(out=ob[:], in_=pov),
                                 reads=[po], writes=[ob])
                        cnt += 1
                        if ATT_STOP <= 3:
                            vcur = vnext
                            continue
                        P.dma("sp", C.d_oacc[s][g, q0:q0 + 127 * d + 1:d, :, :], ob[:],
                              reads=[ob], accs=[C.d_oacc[s]])
                        vcur = vnext


HY_GRP = 32
TWO_PI = 2.0 * math.pi


def alloc_fft(P, C):
    F = Ctx()
    F.w128 = P.sbuf([64, 256], F32, "w128")
    F.tw = P.sbuf([128, 2, 128], F32, "tw")
    F.bd = P.sbuf([128, 3, 128], F32, "bd")
    F.bdc = P.sbuf([128, 2, 256], F32, "bdc")
    F.twi = P.sbuf([128, 2, 128], F32, "twi")
    F.vinv = P.sbuf([128, 2, 64], F32, "vinv")
    for b, d in ((F.w128, C.d_w128), (F.tw, C.d_tw), (F.bd, C.d_bd), (F.bdc, C.d_bdc), (F.twi, C.d_twi),
                 (F.vinv, C.d_vinv)):
        P.dma("act", b[:], d[:], reads=[d], writes=[b])
    F.psA = Rot([P.psum([128, 512], F32, "psA") for _ in range(2)])
    F.psX = Rot([P.psum([128, 512], F32, "psX") for _ in range(2)])
    F.p1 = Rot([P.sbuf([128, 2, 128], F32, "fp1") for _ in range(2)])
    F.p2 = Rot([P.sbuf([128, 2, 128], F32, "fp2") for _ in range(2)])
    F.b = Rot([P.sbuf([128, 2, 128], F32, "fb") for _ in range(2)])
    return F


def cmul(P, F, src_buf, src_v, tab_buf, tab_r, tab_i, out_buf, out_v):
    p1 = F.p1.next()
    p2 = F.p2.next()
    n = src_v.shape[0]
    P.op("dve", lambda e: e.tensor_tensor(out=p1[0:n], in0=src_v, in1=tab_r.unsqueeze(1).to_broadcast([n, 2, 128]),
                                          op=ALU.mult), reads=[src_buf, tab_buf], writes=[p1])
    P.op("dve", lambda e: e.tensor_tensor(out=p2[0:n], in0=src_v, in1=tab_i.unsqueeze(1).to_broadcast([n, 2, 128]),
                                          op=ALU.mult), reads=[src_buf, tab_buf], writes=[p2])
    P.op("dve", lambda e: e.tensor_tensor(out=out_v[:, 0, :], in0=p1[0:n, 0, :], in1=p2[0:n, 1, :], op=ALU.subtract),
         reads=[p1, p2], accs=[out_buf])
    P.op("dve", lambda e: e.tensor_tensor(out=out_v[:, 1, :], in0=p2[0:n, 0, :], in1=p1[0:n, 1, :], op=ALU.add),
         reads=[p1, p2], accs=[out_buf])


def fft_fwd_pair(P, F, xin_buf, xin_ap):
    psA = F.psA.next()
    mm(P, psA, psA[:, 0:256], xin_buf, xin_ap, F.w128, F.w128[:], True, True)
    b = F.b.next()
    cmul(P, F, psA, psA[:, 0:256].rearrange("p (c k) -> p c k", k=128), F.tw, F.tw[:, 0, :], F.tw[:, 1, :], b, b[:])
    psX = F.psX.next()
    mm(P, psX, psX[:, 0:128], F.bd, F.bd[:, 0, :], b, b[:, 0, :], True, False)
    mm(P, psX, psX[:, 0:128], F.bd, F.bd[:, 2, :], b, b[:, 1, :], False, True)
    mm(P, psX, psX[:, 128:256], F.bd, F.bd[:, 1, :], b, b[:, 0, :], True, False)
    mm(P, psX, psX[:, 128:256], F.bd, F.bd[:, 0, :], b, b[:, 1, :], False, True)
    return psX, psX[:, 0:256].rearrange("p (c k) -> p c k", k=128)


def fft_fwd_multi(P, F, inputs):
    psAs = []
    for buf, ap in inputs:
        psA = F.psA.next()
        mm(P, psA, psA[:, 0:256], buf, ap, F.w128, F.w128[:], True, True)
        psAs.append(psA)
    bs = []
    for psA in psAs:
        b = F.b.next()
        cmul(P, F, psA, psA[:, 0:256].rearrange("p (c k) -> p c k", k=128), F.tw, F.tw[:, 0, :], F.tw[:, 1, :], b, b[:])
        bs.append(b)
    outs = []
    for b in bs:
        psX = F.psX.next()
        mm(P, psX, psX[:, 0:128], F.bd, F.bd[:, 0, :], b, b[:, 0, :], True, False)
        mm(P, psX, psX[:, 0:128], F.bd, F.bd[:, 2, :], b, b[:, 1, :], False, True)
        mm(P, psX, psX[:, 128:256], F.bd, F.bd[:, 1, :], b, b[:, 0, :], True, False)
        mm(P, psX, psX[:, 128:256], F.bd, F.bd[:, 0, :], b, b[:, 1, :], False, True)
        outs.append((psX, psX[:, 0:256].rearrange("p (c k) -> p c k", k=128)))
    return outs


def wrap_pi(P, u, m, n, W):
    for _ in range(2):
        P.op("dve", lambda e: e.tensor_single_scalar(out=m[0:n, 0:W], in_=u[0:n, 0:W], scalar=math.pi, op=ALU.is_gt),
             reads=[u], writes=[m])
        P.op("dve", lambda e: e.scalar_tensor_tensor(out=u[0:n, 0:W], in0=m[0:n, 0:W], scalar=-TWO_PI, in1=u[0:n, 0:W],
                                                     op0=ALU.mult, op1=ALU.add), reads=[m, u], writes=[u])
        P.op("dve", lambda e: e.tensor_single_scalar(out=m[0:n, 0:W], in_=u[0:n, 0:W], scalar=-math.pi, op=ALU.is_lt),
             reads=[u], writes=[m])
        P.op("dve", lambda e: e.scalar_tensor_tensor(out=u[0:n, 0:W], in0=m[0:n, 0:W], scalar=TWO_PI, in1=u[0:n, 0:W],
                                                     op0=ALU.mult, op1=ALU.add), reads=[m, u], writes=[u])


def stage_hyena_filter(P, C):
    with P.scope():
        hcol = P.sbuf([64, 8], F32, "hcol")
        P.dma("sp", hcol[:, 0:4], C.d_hcol[:], reads=[C.d_hcol], writes=[hcol])
        for i in range(3):
            P.op("dve", lambda e, i=i: e.tensor_tensor(out=hcol[:, 4 + i:5 + i], in0=hcol[:, i:i + 1],
                                                       in1=hcol[:, 3:4], op=ALU.mult), reads=[hcol], writes=[hcol])
        w1 = P.sbuf([33, 64], F32, "fw1")
        w23 = P.sbuf([64, 2, 64], F32, "fw23")
        w4 = P.sbuf([64, 512], F32, "fw4")
        fbias = P.sbuf([128, 2], F32, "fbias")
        P.dma("sp", w1[:], C.d_fw1[:], reads=[C.d_fw1], writes=[w1])
        P.dma("sp", w23[:], C.d_fw23[:], reads=[C.d_fw23], writes=[w23])
        P.dma("sp", w4[:], C.d_fw4[:], reads=[C.d_fw4], writes=[w4])
        P.dma("sp", fbias[:], C.d_fbias[:], reads=[C.d_fbias], writes=[fbias])
        hT = P.sbuf([128, 4, L], F32, "filt_hT")
        pos = Rot([P.sbuf([33, BLK], F32, "pos") for _ in range(2)])
        win = Rot([P.sbuf([128, 2, BLK], F32, "win") for _ in range(2)])
        us = Rot([P.sbuf([64, BLK], F32, "fu") for _ in range(3)])
        ms = Rot([P.sbuf([64, BLK], F32, "fm") for _ in range(2)])
        pps = Rot([P.psum([128, BLK], F32, "fpp") for _ in range(2)])
        for blk in range(NBLK):
            t0 = blk * BLK
            po = pos.next()
            wn = win.next()
            P.dma("sp", po[:], C.d_posT[:, t0:t0 + BLK], reads=[C.d_posT], writes=[po])
            P.dma("act", wn[:], C.d_winT[:, :, t0:t0 + BLK], reads=[C.d_winT], writes=[wn])
            prev_buf, prev_ap, kdim = po, po[:], 33
            for layer in range(3):
                pp = pps.next()
                if layer == 0:
                    mm(P, pp, pp[0:64, :], w1, w1[:], prev_buf, prev_ap, True, True)
                else:
                    mm(P, pp, pp[0:64, :], w23, w23[:, layer - 1, :], prev_buf, prev_ap, True, True)
                u = us.next()
                m = ms.next()
                P.op("dve", lambda e, pp=pp, u=u, layer=layer: e.tensor_scalar(
                    out=u[:], in0=pp[0:64, :], scalar1=hcol[:, 3:4], scalar2=hcol[:, 4 + layer:5 + layer],
                    op0=ALU.mult, op1=ALU.add), reads=[pp, hcol], writes=[u])
                wrap_pi(P, u, m, 64, BLK)
                P.op("act", lambda e, u=u: e.activation(out=u[:], in_=u[:], func=AF.Sin), reads=[u], writes=[u])
                prev_buf, prev_ap = u, u[:]
            for c in range(4):
                pp = pps.next()
                mm(P, pp, pp[:], w4, w4[:, c * 128:(c + 1) * 128], prev_buf, prev_ap, True, True)
                P.op("dve", lambda e, pp=pp, c=c, wn=wn, t0=t0: e.tensor_tensor(
                    out=hT[:, c, t0:t0 + BLK], in0=pp[:], in1=wn[:, c % 2, :], op=ALU.mult),
                    reads=[pp, wn], accs=[hT])
        junk = P.sbuf([128, L], F32, "fjunk")
        acc = P.sbuf([128, 8], F32, "facc")
        P.op("pool", lambda e: e.memset(acc[:], 0.0), writes=[acc])
        for c in range(4):
            lo = 0 if c < 2 else 1
            P.op("act", lambda e, c=c, lo=lo: e.activation(out=junk[:, lo:L], in_=hT[:, c, lo:L], func=AF.Abs,
                                                           accum_out=acc[:, c:c + 1]),
                 reads=[hT], writes=[junk, acc])
        P.op("dve", lambda e: e.tensor_tensor(out=acc[:, 4:6], in0=acc[:, 0:2], in1=acc[:, 2:4], op=ALU.add),
             reads=[acc], writes=[acc])
        P.op("dve", lambda e: e.reciprocal(out=acc[:, 6:8], in_=acc[:, 4:6]), reads=[acc], writes=[acc])
        for c in range(4):
            P.op("dve", lambda e, c=c: e.tensor_scalar(
                out=hT[:, c, :], in0=hT[:, c, :], scalar1=acc[:, 6 + c % 2:7 + c % 2], scalar2=None, op0=ALU.mult),
                reads=[hT, acc], writes=[hT])
        for j in range(2):
            P.op("dve", lambda e, j=j: e.tensor_tensor(out=hT[:, j, 0:1], in0=hT[:, j, 0:1], in1=fbias[:, j:j + 1],
                                                       op=ALU.add), reads=[hT, fbias], writes=[hT])
            P.op("dve", lambda e, j=j: e.memset(hT[:, 2 + j, 0:1], 0.0), writes=[hT])
        P.dma("sp", C.d_filtT[:], hT[:], reads=[hT], writes=[C.d_filtT])
    with P.scope():
        F = alloc_fft(P, C)
        xf = Rot([P.sbuf([64, HY_GRP, 64], F32, "xf") for _ in range(2)])
        xb = Rot([P.sbuf([64, HY_GRP, 64], F32, "xb") for _ in range(2)])
        fo = Rot([P.sbuf([128, HY_GRP // 2, 2, 128], F32, "fo") for _ in range(2)])
        for j in range(2):
            for c0 in range(0, 128, HY_GRP):
                a = xf.next()
                b = xb.next()
                P.dma("sp", a[:], C.d_filtT[c0:c0 + HY_GRP, j, :].rearrange("c (a b) -> a c b", b=64),
                      reads=[C.d_filtT], writes=[a])
                P.dma("act", b[:], C.d_filtT[c0:c0 + HY_GRP, 2 + j, :].rearrange("c (a b) -> a c b", b=64),
                      reads=[C.d_filtT], writes=[b])
                o = fo.next()
                for i in range(HY_GRP // 2):
                    (psf, vf), (psb, vb) = fft_fwd_multi(P, F, [
                        (a, a[:, 2 * i:2 * i + 2, :].rearrange("p c n -> p (c n)")),
                        (b, b[:, 2 * i:2 * i + 2, :].rearrange("p c n -> p (c n)"))])
                    P.op("act", lambda e, o=o, i=i, vb=vb: e.copy(out=o[:, i], in_=vb), reads=[psb], accs=[o])
                    P.op("dve", lambda e, o=o, i=i, vf=vf: e.tensor_tensor(out=o[:, i, 0, :], in0=vf[:, 0, :],
                                                                           in1=o[:, i, 0, :], op=ALU.add),
                         reads=[psf, o], accs=[o])
                    P.op("dve", lambda e, o=o, i=i, vf=vf: e.tensor_tensor(out=o[:, i, 1, :], in0=vf[:, 1, :],
                                                                           in1=o[:, i, 1, :], op=ALU.subtract),
                         reads=[psf, o], accs=[o])
                pr0 = (j * 128 + c0) // 2
                P.dma("sp", C.d_F[:, pr0:pr0 + HY_GRP // 2], o[:], reads=[o], accs=[C.d_F])


def dwconv3(P, src, dst, W, wt, c):
    P.op("act", lambda e: e.activation(out=dst[:, 0:W], in_=src[:, 0:W], func=AF.Identity,
                                       scale=wt[:, c, 1:2], bias=wt[:, c, 3:4]), reads=[src, wt], writes=[dst])
    P.op("dve", lambda e: e.scalar_tensor_tensor(out=dst[:, 1:W], in0=src[:, 0:W - 1], scalar=wt[:, c, 0:1],
                                                 in1=dst[:, 1:W], op0=ALU.mult, op1=ALU.add),
         reads=[src, wt, dst], writes=[dst])
    P.op("dve", lambda e: e.scalar_tensor_tensor(out=dst[:, 0:W - 1], in0=src[:, 1:W], scalar=wt[:, c, 2:3],
                                                 in1=dst[:, 0:W - 1], op0=ALU.mult, op1=ALU.add),
         reads=[src, wt, dst], writes=[dst])


def stage_hyena(P, C):
    with P.scope():
        swt = P.sbuf([128, 6, 4], F32, "shortw")
        P.dma("sp", swt[:], C.d_shortw[:], reads=[C.d_shortw], writes=[swt])
        srcs = Rot([P.sbuf([128, L], F32, "hysrc") for _ in range(3)])
        dsts = Rot([P.sbuf([128, L], F32, "hydst") for _ in range(4)])
        for s in range(NS):
            for j in range(2):
                res = []
                for part in range(3):
                    c = 2 * part + j
                    src = srcs.next()
                    P.dma("sp" if part != 1 else "act", src[:], C.d_hyT[s][:, c, :], reads=[C.d_hyT[s]], writes=[src])
                    dst = dsts.next()
                    dwconv3(P, src, dst, L, swt, c)
                    res.append(dst)
                x0, x1, v = res
                P.op("dve", lambda e, x1=x1, v=v: e.tensor_tensor(out=v[:], in0=v[:], in1=x1[:], op=ALU.mult),
                     reads=[v, x1], writes=[v])
                P.dma("sp", C.d_zT[s][:, j, :], v[:], reads=[v], accs=[C.d_zT[s]])
                P.dma("sp", C.d_x0T[s][:, j, :], x0[:], reads=[x0], accs=[C.d_x0T[s]])
    with P.scope():
        F = alloc_fft(P, C)
        psC = Rot([P.psum([128, 512], F32, "psC") for _ in range(2)])
        psY = Rot([P.psum([128, 512], F32, "psY") for _ in range(2)])
        zin = Rot([P.sbuf([64, HY_GRP, 64], F32, "zin") for _ in range(2)])
        x0in = Rot([P.sbuf([64, HY_GRP, 64], F32, "x0in") for _ in range(2)])
        oin = Rot([P.sbuf([64, HY_GRP, 64], F32, "oin") for _ in range(2)])
        fsp = Rot([P.sbuf([128, HY_GRP // 2, 2, 128], F32, "fsp") for _ in range(2)])
        ys = Rot([P.sbuf([128, 2, 128], F32, "fy") for _ in range(2)])
        ds = Rot([P.sbuf([128, 2, 128], F32, "fd") for _ in range(2)])
        for s in range(NS):
            for j in range(2):
                for c0 in range(0, 128, HY_GRP):
                    zi = zin.next()
                    xi = x0in.next()
                    fs = fsp.next()
                    oi = oin.next()
                    P.dma("sp", zi[:], C.d_zT[s][c0:c0 + HY_GRP, j, :].rearrange("c (a b) -> a c b", b=64),
                          reads=[C.d_zT[s]], writes=[zi])
                    P.dma("act", xi[:], C.d_x0T[s][c0:c0 + HY_GRP, j, :].rearrange("c (a b) -> a c b", b=64),
                          reads=[C.d_x0T[s]], writes=[xi])
                    pr0 = (j * 128 + c0) // 2
                    P.dma("sp", fs[:], C.d_F[:, pr0:pr0 + HY_GRP // 2], reads=[C.d_F], writes=[fs])
                    for i0 in range(0, HY_GRP // 2, 2):
                        pr = (i0, i0 + 1)
                        st = {}
                        for i in pr:
                            psA = F.psA.next()
                            mm(P, psA, psA[:, 0:256], zi, zi[:, 2 * i:2 * i + 2, :].rearrange("p c n -> p (c n)"),
                               F.w128, F.w128[:], True, True)
                            st[i] = dict(psA=psA)
                        for i in pr:
                            b = F.b.next()
                            psA = st[i]["psA"]
                            cmul(P, F, psA, psA[:, 0:256].rearrange("p (c k) -> p c k", k=128), F.tw, F.tw[:, 0, :],
                                 F.tw[:, 1, :], b, b[:])
                            st[i]["b"] = b
                        for i in pr:
                            b = st[i]["b"]
                            psX = F.psX.next()
                            mm(P, psX, psX[:, 0:128], F.bd, F.bd[:, 0, :], b, b[:, 0, :], True, False)
                            mm(P, psX, psX[:, 0:128], F.bd, F.bd[:, 2, :], b, b[:, 1, :], False, True)
                            mm(P, psX, psX[:, 128:256], F.bd, F.bd[:, 1, :], b, b[:, 0, :], True, False)
                            mm(P, psX, psX[:, 128:256], F.bd, F.bd[:, 0, :], b, b[:, 1, :], False, True)
                            st[i]["psX"] = psX
                        for i in pr:
                            psX = st[i]["psX"]
                            y = ys.next()
                            cmul(P, F, psX, psX[:, 0:256].rearrange("p (c k) -> p c k", k=128), fs, fs[:, i, 0, :],
                                 fs[:, i, 1, :], y, y[:])
                            st[i]["y"] = y
                        for i in pr:
                            y = st[i]["y"]
                            pc = psC.next()
                            mm(P, pc, pc[:, 0:256], y, y[:, 0, :], F.bdc, F.bdc[:, 0, :], True, False)
                            mm(P, pc, pc[:, 0:256], y, y[:, 1, :], F.bdc, F.bdc[:, 1, :], False, True)
                            st[i]["pc"] = pc
                        for i in pr:
                            pc = st[i]["pc"]
                            dd = ds.next()
                            cmul(P, F, pc, pc[:, 0:256].rearrange("p (c k) -> p c k", k=128), F.twi, F.twi[:, 0, :],
                                 F.twi[:, 1, :], dd, dd[:])
                            st[i]["dd"] = dd
                        for i in pr:
                            dd = st[i]["dd"]
                            py = psY.next()
                            mm(P, py, py[0:64, 0:128], F.vinv, F.vinv[:, 0, :], dd, dd[:, 0, :], True, False)
                            mm(P, py, py[0:64, 0:128], F.vinv, F.vinv[:, 1, :], dd, dd[:, 1, :], False, True)
                            st[i]["py"] = py
                        for i in pr:
                            py = st[i]["py"]
                            P.op("dve", lambda e, py=py, oi=oi, xi=xi, i=i: e.tensor_tensor(
                                out=oi[:, 2 * i:2 * i + 2, :].rearrange("p c n -> p (c n)"), in0=py[0:64, 0:128],
                                in1=xi[:, 2 * i:2 * i + 2, :].rearrange("p c n -> p (c n)"), op=ALU.mult),
                                reads=[py, xi], accs=[oi])
                    P.dma("sp", C.d_hyoT[s][c0:c0 + HY_GRP, j, :].rearrange("c (a b) -> a c b", b=64), oi[:],
                          reads=[oi], accs=[C.d_hyoT[s]])


def stage_l0_outproj(P, C):
    with P.scope():
        C.wstage_n = 2048
        C.wstage = Rot([P.sbuf([128, 2048], F32, "wst") for _ in range(2)])
        w_out = load_weight_bf16(P, C, C.d_w_out, 4, D, "w_out")
        identb = P.sbuf([128, 128], BF16, "identb")
        P.op("dve", lambda e: e.tensor_copy(out=identb[:], in_=C.ident32[:]), reads=[C.ident32], writes=[identb])
        alloc_norm_scratch(P, C)
        oas = Rot([P.sbuf([128, 4, 3, 260], F32, "oa") for _ in range(2)])
        o2s = Rot([P.sbuf([128, 4, 260], F32, "o2") for _ in range(2)])
        rds = Rot([P.sbuf([128, 4, 4], F32, "rden") for _ in range(2)])
        abs_ = Rot([P.sbuf([128, 4, 4, 64], BF16, "attnb") for _ in range(2)])
        hyl = Rot([P.sbuf([128, 2, BLK], F32, "hyl") for _ in range(2)])
        mixs = Rot([P.sbuf([128, 4, BLK], BF16, "mixT") for _ in range(2)])
        xrs = Rot([P.sbuf([128, NK, BLK], F32, "xr") for _ in range(2)])
        x1s = Rot([P.sbuf([128, NK, BLK], F32, "x1T") for _ in range(2)])
        h2s = Rot([P.sbuf([128, NK, BLK], BF16, "h2T") for _ in range(2)])
        ptb = Rot([P.psum([128, BLK], BF16, "ptb") for _ in range(2)])
        pps = Rot([P.psum([128, BLK], F32, "pp") for _ in range(2)])
        g1 = C.gatev[0][0]
        for s in range(NS):
            for blk in range(NBLK):
                t0 = blk * BLK
                oa = oas.next()
                for g in range(3):
                    P.dma("sp" if g != 1 else "act", oa[:, :, g, :],
                          C.d_oacc[s][g, t0:t0 + BLK].rearrange("(j p) h d -> p j (h d)", p=128),
                          reads=[C.d_oacc[s]], accs=[oa])
                hy = hyl.next()
                P.dma("act", hy[:], C.d_hyoT[s][:, :, t0:t0 + BLK], reads=[C.d_hyoT[s]], writes=[hy])
                xr = xrs.next()
                P.dma("sp", xr[:], C.d_xT[0][s][:, :, t0:t0 + BLK], reads=[C.d_xT[0][s]], writes=[xr])
                o2 = o2s.next()
                P.op("dve", lambda e, oa=oa, o2=o2: e.tensor_tensor(out=o2[:], in0=oa[:, :, 0, :], in1=oa[:, :, 1, :],
                                                                    op=ALU.add), reads=[oa], writes=[o2])
                P.op("dve", lambda e, oa=oa, o2=o2: e.tensor_tensor(out=o2[:], in0=o2[:], in1=oa[:, :, 2, :],
                                                                    op=ALU.add), reads=[oa, o2], writes=[o2])
                o2v = o2[:].rearrange("p j (h d) -> p j h d", d=65)
                rd = rds.next()
                P.op("dve", lambda e, o2v=o2v, rd=rd: e.reciprocal(out=rd[:], in_=o2v[:, :, :, 64]),
                     reads=[o2], writes=[rd])
                ab = abs_.next()
                P.op("dve", lambda e, o2v=o2v, rd=rd, ab=ab: e.tensor_tensor(
                    out=ab[:], in0=o2v[:, :, :, 0:64], in1=rd[:].unsqueeze(3).to_broadcast([128, 4, 4, 64]),
                    op=ALU.mult), reads=[o2, rd], writes=[ab])
                mix = mixs.next()
                for c in range(2):
                    pt = ptb.next()
                    for j in range(4):
                        P.op("pe", lambda e, pt=pt, j=j, c=c, ab=ab: e.transpose(
                            pt[:, j * 128:(j + 1) * 128],
                            ab[:, j, 2 * c:2 * c + 2, :].rearrange("p h d -> p (h d)"), identb[:]),
                            reads=[ab, identb], writes=[pt])
                    P.op("act", lambda e, pt=pt, c=c, mix=mix: e.copy(out=mix[:, c, :], in_=pt[:]),
                         reads=[pt], accs=[mix])
                P.op("act", lambda e, hy=hy, mix=mix: e.copy(out=mix[:, 2:4, :], in_=hy[:]),
                     reads=[hy], accs=[mix])
                x1 = x1s.next()
                for c in range(NK):
                    pp = pps.next()
                    for k in range(4):
                        mm(P, pp, pp[:], w_out, w_out[:, k, c * 128:(c + 1) * 128], mix, mix[:, k, :], k == 0, k == 3)
                    P.op("dve", lambda e, pp=pp, c=c, x1=x1, xr=xr, s=s: e.scalar_tensor_tensor(
                        out=x1[:, c, :], in0=pp[:], scalar=g1[:, c, s:s + 1], in1=xr[:, c, :],
                        op0=ALU.mult, op1=ALU.add), reads=[pp, g1, xr], accs=[x1])
                P.dma("sp", C.d_xT[1][s][:, :, t0:t0 + BLK], x1[:], reads=[x1], accs=[C.d_xT[1][s]])
                h2 = h2s.next()
                norm_mod(P, C, x1, BLK, C.gain[0][1], C.shiftv[0][1], s, h2)
                P.dma("sp", C.d_h2T[0][s][:, :, t0:t0 + BLK], h2[:], reads=[h2], accs=[C.d_h2T[0][s]])


GELU_C = 0.044715
GELU_S = 2.0 * math.sqrt(2.0 / math.pi)


def stage_ffn(P, C, l, xin_idx, xout_idx):
    with P.scope():
        C.wstage_n = 1024
        C.wstage = Rot([P.sbuf([128, 1024], F32, "wst") for _ in range(2)])
        w_up = load_weight_bf16(P, C, C.d_ffn_up[l], NK, 2 * DFF, "w_up")
        w_dn = load_weight_bf16(P, C, C.d_ffn_dn[l], NFC, D, "w_dn")
        cw = P.sbuf([128, NFC, 4], F32, "convw")
        P.dma("sp", cw[:], C.d_ffn_cw[l][:], reads=[C.d_ffn_cw[l]], writes=[cw])
        hhs = Rot([P.sbuf([128, NK, BLK + 2], BF16, "hh") for _ in range(2)])
        actT = P.sbuf([128, NFC, BLK], BF16, "actT")
        xcs = Rot([P.sbuf([128, BLK], F32, "xc") for _ in range(2)])
        xos = Rot([P.sbuf([128, BLK], F32, "xo") for _ in range(2)])
        tmp = {n: Rot([P.sbuf([128, BLK], F32, n) for _ in range(2)]) for n in ("cv", "sq")}
        pas = Rot([P.psum([128, BLK], F32, "pa") for _ in range(2)])
        phs = Rot([P.psum([128, BLK], F32, "ph") for _ in range(1)])
        pgs = Rot([P.psum([128, BLK], F32, "pg") for _ in range(2)])
        pos = Rot([P.psum([128, BLK], F32, "po") for _ in range(2)])
        g2 = C.gatev[l][1]
        for s in range(NS):
            for blk in range(NBLK):
                t0 = blk * BLK
                hh = hhs.next()
                lo = max(t0 - 1, 0)
                hi = min(t0 + BLK + 1, L)
                c_lo = lo - (t0 - 1)
                P.dma("sp", hh[:, :, c_lo:c_lo + (hi - lo)], C.d_h2T[l][s][:, :, lo:hi], reads=[C.d_h2T[l][s]],
                      writes=[hh])
                if blk == 0:
                    P.op("pool", lambda e, hh=hh: e.memset(hh[:, :, 0:1], 0.0), accs=[hh])
                if blk == NBLK - 1:
                    P.op("pool", lambda e, hh=hh: e.memset(hh[:, :, BLK + 1:BLK + 2], 0.0), accs=[hh])
                for c in range(NFC):
                    pa = pas.next()
                    ph = phs.next()
                    pg = pgs.next()
                    for k in range(NK):
                        mm(P, pa, pa[:], w_up, w_up[:, k, c * 128:(c + 1) * 128], hh, hh[:, k, 1:BLK + 1],
                           k == 0, k == NK - 1)
                    for k in range(NK):
                        mm(P, ph, ph[:, 0:2], w_up, w_up[:, k, c * 128:(c + 1) * 128], hh, hh[:, k, 0:BLK + 2:BLK + 1],
                           k == 0, k == NK - 1)
                    for k in range(NK):
                        mm(P, pg, pg[:], w_up, w_up[:, k, DFF + c * 128:DFF + (c + 1) * 128], hh, hh[:, k, 1:BLK + 1],
                           k == 0, k == NK - 1)
                    cv = tmp["cv"].next()
                    P.op("act", lambda e, pa=pa, cv=cv, c=c: e.activation(
                        out=cv[:], in_=pa[:], func=AF.Identity, scale=cw[:, c, 1:2], bias=cw[:, c, 3:4]),
                        reads=[pa, cw], writes=[cv])
                    P.op("dve", lambda e, pa=pa, cv=cv, c=c: e.scalar_tensor_tensor(
                        out=cv[:, 1:BLK], in0=pa[:, 0:BLK - 1], scalar=cw[:, c, 0:1], in1=cv[:, 1:BLK],
                        op0=ALU.mult, op1=ALU.add), reads=[pa, cw, cv], writes=[cv])
                    P.op("dve", lambda e, pa=pa, cv=cv, c=c: e.scalar_tensor_tensor(
                        out=cv[:, 0:BLK - 1], in0=pa[:, 1:BLK], scalar=cw[:, c, 2:3], in1=cv[:, 0:BLK - 1],
                        op0=ALU.mult, op1=ALU.add), reads=[pa, cw, cv], writes=[cv])
                    P.op("dve", lambda e, ph=ph, cv=cv, c=c: e.scalar_tensor_tensor(
                        out=cv[:, 0:1], in0=ph[:, 0:1], scalar=cw[:, c, 0:1], in1=cv[:, 0:1],
                        op0=ALU.mult, op1=ALU.add), reads=[ph, cw, cv], writes=[cv])
                    P.op("dve", lambda e, ph=ph, cv=cv, c=c: e.scalar_tensor_tensor(
                        out=cv[:, BLK - 1:BLK], in0=ph[:, 1:2], scalar=cw[:, c, 2:3], in1=cv[:, BLK - 1:BLK],
                        op0=ALU.mult, op1=ALU.add), reads=[ph, cw, cv], writes=[cv])
                    tt = tmp["sq"].next()
                    P.op("act", lambda e, cv=cv, tt=tt: e.activation(out=tt[:], in_=cv[:], func=AF.Gelu_apprx_tanh),
                         reads=[cv], writes=[tt])
                    P.op("dve", lambda e, tt=tt, pg=pg, c=c: e.tensor_tensor(out=actT[:, c, :], in0=pg[:], in1=tt[:],
                                                                             op=ALU.mult),
                         reads=[tt, pg], accs=[actT])
                for c in range(NK):
                    xc = xcs.next()
                    P.dma("act", xc[:], C.d_xT[xin_idx][s][:, c, t0:t0 + BLK], reads=[C.d_xT[xin_idx][s]], writes=[xc])
                    po = pos.next()
                    for k in range(NFC):
                        mm(P, po, po[:], w_dn, w_dn[:, k, c * 128:(c + 1) * 128], actT, actT[:, k, :],
                           k == 0, k == NFC - 1)
                    xo = xos.next()
                    P.op("dve", lambda e, po=po, c=c, xo=xo, xc=xc, s=s: e.scalar_tensor_tensor(
                        out=xo[:], in0=po[:], scalar=g2[:, c, s:s + 1], in1=xc[:], op0=ALU.mult, op1=ALU.add),
                        reads=[po, g2, xc], writes=[xo])
                    P.dma("sp", C.d_xT[xout_idx][s][:, c, t0:t0 + BLK], xo[:], reads=[xo],
                          accs=[C.d_xT[xout_idx][s]])


def stage_final(P, C, xin_idx):
    with P.scope():
        alloc_norm_scratch(P, C)
        xrs = Rot([P.sbuf([128, NK, BLK], F32, "xr") for _ in range(2)])
        yTs = Rot([P.sbuf([128, NK, BLK], F32, "yT") for _ in range(2)])
        yts = Rot([P.sbuf([128, 4, D], F32, "ytok") for _ in range(2)])
        pts = Rot([P.psum([128, 1024], F32, "pt2") for _ in range(2)])
        cnt = 0
        for s in range(NS):
            for blk in range(NBLK):
                t0 = blk * BLK
                xr = xrs.next()
                P.dma("sp", xr[:], C.d_xT[xin_idx][s][:, :, t0:t0 + BLK], reads=[C.d_xT[xin_idx][s]], writes=[xr])
                yT = yTs.next()
                norm_mod(P, C, xr, BLK, C.gain_fin, None, s, yT)
                yt = yts.next()
                for j in range(4):
                    pt = pts.next()
                    for c in range(NK):
                        P.op("pe", lambda e, pt=pt, j=j, c=c, yT=yT: e.transpose(
                            pt[:, c * 128:(c + 1) * 128], yT[:, c, j * 128:(j + 1) * 128], C.ident32[:]),
                            reads=[yT, C.ident32], writes=[pt])
                    if cnt % 2 == 0:
                        P.op("act", lambda e, pt=pt, yt=yt, j=j: e.copy(out=yt[:, j, :], in_=pt[:]),
                             reads=[pt], accs=[yt])
                    else:
                        P.op("dve", lambda e, pt=pt, yt=yt, j=j: e.tensor_copy(out=yt[:, j, :], in_=pt[:]),
                             reads=[pt], accs=[yt])
                    cnt += 1
                P.dma("sp", C.d_y[s, t0:t0 + BLK, :].rearrange("(j p) f -> p j f", p=128), yt[:],
                      reads=[yt], accs=[C.d_y])


DECAY_C = -math.exp(-0.5)
R1_STOP = int(os.environ.get("R1_STOP", "9"))
R1_SUB = int(os.environ.get("R1_SUB", "99"))


def stage_rwkv_norm(P, C):
    with P.scope():
        alloc_norm_scratch(P, C)
        xrs = Rot([P.sbuf([128, NK, BLK], F32, "xr") for _ in range(2)])
        hs = Rot([P.sbuf([128, NK, BLK], F32, "h1") for _ in range(2)])
        for s in range(NS):
            for blk in range(NBLK):
                t0 = blk * BLK
                xr = xrs.next()
                P.dma("sp", xr[:], C.d_xT[2][s][:, :, t0:t0 + BLK], reads=[C.d_xT[2][s]], writes=[xr])
                h = hs.next()
                norm_mod(P, C, xr, BLK, C.gain[1][0], C.shiftv[1][0], s, h)
                P.dma("sp", C.d_h1T[s][:, :, t0:t0 + BLK], h[:], reads=[h], accs=[C.d_h1T[s]])


def load_weight_pair(P, C, dram_buf, nk, c_lo, ncols, name, cols, mu_i):
    wb = P.sbuf([128, nk, ncols], BF16, name)
    ws = P.sbuf([128, nk, ncols], BF16, name + "s")
    grp = max(32, (C.wstage_n // nk) // 32 * 32)
    for c0 in range(0, ncols, grp):
        cw = min(grp, ncols - c0)
        st = C.wstage.next()
        stv = st[:, 0:nk * cw].rearrange("p (k c) -> p k c", c=cw)
        P.dma("sp", stv, dram_buf[:, :, c_lo + c0:c_lo + c0 + cw], reads=[dram_buf], writes=[st])
        P.op("act", lambda e, stv=stv, c0=c0, cw=cw: e.copy(out=wb[:, :, c0:c0 + cw], in_=stv), reads=[st], accs=[wb])
        P.op("dve", lambda e, stv=stv, c0=c0, cw=cw: e.tensor_tensor(
            out=ws[:, :, c0:c0 + cw], in0=stv, in1=cols[:, mu_i, :].unsqueeze(2).to_broadcast([128, nk, cw]),
            op=ALU.mult), reads=[st, cols], accs=[ws])
    return wb, ws


def stage_rwkv_proj(P, C):
    with P.scope():
        C.wstage_n = 512
        C.wstage = Rot([P.sbuf([128, 512], F32, "wst") for _ in range(2)])
        cols = P.sbuf([128, 14, NK], F32, "rwcols")
        P.dma("sp", cols[:], C.d_rwcols[:], reads=[C.d_rwcols], writes=[cols])
        w_r = load_weight_pair(P, C, C.d_w_r, NK, 0, D, "w_r", cols, 0)
        w_k = load_weight_pair(P, C, C.d_w_k, NK, 0, D, "w_k", cols, 2)
        w_v = load_weight_pair(P, C, C.d_w_v, NK, 0, D, "w_v", cols, 3)
        w_g1 = load_weight_pair(P, C, C.d_g1, NK, 0, 256, "w_g1", cols, 5)
        w_l1w = load_weight_pair(P, C, C.d_lora1, NK, 0, 128, "w_l1w", cols, 1)
        w_l1a = load_weight_pair(P, C, C.d_lora1, NK, 128, 128, "w_l1a", cols, 4)
        w_g2 = load_weight_bf16(P, C, C.d_g2, 2, D, "w_g2")
        w_l2 = load_weight_bf16(P, C, C.d_lora2, 4, D, "w_l2")
        bones = P.sbuf([128, 128], F32, "bones")
        P.dma("sp", bones[:], C.d_bones[:], reads=[C.d_bones], writes=[bones])
        hls = Rot([P.sbuf([128, NK, BLK + 2], F32, "hl") for _ in range(1)])
        xxh = P.sbuf([128, 4, BLK], F32, "xxh")
        hb = P.sbuf([128, NK, BLK], BF16, "hb")
        xb = P.sbuf([128, NK, BLK], BF16, "xb")
        pps = Rot([P.psum([128, BLK], F32, "pp") for _ in range(3)])
        pvs = Rot([P.psum([128, 1024], F32, "pv") for _ in range(1)])
        pls = Rot([P.psum([128, BLK], F32, "pl") for _ in range(3)])
        ob16 = Rot([P.sbuf([128, BLK], BF16, "ob16") for _ in range(6)])
        of32 = Rot([P.sbuf([128, BLK], F32, "of32") for _ in range(4)])
        kf = Rot([P.sbuf([128, BLK], F32, "kf") for _ in range(3)])
        kkf = Rot([P.sbuf([128, BLK], F32, "kkf") for _ in range(2)])
        vts = Rot([P.sbuf([128, D], BF16, "vtok") for _ in range(2)])
        sgs = Rot([P.sbuf([128, 2, BLK], BF16, "sg") for _ in range(1)])
        lts = Rot([P.sbuf([64, 4, BLK], BF16, "lt") for _ in range(1)])

        def evac(ps_ap, ps_buf, dst_ap, dst_buf):
            P.op("act", lambda e: e.copy(out=dst_ap, in_=ps_ap), reads=[ps_buf], writes=[dst_buf])

        def proj(ps_buf, ps_ap, wpair, csl):
            wb, ws = wpair
            for k in range(NK):
                mm(P, ps_buf, ps_ap, wb, wb[:, k, csl], hb, hb[:, k, :], k == 0, False)
            for k in range(NK):
                mm(P, ps_buf, ps_ap, ws, ws[:, k, csl], xb, xb[:, k, :], False, k == NK - 1)

        for s in range(NS):
            for blk in range(NBLK):
                t0 = blk * BLK
                hl = hls.next()
                lo = max(t0 - 1, 0)
                hi = min(t0 + BLK + 1, L)
                c_lo = lo - (t0 - 1)
                P.dma("sp", hl[:, :, c_lo:c_lo + (hi - lo)], C.d_h1T[s][:, :, lo:hi], reads=[C.d_h1T[s]], writes=[hl])
                if blk == 0:
                    P.op("dve", lambda e, hl=hl: e.memset(hl[:, :, 0:1], 0.0), accs=[hl])
                if blk == NBLK - 1:
                    P.op("dve", lambda e, hl=hl: e.memset(hl[:, :, BLK + 1:BLK + 2], 0.0), accs=[hl])
                P.op("act", lambda e, hl=hl: e.copy(out=hb[:], in_=hl[:, :, 1:BLK + 1]), reads=[hl], writes=[hb])
                for half in range(2):
                    ks = slice(4 * half, 4 * half + 4)
                    P.op("dve", lambda e, hl=hl, ks=ks: e.tensor_tensor(
                        out=xxh[:], in0=hl[:, ks, 0:BLK], in1=hl[:, ks, 2:BLK + 2], op=ALU.add),
                        reads=[hl], writes=[xxh])
                    P.op("dve", lambda e, hl=hl, ks=ks: e.scalar_tensor_tensor(
                        out=xb[:, ks, :], in0=xxh[:], scalar=0.5, in1=hl[:, ks, 1:BLK + 1], op0=ALU.mult,
                        op1=ALU.subtract), reads=[hl, xxh], accs=[xb])
                if R1_STOP <= 1:
                    continue
                for j in range(4):
                    pv = pvs.next()
                    jsl = slice(j * 128, (j + 1) * 128)
                    for half in range(2):
                        hsl = slice(half * 512, (half + 1) * 512)
                        for k in range(NK):
                            mm(P, pv, pv[:, hsl], hb, hb[:, k, jsl], w_v[0], w_v[0][:, k, hsl], k == 0, False)
                        for k in range(NK):
                            mm(P, pv, pv[:, hsl], xb, xb[:, k, jsl], w_v[1], w_v[1][:, k, hsl], False, k == NK - 1)
                    vt = vts.next()
                    evac(pv[:], pv, vt[:], vt)
                    P.dma("sp", C.d_vtok[s][t0 + j * 128:t0 + (j + 1) * 128, :], vt[:], reads=[vt],
                          accs=[C.d_vtok[s]])
                if R1_STOP <= 2:
                    continue
                sg = sgs.next()
                for cc in range(2):
                    pl = pls.next()
                    proj(pl, pl[:], w_g1, slice(cc * 128, (cc + 1) * 128))
                    P.op("act", lambda e, pl=pl, cc=cc, sg=sg: e.activation(
                        out=sg[:, cc, :], in_=pl[:], func=AF.Sigmoid), reads=[pl], accs=[sg])
                lt = lts.next()
                for q in range(4):
                    pl = pls.next()
                    proj(pl, pl[0:64, :], w_l1w if q < 2 else w_l1a, slice((q % 2) * 64, (q % 2) * 64 + 64))
                    if q < 2:
                        P.op("act", lambda e, pl=pl, q=q, lt=lt: e.activation(out=lt[:, q, :], in_=pl[0:64, :],
                                                                              func=AF.Tanh), reads=[pl], accs=[lt])
                    else:
                        P.op("act", lambda e, pl=pl, q=q, lt=lt: e.copy(out=lt[:, q, :], in_=pl[0:64, :]),
                             reads=[pl], accs=[lt])
                if R1_STOP <= 3:
                    continue
                def phase_a(c):
                    csl = slice(c * 128, (c + 1) * 128)
                    pp = pps.next()
                    proj(pp, pp[:], w_r, csl)
                    o = ob16.next()
                    evac(pp[:], pp, o[:], o)
                    P.dma("sp", C.d_rT[s][:, c, t0:t0 + BLK], o[:], reads=[o], accs=[C.d_rT[s]])
                    pp = pps.next()
                    proj(pp, pp[:], w_v, csl)
                    o = ob16.next()
                    evac(pp[:], pp, o[:], o)
                    P.dma("sp", C.d_vT[s][:, c, t0:t0 + BLK], o[:], reads=[o], accs=[C.d_vT[s]])
                    pp = pps.next()
                    mm(P, pp, pp[:], w_g2, w_g2[:, 0, csl], sg, sg[:, 0, :], True, False)
                    mm(P, pp, pp[:], w_g2, w_g2[:, 1, csl], sg, sg[:, 1, :], False, True)
                    o = ob16.next()
                    evac(pp[:], pp, o[:], o)
                    P.dma("sp", C.d_gT[s][:, c, t0:t0 + BLK], o[:], reads=[o], accs=[C.d_gT[s]])
                    pp = pps.next()
                    proj(pp, pp[:], w_k, csl)
                    kk_ = kf.next()
                    P.op("act", lambda e, pp=pp, kk_=kk_: e.copy(out=kk_[:], in_=pp[:]), reads=[pp], writes=[kk_])
                    return kk_

                def phase_b(c, kk_):
                    csl = slice(c * 128, (c + 1) * 128)
                    kq = kkf.next()
                    P.op("dve", lambda e, kk_=kk_, kq=kq, c=c: e.tensor_scalar(
                        out=kq[:], in0=kk_[:], scalar1=cols[:, 10, c:c + 1], scalar2=None, op0=ALU.mult),
                        reads=[kk_, cols], writes=[kq])
                    sq = of32.next()
                    P.op("act", lambda e, kq=kq, sq=sq: e.activation(out=sq[:], in_=kq[:], func=AF.Square),
                         reads=[kq], writes=[sq])
                    pl = pls.next()
                    mm(P, pl, pl[:], bones, bones[:], sq, sq[:], True, True)
                    rn = of32.next()
                    P.op("act", lambda e, pl=pl, rn=rn: e.activation(out=rn[:], in_=pl[:], func=AF.Sqrt),
                         reads=[pl], writes=[rn])
                    P.op("dve", lambda e, rn=rn: e.tensor_scalar_max(out=rn[:], in0=rn[:], scalar1=1e-12),
                         reads=[rn], writes=[rn])
                    P.op("dve", lambda e, rn=rn: e.reciprocal(out=rn[:], in_=rn[:]), reads=[rn], writes=[rn])
                    P.op("dve", lambda e, kq=kq, rn=rn: e.tensor_tensor(out=kq[:], in0=kq[:], in1=rn[:], op=ALU.mult),
                         reads=[kq, rn], writes=[kq])
                    o = ob16.next()
                    P.op("act", lambda e, kq=kq, o=o: e.copy(out=o[:], in_=kq[:]), reads=[kq], writes=[o])
                    P.dma("sp", C.d_kkT[s][:, c, t0:t0 + BLK], o[:], reads=[o], accs=[C.d_kkT[s]])
                    for dd in range(2):
                        pl = pls.next()
                        mm(P, pl, pl[:], w_l2, w_l2[0:64, dd, csl], lt, lt[:, dd, :], True, True)
                        lw = of32.next()
                        P.op("act", lambda e, pl=pl, lw=lw, dd=dd, c=c: e.activation(
                            out=lw[:], in_=pl[:], func=AF.Sigmoid, bias=cols[:, 6 + dd, c:c + 1], scale=1.0),
                            reads=[pl, cols], writes=[lw])
                        P.dma("sp", C.d_lwT[dd][s][:, c, t0:t0 + BLK], lw[:], reads=[lw], accs=[C.d_lwT[dd][s]])
                        pl = pls.next()
                        mm(P, pl, pl[:], w_l2, w_l2[0:64, 2 + dd, csl], lt, lt[:, 2 + dd, :], True, True)
                        aa = of32.next()
                        P.op("act", lambda e, pl=pl, aa=aa, dd=dd, c=c: e.activation(
                            out=aa[:], in_=pl[:], func=AF.Sigmoid, bias=cols[:, 8 + dd, c:c + 1], scale=1.0),
                            reads=[pl, cols], writes=[aa])
                        o = ob16.next()
                        P.op("dve", lambda e, kq=kq, aa=aa, o=o: e.tensor_tensor(out=o[:], in0=kq[:], in1=aa[:],
                                                                                 op=ALU.mult),
                             reads=[kq, aa], writes=[o])
                        P.dma("sp", C.d_bT[dd][s][:, c, t0:t0 + BLK], o[:], reads=[o], accs=[C.d_bT[dd][s]])
                        P.op("dve", lambda e, aa=aa, c=c: e.tensor_scalar(
                            out=aa[:], in0=aa[:], scalar1=-1.0, scalar2=cols[:, 11, c:c + 1], op0=ALU.add,
                            op1=ALU.mult), reads=[aa, cols], writes=[aa])
                        o = ob16.next()
                        P.op("dve", lambda e, aa=aa, kk_=kk_, o=o: e.scalar_tensor_tensor(
                            out=o[:], in0=aa[:], scalar=1.0, in1=kk_[:], op0=ALU.add, op1=ALU.mult),
                            reads=[aa, kk_], writes=[o])
                        P.dma("sp", C.d_kdT[dd][s][:, c, t0:t0 + BLK], o[:], reads=[o], accs=[C.d_kdT[dd][s]])

                prev = None
                for c in range(NK):
                    kk_c = phase_a(c)
                    if prev is not None:
                        phase_b(*prev)
                    prev = (c, kk_c)
                phase_b(*prev)


SBLK = 128
NSB = L // SBLK
CPB = SBLK // 64
SC_LIMIT = int(os.environ.get("SC_LIMIT", "999"))


def stage_rwkv_scan(P, C, dd):
    with P.scope():
        NF = 16 * SBLK
        msk = P.sbuf([64, 192], F32, "scmask")
        P.dma("sp", msk[:], C.d_scanmask[:, dd, :], reads=[C.d_scanmask], writes=[msk])
        rmask = P.sbuf([64, NF], F32, "rmask")
        P.dma("act", rmask[:], C.d_rmask[:, 0:NF], reads=[C.d_rmask], writes=[rmask])
        idb = P.sbuf([64, 64], BF16, "idb")
        P.op("dve", lambda e: e.tensor_copy(out=idb[:], in_=C.ident32[0:64, 0:64]), reads=[C.ident32], writes=[idb])
        Rr = P.sbuf([64, 16, SBLK], BF16, "scR")
        KD = P.sbuf([64, 16, SBLK], BF16, "scKD")
        Bb = P.sbuf([64, 16, SBLK], BF16, "scB")
        KK = P.sbuf([64, 16, SBLK], BF16, "scKK")
        fA = P.sbuf([64, 16, SBLK], F32, "scA")
        fB = P.sbuf([64, 16, SBLK], F32, "scBf")
        fC = P.sbuf([64, 16, SBLK], F32, "scC")
        BS = []
        for _ in range(2):
            b_ = Ctx()
            b_.AR = P.sbuf([64, 16, CPB, 128], BF16, "scAR")
            b_.KT = P.sbuf([64, 16, SBLK], BF16, "scKT")
            b_.BT = P.sbuf([64, 16, SBLK], BF16, "scBT")
            b_.Vt = P.sbuf([64, CPB, D], BF16, "scV")
            b_.Yo = P.sbuf([64, CPB, D], F32, "scY")
            b_.PC = P.sbuf([64, 16, CPB], F32, "scPC")
            BS.append(b_)
        Sf = P.sbuf([64, 16, 64], F32, "scSf")
        Sb = P.sbuf([64, 16, 64], BF16, "scSb")
        MNk = [[P.sbuf([64, 4, 128], BF16, "MNk") for _ in range(4)] for _ in range(2)]
        MNb = [[P.sbuf([64, 4, 128], BF16, "MNb") for _ in range(4)] for _ in range(2)]
        NT0 = [[P.sbuf([64, 4, 64], BF16, "NT0") for _ in range(4)] for _ in range(2)]
        KTt = [[P.sbuf([64, 4, 2, 64], BF16, "KTt") for _ in range(4)] for _ in range(2)]
        Nl = [[[P.sbuf([64, 4, 2, 64], BF16, "Nl") for _ in range(5)] for _ in range(4)] for _ in range(2)]
        Xb = [P.sbuf([64, 4, 64], BF16, "Xb") for _ in range(4)]
        tmpS = [P.sbuf([64, 4, 64], F32, "tmpS") for _ in range(2)]
        psMNk = P.psum([64, 512], F32, "psMNk")
        psMNb = P.psum([64, 512], F32, "psMNb")
        psN = Rot([P.psum([64, 512], F32, "psN") for _ in range(2)])
        psXb = [P.psum([64, 512], F32, "psX") for _ in range(2)]
        psYS = P.psum([64, 512], F32, "psYS")
        psT = P.psum([64, 512], F32, "psT")
        v4 = lambda b, w: b[:, 0:4 * w].rearrange("p (h t) -> p h t", t=w)
        v42 = lambda b: b[:, 0:512].rearrange("p (h a t) -> p h a t", a=2, t=64)
        xview = lambda g: psXb[g // 2][:, (g % 2) * 256:(g % 2) * 256 + 256].rearrange("p (h t) -> p h t", t=64)
        ecnt = [0]

        def cast(src_buf, src_ap, dst_buf, dst_ap):
            ecnt[0] += 1
            if ecnt[0] % 4 != 0:
                P.op("act", lambda e: e.copy(out=dst_ap, in_=src_ap), reads=[src_buf], writes=[dst_buf])
            else:
                P.op("dve", lambda e: e.tensor_copy(out=dst_ap, in_=src_ap), reads=[src_buf], writes=[dst_buf])

        def prep(s, blk, bs):
            t0 = blk * SBLK
            tsl_all = slice(t0, t0 + SBLK)
            for h2 in range(2):
                prt = slice(h2 * 64, (h2 + 1) * 64)
                P.dma("sp", fA[:, h2::2, :], C.d_lwT[dd][s][prt, :, tsl_all], reads=[C.d_lwT[dd][s]], accs=[fA])
                P.dma("sp", KK[:, h2::2, :], C.d_kkT[s][prt, :, tsl_all], reads=[C.d_kkT[s]], accs=[KK])
                P.dma("sp", Rr[:, h2::2, :], C.d_rT[s][prt, :, tsl_all], reads=[C.d_rT[s]], accs=[Rr])
                P.dma("sp", KD[:, h2::2, :], C.d_kdT[dd][s][prt, :, tsl_all], reads=[C.d_kdT[dd][s]], accs=[KD])
                P.dma("sp", Bb[:, h2::2, :], C.d_bT[dd][s][prt, :, tsl_all], reads=[C.d_bT[dd][s]], accs=[Bb])
            P.dma("sp", bs.Vt[:], C.d_vtok[s][tsl_all, :].rearrange("(c i) f -> i c f", i=64),
                  reads=[C.d_vtok[s]], writes=[bs.Vt])
            fl = lambda b: b[:].rearrange("p h t -> p (h t)")
            c4 = lambda b: b[:].rearrange("p h (c t) -> p (h c) t", t=64)
            c5 = lambda b: b[:].rearrange("p h (c t) -> p h c t", t=64)
            P.op("dve", lambda e: e.tensor_tensor_scan(out=fl(fB), data0=rmask[:], data1=fl(fA), initial=0.0,
                                                       op0=ALU.mult, op1=ALU.add), reads=[rmask, fA], writes=[fB])
            if dd == 1:
                P.op("dve", lambda e: e.tensor_tensor(out=fl(fC), in0=fl(fA), in1=fl(fB), op=ALU.subtract),
                     reads=[fA, fB], writes=[fC])
                P.op("dve", lambda e: e.tensor_tensor(
                    out=c4(fB), in0=c4(fC), in1=c4(fB)[:, :, 63:64].to_broadcast([64, 16 * CPB, 64]), op=ALU.add),
                    reads=[fC, fB], writes=[fB])
            P.op("dve", lambda e: e.tensor_tensor(out=fl(fA), in0=fl(fB), in1=fl(fA), op=ALU.subtract),
                 reads=[fA, fB], writes=[fA])
            P.op("act", lambda e: e.activation(out=fl(fA), in_=fl(fA), func=AF.Exp, scale=DECAY_C),
                 reads=[fA], writes=[fA])
            P.op("act", lambda e: e.activation(out=fl(fC), in_=fl(fB), func=AF.Exp, scale=DECAY_C),
                 reads=[fB], writes=[fC])
            P.op("act", lambda e: e.activation(out=fl(fB), in_=fl(fB), func=AF.Exp, scale=-DECAY_C),
                 reads=[fB], writes=[fB])
            pcol = 63 if dd == 0 else 0
            P.op("act", lambda e: e.copy(out=bs.PC[:], in_=c5(fC)[:, :, :, pcol]), reads=[fC], writes=[bs.PC])
            P.op("dve", lambda e: e.scalar_tensor_tensor(out=bs.AR[:, :, :, 0:64], in0=c5(KK), scalar=-1.0,
                                                         in1=c5(fA), op0=ALU.mult, op1=ALU.mult),
                 reads=[KK, fA], accs=[bs.AR])
            P.op("dve", lambda e: e.tensor_tensor(out=bs.AR[:, :, :, 64:128], in0=c5(Rr), in1=c5(fC), op=ALU.mult),
                 reads=[Rr, fC], accs=[bs.AR])
            P.op("dve", lambda e: e.tensor_tensor(out=bs.KT[:], in0=KD[:], in1=fB[:], op=ALU.mult),
                 reads=[KD, fB], writes=[bs.KT])
            P.op("dve", lambda e: e.tensor_tensor(out=bs.BT[:], in0=Bb[:], in1=fB[:], op=ALU.mult),
                 reads=[Bb, fB], writes=[bs.BT])

        def s1(bs, c, par):
            tsl = slice(c * 64, (c + 1) * 64)
            AR, KT, BT = bs.AR, bs.KT, bs.BT
            for g in range(4):
                for hi in range(4):
                    h = 4 * g + hi
                    mm(P, psMNk, v4(psMNk, 128)[:, hi, :], KT, KT[:, h, tsl], AR, AR[:, h, c, :], True, True)
                P.op("dve", lambda e, g=g: e.tensor_tensor(
                    out=MNk[par][g][:], in0=v4(psMNk, 128), in1=msk[:, 0:128].unsqueeze(1).to_broadcast([64, 4, 128]),
                    op=ALU.mult), reads=[psMNk, msk], writes=[MNk[par][g]])
                for hi in range(4):
                    h = 4 * g + hi
                    mm(P, psMNb, v4(psMNb, 128)[:, hi, :], BT, BT[:, h, tsl], AR, AR[:, h, c, :], True, True)
                P.op("dve", lambda e, g=g: e.tensor_tensor(
                    out=MNb[par][g][:], in0=v4(psMNb, 128), in1=msk[:, 0:128].unsqueeze(1).to_broadcast([64, 4, 128]),
                    op=ALU.mult), reads=[psMNb, msk], writes=[MNb[par][g]])
                pn = psN.next()
                for hi in range(4):
                    h = 4 * g + hi
                    mm(P, pn, v4(pn, 64)[:, hi, :], AR, AR[:, h, c, 0:64], BT, BT[:, h, tsl], True, True)
                P.op("dve", lambda e, g=g, pn=pn: e.tensor_tensor(
                    out=NT0[par][g][:], in0=v4(pn, 64), in1=msk[:, 128:192].unsqueeze(1).to_broadcast([64, 4, 64]),
                    op=ALU.mult), reads=[pn, msk], writes=[NT0[par][g]])
                for hi in range(4):
                    h = 4 * g + hi
                    mm(P, psT, v42(psT)[:, hi, 0, :], KT, KT[:, h, tsl], idb, idb[:], True, True)
                    mm(P, psT, v42(psT)[:, hi, 1, :], BT, BT[:, h, tsl], idb, idb[:], True, True)
                P.op("act", lambda e, g=g: e.copy(out=KTt[par][g][:], in_=v42(psT)), reads=[psT],
                     writes=[KTt[par][g]])

        def level_ops(par, g, j):
            if j == 0:
                return (MNb[par][g], (lambda hi: MNb[par][g][:, hi, 0:64]), NT0[par][g], (lambda hi: NT0[par][g][:, hi, :]))
            t = Nl[par][g][j - 1]
            return (t, (lambda hi: t[:, hi, 0, :]), t, (lambda hi: t[:, hi, 1, :]))

        def square(par, j):
            for g in range(4):
                nbuf, nap, tbuf, tap = level_ops(par, g, j)
                pn = psN.next()
                for hi in range(4):
                    mm(P, pn, v42(pn)[:, hi, 0, :], tbuf, tap(hi), nbuf, nap(hi), True, True)
                    if j < 4:
                        mm(P, pn, v42(pn)[:, hi, 1, :], nbuf, nap(hi), tbuf, tap(hi), True, True)
                dst = Nl[par][g][j]
                if j < 4:
                    cast(pn, v42(pn), dst, dst[:])
                else:
                    cast(pn, v42(pn)[:, :, 0, :], dst, dst[:, :, 0, :])

        def g_step(bs, c, par):
            AR, Vt = bs.AR, bs.Vt
            for g in range(4):
                xv = xview(g)
                pb_ = psXb[g // 2]
                for hi in range(4):
                    h = 4 * g + hi
                    first = (g % 2 == 0 and hi == 0)
                    P.op("pe", lambda e, xv=xv, hi=hi, h=h, first=first: e.matmul(
                        xv[:, hi, :], lhsT=AR[:, h, c, 0:64], rhs=Sb[:, h, :], start=first, stop=False,
                        skip_group_check=True), reads=[AR, Sb], writes=[pb_])
                    P.op("pe", lambda e, xv=xv, hi=hi, h=h, g=g: e.matmul(
                        xv[:, hi, :], lhsT=MNk[par][g][:, hi, 0:64], rhs=Vt[:, c, h * 64:(h + 1) * 64], start=False,
                        stop=False, skip_group_check=True), reads=[MNk[par][g], Vt], writes=[pb_])
                cast(pb_, xv, Xb[g], Xb[g][:])

        def apply(par, j):
            for g in range(4):
                xv = xview(g)
                pb_ = psXb[g // 2]
                nbuf, nap, _, _ = level_ops(par, g, j)
                for hi in range(4):
                    P.op("pe", lambda e, xv=xv, hi=hi, g=g, nap=nap: e.matmul(
                        xv[:, hi, :], lhsT=nap(hi), rhs=Xb[g][:, hi, :], start=False, stop=(j == 5),
                        skip_group_check=True), reads=[nbuf, Xb[g]], writes=[pb_])
                cast(pb_, xv, Xb[g], Xb[g][:])

        def ys_step(bs, c, par):
            AR, Vt, Yo = bs.AR, bs.Vt, bs.Yo
            for g in range(4):
                pys = v42(psYS)
                for hi in range(4):
                    h = 4 * g + hi
                    vv = Vt[:, c, h * 64:(h + 1) * 64]
                    mm(P, psYS, pys[:, hi, 0, :], AR, AR[:, h, c, 64:128], Sb, Sb[:, h, :], True, False)
                    mm(P, psYS, pys[:, hi, 0, :], MNk[par][g], MNk[par][g][:, hi, 64:128], Vt, vv, False, False)
                    mm(P, psYS, pys[:, hi, 0, :], MNb[par][g], MNb[par][g][:, hi, 64:128], Xb[g], Xb[g][:, hi, :],
                       False, True)
                    mm(P, psYS, pys[:, hi, 1, :], KTt[par][g], KTt[par][g][:, hi, 0, :], Vt, vv, True, False)
                    mm(P, psYS, pys[:, hi, 1, :], KTt[par][g], KTt[par][g][:, hi, 1, :], Xb[g], Xb[g][:, hi, :],
                       False, True)
                ts_ = tmpS[g % 2]
                P.op("dve", lambda e, g=g, pys=pys, ts_=ts_: e.tensor_tensor(
                    out=ts_[:], in0=pys[:, :, 1, :], in1=Sf[:, 4 * g:4 * g + 4, :], op=ALU.add),
                    reads=[psYS, Sf], writes=[ts_])
                pcb = bs.PC[:, 4 * g:4 * g + 4, c:c + 1].to_broadcast([64, 4, 64])
                P.op("dve", lambda e, g=g, ts_=ts_, pcb=pcb: e.tensor_tensor(
                    out=Sb[:, 4 * g:4 * g + 4, :], in0=ts_[:], in1=pcb, op=ALU.mult),
                    reads=[ts_, bs.PC], accs=[Sb])
                P.op("dve", lambda e, g=g, pys=pys, c=c: e.tensor_copy(
                    out=Yo[:, c, g * 256:(g + 1) * 256].rearrange("p (h v) -> p h v", v=64), in_=pys[:, :, 0, :]),
                    reads=[psYS], accs=[Yo])
                P.op("dve", lambda e, g=g, ts_=ts_, pcb=pcb: e.tensor_tensor(
                    out=Sf[:, 4 * g:4 * g + 4, :], in0=ts_[:], in1=pcb, op=ALU.mult),
                    reads=[ts_, bs.PC], accs=[Sf])

        for s in range(NS):
            P.op("dve", lambda e: e.memset(Sf[:], 0.0), writes=[Sf])
            P.op("dve", lambda e: e.memset(Sb[:], 0.0), writes=[Sb])
            blocks = list(range(NSB) if dd == 0 else range(NSB - 1, -1, -1))
            seq = []
            for bp, blk in enumerate(blocks):
                for c in (range(CPB) if dd == 0 else range(CPB - 1, -1, -1)):
                    seq.append((bp, blk, c))
            seq = seq[:SC_LIMIT]
            prep(s, seq[0][1], BS[0])
            s1(BS[0], seq[0][2], 0)
            for j in range(5):
                square(0, j)
            for n, (bp, blk, c) in enumerate(seq):
                par = n % 2
                bs = BS[bp % 2]
                nxt = seq[n + 1] if n + 1 < len(seq) else None
                if nxt is not None:
                    nbs = BS[nxt[0] % 2]
                    if nxt[0] != bp:
                        prep(s, nxt[1], nbs)
                    s1(nbs, nxt[2], 1 - par)
                g_step(bs, c, par)
                for j in range(6):
                    apply(par, j)
                    if nxt is not None and j < 5:
                        square(1 - par, j)
                ys_step(bs, c, par)
                last_of_block = (nxt is None) or (nxt[0] != bp)
                if last_of_block:
                    t0 = blk * SBLK
                    P.dma("sp", C.d_ytok[dd][s][t0:t0 + SBLK, :].rearrange("(c i) f -> i c f", i=64), bs.Yo[:],
                          reads=[bs.Yo], accs=[C.d_ytok[dd][s]])


GN_EPS = 64e-5


def stage_rwkv_post(P, C):
    with P.scope():
        C.wstage_n = 1024
        C.wstage = Rot([P.sbuf([128, 1024], F32, "wst") for _ in range(2)])
        w_o = load_weight_bf16(P, C, C.d_w_o, NK, D, "w_o")
        cols = P.sbuf([128, 14, NK], F32, "rwcols")
        P.dma("sp", cols[:], C.d_rwcols[:], reads=[C.d_rwcols], writes=[cols])
        lnb = P.sbuf([128, NK], F32, "lnb")
        P.dma("sp", lnb[:], C.d_lnb[:], reads=[C.d_lnb], writes=[lnb])
        bones = P.sbuf([128, 128], F32, "bones")
        P.dma("sp", bones[:], C.d_bones[:], reads=[C.d_bones], writes=[bones])
        alloc_norm_scratch(P, C)
        yf = P.sbuf([128, 4, D], F32, "yf")
        yb = P.sbuf([128, 4, D], F32, "yb")
        st = P.sbuf([128, 6, 64], F32, "gnst")
        ynT = P.sbuf([128, NK, BLK], F32, "ynT")
        rB = P.sbuf([128, NK, BLK], BF16, "rB")
        k0B = P.sbuf([128, NK, BLK], BF16, "k0B")
        k1B = P.sbuf([128, NK, BLK], BF16, "k1B")
        vB = P.sbuf([128, NK, BLK], BF16, "vB")
        gB = P.sbuf([128, NK, BLK], BF16, "gB")
        xr = P.sbuf([128, NK, BLK], F32, "xr")
        outT = P.sbuf([128, NK, BLK], BF16, "outT")
        x3 = P.sbuf([128, NK, BLK], F32, "x3T")
        h2 = P.sbuf([128, NK, BLK], BF16, "h2T")
        kms = Rot([P.sbuf([128, BLK], F32, "km") for _ in range(2)])
        qs = Rot([P.sbuf([128, BLK], F32, "qq") for _ in range(2)])
        bns = Rot([P.sbuf([128, BLK], F32, "bn") for _ in range(2)])
        pts = Rot([P.psum([128, BLK], F32, "pt") for _ in range(2)])
        pbs = Rot([P.psum([128, BLK], F32, "pb") for _ in range(2)])
        pps = Rot([P.psum([128, BLK], F32, "pp") for _ in range(2)])
        g1 = C.gatev[1][0]
        for s in range(NS):
            for blk in range(NBLK):
                t0 = blk * BLK
                tsl = slice(t0, t0 + BLK)
                P.dma("sp", yf[:], C.d_ytok[0][s][tsl, :].rearrange("(j p) f -> p j f", p=128),
                      reads=[C.d_ytok[0][s]], writes=[yf])
                P.dma("act", yb[:], C.d_ytok[1][s][tsl, :].rearrange("(j p) f -> p j f", p=128),
                      reads=[C.d_ytok[1][s]], writes=[yb])
                P.dma("sp", rB[:], C.d_rT[s][:, :, tsl], reads=[C.d_rT[s]], writes=[rB])
                P.dma("act", k0B[:], C.d_kdT[0][s][:, :, tsl], reads=[C.d_kdT[0][s]], writes=[k0B])
                P.dma("sp", k1B[:], C.d_kdT[1][s][:, :, tsl], reads=[C.d_kdT[1][s]], writes=[k1B])
                P.dma("act", vB[:], C.d_vT[s][:, :, tsl], reads=[C.d_vT[s]], writes=[vB])
                P.dma("sp", gB[:], C.d_gT[s][:, :, tsl], reads=[C.d_gT[s]], writes=[gB])
                P.dma("act", xr[:], C.d_xT[2][s][:, :, tsl], reads=[C.d_xT[2][s]], writes=[xr])
                yv = yf[:].rearrange("p j (h v) -> p (j h) v", v=64)
                ybv = yb[:].rearrange("p j (h v) -> p (j h) v", v=64)
                P.op("dve", lambda e: e.tensor_tensor(out=yf[:], in0=yf[:], in1=yb[:], op=ALU.add),
                     reads=[yf, yb], writes=[yf])
                P.op("dve", lambda e: e.tensor_reduce(out=st[:, 0, :], in_=yv, axis=mybir.AxisListType.X, op=ALU.add),
                     reads=[yf], writes=[st])
                P.op("act", lambda e: e.activation(out=yb[:], in_=yf[:], func=AF.Square), reads=[yf], writes=[yb])
                P.op("dve", lambda e: e.tensor_reduce(out=st[:, 1, :], in_=ybv, axis=mybir.AxisListType.X, op=ALU.add),
                     reads=[yb], writes=[st])
                P.op("dve", lambda e: e.tensor_scalar(out=st[:, 2, :], in0=st[:, 0, :], scalar1=1.0 / 64, scalar2=None,
                                                      op0=ALU.mult), reads=[st], writes=[st])
                P.op("dve", lambda e: e.tensor_tensor(out=st[:, 3, :], in0=st[:, 2, :], in1=st[:, 2, :], op=ALU.mult),
                     reads=[st], writes=[st])
                P.op("dve", lambda e: e.scalar_tensor_tensor(out=st[:, 4, :], in0=st[:, 1, :], scalar=1.0 / 64,
                                                             in1=st[:, 3, :], op0=ALU.mult, op1=ALU.subtract),
                     reads=[st], writes=[st])
                P.op("dve", lambda e: e.tensor_scalar_add(out=st[:, 4, :], in0=st[:, 4, :], scalar1=GN_EPS),
                     reads=[st], writes=[st])
                P.op("act", lambda e: e.activation(out=st[:, 5, :], in_=st[:, 4, :], func=AF.Sqrt),
                     reads=[st], writes=[st])
                P.op("dve", lambda e: e.reciprocal(out=st[:, 5, :], in_=st[:, 5, :]), reads=[st], writes=[st])
                P.op("dve", lambda e: e.tensor_tensor(out=yv, in0=yv, in1=st[:, 2, :].unsqueeze(2).to_broadcast(
                    [128, 64, 64]), op=ALU.subtract), reads=[yf, st], writes=[yf])
                P.op("dve", lambda e: e.tensor_tensor(out=yv, in0=yv, in1=st[:, 5, :].unsqueeze(2).to_broadcast(
                    [128, 64, 64]), op=ALU.mult), reads=[yf, st], writes=[yf])
                for k in range(NK):
                    pt = pts.next()
                    for j in range(4):
                        P.op("pe", lambda e, pt=pt, j=j, k=k: e.transpose(
                            pt[:, j * 128:(j + 1) * 128], yf[:, j, k * 128:(k + 1) * 128], C.ident32[:]),
                            reads=[yf, C.ident32], writes=[pt])
                    P.op("act", lambda e, pt=pt, k=k: e.activation(
                        out=ynT[:, k, :], in_=pt[:], func=AF.Identity, scale=cols[:, 13, k:k + 1],
                        bias=lnb[:, k:k + 1]), reads=[pt, cols, lnb], accs=[ynT])
                    km = kms.next()
                    P.op("dve", lambda e, km=km, k=k: e.tensor_tensor(out=km[:], in0=k0B[:, k, :], in1=k1B[:, k, :],
                                                                       op=ALU.add), reads=[k0B, k1B], writes=[km])
                    q = qs.next()
                    P.op("dve", lambda e, km=km, q=q, k=k: e.scalar_tensor_tensor(
                        out=q[:], in0=rB[:, k, :], scalar=cols[:, 12, k:k + 1], in1=km[:], op0=ALU.mult, op1=ALU.mult),
                        reads=[rB, cols, km], writes=[q])
                    pb = pbs.next()
                    mm(P, pb, pb[:], bones, bones[:], q, q[:], True, True)
                    bn = bns.next()
                    P.op("dve", lambda e, pb=pb, bn=bn, k=k: e.scalar_tensor_tensor(
                        out=bn[:], in0=pb[:], scalar=0.5, in1=vB[:, k, :], op0=ALU.mult, op1=ALU.mult),
                        reads=[pb, vB], writes=[bn])
                    P.op("dve", lambda e, bn=bn, k=k: e.tensor_tensor(out=bn[:], in0=bn[:], in1=ynT[:, k, :],
                                                                       op=ALU.add), reads=[bn, ynT], writes=[bn])
                    P.op("dve", lambda e, bn=bn, k=k: e.tensor_tensor(out=outT[:, k, :], in0=bn[:], in1=gB[:, k, :],
                                                                       op=ALU.mult), reads=[bn, gB], accs=[outT])
                for c in range(NK):
                    pp = pps.next()
                    for k in range(NK):
                        mm(P, pp, pp[:], w_o, w_o[:, k, c * 128:(c + 1) * 128], outT, outT[:, k, :], k == 0, k == NK - 1)
                    P.op("dve", lambda e, pp=pp, c=c, s=s: e.scalar_tensor_tensor(
                        out=x3[:, c, :], in0=pp[:], scalar=g1[:, c, s:s + 1], in1=xr[:, c, :],
                        op0=ALU.mult, op1=ALU.add), reads=[pp, g1, xr], accs=[x3])
                P.dma("sp", C.d_xT[3][s][:, :, tsl], x3[:], reads=[x3], accs=[C.d_xT[3][s]])
                norm_mod(P, C, x3, BLK, C.gain[1][1], C.shiftv[1][1], s, h2)
                P.dma("sp", C.d_h2T[1][s][:, :, tsl], h2[:], reads=[h2], accs=[C.d_h2T[1][s]])

class _ModView:
    def __init__(self, buf, j):
        self.buf = buf
        self.j = j

    @property
    def w(self):
        return self.buf.w

    @property
    def r(self):
        return self.buf.r

    @property
    def a(self):
        return self.buf.a

    def __getitem__(self, idx):
        p, k, s = idx
        return self.buf[p, self.j * 8 + k, s]


def mod_shift(C, l, which):
    return C.shiftv[l][which]


def host_consts():
    c = {}
    c["ident32"] = np.eye(128, dtype=np.float32)
    c["ones32"] = np.ones((128, 128), dtype=np.float32)
    bo = np.zeros((128, 128), np.float32)
    bo[0:64, 0:64] = 1.0
    bo[64:128, 64:128] = 1.0
    c["c_bones"] = bo
    ii = np.arange(64)[:, None]
    tt = np.arange(64)[None, :]
    sm = np.zeros((64, 2, 192), np.float32)
    sm[:, 0, 0:64] = (ii < tt)
    sm[:, 0, 64:128] = (ii <= tt)
    sm[:, 0, 128:192] = (tt < ii)
    sm[:, 1, 0:64] = (ii > tt)
    sm[:, 1, 64:128] = (ii >= tt)
    sm[:, 1, 128:192] = (tt > ii)
    c["c_scanmask"] = sm
    rm = np.ones((64, 16 * 256), np.float32)
    rm[:, ::64] = 0.0
    c["c_rmask"] = rm
    slopes = np.exp2(-8.0 * (np.arange(12, dtype=np.float32) + 1.0) / 12).astype(np.float32).reshape(3, 4)
    tab = np.zeros((128, 9, 4, 128), np.float32)
    kk = np.arange(128)[:, None]
    qq = np.arange(128)[None, :]
    NEG = -30000.0
    for g, d in enumerate((1, 4, 16)):
        for h in range(4):
            sl = slopes[g, h] * d
            lo = np.where(kk >= qq, -sl * np.abs(kk - qq - 64), NEG)
            up = np.where(kk <= qq, -sl * np.abs(kk - qq + 64), NEG)
            tab[:, 3 * g + 0, h, :] = lo
            tab[:, 3 * g + 1, h, :] = up
            tab[0:64, 3 * g + 2, h, :] = lo[64:128]
    c["abias"] = tab.astype(np.float32)
    n1 = np.arange(64, dtype=np.float64)[:, None]
    k1 = np.arange(128, dtype=np.float64)[None, :]
    ang = 2 * np.pi * n1 * k1 / 128.0
    c["c_w128"] = np.concatenate([np.cos(ang), -np.sin(ang)], 1).astype(np.float32)
    n2 = np.tile(np.arange(64, dtype=np.float64), 2)[:, None]
    ang = 2 * np.pi * n2 * k1 / 8192.0
    c["c_tw"] = np.stack([np.cos(ang), -np.sin(ang)], 1).astype(np.float32)
    a64 = 2 * np.pi * np.outer(np.arange(64), np.arange(64)) / 64.0
    def bdiag(m):
        z = np.zeros((128, 128))
        z[0:64, 0:64] = m
        z[64:128, 64:128] = m
        return z
    bdr, bdi = bdiag(np.cos(a64)), bdiag(-np.sin(a64))
    c["c_bd"] = np.stack([bdr, bdi, -bdi], 1).astype(np.float32)
    cr, ci = bdiag(np.cos(a64)), bdiag(np.sin(a64))
    c["c_bdc"] = np.stack([np.concatenate([cr, ci], 1), np.concatenate([-ci, cr], 1)], 1).astype(np.float32)
    kk1 = np.arange(128, dtype=np.float64)[:, None]
    nn2 = np.tile(np.arange(64, dtype=np.float64), 2)[None, :]
    ang = 2 * np.pi * kk1 * nn2 / 8192.0
    c["c_twi"] = np.stack([np.cos(ang), np.sin(ang)], 1).astype(np.float32)
    ang = 2 * np.pi * np.outer(np.arange(128), np.arange(64)) / 128.0
    c["c_vinv"] = np.stack([np.cos(ang) / 8192.0, -np.sin(ang) / 8192.0], 1).astype(np.float32)
    t = np.linspace(0.0, 1.0, L, dtype=np.float32)[:, None]
    w = (2.0 * np.float32(math.pi) * np.arange(L, dtype=np.float32)[:, None] / np.float32(L)).astype(np.float32)
    f = np.linspace(1e-4, 15, 16, dtype=np.float32)[None, :]
    z = (f * w).astype(np.float32)
    pos = np.concatenate([t, np.cos(z), -np.sin(z)], -1).astype(np.float32)
    c["c_posT"] = np.ascontiguousarray(pos.T)
    min_decay = math.log(1e-2) / 1.5
    max_decay = math.log(1e-2) / 0.3
    deltas = np.linspace(min_decay, max_decay, 256, dtype=np.float32)[None, :]
    win = np.exp(-t * np.abs(deltas)).astype(np.float32)
    c["c_winT"] = np.ascontiguousarray(win.T.reshape(2, 128, L).transpose(1, 0, 2))
    return c


def build_program(stages, dbg=()):
    nc = bass.Bass("TRN2", target_bir_lowering=False)
    P = Prog(nc)
    C = Ctx()
    C.dbg = {}
    din = lambda name, shape, dt=F32: P.dram(name, shape, dt, kind="ExternalInput")
    C.d_x = din("x_in", [NS, L, D])
    C.d_cT = din("cT", [128, NK, NS])
    C.d_adaw = [din("ada_w%d" % l, [128, NK, 6 * D]) for l in range(2)]
    C.d_adab = din("ada_b", [128, 2, 48])
    C.d_normw = din("normw", [128, 5, NK])
    C.d_w_in = din("w_in", [128, NK, 3072])
    C.d_abias = din("abias", [128, 9, 4, 128])
    C.d_w128 = din("c_w128", [64, 256])
    C.d_tw = din("c_tw", [128, 2, 128])
    C.d_bd = din("c_bd", [128, 3, 128])
    C.d_bdc = din("c_bdc", [128, 2, 256])
    C.d_twi = din("c_twi", [128, 2, 128])
    C.d_vinv = din("c_vinv", [128, 2, 64])
    C.d_posT = din("c_posT", [33, L])
    C.d_winT = din("c_winT", [128, 2, L])
    C.d_hcol = din("hcol", [64, 4])
    C.d_fw1 = din("fw1", [33, 64])
    C.d_fw23 = din("fw23", [64, 2, 64])
    C.d_fw4 = din("fw4", [64, 512])
    C.d_fbias = din("fbias", [128, 2])
    C.d_shortw = din("shortw", [128, 6, 4])
    C.d_w_out = din("w_out", [128, 4, D])
    C.d_ffn_up = [din("ffn_up%d" % l, [128, NK, 2 * DFF]) for l in range(2)]
    C.d_ffn_dn = [din("ffn_dn%d" % l, [128, NFC, D]) for l in range(2)]
    C.d_ffn_cw = [din("ffn_cw%d" % l, [128, NFC, 4]) for l in range(2)]
    C.d_w_r = din("w_r", [128, NK, D])
    C.d_w_k = din("w_k", [128, NK, D])
    C.d_w_v = din("w_v", [128, NK, D])
    C.d_w_o = din("w_o", [128, NK, D])
    C.d_g1 = din("rw_g1", [128, NK, 256])
    C.d_g2 = din("rw_g2", [128, 2, D])
    C.d_lora1 = din("rw_lora1", [128, NK, 256])
    C.d_lora2 = din("rw_lora2", [128, 4, D])
    C.d_rwcols = din("rw_cols", [128, 14, NK])
    C.d_bones = din("c_bones", [128, 128])
    C.d_lnb = din("rw_lnb", [128, NK])
    C.d_scanmask = din("c_scanmask", [64, 2, 192])
    C.d_rmask = din("c_rmask", [64, 16 * 256])
    C.d_ident32 = din("ident32", [128, 128])
    C.d_ones32 = din("ones32", [128, 128])
    C.d_y = P.dram("y_out", [NS, L, D], F32, kind="ExternalOutput")
    outs = []

    def scratch(name, shape, dt):
        if name in dbg:
            b = P.dram(name, shape, dt, kind="ExternalOutput")
            outs.append(b)
            return b
        return P.dram(name, shape, dt)

    C.d_xT = [[scratch("xT%d_%d" % (i, s), [128, NK, L], F32) for s in range(NS)] for i in range(5)]
    C.d_qkT = [scratch("qkT_%d" % s, [128, 12, L], BF16) for s in range(NS)]
    C.d_vaug = [scratch("vaug_%d" % s, [L, 12, 65], BF16) for s in range(NS)]
    C.d_hyT = [scratch("hyT_%d" % s, [128, 6, L], F32) for s in range(NS)]
    C.d_oacc = [scratch("oacc_%d" % s, [3, L, 4, 65], F32) for s in range(NS)]
    C.d_h2T = [[scratch("h2T%d_%d" % (l, s), [128, NK, L], BF16) for s in range(NS)] for l in range(2)]
    C.d_h1T = [scratch("h1T_%d" % s, [128, NK, L], F32) for s in range(NS)]
    C.d_vtok = [scratch("vtok_%d" % s, [L, D], BF16) for s in range(NS)]
    C.d_rT = [scratch("rT_%d" % s, [128, NK, L], BF16) for s in range(NS)]
    C.d_vT = [scratch("vT_%d" % s, [128, NK, L], BF16) for s in range(NS)]
    C.d_gT = [scratch("gT_%d" % s, [128, NK, L], BF16) for s in range(NS)]
    C.d_kkT = [scratch("kkT_%d" % s, [128, NK, L], BF16) for s in range(NS)]
    C.d_lwT = [[scratch("lwT%d_%d" % (dd, s), [128, NK, L], F32) for s in range(NS)] for dd in range(2)]
    C.d_bT = [[scratch("bT%d_%d" % (dd, s), [128, NK, L], BF16) for s in range(NS)] for dd in range(2)]
    C.d_kdT = [[scratch("kdT%d_%d" % (dd, s), [128, NK, L], BF16) for s in range(NS)] for dd in range(2)]
    C.d_ytok = [[scratch("ytok%d_%d" % (dd, s), [L, D], F32) for s in range(NS)] for dd in range(2)]
    C.d_filtT = scratch("filtT", [128, 4, L], F32)
    C.d_F = scratch("Fspec", [128, 128, 2, 128], F32)
    C.d_zT = [scratch("zT_%d" % s, [128, 2, L], F32) for s in range(NS)]
    C.d_x0T = [scratch("x0T_%d" % s, [128, 2, L], F32) for s in range(NS)]
    C.d_hyoT = [scratch("hyoT_%d" % s, [128, 2, L], F32) for s in range(NS)]
    C.ident32 = P.sbuf([128, 128], F32, "ident32")
    C.ones32 = P.sbuf([128, 128], F32, "ones32")
    C.eps_col = P.sbuf([128, 1], F32, "eps")
    P.dma("sp", C.ident32[:], C.d_ident32[:], reads=[C.d_ident32], writes=[C.ident32])
    P.dma("sp", C.ones32[:], C.d_ones32[:], reads=[C.d_ones32], writes=[C.ones32])
    P.op("pool", lambda e: e.memset(C.eps_col[:], RMS_EPS), writes=[C.eps_col])
    C.modT = [P.sbuf([128, 48, NS], F32, "modT%d" % l) for l in range(2)]
    C.gain = [[P.sbuf([128, NK, NS], F32, "gain%d%d" % (l, w)) for w in range(2)] for l in range(2)]
    C.gain_fin = P.sbuf([128, NK, NS], F32, "gainf")
    C.shiftv = [[_ModView(C.modT[l], 0), _ModView(C.modT[l], 3)] for l in range(2)]
    C.gatev = [[_ModView(C.modT[l], 2), _ModView(C.modT[l], 5)] for l in range(2)]

    def dbg_out(name, src_buf, src_ap, shape, dt=F32):
        if name in dbg:
            o = P.dram("dbg_" + name, shape, dt, kind="ExternalOutput")
            P.dma("sp", o[:], src_ap, reads=[src_buf], writes=[o])
            outs.append(o)

    C.dbg_out = dbg_out
    if "adaln" in stages:
        stage_adaln(P, C)
        dbg_out("modT0", C.modT[0], C.modT[0][:], [128, 48, NS])
        dbg_out("gain00", C.gain[0][0], C.gain[0][0][:], [128, NK, NS])
    if "l0_inproj" in stages:
        stage_l0_inproj(P, C)
    if "attn" in stages:
        stage_attention(P, C)
    if "hyfilt" in stages:
        stage_hyena_filter(P, C)
    if "hyena" in stages:
        stage_hyena(P, C)
    if "l0_outproj" in stages:
        stage_l0_outproj(P, C)
    if "ffn0" in stages:
        stage_ffn(P, C, 0, 1, 2)
    if "rw_norm" in stages:
        stage_rwkv_norm(P, C)
    if "rw_proj" in stages:
        stage_rwkv_proj(P, C)
    if "rw_scan0" in stages:
        stage_rwkv_scan(P, C, 0)
    if "rw_scan1" in stages:
        stage_rwkv_scan(P, C, 1)
    if "rw_post" in stages:
        stage_rwkv_post(P, C)
    if "ffn1" in stages:
        stage_ffn(P, C, 1, 3, 4)
    if "final" in stages:
        stage_final(P, C, FINAL_SRC)
    outs.append(C.d_y)
    final = [o for o in outs if o is not None]
    C.final_bufs = final
    return nc, P, C


def finish_program(P, C, extra=()):
    bufs = list(C.final_bufs) + list(extra)
    P.wait_all("sp", bufs)
    P.close()


def arrange_w(w, nk=None):
    K, N = w.shape
    nk = K // 128
    return np.ascontiguousarray(w.reshape(nk, 128, N).transpose(1, 0, 2))


def col_layout(v):
    return np.ascontiguousarray(v.reshape(-1, 128).T)


def prep_shared(inp):
    m = {}
    m["ada_w0"] = arrange_w(inp["l0_ada_w"])
    m["ada_w1"] = arrange_w(inp["l1_ada_w"])
    m["ada_b"] = np.ascontiguousarray(np.stack([col_layout(inp["l0_ada_b"]), col_layout(inp["l1_ada_b"])], 1))
    m["normw"] = np.ascontiguousarray(np.stack([col_layout(inp[k]) for k in
                                               ("l0_norm1", "l0_norm2", "l1_norm1", "l1_norm2", "final_norm")], 1))
    m["w_in"] = arrange_w(inp["l0_w_in"])
    m["w_out"] = arrange_w(inp["l0_w_out"])
    ffn = [(inp["l0_ffn_up"], inp["l0_ffn_down"], inp["l0_ffn_conv_w"], inp["l0_ffn_conv_b"]),
           (inp["l1_ffn_up"], inp["l1_ffn_down"], inp["l1_ffn_conv_w"], inp["l1_ffn_conv_b"])]
    for l in range(2):
        up, dn, cw_, cb_ = ffn[l]
        m["ffn_up%d" % l] = arrange_w(up)
        m["ffn_dn%d" % l] = arrange_w(dn)
        cwb = np.concatenate([cw_, cb_[None, :]], 0)
        m["ffn_cw%d" % l] = np.ascontiguousarray(cwb.T.reshape(NFC, 128, 4).transpose(1, 0, 2))
    for nm in ("w_r", "w_k", "w_v", "w_o"):
        m[nm] = arrange_w(inp["l1_" + nm])
    g1p = np.zeros((D, 256), np.float32)
    g1p[:, :160] = inp["l1_g1"]
    m["rw_g1"] = arrange_w(g1p)
    g2 = np.zeros((256, D), np.float32)
    g2[:160] = inp["l1_g2"]
    m["rw_g2"] = arrange_w(g2)
    m["rw_lora1"] = arrange_w(np.concatenate([inp["l1_w1"][0], inp["l1_w1"][1], inp["l1_a1"][0], inp["l1_a1"][1]], 1))
    l2 = np.zeros((128, 4, D), np.float32)
    l2[:64, 0] = inp["l1_w2"][0]
    l2[:64, 1] = inp["l1_w2"][1]
    l2[:64, 2] = inp["l1_a2"][0]
    l2[:64, 3] = inp["l1_a2"][1]
    m["rw_lora2"] = l2
    cl = [col_layout(inp["l1_mu"][i]) for i in range(6)]
    cl += [col_layout(inp["l1_w0"][0]), col_layout(inp["l1_w0"][1]), col_layout(inp["l1_a0"][0]),
           col_layout(inp["l1_a0"][1]), col_layout(inp["l1_k_k"]), col_layout(inp["l1_k_a"]),
           col_layout(inp["l1_r_k"].reshape(-1)), col_layout(inp["l1_ln_w"])]
    m["rw_cols"] = np.ascontiguousarray(np.stack(cl, 1))
    m["rw_lnb"] = col_layout(inp["l1_ln_b"])
    m["hcol"] = np.ascontiguousarray(np.stack([inp["l0_filt_b1"], inp["l0_filt_b2"], inp["l0_filt_b3"],
                                               inp["l0_filt_freq"]], 1))
    m["fw1"] = np.ascontiguousarray(inp["l0_filt_w1"])
    m["fw23"] = np.ascontiguousarray(np.stack([inp["l0_filt_w2"], inp["l0_filt_w3"]], 1))
    m["fw4"] = np.ascontiguousarray(inp["l0_filt_w4"])
    m["fbias"] = col_layout(inp["l0_filt_bias"])
    sw = np.concatenate([inp["l0_short_w"], inp["l0_short_b"][None, :]], 0)
    m["shortw"] = np.ascontiguousarray(sw.T.reshape(6, 128, 4).transpose(1, 0, 2))
    m.update(host_consts())
    return m


def prep_core(xs, cs):
    m = {}
    m["x_in"] = np.ascontiguousarray(np.stack(xs, 0))
    c = np.stack(cs, 0)
    m["cT"] = np.ascontiguousarray(c.reshape(len(xs), NK, 128).transpose(2, 1, 0))
    return m


ALL_STAGES = ("adaln", "l0_inproj", "attn", "hyfilt", "hyena", "l0_outproj", "ffn0", "rw_norm", "rw_proj",
              "rw_scan0", "rw_scan1", "rw_post", "ffn1", "final")


def kernel(**inputs):
    inp = {k: np.asarray(v) for k, v in inputs.items()}
    xs = [inp["x_prompt"][i] for i in range(inp["x_prompt"].shape[0])] + \
         [inp["x_sample"][i] for i in range(inp["x_sample"].shape[0])]
    cs = [inp["c_prompt"][i] for i in range(inp["c_prompt"].shape[0])] + \
         [inp["c_sample"][i] for i in range(inp["c_sample"].shape[0])]
    nseq = len(xs)
    nb = inp["x_prompt"].shape[0]
    assign = []
    for core in range(NCORES):
        ids = []
        for slot in range(NS):
            sid = core + NCORES * slot
            ids.append(sid if sid < nseq else core)
        assign.append(ids)
    nc, P, C = build_program(ALL_STAGES, ())
    finish_program(P, C)
    shared = prep_shared(inp)
    in_maps = []
    for core in range(NCORES):
        m = dict(shared)
        m.update(prep_core([xs[i] for i in assign[core]], [cs[i] for i in assign[core]]))
        in_maps.append(m)
    res = run_bass_kernel_spmd(nc, in_maps, core_ids=list(range(NCORES)))
    outs = [None] * nseq
    for core in range(NCORES):
        y = np.asarray(res.results[core]["y_out"])
        for slot in range(NS):
            sid = core + NCORES * slot
            if sid < nseq:
                outs[sid] = y[slot]
    y_prompt = np.stack(outs[:nb], 0).astype(np.float32)
    y_sample = np.stack(outs[nb:], 0).astype(np.float32)
    return (y_prompt, y_sample)
```

```python
import contextlib
import math
import numpy as np
import concourse.bass as bass
import concourse.mybir as mybir
from concourse.bass_utils import run_bass_kernel_spmd

F32 = mybir.dt.float32
BF16 = mybir.dt.bfloat16
AF = mybir.ActivationFunctionType
ALU = mybir.AluOpType

D = 1024
L = 4096
NK = 8
BLK = 512
NBLK = L // BLK
NCORES = 8
NS = 3
DFF = 2816
NFC = DFF // 128
RMS_EPS = 1e-6

ENGS = ("pe", "act", "dve", "pool", "sp")
NDMASEM = 6


class Buf:
    __slots__ = ("t", "name", "w", "r", "a")

    def __init__(self, t, name):
        self.t = t
        self.name = name
        self.w = {}
        self.r = {}
        self.a = {}

    def __getitem__(self, idx):
        return self.t[idx]


class Prog:
    def __init__(self, nc):
        self.nc = nc
        self.base = contextlib.ExitStack()
        self.es = self.base
        self.streams = {e: [] for e in ENGS}
        self.cnt = {e: 0 for e in ENGS}
        self.seen = {e: {} for e in ENGS}
        self.sem = {}
        self.dtot = {}
        self.rr = {e: 0 for e in ENGS}
        for e in ENGS:
            self.sem[e] = self.base.enter_context(nc.semaphore("c_" + e))
        for q in ("sp", "act", "pool"):
            for i in range(NDMASEM):
                k = "d_%s%d" % (q, i)
                self.sem[k] = self.base.enter_context(nc.semaphore(k))
                self.dtot[k] = 0
        self.nbuf = 0
        self.ninst = 0

    @contextlib.contextmanager
    def scope(self):
        old = self.es
        es = contextlib.ExitStack()
        self.es = es
        try:
            yield
            self.barrier()
            self.emit()
        finally:
            es.close()
            self.es = old

    def sbuf(self, shape, dt, name=None):
        self.nbuf += 1
        name = (name or "sb") + "_%d" % self.nbuf
        t = self.es.enter_context(self.nc.sbuf_tensor(name, list(shape), dt))
        return Buf(t, name)

    def psum(self, shape, dt=F32, name=None):
        self.nbuf += 1
        name = (name or "ps") + "_%d" % self.nbuf
        t = self.es.enter_context(self.nc.psum_tensor(name, list(shape), dt))
        return Buf(t, name)

    def dram(self, name, shape, dt, kind="Internal"):
        t = self.nc.dram_tensor(name, list(shape), dt, kind=kind)
        return Buf(t.ap(), name)

    def _deps(self, eng, reads, writes, accs=()):
        need = {}
        seen = self.seen[eng]

        def add(k, v):
            if k == eng and eng == "pe":
                return
            if seen.get(k, 0) >= v:
                return
            if need.get(k, 0) < v:
                need[k] = v

        for b in reads:
            for k, v in b.w.items():
                add(k, v)
            for k, v in b.a.items():
                add(k, v)
        for b in writes:
            for k, v in b.w.items():
                add(k, v)
            for k, v in b.a.items():
                add(k, v)
            for k, v in b.r.items():
                add(k, v)
        for b in accs:
            for k, v in b.w.items():
                add(k, v)
            for k, v in b.r.items():
                add(k, v)
        for k, v in need.items():
            seen[k] = v
        return list(need.items())

    @staticmethod
    def _commit(ev, reads, writes, accs):
        k, v = ev
        for b in reads:
            if b.r.get(k, 0) < v:
                b.r[k] = v
        for b in writes:
            b.w.clear()
            b.w[k] = v
            b.r.clear()
            b.a.clear()
        for b in accs:
            if b.a.get(k, 0) < v:
                b.a[k] = v

    def op(self, eng, fn, reads=(), writes=(), accs=()):
        waits = self._deps(eng, reads, writes, accs)
        self.cnt[eng] += 1
        ev = (eng, self.cnt[eng])
        self.streams[eng].append((waits, fn, eng, 1))
        self._commit(ev, reads, writes, accs)
        self.ninst += 1 + len(waits)

    def dma(self, q, out, in_, reads=(), writes=(), accs=(), **kw):
        i = self.rr[q]
        self.rr[q] = (i + 1) % NDMASEM
        k = "d_%s%d" % (q, i)
        waits = self._deps(q, reads, writes, accs)
        prev = self.dtot[k]
        if prev > 0 and self.seen[q].get(k, 0) < prev:
            waits.append((k, prev))
            self.seen[q][k] = prev
        self.dtot[k] = prev + 16
        ev = (k, prev + 16)

        def fn(e, out=out, in_=in_, kw=kw):
            return e.dma_start(out=out, in_=in_, **kw)

        self.streams[q].append((waits, fn, k, 16))
        self._commit(ev, reads, writes, accs)
        self.ninst += 1 + len(waits)

    def barrier(self):
        tot = dict(self.dtot)
        for e in ENGS:
            tot[e] = self.cnt[e]
        for e in ENGS:
            waits = []
            for k, v in tot.items():
                if v > 0 and k != e and self.seen[e].get(k, 0) < v:
                    waits.append((k, v))
                    self.seen[e][k] = v
            if waits:
                self.streams[e].append((waits, None, None, 0))
                self.ninst += len(waits)

    def wait_all(self, eng, bufs):
        waits = self._deps(eng, bufs, ())
        self.streams[eng].append((waits, None, None, 0))

    def emit(self):
        if not any(self.streams.values()):
            return
        nc = self.nc
        engobj = {"pe": "tensor", "act": "scalar", "dve": "vector", "pool": "gpsimd", "sp": "sync"}
        with nc.Block() as block:
            for e in ENGS:
                stream = self.streams[e]

                def body(eng, stream=stream):
                    for waits, fn, sk, inc in stream:
                        for k, v in waits:
                            eng.wait_ge(self.sem[k], v)
                        if fn is not None:
                            fn(eng).then_inc(self.sem[sk], inc)

                getattr(block, engobj[e])(body)
        self.streams = {e: [] for e in ENGS}

    def close(self):
        self.emit()
        self.base.close()


class Rot:
    def __init__(self, bufs):
        self.bufs = bufs
        self.i = 0

    def next(self):
        b = self.bufs[self.i % len(self.bufs)]
        self.i += 1
        return b


class Ctx:
    pass


def mm(P, out_buf, out_ap, lhsT_buf, lhsT_ap, rhs_buf, rhs_ap, start, stop):
    P.op("pe", lambda e: e.matmul(out_ap, lhsT=lhsT_ap, rhs=rhs_ap, start=start, stop=stop),
         reads=[lhsT_buf, rhs_buf], writes=[out_buf])


def load_weight_bf16(P, C, dram_buf, nk, ncols, name):
    wb = P.sbuf([128, nk, ncols], BF16, name)
    grp = max(32, (C.wstage_n // nk) // 32 * 32)
    i = 0
    for c0 in range(0, ncols, grp):
        cw = min(grp, ncols - c0)
        st = C.wstage.next()
        P.dma("sp", st[:, 0:nk * cw].rearrange("p (k c) -> p k c", c=cw),
              dram_buf[:, :, c0:c0 + cw], reads=[dram_buf], writes=[st])
        if i % 2 == 0:
            P.op("dve", lambda e, st=st, c0=c0, cw=cw: e.tensor_copy(
                out=wb[:, :, c0:c0 + cw], in_=st[:, 0:nk * cw].rearrange("p (k c) -> p k c", c=cw)),
                reads=[st], accs=[wb])
        else:
            P.op("act", lambda e, st=st, c0=c0, cw=cw: e.copy(
                out=wb[:, :, c0:c0 + cw], in_=st[:, 0:nk * cw].rearrange("p (k c) -> p k c", c=cw)),
                reads=[st], accs=[wb])
        i += 1
    return wb


def norm_mod(P, C, xT, W, gain, shift, s, out_buf, out_dt_is_bf16=True):
    ssp = C.ps_ss.next()
    for k in range(NK):
        sq = C.nm_sq.next()
        P.op("act", lambda e, sq=sq, k=k: e.activation(out=sq[:, 0:W], in_=xT[:, k, 0:W], func=AF.Square),
             reads=[xT], writes=[sq])
        mm(P, ssp, ssp[:, 0:W], C.ones32, C.ones32[:], sq, sq[:, 0:W], k == 0, k == NK - 1)
    rstd = C.nm_rstd.next()
    P.op("act", lambda e: e.activation(out=rstd[:, 0:W], in_=ssp[:, 0:W], func=AF.Sqrt,
                                       scale=1.0 / D, bias=C.eps_col[:, 0:1]),
         reads=[ssp, C.eps_col], writes=[rstd])
    P.op("dve", lambda e: e.reciprocal(out=rstd[:, 0:W], in_=rstd[:, 0:W]), reads=[rstd], writes=[rstd])
    for k in range(NK):
        if shift is None:
            P.op("dve", lambda e, k=k: e.scalar_tensor_tensor(
                out=out_buf[:, k, 0:W], in0=xT[:, k, 0:W], scalar=gain[:, k, s:s + 1], in1=rstd[:, 0:W],
                op0=ALU.mult, op1=ALU.mult), reads=[xT, gain, rstd], accs=[out_buf])
        else:
            tmp = C.nm_tmp.next()
            P.op("dve", lambda e, k=k, tmp=tmp: e.scalar_tensor_tensor(
                out=tmp[:, 0:W], in0=xT[:, k, 0:W], scalar=gain[:, k, s:s + 1], in1=rstd[:, 0:W],
                op0=ALU.mult, op1=ALU.mult), reads=[xT, gain, rstd], writes=[tmp])
            P.op("act", lambda e, k=k, tmp=tmp: e.activation(
                out=out_buf[:, k, 0:W], in_=tmp[:, 0:W], func=AF.Identity, bias=shift[:, k, s:s + 1], scale=1.0),
                reads=[tmp, shift], accs=[out_buf])


def alloc_norm_scratch(P, C, W=BLK):
    C.nm_sq = Rot([P.sbuf([128, W], F32, "nmsq") for _ in range(2)])
    C.nm_rstd = Rot([P.sbuf([128, W], F32, "nmrs") for _ in range(2)])
    C.nm_tmp = Rot([P.sbuf([128, W], F32, "nmtmp") for _ in range(2)])
    C.ps_ss = Rot([P.psum([128, W], F32, "psss")])


def fence(P, buf):
    return buf


def stage_adaln(P, C):
    with P.scope():
        cT = P.sbuf([128, NK, NS], F32, "cT")
        P.dma("sp", cT[:], C.d_cT[:], reads=[C.d_cT], writes=[cT])
        sc = P.sbuf([128, NK, NS], F32, "silu_c")
        P.op("act", lambda e: e.activation(out=sc[:], in_=cT[:], func=AF.Silu), reads=[cT], writes=[sc])
        wts = Rot([P.sbuf([128, NK, 1024], F32, "adaw") for _ in range(2)])
        pss = Rot([P.psum([128, 512], F32, "adaps") for _ in range(2)])
        adab = P.sbuf([128, 2, 48], F32, "adab")
        P.dma("act", adab[:], C.d_adab[:], reads=[C.d_adab], writes=[adab])
        for l in range(2):
            for j in range(6):
                wt = wts.next()
                P.dma("sp" if j % 2 == 0 else "act", wt[:], C.d_adaw[l][:, :, j * 1024:(j + 1) * 1024],
                      reads=[C.d_adaw[l]], writes=[wt])
                psb = pss.next()
                ps = psb[:, 0:8 * NS].rearrange("p (c s) -> p c s", s=NS)
                for cc in range(8):
                    for k in range(NK):
                        mm(P, psb, ps[:, cc, :], wt, wt[:, k, cc * 128:(cc + 1) * 128], sc, sc[:, k, :],
                           k == 0, k == NK - 1)
                P.op("dve", lambda e, l=l, j=j, ps=ps: e.tensor_tensor(
                    out=C.modT[l][:, j * 8:(j + 1) * 8, :], in0=ps,
                    in1=adab[:, l, j * 8:(j + 1) * 8].unsqueeze(2).to_broadcast([128, 8, NS]), op=ALU.add),
                    reads=[psb, adab], accs=[C.modT[l]])
        nw = P.sbuf([128, 5, NK], F32, "normw")
        P.dma("act", nw[:], C.d_normw[:], reads=[C.d_normw], writes=[nw])
        for l in range(2):
            for which in range(2):
                jsc = 1 + 3 * which
                g = C.gain[l][which]
                P.op("dve", lambda e, l=l, which=which, jsc=jsc, g=g: e.scalar_tensor_tensor(
                    out=g[:], in0=C.modT[l][:, jsc * 8:(jsc + 1) * 8, :], scalar=1.0,
                    in1=nw[:, 2 * l + which, :].unsqueeze(2).to_broadcast([128, NK, NS]),
                    op0=ALU.add, op1=ALU.mult), reads=[C.modT[l], nw], writes=[g])
        P.op("dve", lambda e: e.tensor_copy(
            out=C.gain_fin[:], in_=nw[:, 4, :].unsqueeze(2).to_broadcast([128, NK, NS])),
            reads=[nw], writes=[C.gain_fin])


def mod_vec(C, l, j):
    return C.modT[l]


def stage_l0_inproj(P, C):
    with P.scope():
        C.wstage_n = 2048
        C.wstage = Rot([P.sbuf([128, 2048], F32, "wst") for _ in range(2)])
        w_in = load_weight_bf16(P, C, C.d_w_in, NK, 3072, "w_in")
        alloc_norm_scratch(P, C)
        xins = Rot([P.sbuf([128, 4, D], F32, "xin") for _ in range(2)])
        xTs = Rot([P.sbuf([128, NK, BLK], F32, "xT") for _ in range(2)])
        hTs = Rot([P.sbuf([128, NK, BLK], BF16, "hT") for _ in range(2)])
        pts = Rot([P.psum([128, BLK], F32, "pt") for _ in range(2)])
        pps = Rot([P.psum([128, BLK], F32, "pp") for _ in range(2)])
        pvs = Rot([P.psum([128, 1024], F32, "pv") for _ in range(1)])
        qks = Rot([P.sbuf([128, 12, BLK], BF16, "qk") for _ in range(1)])
        hys = Rot([P.sbuf([128, 6, BLK], F32, "hy") for _ in range(1)])
        vas = []
        for _ in range(2):
            va = P.sbuf([128, 4, 12, 65], BF16, "vaug")
            P.op("pool", lambda e, va=va: e.memset(va[:], 1.0), writes=[va])
            vas.append(va)
        vas = Rot(vas)
        mod = C.modT[0]
        ev = 0
        for s in range(NS):
            for blk in range(NBLK):
                t0 = blk * BLK
                xin = xins.next()
                P.dma("sp", xin[:], C.d_x[s, t0:t0 + BLK, :].rearrange("(j p) f -> p j f", p=128),
                      reads=[C.d_x], writes=[xin])
                xT = xTs.next()
                for k in range(NK):
                    pt = pts.next()
                    for j in range(4):
                        P.op("pe", lambda e, pt=pt, j=j, k=k, xin=xin: e.transpose(
                            pt[:, j * 128:(j + 1) * 128], xin[:, j, k * 128:(k + 1) * 128], C.ident32[:]),
                            reads=[xin, C.ident32], writes=[pt])
                    if k % 2 == 0:
                        P.op("act", lambda e, pt=pt, k=k, xT=xT: e.copy(out=xT[:, k, :], in_=pt[:]),
                             reads=[pt], accs=[xT])
                    else:
                        P.op("dve", lambda e, pt=pt, k=k, xT=xT: e.tensor_copy(out=xT[:, k, :], in_=pt[:]),
                             reads=[pt], accs=[xT])
                P.dma("sp", C.d_xT[0][s][:, :, t0:t0 + BLK], xT[:], reads=[xT], accs=[C.d_xT[0][s]])
                hT = hTs.next()
                norm_mod(P, C, xT, BLK, C.gain[0][0], mod_shift(C, 0, 0), s, hT)
                qk = qks.next()
                for c in range(12):
                    pp = pps.next()
                    for k in range(NK):
                        mm(P, pp, pp[:], w_in, w_in[:, k, c * 128:(c + 1) * 128], hT, hT[:, k, :], k == 0, k == NK - 1)
                    if c % 2 == 0:
                        P.op("act", lambda e, pp=pp, c=c, qk=qk: e.copy(out=qk[:, c, :], in_=pp[:]),
                             reads=[pp], accs=[qk])
                    else:
                        P.op("dve", lambda e, pp=pp, c=c, qk=qk: e.tensor_copy(out=qk[:, c, :], in_=pp[:]),
                             reads=[pp], accs=[qk])
                P.dma("sp", C.d_qkT[s][:, :, t0:t0 + BLK], qk[:], reads=[qk], accs=[C.d_qkT[s]])
                va = vas.next()
                for j in range(4):
                    pv = pvs.next()
                    for k in range(NK):
                        mm(P, pv, pv[:, 0:512], hT, hT[:, k, j * 128:(j + 1) * 128], w_in, w_in[:, k, 1536:2048],
                           k == 0, k == NK - 1)
                    for k in range(NK):
                        mm(P, pv, pv[:, 512:768], hT, hT[:, k, j * 128:(j + 1) * 128], w_in, w_in[:, k, 2048:2304],
                           k == 0, k == NK - 1)
                    P.op("act" if j % 2 == 0 else "dve",
                         (lambda e, pv=pv, j=j, va=va: e.copy(
                             out=va[:, j, :, 0:64], in_=pv[:, 0:768].rearrange("p (h d) -> p h d", d=64)))
                         if j % 2 == 0 else
                         (lambda e, pv=pv, j=j, va=va: e.tensor_copy(
                             out=va[:, j, :, 0:64], in_=pv[:, 0:768].rearrange("p (h d) -> p h d", d=64))),
                         reads=[pv], accs=[va])
                P.dma("sp", C.d_vaug[s][t0:t0 + BLK].rearrange("(j p) h d -> p j h d", p=128), va[:],
                      reads=[va], accs=[C.d_vaug[s]])
                hy = hys.next()
                for c in range(6):
                    pp = pps.next()
                    for k in range(NK):
                        mm(P, pp, pp[:], w_in, w_in[:, k, 2304 + c * 128:2304 + (c + 1) * 128], hT, hT[:, k, :],
                           k == 0, k == NK - 1)
                    if c % 2 == 0:
                        P.op("act", lambda e, pp=pp, c=c, hy=hy: e.copy(out=hy[:, c, :], in_=pp[:]),
                             reads=[pp], accs=[hy])
                    else:
                        P.op("dve", lambda e, pp=pp, c=c, hy=hy: e.tensor_copy(out=hy[:, c, :], in_=pp[:]),
                             reads=[pp], accs=[hy])
                P.dma("sp", C.d_hyT[s][:, :, t0:t0 + BLK], hy[:], reads=[hy], accs=[C.d_hyT[s]])


DILS = (1, 4, 16)
FINAL_SRC = 4
import os
ATT_STOP = int(os.environ.get("ATT_STOP", "9"))
ATT_GROUPS = tuple(int(x) for x in os.environ.get("ATT_GROUPS", "0,1,2").split(","))


def stage_attention(P, C):
    with P.scope():
        tabs = P.sbuf([128, 9, 4, 128], F32, "abias")
        P.dma("sp", tabs[:], C.d_abias[:], reads=[C.d_abias], writes=[tabs])
        qTs = Rot([P.sbuf([128, 2, L], BF16, "qT") for _ in range(2)])
        kTs = Rot([P.sbuf([128, 2, L], BF16, "kT") for _ in range(2)])
        vts = Rot([P.sbuf([128, 4, 65], BF16, "vt") for _ in range(4)])
        pss = Rot([P.psum([128, 512], F32, "pss") for _ in range(4)])
        pos = Rot([P.psum([128, 512], F32, "pso") for _ in range(2)])
        sbs = Rot([P.sbuf([128, 4, 128], F32, "ssb") for _ in range(2)])
        pTs = Rot([P.sbuf([128, 4, 128], BF16, "pT") for _ in range(4)])
        osb = Rot([P.sbuf([128, 4, 65], F32, "osb") for _ in range(3)])
        cnt = 0
        for s in range(NS):
            for g, d in enumerate(DILS):
                if g not in ATT_GROUPS:
                    continue
                qT = qTs.next()
                kT = kTs.next()
                P.dma("sp", qT[:], C.d_qkT[s][:, 2 * g:2 * g + 2, :], reads=[C.d_qkT[s]], writes=[qT])
                P.dma("act", kT[:], C.d_qkT[s][:, 6 + 2 * g:6 + 2 * g + 2, :], reads=[C.d_qkT[s]], writes=[kT])
                n = L // d
                ntile = n // 128
                for r in range(d):
                    def load_v(m):
                        vt = vts.next()
                        if m == 0:
                            ks, nk = 0, 64
                        elif m == ntile:
                            ks, nk = n - 64, 64
                        else:
                            ks, nk = 128 * m - 64, 128
                        t0 = ks * d + r
                        P.dma("sp", vt[0:nk], C.d_vaug[s][t0:t0 + (nk - 1) * d + 1:d, 4 * g:4 * g + 4, :],
                              reads=[C.d_vaug[s]], writes=[vt])
                        return vt, ks, nk
                    vcur = load_v(0)
                    for qt in range(ntile):
                        vnext = load_v(qt + 1)
                        i0 = qt * 128
                        q0 = i0 * d + r
                        qsl = slice(q0, q0 + 127 * d + 1, d)
                        pTl = []
                        if ATT_STOP <= 1:
                            vcur = vnext
                            continue
                        for bi, (vt, ks, nk) in enumerate((vcur, vnext)):
                            if bi == 0:
                                ti = 3 * g + (2 if qt == 0 else 0)
                            else:
                                ti = 3 * g + 1
                            k0 = ks * d + r
                            ksl = slice(k0, k0 + (nk - 1) * d + 1, d)
                            sb = sbs.next()
                            pT = pTs.next()
                            for hh in range(2):
                                ps = pss.next()
                                psv = ps[:, 0:256].rearrange("p (a q) -> p a q", q=128)
                                for hp in range(2):
                                    mm(P, ps, psv[0:nk, hp, :], kT, kT[hh * 64:(hh + 1) * 64, hp, ksl],
                                       qT, qT[hh * 64:(hh + 1) * 64, hp, qsl], True, True)
                                P.op("dve", lambda e, ps=ps, psv=psv, sb=sb, nk=nk, ti=ti, hh=hh: e.scalar_tensor_tensor(
                                    out=sb[0:nk, hh::2, :], in0=psv[0:nk], scalar=0.125, in1=tabs[0:nk, ti, hh::2, :],
                                    op0=ALU.mult, op1=ALU.add), reads=[ps, tabs], accs=[sb])
                            P.op("act", lambda e, sb=sb, pT=pT, nk=nk: e.activation(
                                out=pT[0:nk], in_=sb[0:nk], func=AF.Exp), reads=[sb], writes=[pT])
                            pTl.append((pT, vt, nk))
                        if ATT_STOP <= 2:
                            vcur = vnext
                            continue
                        po = pos.next()
                        pov = po[:, 0:260].rearrange("p (h d) -> p h d", d=65)
                        for h in range(4):
                            for bi, (pT, vt, nk) in enumerate(pTl):
                                mm(P, po, pov[:, h, :], pT, pT[0:nk, h, :], vt, vt[0:nk, h, :], bi == 0, bi == 1)
                        ob = osb.next()
                        if cnt % 2 == 0:
                            P.op("act", lambda e, ob=ob, pov=pov: e.copy(out=ob[:], in_=pov), reads=[po], writes=[ob])
                        else:
                            P.op("dve", lambda e, ob=ob, pov=pov: e.tensor_copy(out=ob[:], in_=pov),
                                 reads=[po], writes=[ob])
                        cnt += 1
                        if ATT_STOP <= 3:
                            vcur = vnext
                            continue
                        P.dma("sp", C.d_oacc[s][g, q0:q0 + 127 * d + 1:d, :, :], ob[:],
                              reads=[ob], accs=[C.d_oacc[s]])
                        vcur = vnext


HY_GRP = 32
TWO_PI = 2.0 * math.pi


def alloc_fft(P, C):
    F = Ctx()
    F.w128 = P.sbuf([64, 256], F32, "w128")
    F.tw = P.sbuf([128, 2, 128], F32, "tw")
    F.bd = P.sbuf([128, 3, 128], F32, "bd")
    F.bdc = P.sbuf([128, 2, 256], F32, "bdc")
    F.twi = P.sbuf([128, 2, 128], F32, "twi")
    F.vinv = P.sbuf([128, 2, 64], F32, "vinv")
    for b, d in ((F.w128, C.d_w128), (F.tw, C.d_tw), (F.bd, C.d_bd), (F.bdc, C.d_bdc), (F.twi, C.d_twi),
                 (F.vinv, C.d_vinv)):
        P.dma("act", b[:], d[:], reads=[d], writes=[b])
    F.psA = Rot([P.psum([128, 512], F32, "psA") for _ in range(2)])
    F.psX = Rot([P.psum([128, 512], F32, "psX") for _ in range(2)])
    F.p1 = Rot([P.sbuf([128, 2, 128], F32, "fp1") for _ in range(2)])
    F.p2 = Rot([P.sbuf([128, 2, 128], F32, "fp2") for _ in range(2)])
    F.b = Rot([P.sbuf([128, 2, 128], F32, "fb") for _ in range(2)])
    return F


def cmul(P, F, src_buf, src_v, tab_buf, tab_r, tab_i, out_buf, out_v):
    p1 = F.p1.next()
    p2 = F.p2.next()
    n = src_v.shape[0]
    P.op("dve", lambda e: e.tensor_tensor(out=p1[0:n], in0=src_v, in1=tab_r.unsqueeze(1).to_broadcast([n, 2, 128]),
                                          op=ALU.mult), reads=[src_buf, tab_buf], writes=[p1])
    P.op("dve", lambda e: e.tensor_tensor(out=p2[0:n], in0=src_v, in1=tab_i.unsqueeze(1).to_broadcast([n, 2, 128]),
                                          op=ALU.mult), reads=[src_buf, tab_buf], writes=[p2])
    P.op("dve", lambda e: e.tensor_tensor(out=out_v[:, 0, :], in0=p1[0:n, 0, :], in1=p2[0:n, 1, :], op=ALU.subtract),
         reads=[p1, p2], accs=[out_buf])
    P.op("dve", lambda e: e.tensor_tensor(out=out_v[:, 1, :], in0=p2[0:n, 0, :], in1=p1[0:n, 1, :], op=ALU.add),
         reads=[p1, p2], accs=[out_buf])


def fft_fwd_pair(P, F, xin_buf, xin_ap):
    psA = F.psA.next()
    mm(P, psA, psA[:, 0:256], xin_buf, xin_ap, F.w128, F.w128[:], True, True)
    b = F.b.next()
    cmul(P, F, psA, psA[:, 0:256].rearrange("p (c k) -> p c k", k=128), F.tw, F.tw[:, 0, :], F.tw[:, 1, :], b, b[:])
    psX = F.psX.next()
    mm(P, psX, psX[:, 0:128], F.bd, F.bd[:, 0, :], b, b[:, 0, :], True, False)
    mm(P, psX, psX[:, 0:128], F.bd, F.bd[:, 2, :], b, b[:, 1, :], False, True)
    mm(P, psX, psX[:, 128:256], F.bd, F.bd[:, 1, :], b, b[:, 0, :], True, False)
    mm(P, psX, psX[:, 128:256], F.bd, F.bd[:, 0, :], b, b[:, 1, :], False, True)
    return psX, psX[:, 0:256].rearrange("p (c k) -> p c k", k=128)


def fft_fwd_multi(P, F, inputs):
    psAs = []
    for buf, ap in inputs:
        psA = F.psA.next()
        mm(P, psA, psA[:, 0:256], buf, ap, F.w128, F.w128[:], True, True)
        psAs.append(psA)
    bs = []
    for psA in psAs:
        b = F.b.next()
        cmul(P, F, psA, psA[:, 0:256].rearrange("p (c k) -> p c k", k=128), F.tw, F.tw[:, 0, :], F.tw[:, 1, :], b, b[:])
        bs.append(b)
    outs = []
    for b in bs:
        psX = F.psX.next()
        mm(P, psX, psX[:, 0:128], F.bd, F.bd[:, 0, :], b, b[:, 0, :], True, False)
        mm(P, psX, psX[:, 0:128], F.bd, F.bd[:, 2, :], b, b[:, 1, :], False, True)
        mm(P, psX, psX[:, 128:256], F.bd, F.bd[:, 1, :], b, b[:, 0, :], True, False)
        mm(P, psX, psX[:, 128:256], F.bd, F.bd[:, 0, :], b, b[:, 1, :], False, True)
        outs.append((psX, psX[:, 0:256].rearrange("p (c k) -> p c k", k=128)))
    return outs


def wrap_pi(P, u, m, n, W):
    for _ in range(2):
        P.op("dve", lambda e: e.tensor_single_scalar(out=m[0:n, 0:W], in_=u[0:n, 0:W], scalar=math.pi, op=ALU.is_gt),
             reads=[u], writes=[m])
        P.op("dve", lambda e: e.scalar_tensor_tensor(out=u[0:n, 0:W], in0=m[0:n, 0:W], scalar=-TWO_PI, in1=u[0:n, 0:W],
                                                     op0=ALU.mult, op1=ALU.add), reads=[m, u], writes=[u])
        P.op("dve", lambda e: e.tensor_single_scalar(out=m[0:n, 0:W], in_=u[0:n, 0:W], scalar=-math.pi, op=ALU.is_lt),
             reads=[u], writes=[m])
        P.op("dve", lambda e: e.scalar_tensor_tensor(out=u[0:n, 0:W], in0=m[0:n, 0:W], scalar=TWO_PI, in1=u[0:n, 0:W],
                                                     op0=ALU.mult, op1=ALU.add), reads=[m, u], writes=[u])


def stage_hyena_filter(P, C):
    with P.scope():
        hcol = P.sbuf([64, 8], F32, "hcol")
        P.dma("sp", hcol[:, 0:4], C.d_hcol[:], reads=[C.d_hcol], writes=[hcol])
        for i in range(3):
            P.op("dve", lambda e, i=i: e.tensor_tensor(out=hcol[:, 4 + i:5 + i], in0=hcol[:, i:i + 1],
                                                       in1=hcol[:, 3:4], op=ALU.mult), reads=[hcol], writes=[hcol])
        w1 = P.sbuf([33, 64], F32, "fw1")
        w23 = P.sbuf([64, 2, 64], F32, "fw23")
        w4 = P.sbuf([64, 512], F32, "fw4")
        fbias = P.sbuf([128, 2], F32, "fbias")
        P.dma("sp", w1[:], C.d_fw1[:], reads=[C.d_fw1], writes=[w1])
        P.dma("sp", w23[:], C.d_fw23[:], reads=[C.d_fw23], writes=[w23])
        P.dma("sp", w4[:], C.d_fw4[:], reads=[C.d_fw4], writes=[w4])
        P.dma("sp", fbias[:], C.d_fbias[:], reads=[C.d_fbias], writes=[fbias])
        hT = P.sbuf([128, 4, L], F32, "filt_hT")
        pos = Rot([P.sbuf([33, BLK], F32, "pos") for _ in range(2)])
        win = Rot([P.sbuf([128, 2, BLK], F32, "win") for _ in range(2)])
        us = Rot([P.sbuf([64, BLK], F32, "fu") for _ in range(3)])
        ms = Rot([P.sbuf([64, BLK], F32, "fm") for _ in range(2)])
        pps = Rot([P.psum([128, BLK], F32, "fpp") for _ in range(2)])
        for blk in range(NBLK):
            t0 = blk * BLK
            po = pos.next()
            wn = win.next()
            P.dma("sp", po[:], C.d_posT[:, t0:t0 + BLK], reads=[C.d_posT], writes=[po])
            P.dma("act", wn[:], C.d_winT[:, :, t0:t0 + BLK], reads=[C.d_winT], writes=[wn])
            prev_buf, prev_ap, kdim = po, po[:], 33
            for layer in range(3):
                pp = pps.next()
                if layer == 0:
                    mm(P, pp, pp[0:64, :], w1, w1[:], prev_buf, prev_ap, True, True)
                else:
                    mm(P, pp, pp[0:64, :], w23, w23[:, layer - 1, :], prev_buf, prev_ap, True, True)
                u = us.next()
                m = ms.next()
                P.op("dve", lambda e, pp=pp, u=u, layer=layer: e.tensor_scalar(
                    out=u[:], in0=pp[0:64, :], scalar1=hcol[:, 3:4], scalar2=hcol[:, 4 + layer:5 + layer],
                    op0=ALU.mult, op1=ALU.add), reads=[pp, hcol], writes=[u])
                wrap_pi(P, u, m, 64, BLK)
                P.op("act", lambda e, u=u: e.activation(out=u[:], in_=u[:], func=AF.Sin), reads=[u], writes=[u])
                prev_buf, prev_ap = u, u[:]
            for c in range(4):
                pp = pps.next()
                mm(P, pp, pp[:], w4, w4[:, c * 128:(c + 1) * 128], prev_buf, prev_ap, True, True)
                P.op("dve", lambda e, pp=pp, c=c, wn=wn, t0=t0: e.tensor_tensor(
                    out=hT[:, c, t0:t0 + BLK], in0=pp[:], in1=wn[:, c % 2, :], op=ALU.mult),
                    reads=[pp, wn], accs=[hT])
        junk = P.sbuf([128, L], F32, "fjunk")
        acc = P.sbuf([128, 8], F32, "facc")
        P.op("pool", lambda e: e.memset(acc[:], 0.0), writes=[acc])
        for c in range(4):
            lo = 0 if c < 2 else 1
            P.op("act", lambda e, c=c, lo=lo: e.activation(out=junk[:, lo:L], in_=hT[:, c, lo:L], func=AF.Abs,
                                                           accum_out=acc[:, c:c + 1]),
                 reads=[hT], writes=[junk, acc])
        P.op("dve", lambda e: e.tensor_tensor(out=acc[:, 4:6], in0=acc[:, 0:2], in1=acc[:, 2:4], op=ALU.add),
             reads=[acc], writes=[acc])
        P.op("dve", lambda e: e.reciprocal(out=acc[:, 6:8], in_=acc[:, 4:6]), reads=[acc], writes=[acc])
        for c in range(4):
            P.op("dve", lambda e, c=c: e.tensor_scalar(
                out=hT[:, c, :], in0=hT[:, c, :], scalar1=acc[:, 6 + c % 2:7 + c % 2], scalar2=None, op0=ALU.mult),
                reads=[hT, acc], writes=[hT])
        for j in range(2):
            P.op("dve", lambda e, j=j: e.tensor_tensor(out=hT[:, j, 0:1], in0=hT[:, j, 0:1], in1=fbias[:, j:j + 1],
                                                       op=ALU.add), reads=[hT, fbias], writes=[hT])
            P.op("dve", lambda e, j=j: e.memset(hT[:, 2 + j, 0:1], 0.0), writes=[hT])
        P.dma("sp", C.d_filtT[:], hT[:], reads=[hT], writes=[C.d_filtT])
    with P.scope():
        F = alloc_fft(P, C)
        xf = Rot([P.sbuf([64, HY_GRP, 64], F32, "xf") for _ in range(2)])
        xb = Rot([P.sbuf([64, HY_GRP, 64], F32, "xb") for _ in range(2)])
        fo = Rot([P.sbuf([128, HY_GRP // 2, 2, 128], F32, "fo") for _ in range(2)])
        for j in range(2):
            for c0 in range(0, 128, HY_GRP):
                a = xf.next()
                b = xb.next()
                P.dma("sp", a[:], C.d_filtT[c0:c0 + HY_GRP, j, :].rearrange("c (a b) -> a c b", b=64),
                      reads=[C.d_filtT], writes=[a])
                P.dma("act", b[:], C.d_filtT[c0:c0 + HY_GRP, 2 + j, :].rearrange("c (a b) -> a c b", b=64),
                      reads=[C.d_filtT], writes=[b])
                o = fo.next()
                for i in range(HY_GRP // 2):
                    (psf, vf), (psb, vb) = fft_fwd_multi(P, F, [
                        (a, a[:, 2 * i:2 * i + 2, :].rearrange("p c n -> p (c n)")),
                        (b, b[:, 2 * i:2 * i + 2, :].rearrange("p c n -> p (c n)"))])
                    P.op("act", lambda e, o=o, i=i, vb=vb: e.copy(out=o[:, i], in_=vb), reads=[psb], accs=[o])
                    P.op("dve", lambda e, o=o, i=i, vf=vf: e.tensor_tensor(out=o[:, i, 0, :], in0=vf[:, 0, :],
                                                                           in1=o[:, i, 0, :], op=ALU.add),
                         reads=[psf, o], accs=[o])
                    P.op("dve", lambda e, o=o, i=i, vf=vf: e.tensor_tensor(out=o[:, i, 1, :], in0=vf[:, 1, :],
                                                                           in1=o[:, i, 1, :], op=ALU.subtract),
                         reads=[psf, o], accs=[o])
                pr0 = (j * 128 + c0) // 2
                P.dma("sp", C.d_F[:, pr0:pr0 + HY_GRP // 2], o[:], reads=[o], accs=[C.d_F])


def dwconv3(P, src, dst, W, wt, c):
    P.op("act", lambda e: e.activation(out=dst[:, 0:W], in_=src[:, 0:W], func=AF.Identity,
                                       scale=wt[:, c, 1:2], bias=wt[:, c, 3:4]), reads=[src, wt], writes=[dst])
    P.op("dve", lambda e: e.scalar_tensor_tensor(out=dst[:, 1:W], in0=src[:, 0:W - 1], scalar=wt[:, c, 0:1],
                                                 in1=dst[:, 1:W], op0=ALU.mult, op1=ALU.add),
         reads=[src, wt, dst], writes=[dst])
    P.op("dve", lambda e: e.scalar_tensor_tensor(out=dst[:, 0:W - 1], in0=src[:, 1:W], scalar=wt[:, c, 2:3],
                                                 in1=dst[:, 0:W - 1], op0=ALU.mult, op1=ALU.add),
         reads=[src, wt, dst], writes=[dst])


def stage_hyena(P, C):
    with P.scope():
        swt = P.sbuf([128, 6, 4], F32, "shortw")
        P.dma("sp", swt[:], C.d_shortw[:], reads=[C.d_shortw], writes=[swt])
        srcs = Rot([P.sbuf([128, L], F32, "hysrc") for _ in range(3)])
        dsts = Rot([P.sbuf([128, L], F32, "hydst") for _ in range(4)])
        for s in range(NS):
            for j in range(2):
                res = []
                for part in range(3):
                    c = 2 * part + j
                    src = srcs.next()
                    P.dma("sp" if part != 1 else "act", src[:], C.d_hyT[s][:, c, :], reads=[C.d_hyT[s]], writes=[src])
                    dst = dsts.next()
                    dwconv3(P, src, dst, L, swt, c)
                    res.append(dst)
                x0, x1, v = res
                P.op("dve", lambda e, x1=x1, v=v: e.tensor_tensor(out=v[:], in0=v[:], in1=x1[:], op=ALU.mult),
                     reads=[v, x1], writes=[v])
                P.dma("sp", C.d_zT[s][:, j, :], v[:], reads=[v], accs=[C.d_zT[s]])
                P.dma("sp", C.d_x0T[s][:, j, :], x0[:], reads=[x0], accs=[C.d_x0T[s]])
    with P.scope():
        F = alloc_fft(P, C)
        psC = Rot([P.psum([128, 512], F32, "psC") for _ in range(2)])
        psY = Rot([P.psum([128, 512], F32, "psY") for _ in range(2)])
        zin = Rot([P.sbuf([64, HY_GRP, 64], F32, "zin") for _ in range(2)])
        x0in = Rot([P.sbuf([64, HY_GRP, 64], F32, "x0in") for _ in range(2)])
        oin = Rot([P.sbuf([64, HY_GRP, 64], F32, "oin") for _ in range(2)])
        fsp = Rot([P.sbuf([128, HY_GRP // 2, 2, 128], F32, "fsp") for _ in range(2)])
        ys = Rot([P.sbuf([128, 2, 128], F32, "fy") for _ in range(2)])
        ds = Rot([P.sbuf([128, 2, 128], F32, "fd") for _ in range(2)])
        for s in range(NS):
            for j in range(2):
                for c0 in range(0, 128, HY_GRP):
                    zi = zin.next()
                    xi = x0in.next()
                    fs = fsp.next()
                    oi = oin.next()
                    P.dma("sp", zi[:], C.d_zT[s][c0:c0 + HY_GRP, j, :].rearrange("c (a b) -> a c b", b=64),
                          reads=[C.d_zT[s]], writes=[zi])
                    P.dma("act", xi[:], C.d_x0T[s][c0:c0 + HY_GRP, j, :].rearrange("c (a b) -> a c b", b=64),
                          reads=[C.d_x0T[s]], writes=[xi])
                    pr0 = (j * 128 + c0) // 2
                    P.dma("sp", fs[:], C.d_F[:, pr0:pr0 + HY_GRP // 2], reads=[C.d_F], writes=[fs])
                    for i0 in range(0, HY_GRP // 2, 2):
                        pr = (i0, i0 + 1)
                        st = {}
                        for i in pr:
                            psA = F.psA.next()
                            mm(P, psA, psA[:, 0:256], zi, zi[:, 2 * i:2 * i + 2, :].rearrange("p c n -> p (c n)"),
                               F.w128, F.w128[:], True, True)
                            st[i] = dict(psA=psA)
                        for i in pr:
                            b = F.b.next()
                            psA = st[i]["psA"]
                            cmul(P, F, psA, psA[:, 0:256].rearrange("p (c k) -> p c k", k=128), F.tw, F.tw[:, 0, :],
                                 F.tw[:, 1, :], b, b[:])
                            st[i]["b"] = b
                        for i in pr:
                            b = st[i]["b"]
                            psX = F.psX.next()
                            mm(P, psX, psX[:, 0:128], F.bd, F.bd[:, 0, :], b, b[:, 0, :], True, False)
                            mm(P, psX, psX[:, 0:128], F.bd, F.bd[:, 2, :], b, b[:, 1, :], False, True)
                            mm(P, psX, psX[:, 128:256], F.bd, F.bd[:, 1, :], b, b[:, 0, :], True, False)
                            mm(P, psX, psX[:, 128:256], F.bd, F.bd[:, 0, :], b, b[:, 1, :], False, True)
                            st[i]["psX"] = psX
                        for i in pr:
                            psX = st[i]["psX"]
                            y = ys.next()
                            cmul(P, F, psX, psX[:, 0:256].rearrange("p (c k) -> p c k", k=128), fs, fs[:, i, 0, :],
                                 fs[:, i, 1, :], y, y[:])
                            st[i]["y"] = y
                        for i in pr:
                            y = st[i]["y"]
                            pc = psC.next()
                            mm(P, pc, pc[:, 0:256], y, y[:, 0, :], F.bdc, F.bdc[:, 0, :], True, False)
                            mm(P, pc, pc[:, 0:256], y, y[:, 1, :], F.bdc, F.bdc[:, 1, :], False, True)
                            st[i]["pc"] = pc
                        for i in pr:
                            pc = st[i]["pc"]
                            dd = ds.next()
                            cmul(P, F, pc, pc[:, 0:256].rearrange("p (c k) -> p c k", k=128), F.twi, F.twi[:, 0, :],
                                 F.twi[:, 1, :], dd, dd[:])
                            st[i]["dd"] = dd
                        for i in pr:
                            dd = st[i]["dd"]
                            py = psY.next()
                            mm(P, py, py[0:64, 0:128], F.vinv, F.vinv[:, 0, :], dd, dd[:, 0, :], True, False)
                            mm(P, py, py[0:64, 0:128], F.vinv, F.vinv[:, 1, :], dd, dd[:, 1, :], False, True)
                            st[i]["py"] = py
                        for i in pr:
                            py = st[i]["py"]
                            P.op("dve", lambda e, py=py, oi=oi, xi=xi, i=i: e.tensor_tensor(
                                out=oi[:, 2 * i:2 * i + 2, :].rearrange("p c n -> p (c n)"), in0=py[0:64, 0:128],
                                in1=xi[:, 2 * i:2 * i + 2, :].rearrange("p c n -> p (c n)"), op=ALU.mult),
                                reads=[py, xi], accs=[oi])
                    P.dma("sp", C.d_hyoT[s][c0:c0 + HY_GRP, j, :].rearrange("c (a b) -> a c b", b=64), oi[:],
                          reads=[oi], accs=[C.d_hyoT[s]])


def stage_l0_outproj(P, C):
    with P.scope():
        C.wstage_n = 2048
        C.wstage = Rot([P.sbuf([128, 2048], F32, "wst") for _ in range(2)])
        w_out = load_weight_bf16(P, C, C.d_w_out, 4, D, "w_out")
        identb = P.sbuf([128, 128], BF16, "identb")
        P.op("dve", lambda e: e.tensor_copy(out=identb[:], in_=C.ident32[:]), reads=[C.ident32], writes=[identb])
        alloc_norm_scratch(P, C)
        oas = Rot([P.sbuf([128, 4, 3, 260], F32, "oa") for _ in range(2)])
        o2s = Rot([P.sbuf([128, 4, 260], F32, "o2") for _ in range(2)])
        rds = Rot([P.sbuf([128, 4, 4], F32, "rden") for _ in range(2)])
        abs_ = Rot([P.sbuf([128, 4, 4, 64], BF16, "attnb") for _ in range(2)])
        hyl = Rot([P.sbuf([128, 2, BLK], F32, "hyl") for _ in range(2)])
        mixs = Rot([P.sbuf([128, 4, BLK], BF16, "mixT") for _ in range(2)])
        xrs = Rot([P.sbuf([128, NK, BLK], F32, "xr") for _ in range(2)])
        x1s = Rot([P.sbuf([128, NK, BLK], F32, "x1T") for _ in range(2)])
        h2s = Rot([P.sbuf([128, NK, BLK], BF16, "h2T") for _ in range(2)])
        ptb = Rot([P.psum([128, BLK], BF16, "ptb") for _ in range(2)])
        pps = Rot([P.psum([128, BLK], F32, "pp") for _ in range(2)])
        g1 = C.gatev[0][0]
        for s in range(NS):
            for blk in range(NBLK):
                t0 = blk * BLK
                oa = oas.next()
                for g in range(3):
                    P.dma("sp" if g != 1 else "act", oa[:, :, g, :],
                          C.d_oacc[s][g, t0:t0 + BLK].rearrange("(j p) h d -> p j (h d)", p=128),
                          reads=[C.d_oacc[s]], accs=[oa])
                hy = hyl.next()
                P.dma("act", hy[:], C.d_hyoT[s][:, :, t0:t0 + BLK], reads=[C.d_hyoT[s]], writes=[hy])
                xr = xrs.next()
                P.dma("sp", xr[:], C.d_xT[0][s][:, :, t0:t0 + BLK], reads=[C.d_xT[0][s]], writes=[xr])
                o2 = o2s.next()
                P.op("dve", lambda e, oa=oa, o2=o2: e.tensor_tensor(out=o2[:], in0=oa[:, :, 0, :], in1=oa[:, :, 1, :],
                                                                    op=ALU.add), reads=[oa], writes=[o2])
                P.op("dve", lambda e, oa=oa, o2=o2: e.tensor_tensor(out=o2[:], in0=o2[:], in1=oa[:, :, 2, :],
                                                                    op=ALU.add), reads=[oa, o2], writes=[o2])
                o2v = o2[:].rearrange("p j (h d) -> p j h d", d=65)
                rd = rds.next()
                P.op("dve", lambda e, o2v=o2v, rd=rd: e.reciprocal(out=rd[:], in_=o2v[:, :, :, 64]),
                     reads=[o2], writes=[rd])
                ab = abs_.next()
                P.op("dve", lambda e, o2v=o2v, rd=rd, ab=ab: e.tensor_tensor(
                    out=ab[:], in0=o2v[:, :, :, 0:64], in1=rd[:].unsqueeze(3).to_broadcast([128, 4, 4, 64]),
                    op=ALU.mult), reads=[o2, rd], writes=[ab])
                mix = mixs.next()
                for c in range(2):
                    pt = ptb.next()
                    for j in range(4):
                        P.op("pe", lambda e, pt=pt, j=j, c=c, ab=ab: e.transpose(
                            pt[:, j * 128:(j + 1) * 128],
                            ab[:, j, 2 * c:2 * c + 2, :].rearrange("p h d -> p (h d)"), identb[:]),
                            reads=[ab, identb], writes=[pt])
                    P.op("act", lambda e, pt=pt, c=c, mix=mix: e.copy(out=mix[:, c, :], in_=pt[:]),
                         reads=[pt], accs=[mix])
                P.op("act", lambda e, hy=hy, mix=mix: e.copy(out=mix[:, 2:4, :], in_=hy[:]),
                     reads=[hy], accs=[mix])
                x1 = x1s.next()
                for c in range(NK):
                    pp = pps.next()
                    for k in range(4):
                        mm(P, pp, pp[:], w_out, w_out[:, k, c * 128:(c + 1) * 128], mix, mix[:, k, :], k == 0, k == 3)
                    P.op("dve", lambda e, pp=pp, c=c, x1=x1, xr=xr, s=s: e.scalar_tensor_tensor(
                        out=x1[:, c, :], in0=pp[:], scalar=g1[:, c, s:s + 1], in1=xr[:, c, :],
                        op0=ALU.mult, op1=ALU.add), reads=[pp, g1, xr], accs=[x1])
                P.dma("sp", C.d_xT[1][s][:, :, t0:t0 + BLK], x1[:], reads=[x1], accs=[C.d_xT[1][s]])
                h2 = h2s.next()
                norm_mod(P, C, x1, BLK, C.gain[0][1], C.shiftv[0][1], s, h2)
                P.dma("sp", C.d_h2T[0][s][:, :, t0:t0 + BLK], h2[:], reads=[h2], accs=[C.d_h2T[0][s]])


GELU_C = 0.044715
GELU_S = 2.0 * math.sqrt(2.0 / math.pi)


def stage_ffn(P, C, l, xin_idx, xout_idx):
    with P.scope():
        C.wstage_n = 1024
        C.wstage = Rot([P.sbuf([128, 1024], F32, "wst") for _ in range(2)])
        w_up = load_weight_bf16(P, C, C.d_ffn_up[l], NK, 2 * DFF, "w_up")
        w_dn = load_weight_bf16(P, C, C.d_ffn_dn[l], NFC, D, "w_dn")
        cw = P.sbuf([128, NFC, 4], F32, "convw")
        P.dma("sp", cw[:], C.d_ffn_cw[l][:], reads=[C.d_ffn_cw[l]], writes=[cw])
        hhs = Rot([P.sbuf([128, NK, BLK + 2], BF16, "hh") for _ in range(2)])
        actT = P.sbuf([128, NFC, BLK], BF16, "actT")
        xcs = Rot([P.sbuf([128, BLK], F32, "xc") for _ in range(2)])
        xos = Rot([P.sbuf([128, BLK], F32, "xo") for _ in range(2)])
        tmp = {n: Rot([P.sbuf([128, BLK], F32, n) for _ in range(2)]) for n in ("cv", "sq")}
        pas = Rot([P.psum([128, BLK], F32, "pa") for _ in range(2)])
        phs = Rot([P.psum([128, BLK], F32, "ph") for _ in range(1)])
        pgs = Rot([P.psum([128, BLK], F32, "pg") for _ in range(2)])
        pos = Rot([P.psum([128, BLK], F32, "po") for _ in range(2)])
        g2 = C.gatev[l][1]
        for s in range(NS):
            for blk in range(NBLK):
                t0 = blk * BLK
                hh = hhs.next()
                lo = max(t0 - 1, 0)
                hi = min(t0 + BLK + 1, L)
                c_lo = lo - (t0 - 1)
                P.dma("sp", hh[:, :, c_lo:c_lo + (hi - lo)], C.d_h2T[l][s][:, :, lo:hi], reads=[C.d_h2T[l][s]],
                      writes=[hh])
                if blk == 0:
                    P.op("pool", lambda e, hh=hh: e.memset(hh[:, :, 0:1], 0.0), accs=[hh])
                if blk == NBLK - 1:
                    P.op("pool", lambda e, hh=hh: e.memset(hh[:, :, BLK + 1:BLK + 2], 0.0), accs=[hh])
                for c in range(NFC):
                    pa = pas.next()
                    ph = phs.next()
                    pg = pgs.next()
                    for k in range(NK):
                        mm(P, pa, pa[:], w_up, w_up[:, k, c * 128:(c + 1) * 128], hh, hh[:, k, 1:BLK + 1],
                           k == 0, k == NK - 1)
                    for k in range(NK):
                        mm(P, ph, ph[:, 0:2], w_up, w_up[:, k, c * 128:(c + 1) * 128], hh, hh[:, k, 0:BLK + 2:BLK + 1],
                           k == 0, k == NK - 1)
                    for k in range(NK):
                        mm(P, pg, pg[:], w_up, w_up[:, k, DFF + c * 128:DFF + (c + 1) * 128], hh, hh[:, k, 1:BLK + 1],
                           k == 0, k == NK - 1)
                    cv = tmp["cv"].next()
                    P.op("act", lambda e, pa=pa, cv=cv, c=c: e.activation(
                        out=cv[:], in_=pa[:], func=AF.Identity, scale=cw[:, c, 1:2], bias=cw[:, c, 3:4]),
                        reads=[pa, cw], writes=[cv])
                    P.op("dve", lambda e, pa=pa, cv=cv, c=c: e.scalar_tensor_tensor(
                        out=cv[:, 1:BLK], in0=pa[:, 0:BLK - 1], scalar=cw[:, c, 0:1], in1=cv[:, 1:BLK],
                        op0=ALU.mult, op1=ALU.add), reads=[pa, cw, cv], writes=[cv])
                    P.op("dve", lambda e, pa=pa, cv=cv, c=c: e.scalar_tensor_tensor(
                        out=cv[:, 0:BLK - 1], in0=pa[:, 1:BLK], scalar=cw[:, c, 2:3], in1=cv[:, 0:BLK - 1],
                        op0=ALU.mult, op1=ALU.add), reads=[pa, cw, cv], writes=[cv])
                    P.op("dve", lambda e, ph=ph, cv=cv, c=c: e.scalar_tensor_tensor(
                        out=cv[:, 0:1], in0=ph[:, 0:1], scalar=cw[:, c, 0:1], in1=cv[:, 0:1],
                        op0=ALU.mult, op1=ALU.add), reads=[ph, cw, cv], writes=[cv])
                    P.op("dve", lambda e, ph=ph, cv=cv, c=c: e.scalar_tensor_tensor(
                        out=cv[:, BLK - 1:BLK], in0=ph[:, 1:2], scalar=cw[:, c, 2:3], in1=cv[:, BLK - 1:BLK],
                        op0=ALU.mult, op1=ALU.add), reads=[ph, cw, cv], writes=[cv])
                    tt = tmp["sq"].next()
                    P.op("act", lambda e, cv=cv, tt=tt: e.activation(out=tt[:], in_=cv[:], func=AF.Gelu_apprx_tanh),
                         reads=[cv], writes=[tt])
                    P.op("dve", lambda e, tt=tt, pg=pg, c=c: e.tensor_tensor(out=actT[:, c, :], in0=pg[:], in1=tt[:],
                                                                             op=ALU.mult),
                         reads=[tt, pg], accs=[actT])
                for c in range(NK):
                    xc = xcs.next()
                    P.dma("act", xc[:], C.d_xT[xin_idx][s][:, c, t0:t0 + BLK], reads=[C.d_xT[xin_idx][s]], writes=[xc])
                    po = pos.next()
                    for k in range(NFC):
                        mm(P, po, po[:], w_dn, w_dn[:, k, c * 128:(c + 1) * 128], actT, actT[:, k, :],
                           k == 0, k == NFC - 1)
                    xo = xos.next()
                    P.op("dve", lambda e, po=po, c=c, xo=xo, xc=xc, s=s: e.scalar_tensor_tensor(
                        out=xo[:], in0=po[:], scalar=g2[:, c, s:s + 1], in1=xc[:], op0=ALU.mult, op1=ALU.add),
                        reads=[po, g2, xc], writes=[xo])
                    P.dma("sp", C.d_xT[xout_idx][s][:, c, t0:t0 + BLK], xo[:], reads=[xo],
                          accs=[C.d_xT[xout_idx][s]])


def stage_final(P, C, xin_idx):
    with P.scope():
        alloc_norm_scratch(P, C)
        xrs = Rot([P.sbuf([128, NK, BLK], F32, "xr") for _ in range(2)])
        yTs = Rot([P.sbuf([128, NK, BLK], F32, "yT") for _ in range(2)])
        yts = Rot([P.sbuf([128, 4, D], F32, "ytok") for _ in range(2)])
        pts = Rot([P.psum([128, 1024], F32, "pt2") for _ in range(2)])
        cnt = 0
        for s in range(NS):
            for blk in range(NBLK):
                t0 = blk * BLK
                xr = xrs.next()
                P.dma("sp", xr[:], C.d_xT[xin_idx][s][:, :, t0:t0 + BLK], reads=[C.d_xT[xin_idx][s]], writes=[xr])
                yT = yTs.next()
                norm_mod(P, C, xr, BLK, C.gain_fin, None, s, yT)
                yt = yts.next()
                for j in range(4):
                    pt = pts.next()
                    for c in range(NK):
                        P.op("pe", lambda e, pt=pt, j=j, c=c, yT=yT: e.transpose(
                            pt[:, c * 128:(c + 1) * 128], yT[:, c, j * 128:(j + 1) * 128], C.ident32[:]),
                            reads=[yT, C.ident32], writes=[pt])
                    if cnt % 2 == 0:
                        P.op("act", lambda e, pt=pt, yt=yt, j=j: e.copy(out=yt[:, j, :], in_=pt[:]),
                             reads=[pt], accs=[yt])
                    else:
                        P.op("dve", lambda e, pt=pt, yt=yt, j=j: e.tensor_copy(out=yt[:, j, :], in_=pt[:]),
                             reads=[pt], accs=[yt])
                    cnt += 1
                P.dma("sp", C.d_y[s, t0:t0 + BLK, :].rearrange("(j p) f -> p j f", p=128), yt[:],
                      reads=[yt], accs=[C.d_y])


DECAY_C = -math.exp(-0.5)
R1_STOP = int(os.environ.get("R1_STOP", "9"))
R1_SUB = int(os.environ.get("R1_SUB", "99"))


def stage_rwkv_norm(P, C):
    with P.scope():
        alloc_norm_scratch(P, C)
        xrs = Rot([P.sbuf([128, NK, BLK], F32, "xr") for _ in range(2)])
        hs = Rot([P.sbuf([128, NK, BLK], F32, "h1") for _ in range(2)])
        for s in range(NS):
            for blk in range(NBLK):
                t0 = blk * BLK
                xr = xrs.next()
                P.dma("sp", xr[:], C.d_xT[2][s][:, :, t0:t0 + BLK], reads=[C.d_xT[2][s]], writes=[xr])
                h = hs.next()
                norm_mod(P, C, xr, BLK, C.gain[1][0], C.shiftv[1][0], s, h)
                P.dma("sp", C.d_h1T[s][:, :, t0:t0 + BLK], h[:], reads=[h], accs=[C.d_h1T[s]])


def load_weight_pair(P, C, dram_buf, nk, c_lo, ncols, name, cols, mu_i):
    wb = P.sbuf([128, nk, ncols], BF16, name)
    ws = P.sbuf([128, nk, ncols], BF16, name + "s")
    grp = max(32, (C.wstage_n // nk) // 32 * 32)
    for c0 in range(0, ncols, grp):
        cw = min(grp, ncols - c0)
        st = C.wstage.next()
        stv = st[:, 0:nk * cw].rearrange("p (k c) -> p k c", c=cw)
        P.dma("sp", stv, dram_buf[:, :, c_lo + c0:c_lo + c0 + cw], reads=[dram_buf], writes=[st])
        P.op("act", lambda e, stv=stv, c0=c0, cw=cw: e.copy(out=wb[:, :, c0:c0 + cw], in_=stv), reads=[st], accs=[wb])
        P.op("dve", lambda e, stv=stv, c0=c0, cw=cw: e.tensor_tensor(
            out=ws[:, :, c0:c0 + cw], in0=stv, in1=cols[:, mu_i, :].unsqueeze(2).to_broadcast([128, nk, cw]),
            op=ALU.mult), reads=[st, cols], accs=[ws])
    return wb, ws


def stage_rwkv_proj(P, C):
    with P.scope():
        C.wstage_n = 512
        C.wstage = Rot([P.sbuf([128, 512], F32, "wst") for _ in range(2)])
        cols = P.sbuf([128, 14, NK], F32, "rwcols")
        P.dma("sp", cols[:], C.d_rwcols[:], reads=[C.d_rwcols], writes=[cols])
        w_r = load_weight_pair(P, C, C.d_w_r, NK, 0, D, "w_r", cols, 0)
        w_k = load_weight_pair(P, C, C.d_w_k, NK, 0, D, "w_k", cols, 2)
        w_v = load_weight_pair(P, C, C.d_w_v, NK, 0, D, "w_v", cols, 3)
        w_g1 = load_weight_pair(P, C, C.d_g1, NK, 0, 256, "w_g1", cols, 5)
        w_l1w = load_weight_pair(P, C, C.d_lora1, NK, 0, 128, "w_l1w", cols, 1)
        w_l1a = load_weight_pair(P, C, C.d_lora1, NK, 128, 128, "w_l1a", cols, 4)
        w_g2 = load_weight_bf16(P, C, C.d_g2, 2, D, "w_g2")
        w_l2 = load_weight_bf16(P, C, C.d_lora2, 4, D, "w_l2")
        bones = P.sbuf([128, 128], F32, "bones")
        P.dma("sp", bones[:], C.d_bones[:], reads=[C.d_bones], writes=[bones])
        hls = Rot([P.sbuf([128, NK, BLK + 2], F32, "hl") for _ in range(1)])
        xxh = P.sbuf([128, 4, BLK], F32, "xxh")
        hb = P.sbuf([128, NK, BLK], BF16, "hb")
        xb = P.sbuf([128, NK, BLK], BF16, "xb")
        pps = Rot([P.psum([128, BLK], F32, "pp") for _ in range(3)])
        pvs = Rot([P.psum([128, 1024], F32, "pv") for _ in range(1)])
        pls = Rot([P.psum([128, BLK], F32, "pl") for _ in range(3)])
        ob16 = Rot([P.sbuf([128, BLK], BF16, "ob16") for _ in range(6)])
        of32 = Rot([P.sbuf([128, BLK], F32, "of32") for _ in range(4)])
        kf = Rot([P.sbuf([128, BLK], F32, "kf") for _ in range(3)])
        kkf = Rot([P.sbuf([128, BLK], F32, "kkf") for _ in range(2)])
        vts = Rot([P.sbuf([128, D], BF16, "vtok") for _ in range(2)])
        sgs = Rot([P.sbuf([128, 2, BLK], BF16, "sg") for _ in range(1)])
        lts = Rot([P.sbuf([64, 4, BLK], BF16, "lt") for _ in range(1)])

        def evac(ps_ap, ps_buf, dst_ap, dst_buf):
            P.op("act", lambda e: e.copy(out=dst_ap, in_=ps_ap), reads=[ps_buf], writes=[dst_buf])

        def proj(ps_buf, ps_ap, wpair, csl):
            wb, ws = wpair
            for k in range(NK):
                mm(P, ps_buf, ps_ap, wb, wb[:, k, csl], hb, hb[:, k, :], k == 0, False)
            for k in range(NK):
                mm(P, ps_buf, ps_ap, ws, ws[:, k, csl], xb, xb[:, k, :], False, k == NK - 1)

        for s in range(NS):
            for blk in range(NBLK):
                t0 = blk * BLK
                hl = hls.next()
                lo = max(t0 - 1, 0)
                hi = min(t0 + BLK + 1, L)
                c_lo = lo - (t0 - 1)
                P.dma("sp", hl[:, :, c_lo:c_lo + (hi - lo)], C.d_h1T[s][:, :, lo:hi], reads=[C.d_h1T[s]], writes=[hl])
                if blk == 0:
                    P.op("dve", lambda e, hl=hl: e.memset(hl[:, :, 0:1], 0.0), accs=[hl])
                if blk == NBLK - 1:
                    P.op("dve", lambda e, hl=hl: e.memset(hl[:, :, BLK + 1:BLK + 2], 0.0), accs=[hl])
                P.op("act", lambda e, hl=hl: e.copy(out=hb[:], in_=hl[:, :, 1:BLK + 1]), reads=[hl], writes=[hb])
                for half in range(2):
                    ks = slice(4 * half, 4 * half + 4)
                    P.op("dve", lambda e, hl=hl, ks=ks: e.tensor_tensor(
                        out=xxh[:], in0=hl[:, ks, 0:BLK], in1=hl[:, ks, 2:BLK + 2], op=ALU.add),
                        reads=[hl], writes=[xxh])
                    P.op("dve", lambda e, hl=hl, ks=ks: e.scalar_tensor_tensor(
                        out=xb[:, ks, :], in0=xxh[:], scalar=0.5, in1=hl[:, ks, 1:BLK + 1], op0=ALU.mult,
                        op1=ALU.subtract), reads=[hl, xxh], accs=[xb])
                if R1_STOP <= 1:
                    continue
                for j in range(4):
                    pv = pvs.next()
                    jsl = slice(j * 128, (j + 1) * 128)
                    for half in range(2):
                        hsl = slice(half * 512, (half + 1) * 512)
                        for k in range(NK):
                            mm(P, pv, pv[:, hsl], hb, hb[:, k, jsl], w_v[0], w_v[0][:, k, hsl], k == 0, False)
                        for k in range(NK):
                            mm(P, pv, pv[:, hsl], xb, xb[:, k, jsl], w_v[1], w_v[1][:, k, hsl], False, k == NK - 1)
                    vt = vts.next()
                    evac(pv[:], pv, vt[:], vt)
                    P.dma("sp", C.d_vtok[s][t0 + j * 128:t0 + (j + 1) * 128, :], vt[:], reads=[vt],
                          accs=[C.d_vtok[s]])
                if R1_STOP <= 2:
                    continue
                sg = sgs.next()
                for cc in range(2):
                    pl = pls.next()
                    proj(pl, pl[:], w_g1, slice(cc * 128, (cc + 1) * 128))
                    P.op("act", lambda e, pl=pl, cc=cc, sg=sg: e.activation(
                        out=sg[:, cc, :], in_=pl[:], func=AF.Sigmoid), reads=[pl], accs=[sg])
                lt = lts.next()
                for q in range(4):
                    pl = pls.next()
                    proj(pl, pl[0:64, :], w_l1w if q < 2 else w_l1a, slice((q % 2) * 64, (q % 2) * 64 + 64))
                    if q < 2:
                        P.op("act", lambda e, pl=pl, q=q, lt=lt: e.activation(out=lt[:, q, :], in_=pl[0:64, :],
                                                                              func=AF.Tanh), reads=[pl], accs=[lt])
                    else:
                        P.op("act", lambda e, pl=pl, q=q, lt=lt: e.copy(out=lt[:, q, :], in_=pl[0:64, :]),
                             reads=[pl], accs=[lt])
                if R1_STOP <= 3:
                    continue
                def phase_a(c):
                    csl = slice(c * 128, (c + 1) * 128)
                    pp = pps.next()
                    proj(pp, pp[:], w_r, csl)
                    o = ob16.next()
                    evac(pp[:], pp, o[:], o)
                    P.dma("sp", C.d_rT[s][:, c, t0:t0 + BLK], o[:], reads=[o], accs=[C.d_rT[s]])
                    pp = pps.next()
                    proj(pp, pp[:], w_v, csl)
                    o = ob16.next()
                    evac(pp[:], pp, o[:], o)
                    P.dma("sp", C.d_vT[s][:, c, t0:t0 + BLK], o[:], reads=[o], accs=[C.d_vT[s]])
                    pp = pps.next()
                    mm(P, pp, pp[:], w_g2, w_g2[:, 0, csl], sg, sg[:, 0, :], True, False)
                    mm(P, pp, pp[:], w_g2, w_g2[:, 1, csl], sg, sg[:, 1, :], False, True)
                    o = ob16.next()
                    evac(pp[:], pp, o[:], o)
                    P.dma("sp", C.d_gT[s][:, c, t0:t0 + BLK], o[:], reads=[o], accs=[C.d_gT[s]])
                    pp = pps.next()
                    proj(pp, pp[:], w_k, csl)
                    kk_ = kf.next()
                    P.op("act", lambda e, pp=pp, kk_=kk_: e.copy(out=kk_[:], in_=pp[:]), reads=[pp], writes=[kk_])
                    return kk_

                def phase_b(c, kk_):
                    csl = slice(c * 128, (c + 1) * 128)
                    kq = kkf.next()
                    P.op("dve", lambda e, kk_=kk_, kq=kq, c=c: e.tensor_scalar(
                        out=kq[:], in0=kk_[:], scalar1=cols[:, 10, c:c + 1], scalar2=None, op0=ALU.mult),
                        reads=[kk_, cols], writes=[kq])
                    sq = of32.next()
                    P.op("act", lambda e, kq=kq, sq=sq: e.activation(out=sq[:], in_=kq[:], func=AF.Square),
                         reads=[kq], writes=[sq])
                    pl = pls.next()
                    mm(P, pl, pl[:], bones, bones[:], sq, sq[:], True, True)
                    rn = of32.next()
                    P.op("act", lambda e, pl=pl, rn=rn: e.activation(out=rn[:], in_=pl[:], func=AF.Sqrt),
                         reads=[pl], writes=[rn])
                    P.op("dve", lambda e, rn=rn: e.tensor_scalar_max(out=rn[:], in0=rn[:], scalar1=1e-12),
                         reads=[rn], writes=[rn])
                    P.op("dve", lambda e, rn=rn: e.reciprocal(out=rn[:], in_=rn[:]), reads=[rn], writes=[rn])
                    P.op("dve", lambda e, kq=kq, rn=rn: e.tensor_tensor(out=kq[:], in0=kq[:], in1=rn[:], op=ALU.mult),
                         reads=[kq, rn], writes=[kq])
                    o = ob16.next()
                    P.op("dve", lambda e, kq=kq, o=o: e.tensor_copy(out=o[:], in_=kq[:]), reads=[kq], writes=[o])
                    P.dma("sp", C.d_kkT[s][:, c, t0:t0 + BLK], o[:], reads=[o], accs=[C.d_kkT[s]])
                    for dd in range(2):
                        pl = pls.next()
                        mm(P, pl, pl[:], w_l2, w_l2[0:64, dd, csl], lt, lt[:, dd, :], True, True)
                        lw = of32.next()
                        P.op("act", lambda e, pl=pl, lw=lw, dd=dd, c=c: e.activation(
                            out=lw[:], in_=pl[:], func=AF.Sigmoid, bias=cols[:, 6 + dd, c:c + 1], scale=1.0),
                            reads=[pl, cols], writes=[lw])
                        P.dma("sp", C.d_lwT[dd][s][:, c, t0:t0 + BLK], lw[:], reads=[lw], accs=[C.d_lwT[dd][s]])
                        pl = pls.next()
                        mm(P, pl, pl[:], w_l2, w_l2[0:64, 2 + dd, csl], lt, lt[:, 2 + dd, :], True, True)
                        aa = of32.next()
                        P.op("act", lambda e, pl=pl, aa=aa, dd=dd, c=c: e.activation(
                            out=aa[:], in_=pl[:], func=AF.Sigmoid, bias=cols[:, 8 + dd, c:c + 1], scale=1.0),
                            reads=[pl, cols], writes=[aa])
                        o = ob16.next()
                        P.op("dve", lambda e, kq=kq, aa=aa, o=o: e.tensor_tensor(out=o[:], in0=kq[:], in1=aa[:],
                                                                                 op=ALU.mult),
                             reads=[kq, aa], writes=[o])
                        P.dma("sp", C.d_bT[dd][s][:, c, t0:t0 + BLK], o[:], reads=[o], accs=[C.d_bT[dd][s]])
                        P.op("dve", lambda e, aa=aa, c=c: e.tensor_scalar(
                            out=aa[:], in0=aa[:], scalar1=-1.0, scalar2=cols[:, 11, c:c + 1], op0=ALU.add,
                            op1=ALU.mult), reads=[aa, cols], writes=[aa])
                        o = ob16.next()
                        P.op("dve", lambda e, aa=aa, kk_=kk_, o=o: e.scalar_tensor_tensor(
                            out=o[:], in0=aa[:], scalar=1.0, in1=kk_[:], op0=ALU.add, op1=ALU.mult),
                            reads=[aa, kk_], writes=[o])
                        P.dma("sp", C.d_kdT[dd][s][:, c, t0:t0 + BLK], o[:], reads=[o], accs=[C.d_kdT[dd][s]])

                prev = None
                for c in range(NK):
                    kk_c = phase_a(c)
                    if prev is not None:
                        phase_b(*prev)
                    prev = (c, kk_c)
                phase_b(*prev)


SBLK = 128
NSB = L // SBLK
CPB = SBLK // 64
SC_LIMIT = int(os.environ.get("SC_LIMIT", "999"))
CAST_MOD = int(os.environ.get("CAST_MOD", "4"))


def stage_rwkv_scan(P, C, dd):
    with P.scope():
        NF = 16 * SBLK
        msk = P.sbuf([64, 192], F32, "scmask")
        P.dma("sp", msk[:], C.d_scanmask[:, dd, :], reads=[C.d_scanmask], writes=[msk])
        rmask = P.sbuf([64, NF], F32, "rmask")
        P.dma("act", rmask[:], C.d_rmask[:, 0:NF], reads=[C.d_rmask], writes=[rmask])
        idb = P.sbuf([64, 64], BF16, "idb")
        P.op("dve", lambda e: e.tensor_copy(out=idb[:], in_=C.ident32[0:64, 0:64]), reads=[C.ident32], writes=[idb])
        Rr = P.sbuf([64, 16, SBLK], BF16, "scR")
        KD = P.sbuf([64, 16, SBLK], BF16, "scKD")
        Bb = P.sbuf([64, 16, SBLK], BF16, "scB")
        KK = P.sbuf([64, 16, SBLK], BF16, "scKK")
        fA = P.sbuf([64, 16, SBLK], F32, "scA")
        fB = P.sbuf([64, 16, SBLK], F32, "scBf")
        fC = P.sbuf([64, 16, SBLK], F32, "scC")
        BS = []
        for _ in range(2):
            b_ = Ctx()
            b_.AR = P.sbuf([64, 16, CPB, 128], BF16, "scAR")
            b_.KT = P.sbuf([64, 16, SBLK], BF16, "scKT")
            b_.BT = P.sbuf([64, 16, SBLK], BF16, "scBT")
            b_.Vt = P.sbuf([64, CPB, D], BF16, "scV")
            b_.Yo = P.sbuf([64, CPB, D], F32, "scY")
            b_.PC = P.sbuf([64, 16, CPB], F32, "scPC")
            BS.append(b_)
        Sf = P.sbuf([64, 16, 64], F32, "scSf")
        Sb = P.sbuf([64, 16, 64], BF16, "scSb")
        MNk = [[P.sbuf([64, 4, 128], BF16, "MNk") for _ in range(4)] for _ in range(2)]
        MNb = [[P.sbuf([64, 4, 128], BF16, "MNb") for _ in range(4)] for _ in range(2)]
        NT0 = [[P.sbuf([64, 4, 64], BF16, "NT0") for _ in range(4)] for _ in range(2)]
        KTt = [[P.sbuf([64, 4, 2, 64], BF16, "KTt") for _ in range(4)] for _ in range(2)]
        Nl = [[[P.sbuf([64, 4, 2, 64], BF16, "Nl") for _ in range(5)] for _ in range(4)] for _ in range(2)]
        Xb = [P.sbuf([64, 4, 64], BF16, "Xb") for _ in range(4)]
        tmpS = [P.sbuf([64, 4, 64], F32, "tmpS") for _ in range(2)]
        psMNk = P.psum([64, 512], F32, "psMNk")
        psMNb = P.psum([64, 512], F32, "psMNb")
        psN = Rot([P.psum([64, 512], F32, "psN") for _ in range(2)])
        psXb = [P.psum([64, 512], F32, "psX") for _ in range(2)]
        psYS = P.psum([64, 512], F32, "psYS")
        psT = P.psum([64, 512], F32, "psT")
        v4 = lambda b, w: b[:, 0:4 * w].rearrange("p (h t) -> p h t", t=w)
        v42 = lambda b: b[:, 0:512].rearrange("p (h a t) -> p h a t", a=2, t=64)
        xview = lambda g: psXb[g // 2][:, (g % 2) * 256:(g % 2) * 256 + 256].rearrange("p (h t) -> p h t", t=64)
        ecnt = [0]

        def cast(src_buf, src_ap, dst_buf, dst_ap):
            ecnt[0] += 1
            if ecnt[0] % CAST_MOD != 0:
                P.op("act", lambda e: e.copy(out=dst_ap, in_=src_ap), reads=[src_buf], writes=[dst_buf])
            else:
                P.op("dve", lambda e: e.tensor_copy(out=dst_ap, in_=src_ap), reads=[src_buf], writes=[dst_buf])

        def prep(s, blk, bs):
            t0 = blk * SBLK
            tsl_all = slice(t0, t0 + SBLK)
            for h2 in range(2):
                prt = slice(h2 * 64, (h2 + 1) * 64)
                P.dma("sp", fA[:, h2::2, :], C.d_lwT[dd][s][prt, :, tsl_all], reads=[C.d_lwT[dd][s]], accs=[fA])
                P.dma("sp", KK[:, h2::2, :], C.d_kkT[s][prt, :, tsl_all], reads=[C.d_kkT[s]], accs=[KK])
                P.dma("sp", Rr[:, h2::2, :], C.d_rT[s][prt, :, tsl_all], reads=[C.d_rT[s]], accs=[Rr])
                P.dma("sp", KD[:, h2::2, :], C.d_kdT[dd][s][prt, :, tsl_all], reads=[C.d_kdT[dd][s]], accs=[KD])
                P.dma("sp", Bb[:, h2::2, :], C.d_bT[dd][s][prt, :, tsl_all], reads=[C.d_bT[dd][s]], accs=[Bb])
            P.dma("sp", bs.Vt[:], C.d_vtok[s][tsl_all, :].rearrange("(c i) f -> i c f", i=64),
                  reads=[C.d_vtok[s]], writes=[bs.Vt])
            fl = lambda b: b[:].rearrange("p h t -> p (h t)")
            c4 = lambda b: b[:].rearrange("p h (c t) -> p (h c) t", t=64)
            c5 = lambda b: b[:].rearrange("p h (c t) -> p h c t", t=64)
            P.op("dve", lambda e: e.tensor_tensor_scan(out=fl(fB), data0=rmask[:], data1=fl(fA), initial=0.0,
                                                       op0=ALU.mult, op1=ALU.add), reads=[rmask, fA], writes=[fB])
            if dd == 1:
                P.op("dve", lambda e: e.tensor_tensor(out=fl(fC), in0=fl(fA), in1=fl(fB), op=ALU.subtract),
                     reads=[fA, fB], writes=[fC])
                P.op("dve", lambda e: e.tensor_tensor(
                    out=c4(fB), in0=c4(fC), in1=c4(fB)[:, :, 63:64].to_broadcast([64, 16 * CPB, 64]), op=ALU.add),
                    reads=[fC, fB], writes=[fB])
            P.op("dve", lambda e: e.tensor_tensor(out=fl(fA), in0=fl(fB), in1=fl(fA), op=ALU.subtract),
                 reads=[fA, fB], writes=[fA])
            P.op("act", lambda e: e.activation(out=fl(fA), in_=fl(fA), func=AF.Exp, scale=DECAY_C),
                 reads=[fA], writes=[fA])
            P.op("act", lambda e: e.activation(out=fl(fC), in_=fl(fB), func=AF.Exp, scale=DECAY_C),
                 reads=[fB], writes=[fC])
            P.op("act", lambda e: e.activation(out=fl(fB), in_=fl(fB), func=AF.Exp, scale=-DECAY_C),
                 reads=[fB], writes=[fB])
            pcol = 63 if dd == 0 else 0
            P.op("act", lambda e: e.copy(out=bs.PC[:], in_=c5(fC)[:, :, :, pcol]), reads=[fC], writes=[bs.PC])
            P.op("dve", lambda e: e.scalar_tensor_tensor(out=bs.AR[:, :, :, 0:64], in0=c5(KK), scalar=-1.0,
                                                         in1=c5(fA), op0=ALU.mult, op1=ALU.mult),
                 reads=[KK, fA], accs=[bs.AR])
            P.op("dve", lambda e: e.tensor_tensor(out=bs.AR[:, :, :, 64:128], in0=c5(Rr), in1=c5(fC), op=ALU.mult),
                 reads=[Rr, fC], accs=[bs.AR])
            P.op("dve", lambda e: e.tensor_tensor(out=bs.KT[:], in0=KD[:], in1=fB[:], op=ALU.mult),
                 reads=[KD, fB], writes=[bs.KT])
            P.op("dve", lambda e: e.tensor_tensor(out=bs.BT[:], in0=Bb[:], in1=fB[:], op=ALU.mult),
                 reads=[Bb, fB], writes=[bs.BT])

        def s1(bs, c, par):
            tsl = slice(c * 64, (c + 1) * 64)
            AR, KT, BT = bs.AR, bs.KT, bs.BT
            for g in range(4):
                for hi in range(4):
                    h = 4 * g + hi
                    mm(P, psMNk, v4(psMNk, 128)[:, hi, :], KT, KT[:, h, tsl], AR, AR[:, h, c, :], True, True)
                P.op("dve", lambda e, g=g: e.tensor_tensor(
                    out=MNk[par][g][:], in0=v4(psMNk, 128), in1=msk[:, 0:128].unsqueeze(1).to_broadcast([64, 4, 128]),
                    op=ALU.mult), reads=[psMNk, msk], writes=[MNk[par][g]])
                for hi in range(4):
                    h = 4 * g + hi
                    mm(P, psMNb, v4(psMNb, 128)[:, hi, :], BT, BT[:, h, tsl], AR, AR[:, h, c, :], True, True)
                P.op("dve", lambda e, g=g: e.tensor_tensor(
                    out=MNb[par][g][:], in0=v4(psMNb, 128), in1=msk[:, 0:128].unsqueeze(1).to_broadcast([64, 4, 128]),
                    op=ALU.mult), reads=[psMNb, msk], writes=[MNb[par][g]])
                pn = psN.next()
                for hi in range(4):
                    h = 4 * g + hi
                    mm(P, pn, v4(pn, 64)[:, hi, :], AR, AR[:, h, c, 0:64], BT, BT[:, h, tsl], True, True)
                P.op("dve", lambda e, g=g, pn=pn: e.tensor_tensor(
                    out=NT0[par][g][:], in0=v4(pn, 64), in1=msk[:, 128:192].unsqueeze(1).to_broadcast([64, 4, 64]),
                    op=ALU.mult), reads=[pn, msk], writes=[NT0[par][g]])
                for hi in range(4):
                    h = 4 * g + hi
                    mm(P, psT, v42(psT)[:, hi, 0, :], KT, KT[:, h, tsl], idb, idb[:], True, True)
                    mm(P, psT, v42(psT)[:, hi, 1, :], BT, BT[:, h, tsl], idb, idb[:], True, True)
                P.op("act", lambda e, g=g: e.copy(out=KTt[par][g][:], in_=v42(psT)), reads=[psT],
                     writes=[KTt[par][g]])

        def level_ops(par, g, j):
            if j == 0:
                return (MNb[par][g], (lambda hi: MNb[par][g][:, hi, 0:64]), NT0[par][g], (lambda hi: NT0[par][g][:, hi, :]))
            t = Nl[par][g][j - 1]
            return (t, (lambda hi: t[:, hi, 0, :]), t, (lambda hi: t[:, hi, 1, :]))

        def square(par, j):
            for g in range(4):
                nbuf, nap, tbuf, tap = level_ops(par, g, j)
                pn = psN.next()
                for hi in range(4):
                    mm(P, pn, v42(pn)[:, hi, 0, :], tbuf, tap(hi), nbuf, nap(hi), True, True)
                    if j < 4:
                        mm(P, pn, v42(pn)[:, hi, 1, :], nbuf, nap(hi), tbuf, tap(hi), True, True)
                dst = Nl[par][g][j]
                if j < 4:
                    cast(pn, v42(pn), dst, dst[:])
                else:
                    cast(pn, v42(pn)[:, :, 0, :], dst, dst[:, :, 0, :])

        def g_step(bs, c, par):
            AR, Vt = bs.AR, bs.Vt
            for g in range(4):
                xv = xview(g)
                pb_ = psXb[g // 2]
                for hi in range(4):
                    h = 4 * g + hi
                    first = (g % 2 == 0 and hi == 0)
                    P.op("pe", lambda e, xv=xv, hi=hi, h=h, first=first: e.matmul(
                        xv[:, hi, :], lhsT=AR[:, h, c, 0:64], rhs=Sb[:, h, :], start=first, stop=False,
                        skip_group_check=True), reads=[AR, Sb], writes=[pb_])
                    P.op("pe", lambda e, xv=xv, hi=hi, h=h, g=g: e.matmul(
                        xv[:, hi, :], lhsT=MNk[par][g][:, hi, 0:64], rhs=Vt[:, c, h * 64:(h + 1) * 64], start=False,
                        stop=False, skip_group_check=True), reads=[MNk[par][g], Vt], writes=[pb_])
                cast(pb_, xv, Xb[g], Xb[g][:])

        def apply(par, j):
            for g in range(4):
                xv = xview(g)
                pb_ = psXb[g // 2]
                nbuf, nap, _, _ = level_ops(par, g, j)
                for hi in range(4):
                    P.op("pe", lambda e, xv=xv, hi=hi, g=g, nap=nap: e.matmul(
                        xv[:, hi, :], lhsT=nap(hi), rhs=Xb[g][:, hi, :], start=False, stop=(j == 5),
                        skip_group_check=True), reads=[nbuf, Xb[g]], writes=[pb_])
                cast(pb_, xv, Xb[g], Xb[g][:])

        def ys_step(bs, c, par):
            AR, Vt, Yo = bs.AR, bs.Vt, bs.Yo
            for g in range(4):
                pys = v42(psYS)
                for hi in range(4):
                    h = 4 * g + hi
                    vv = Vt[:, c, h * 64:(h + 1) * 64]
                    mm(P, psYS, pys[:, hi, 0, :], AR, AR[:, h, c, 64:128], Sb, Sb[:, h, :], True, False)
                    mm(P, psYS, pys[:, hi, 0, :], MNk[par][g], MNk[par][g][:, hi, 64:128], Vt, vv, False, False)
                    mm(P, psYS, pys[:, hi, 0, :], MNb[par][g], MNb[par][g][:, hi, 64:128], Xb[g], Xb[g][:, hi, :],
                       False, True)
                    mm(P, psYS, pys[:, hi, 1, :], KTt[par][g], KTt[par][g][:, hi, 0, :], Vt, vv, True, False)
                    mm(P, psYS, pys[:, hi, 1, :], KTt[par][g], KTt[par][g][:, hi, 1, :], Xb[g], Xb[g][:, hi, :],
                       False, True)
                ts_ = tmpS[g % 2]
                P.op("dve", lambda e, g=g, pys=pys, ts_=ts_: e.tensor_tensor(
                    out=ts_[:], in0=pys[:, :, 1, :], in1=Sf[:, 4 * g:4 * g + 4, :], op=ALU.add),
                    reads=[psYS, Sf], writes=[ts_])
                pcb = bs.PC[:, 4 * g:4 * g + 4, c:c + 1].to_broadcast([64, 4, 64])
                P.op("dve", lambda e, g=g, ts_=ts_, pcb=pcb: e.tensor_tensor(
                    out=Sb[:, 4 * g:4 * g + 4, :], in0=ts_[:], in1=pcb, op=ALU.mult),
                    reads=[ts_, bs.PC], accs=[Sb])
                P.op("dve", lambda e, g=g, pys=pys, c=c: e.tensor_copy(
                    out=Yo[:, c, g * 256:(g + 1) * 256].rearrange("p (h v) -> p h v", v=64), in_=pys[:, :, 0, :]),
                    reads=[psYS], accs=[Yo])
                P.op("dve", lambda e, g=g, ts_=ts_, pcb=pcb: e.tensor_tensor(
                    out=Sf[:, 4 * g:4 * g + 4, :], in0=ts_[:], in1=pcb, op=ALU.mult),
                    reads=[ts_, bs.PC], accs=[Sf])

        for s in range(NS):
            P.op("dve", lambda e: e.memset(Sf[:], 0.0), writes=[Sf])
            P.op("dve", lambda e: e.memset(Sb[:], 0.0), writes=[Sb])
            blocks = list(range(NSB) if dd == 0 else range(NSB - 1, -1, -1))
            seq = []
            for bp, blk in enumerate(blocks):
                for c in (range(CPB) if dd == 0 else range(CPB - 1, -1, -1)):
                    seq.append((bp, blk, c))
            seq = seq[:SC_LIMIT]
            prep(s, seq[0][1], BS[0])
            s1(BS[0], seq[0][2], 0)
            for j in range(5):
                square(0, j)
            for n, (bp, blk, c) in enumerate(seq):
                par = n % 2
                bs = BS[bp % 2]
                nxt = seq[n + 1] if n + 1 < len(seq) else None
                if nxt is not None:
                    nbs = BS[nxt[0] % 2]
                    if nxt[0] != bp:
                        prep(s, nxt[1], nbs)
                    s1(nbs, nxt[2], 1 - par)
                g_step(bs, c, par)
                for j in range(6):
                    apply(par, j)
                    if nxt is not None and j < 5:
                        square(1 - par, j)
                ys_step(bs, c, par)
                last_of_block = (nxt is None) or (nxt[0] != bp)
                if last_of_block:
                    t0 = blk * SBLK
                    P.dma("sp", C.d_ytok[dd][s][t0:t0 + SBLK, :].rearrange("(c i) f -> i c f", i=64), bs.Yo[:],
                          reads=[bs.Yo], accs=[C.d_ytok[dd][s]])


GN_EPS = 64e-5


def stage_rwkv_post(P, C):
    with P.scope():
        C.wstage_n = 1024
        C.wstage = Rot([P.sbuf([128, 1024], F32, "wst") for _ in range(2)])
        w_o = load_weight_bf16(P, C, C.d_w_o, NK, D, "w_o")
        cols = P.sbuf([128, 14, NK], F32, "rwcols")
        P.dma("sp", cols[:], C.d_rwcols[:], reads=[C.d_rwcols], writes=[cols])
        lnb = P.sbuf([128, NK], F32, "lnb")
        P.dma("sp", lnb[:], C.d_lnb[:], reads=[C.d_lnb], writes=[lnb])
        bones = P.sbuf([128, 128], F32, "bones")
        P.dma("sp", bones[:], C.d_bones[:], reads=[C.d_bones], writes=[bones])
        alloc_norm_scratch(P, C)
        yf = P.sbuf([128, 4, D], F32, "yf")
        yb = P.sbuf([128, 4, D], F32, "yb")
        st = P.sbuf([128, 6, 64], F32, "gnst")
        ynT = P.sbuf([128, NK, BLK], F32, "ynT")
        rB = P.sbuf([128, NK, BLK], BF16, "rB")
        k0B = P.sbuf([128, NK, BLK], BF16, "k0B")
        k1B = P.sbuf([128, NK, BLK], BF16, "k1B")
        vB = P.sbuf([128, NK, BLK], BF16, "vB")
        gB = P.sbuf([128, NK, BLK], BF16, "gB")
        xr = P.sbuf([128, NK, BLK], F32, "xr")
        outT = P.sbuf([128, NK, BLK], BF16, "outT")
        x3 = P.sbuf([128, NK, BLK], F32, "x3T")
        h2 = P.sbuf([128, NK, BLK], BF16, "h2T")
        kms = Rot([P.sbuf([128, BLK], F32, "km") for _ in range(2)])
        qs = Rot([P.sbuf([128, BLK], F32, "qq") for _ in range(2)])
        bns = Rot([P.sbuf([128, BLK], F32, "bn") for _ in range(2)])
        pts = Rot([P.psum([128, BLK], F32, "pt") for _ in range(2)])
        pbs = Rot([P.psum([128, BLK], F32, "pb") for _ in range(2)])
        pps = Rot([P.psum([128, BLK], F32, "pp") for _ in range(2)])
        g1 = C.gatev[1][0]
        for s in range(NS):
            for blk in range(NBLK):
                t0 = blk * BLK
                tsl = slice(t0, t0 + BLK)
                P.dma("sp", yf[:], C.d_ytok[0][s][tsl, :].rearrange("(j p) f -> p j f", p=128),
                      reads=[C.d_ytok[0][s]], writes=[yf])
                P.dma("act", yb[:], C.d_ytok[1][s][tsl, :].rearrange("(j p) f -> p j f", p=128),
                      reads=[C.d_ytok[1][s]], writes=[yb])
                P.dma("sp", rB[:], C.d_rT[s][:, :, tsl], reads=[C.d_rT[s]], writes=[rB])
                P.dma("act", k0B[:], C.d_kdT[0][s][:, :, tsl], reads=[C.d_kdT[0][s]], writes=[k0B])
                P.dma("sp", k1B[:], C.d_kdT[1][s][:, :, tsl], reads=[C.d_kdT[1][s]], writes=[k1B])
                P.dma("act", vB[:], C.d_vT[s][:, :, tsl], reads=[C.d_vT[s]], writes=[vB])
                P.dma("sp", gB[:], C.d_gT[s][:, :, tsl], reads=[C.d_gT[s]], writes=[gB])
                P.dma("act", xr[:], C.d_xT[2][s][:, :, tsl], reads=[C.d_xT[2][s]], writes=[xr])
                yv = yf[:].rearrange("p j (h v) -> p (j h) v", v=64)
                ybv = yb[:].rearrange("p j (h v) -> p (j h) v", v=64)
                P.op("dve", lambda e: e.tensor_tensor(out=yf[:], in0=yf[:], in1=yb[:], op=ALU.add),
                     reads=[yf, yb], writes=[yf])
                P.op("dve", lambda e: e.tensor_reduce(out=st[:, 0, :], in_=yv, axis=mybir.AxisListType.X, op=ALU.add),
                     reads=[yf], writes=[st])
                P.op("act", lambda e: e.activation(out=yb[:], in_=yf[:], func=AF.Square), reads=[yf], writes=[yb])
                P.op("dve", lambda e: e.tensor_reduce(out=st[:, 1, :], in_=ybv, axis=mybir.AxisListType.X, op=ALU.add),
                     reads=[yb], writes=[st])
                P.op("dve", lambda e: e.tensor_scalar(out=st[:, 2, :], in0=st[:, 0, :], scalar1=1.0 / 64, scalar2=None,
                                                      op0=ALU.mult), reads=[st], writes=[st])
                P.op("dve", lambda e: e.tensor_tensor(out=st[:, 3, :], in0=st[:, 2, :], in1=st[:, 2, :], op=ALU.mult),
                     reads=[st], writes=[st])
                P.op("dve", lambda e: e.scalar_tensor_tensor(out=st[:, 4, :], in0=st[:, 1, :], scalar=1.0 / 64,
                                                             in1=st[:, 3, :], op0=ALU.mult, op1=ALU.subtract),
                     reads=[st], writes=[st])
                P.op("dve", lambda e: e.tensor_scalar_add(out=st[:, 4, :], in0=st[:, 4, :], scalar1=GN_EPS),
                     reads=[st], writes=[st])
                P.op("act", lambda e: e.activation(out=st[:, 5, :], in_=st[:, 4, :], func=AF.Sqrt),
                     reads=[st], writes=[st])
                P.op("dve", lambda e: e.reciprocal(out=st[:, 5, :], in_=st[:, 5, :]), reads=[st], writes=[st])
                P.op("dve", lambda e: e.tensor_tensor(out=yv, in0=yv, in1=st[:, 2, :].unsqueeze(2).to_broadcast(
                    [128, 64, 64]), op=ALU.subtract), reads=[yf, st], writes=[yf])
                P.op("dve", lambda e: e.tensor_tensor(out=yv, in0=yv, in1=st[:, 5, :].unsqueeze(2).to_broadcast(
                    [128, 64, 64]), op=ALU.mult), reads=[yf, st], writes=[yf])
                for k in range(NK):
                    pt = pts.next()
                    for j in range(4):
                        P.op("pe", lambda e, pt=pt, j=j, k=k: e.transpose(
                            pt[:, j * 128:(j + 1) * 128], yf[:, j, k * 128:(k + 1) * 128], C.ident32[:]),
                            reads=[yf, C.ident32], writes=[pt])
                    P.op("act", lambda e, pt=pt, k=k: e.activation(
                        out=ynT[:, k, :], in_=pt[:], func=AF.Identity, scale=cols[:, 13, k:k + 1],
                        bias=lnb[:, k:k + 1]), reads=[pt, cols, lnb], accs=[ynT])
                    km = kms.next()
                    P.op("dve", lambda e, km=km, k=k: e.tensor_tensor(out=km[:], in0=k0B[:, k, :], in1=k1B[:, k, :],
                                                                       op=ALU.add), reads=[k0B, k1B], writes=[km])
                    q = qs.next()
                    P.op("dve", lambda e, km=km, q=q, k=k: e.scalar_tensor_tensor(
                        out=q[:], in0=rB[:, k, :], scalar=cols[:, 12, k:k + 1], in1=km[:], op0=ALU.mult, op1=ALU.mult),
                        reads=[rB, cols, km], writes=[q])
                    pb = pbs.next()
                    mm(P, pb, pb[:], bones, bones[:], q, q[:], True, True)
                    bn = bns.next()
                    P.op("dve", lambda e, pb=pb, bn=bn, k=k: e.scalar_tensor_tensor(
                        out=bn[:], in0=pb[:], scalar=0.5, in1=vB[:, k, :], op0=ALU.mult, op1=ALU.mult),
                        reads=[pb, vB], writes=[bn])
                    P.op("dve", lambda e, bn=bn, k=k: e.tensor_tensor(out=bn[:], in0=bn[:], in1=ynT[:, k, :],
                                                                       op=ALU.add), reads=[bn, ynT], writes=[bn])
                    P.op("dve", lambda e, bn=bn, k=k: e.tensor_tensor(out=outT[:, k, :], in0=bn[:], in1=gB[:, k, :],
                                                                       op=ALU.mult), reads=[bn, gB], accs=[outT])
                for c in range(NK):
                    pp = pps.next()
                    for k in range(NK):
                        mm(P, pp, pp[:], w_o, w_o[:, k, c * 128:(c + 1) * 128], outT, outT[:, k, :], k == 0, k == NK - 1)
                    P.op("dve", lambda e, pp=pp, c=c, s=s: e.scalar_tensor_tensor(
                        out=x3[:, c, :], in0=pp[:], scalar=g1[:, c, s:s + 1], in1=xr[:, c, :],
                        op0=ALU.mult, op1=ALU.add), reads=[pp, g1, xr], accs=[x3])
                P.dma("sp", C.d_xT[3][s][:, :, tsl], x3[:], reads=[x3], accs=[C.d_xT[3][s]])
                norm_mod(P, C, x3, BLK, C.gain[1][1], C.shiftv[1][1], s, h2)
                P.dma("sp", C.d_h2T[1][s][:, :, tsl], h2[:], reads=[h2], accs=[C.d_h2T[1][s]])

class _ModView:
    def __init__(self, buf, j):
        self.buf = buf
        self.j = j

    @property
    def w(self):
        return self.buf.w

    @property
    def r(self):
        return self.buf.r

    @property
    def a(self):
        return self.buf.a

    def __getitem__(self, idx):
        p, k, s = idx
        return self.buf[p, self.j * 8 + k, s]


def mod_shift(C, l, which):
    return C.shiftv[l][which]


def host_consts():
    c = {}
    c["ident32"] = np.eye(128, dtype=np.float32)
    c["ones32"] = np.ones((128, 128), dtype=np.float32)
    bo = np.zeros((128, 128), np.float32)
    bo[0:64, 0:64] = 1.0
    bo[64:128, 64:128] = 1.0
    c["c_bones"] = bo
    ii = np.arange(64)[:, None]
    tt = np.arange(64)[None, :]
    sm = np.zeros((64, 2, 192), np.float32)
    sm[:, 0, 0:64] = (ii < tt)
    sm[:, 0, 64:128] = (ii <= tt)
    sm[:, 0, 128:192] = (tt < ii)
    sm[:, 1, 0:64] = (ii > tt)
    sm[:, 1, 64:128] = (ii >= tt)
    sm[:, 1, 128:192] = (tt > ii)
    c["c_scanmask"] = sm
    rm = np.ones((64, 16 * 256), np.float32)
    rm[:, ::64] = 0.0
    c["c_rmask"] = rm
    slopes = np.exp2(-8.0 * (np.arange(12, dtype=np.float32) + 1.0) / 12).astype(np.float32).reshape(3, 4)
    tab = np.zeros((128, 9, 4, 128), np.float32)
    kk = np.arange(128)[:, None]
    qq = np.arange(128)[None, :]
    NEG = -30000.0
    for g, d in enumerate((1, 4, 16)):
        for h in range(4):
            sl = slopes[g, h] * d
            lo = np.where(kk >= qq, -sl * np.abs(kk - qq - 64), NEG)
            up = np.where(kk <= qq, -sl * np.abs(kk - qq + 64), NEG)
            tab[:, 3 * g + 0, h, :] = lo
            tab[:, 3 * g + 1, h, :] = up
            tab[0:64, 3 * g + 2, h, :] = lo[64:128]
    c["abias"] = tab.astype(np.float32)
    n1 = np.arange(64, dtype=np.float64)[:, None]
    k1 = np.arange(128, dtype=np.float64)[None, :]
    ang = 2 * np.pi * n1 * k1 / 128.0
    c["c_w128"] = np.concatenate([np.cos(ang), -np.sin(ang)], 1).astype(np.float32)
    n2 = np.tile(np.arange(64, dtype=np.float64), 2)[:, None]
    ang = 2 * np.pi * n2 * k1 / 8192.0
    c["c_tw"] = np.stack([np.cos(ang), -np.sin(ang)], 1).astype(np.float32)
    a64 = 2 * np.pi * np.outer(np.arange(64), np.arange(64)) / 64.0
    def bdiag(m):
        z = np.zeros((128, 128))
        z[0:64, 0:64] = m
        z[64:128, 64:128] = m
        return z
    bdr, bdi = bdiag(np.cos(a64)), bdiag(-np.sin(a64))
    c["c_bd"] = np.stack([bdr, bdi, -bdi], 1).astype(np.float32)
    cr, ci = bdiag(np.cos(a64)), bdiag(np.sin(a64))
    c["c_bdc"] = np.stack([np.concatenate([cr, ci], 1), np.concatenate([-ci, cr], 1)], 1).astype(np.float32)
    kk1 = np.arange(128, dtype=np.float64)[:, None]
    nn2 = np.tile(np.arange(64, dtype=np.float64), 2)[None, :]
    ang = 2 * np.pi * kk1 * nn2 / 8192.0
    c["c_twi"] = np.stack([np.cos(ang), np.sin(ang)], 1).astype(np.float32)
    ang = 2 * np.pi * np.outer(np.arange(128), np.arange(64)) / 128.0
    c["c_vinv"] = np.stack([np.cos(ang) / 8192.0, -np.sin(ang) / 8192.0], 1).astype(np.float32)
    t = np.linspace(0.0, 1.0, L, dtype=np.float32)[:, None]
    w = (2.0 * np.float32(math.pi) * np.arange(L, dtype=np.float32)[:, None] / np.float32(L)).astype(np.float32)
    f = np.linspace(1e-4, 15, 16, dtype=np.float32)[None, :]
    z = (f * w).astype(np.float32)
    pos = np.concatenate([t, np.cos(z), -np.sin(z)], -1).astype(np.float32)
    c["c_posT"] = np.ascontiguousarray(pos.T)
    min_decay = math.log(1e-2) / 1.5
    max_decay = math.log(1e-2) / 0.3
    deltas = np.linspace(min_decay, max_decay, 256, dtype=np.float32)[None, :]
    win = np.exp(-t * np.abs(deltas)).astype(np.float32)
    c["c_winT"] = np.ascontiguousarray(win.T.reshape(2, 128, L).transpose(1, 0, 2))
    return c


def build_program(stages, dbg=()):
    nc = bass.Bass("TRN2", target_bir_lowering=False)
    P = Prog(nc)
    C = Ctx()
    C.dbg = {}
    din = lambda name, shape, dt=F32: P.dram(name, shape, dt, kind="ExternalInput")
    C.d_x = din("x_in", [NS, L, D])
    C.d_cT = din("cT", [128, NK, NS])
    C.d_adaw = [din("ada_w%d" % l, [128, NK, 6 * D]) for l in range(2)]
    C.d_adab = din("ada_b", [128, 2, 48])
    C.d_normw = din("normw", [128, 5, NK])
    C.d_w_in = din("w_in", [128, NK, 3072])
    C.d_abias = din("abias", [128, 9, 4, 128])
    C.d_w128 = din("c_w128", [64, 256])
    C.d_tw = din("c_tw", [128, 2, 128])
    C.d_bd = din("c_bd", [128, 3, 128])
    C.d_bdc = din("c_bdc", [128, 2, 256])
    C.d_twi = din("c_twi", [128, 2, 128])
    C.d_vinv = din("c_vinv", [128, 2, 64])
    C.d_posT = din("c_posT", [33, L])
    C.d_winT = din("c_winT", [128, 2, L])
    C.d_hcol = din("hcol", [64, 4])
    C.d_fw1 = din("fw1", [33, 64])
    C.d_fw23 = din("fw23", [64, 2, 64])
    C.d_fw4 = din("fw4", [64, 512])
    C.d_fbias = din("fbias", [128, 2])
    C.d_shortw = din("shortw", [128, 6, 4])
    C.d_w_out = din("w_out", [128, 4, D])
    C.d_ffn_up = [din("ffn_up%d" % l, [128, NK, 2 * DFF]) for l in range(2)]
    C.d_ffn_dn = [din("ffn_dn%d" % l, [128, NFC, D]) for l in range(2)]
    C.d_ffn_cw = [din("ffn_cw%d" % l, [128, NFC, 4]) for l in range(2)]
    C.d_w_r = din("w_r", [128, NK, D])
    C.d_w_k = din("w_k", [128, NK, D])
    C.d_w_v = din("w_v", [128, NK, D])
    C.d_w_o = din("w_o", [128, NK, D])
    C.d_g1 = din("rw_g1", [128, NK, 256])
    C.d_g2 = din("rw_g2", [128, 2, D])
    C.d_lora1 = din("rw_lora1", [128, NK, 256])
    C.d_lora2 = din("rw_lora2", [128, 4, D])
    C.d_rwcols = din("rw_cols", [128, 14, NK])
    C.d_bones = din("c_bones", [128, 128])
    C.d_lnb = din("rw_lnb", [128, NK])
    C.d_scanmask = din("c_scanmask", [64, 2, 192])
    C.d_rmask = din("c_rmask", [64, 16 * 256])
    C.d_ident32 = din("ident32", [128, 128])
    C.d_ones32 = din("ones32", [128, 128])
    C.d_y = P.dram("y_out", [NS, L, D], F32, kind="ExternalOutput")
    outs = []

    def scratch(name, shape, dt):
        if name in dbg:
            b = P.dram(name, shape, dt, kind="ExternalOutput")
            outs.append(b)
            return b
        return P.dram(name, shape, dt)

    C.d_xT = [[scratch("xT%d_%d" % (i, s), [128, NK, L], F32) for s in range(NS)] for i in range(5)]
    C.d_qkT = [scratch("qkT_%d" % s, [128, 12, L], BF16) for s in range(NS)]
    C.d_vaug = [scratch("vaug_%d" % s, [L, 12, 65], BF16) for s in range(NS)]
    C.d_hyT = [scratch("hyT_%d" % s, [128, 6, L], F32) for s in range(NS)]
    C.d_oacc = [scratch("oacc_%d" % s, [3, L, 4, 65], F32) for s in range(NS)]
    C.d_h2T = [[scratch("h2T%d_%d" % (l, s), [128, NK, L], BF16) for s in range(NS)] for l in range(2)]
    C.d_h1T = [scratch("h1T_%d" % s, [128, NK, L], F32) for s in range(NS)]
    C.d_vtok = [scratch("vtok_%d" % s, [L, D], BF16) for s in range(NS)]
    C.d_rT = [scratch("rT_%d" % s, [128, NK, L], BF16) for s in range(NS)]
    C.d_vT = [scratch("vT_%d" % s, [128, NK, L], BF16) for s in range(NS)]
    C.d_gT = [scratch("gT_%d" % s, [128, NK, L], BF16) for s in range(NS)]
    C.d_kkT = [scratch("kkT_%d" % s, [128, NK, L], BF16) for s in range(NS)]
    C.d_lwT = [[scratch("lwT%d_%d" % (dd, s), [128, NK, L], F32) for s in range(NS)] for dd in range(2)]
    C.d_bT = [[scratch("bT%d_%d" % (dd, s), [128, NK, L], BF16) for s in range(NS)] for dd in range(2)]
    C.d_kdT = [[scratch("kdT%d_%d" % (dd, s), [128, NK, L], BF16) for s in range(NS)] for dd in range(2)]
    C.d_ytok = [[scratch("ytok%d_%d" % (dd, s), [L, D], F32) for s in range(NS)] for dd in range(2)]
    C.d_filtT = scratch("filtT", [128, 4, L], F32)
    C.d_F = scratch("Fspec", [128, 128, 2, 128], F32)
    C.d_zT = [scratch("zT_%d" % s, [128, 2, L], F32) for s in range(NS)]
    C.d_x0T = [scratch("x0T_%d" % s, [128, 2, L], F32) for s in range(NS)]
    C.d_hyoT = [scratch("hyoT_%d" % s, [128, 2, L], F32) for s in range(NS)]
    C.ident32 = P.sbuf([128, 128], F32, "ident32")
    C.ones32 = P.sbuf([128, 128], F32, "ones32")
    C.eps_col = P.sbuf([128, 1], F32, "eps")
    P.dma("sp", C.ident32[:], C.d_ident32[:], reads=[C.d_ident32], writes=[C.ident32])
    P.dma("sp", C.ones32[:], C.d_ones32[:], reads=[C.d_ones32], writes=[C.ones32])
    P.op("pool", lambda e: e.memset(C.eps_col[:], RMS_EPS), writes=[C.eps_col])
    C.modT = [P.sbuf([128, 48, NS], F32, "modT%d" % l) for l in range(2)]
    C.gain = [[P.sbuf([128, NK, NS], F32, "gain%d%d" % (l, w)) for w in range(2)] for l in range(2)]
    C.gain_fin = P.sbuf([128, NK, NS], F32, "gainf")
    C.shiftv = [[_ModView(C.modT[l], 0), _ModView(C.modT[l], 3)] for l in range(2)]
    C.gatev = [[_ModView(C.modT[l], 2), _ModView(C.modT[l], 5)] for l in range(2)]

    def dbg_out(name, src_buf, src_ap, shape, dt=F32):
        if name in dbg:
            o = P.dram("dbg_" + name, shape, dt, kind="ExternalOutput")
            P.dma("sp", o[:], src_ap, reads=[src_buf], writes=[o])
            outs.append(o)

    C.dbg_out = dbg_out
    if "adaln" in stages:
        stage_adaln(P, C)
        dbg_out("modT0", C.modT[0], C.modT[0][:], [128, 48, NS])
        dbg_out("gain00", C.gain[0][0], C.gain[0][0][:], [128, NK, NS])
    if "l0_inproj" in stages:
        stage_l0_inproj(P, C)
    if "attn" in stages:
        stage_attention(P, C)
    if "hyfilt" in stages:
        stage_hyena_filter(P, C)
    if "hyena" in stages:
        stage_hyena(P, C)
    if "l0_outproj" in stages:
        stage_l0_outproj(P, C)
    if "ffn0" in stages:
        stage_ffn(P, C, 0, 1, 2)
    if "rw_norm" in stages:
        stage_rwkv_norm(P, C)
    if "rw_proj" in stages:
        stage_rwkv_proj(P, C)
    if "rw_scan0" in stages:
        stage_rwkv_scan(P, C, 0)
    if "rw_scan1" in stages:
        stage_rwkv_scan(P, C, 1)
    if "rw_post" in stages:
        stage_rwkv_post(P, C)
    if "ffn1" in stages:
        stage_ffn(P, C, 1, 3, 4)
    if "final" in stages:
        stage_final(P, C, FINAL_SRC)
    outs.append(C.d_y)
    final = [o for o in outs if o is not None]
    C.final_bufs = final
    return nc, P, C


def finish_program(P, C, extra=()):
    bufs = list(C.final_bufs) + list(extra)
    P.wait_all("sp", bufs)
    P.close()


def arrange_w(w, nk=None):
    K, N = w.shape
    nk = K // 128
    return np.ascontiguousarray(w.reshape(nk, 128, N).transpose(1, 0, 2))


def col_layout(v):
    return np.ascontiguousarray(v.reshape(-1, 128).T)


def prep_shared(inp):
    m = {}
    m["ada_w0"] = arrange_w(inp["l0_ada_w"])
    m["ada_w1"] = arrange_w(inp["l1_ada_w"])
    m["ada_b"] = np.ascontiguousarray(np.stack([col_layout(inp["l0_ada_b"]), col_layout(inp["l1_ada_b"])], 1))
    m["normw"] = np.ascontiguousarray(np.stack([col_layout(inp[k]) for k in
                                               ("l0_norm1", "l0_norm2", "l1_norm1", "l1_norm2", "final_norm")], 1))
    m["w_in"] = arrange_w(inp["l0_w_in"])
    m["w_out"] = arrange_w(inp["l0_w_out"])
    ffn = [(inp["l0_ffn_up"], inp["l0_ffn_down"], inp["l0_ffn_conv_w"], inp["l0_ffn_conv_b"]),
           (inp["l1_ffn_up"], inp["l1_ffn_down"], inp["l1_ffn_conv_w"], inp["l1_ffn_conv_b"])]
    for l in range(2):
        up, dn, cw_, cb_ = ffn[l]
        m["ffn_up%d" % l] = arrange_w(up)
        m["ffn_dn%d" % l] = arrange_w(dn)
        cwb = np.concatenate([cw_, cb_[None, :]], 0)
        m["ffn_cw%d" % l] = np.ascontiguousarray(cwb.T.reshape(NFC, 128, 4).transpose(1, 0, 2))
    for nm in ("w_r", "w_k", "w_v", "w_o"):
        m[nm] = arrange_w(inp["l1_" + nm])
    g1p = np.zeros((D, 256), np.float32)
    g1p[:, :160] = inp["l1_g1"]
    m["rw_g1"] = arrange_w(g1p)
    g2 = np.zeros((256, D), np.float32)
    g2[:160] = inp["l1_g2"]
    m["rw_g2"] = arrange_w(g2)
    m["rw_lora1"] = arrange_w(np.concatenate([inp["l1_w1"][0], inp["l1_w1"][1], inp["l1_a1"][0], inp["l1_a1"][1]], 1))
    l2 = np.zeros((128, 4, D), np.float32)
    l2[:64, 0] = inp["l1_w2"][0]
    l2[:64, 1] = inp["l1_w2"][1]
    l2[:64, 2] = inp["l1_a2"][0]
    l2[:64, 3] = inp["l1_a2"][1]
    m["rw_lora2"] = l2
    cl = [col_layout(inp["l1_mu"][i]) for i in range(6)]
    cl += [col_layout(inp["l1_w0"][0]), col_layout(inp["l1_w0"][1]), col_layout(inp["l1_a0"][0]),
           col_layout(inp["l1_a0"][1]), col_layout(inp["l1_k_k"]), col_layout(inp["l1_k_a"]),
           col_layout(inp["l1_r_k"].reshape(-1)), col_layout(inp["l1_ln_w"])]
    m["rw_cols"] = np.ascontiguousarray(np.stack(cl, 1))
    m["rw_lnb"] = col_layout(inp["l1_ln_b"])
    m["hcol"] = np.ascontiguousarray(np.stack([inp["l0_filt_b1"], inp["l0_filt_b2"], inp["l0_filt_b3"],
                                               inp["l0_filt_freq"]], 1))
    m["fw1"] = np.ascontiguousarray(inp["l0_filt_w1"])
    m["fw23"] = np.ascontiguousarray(np.stack([inp["l0_filt_w2"], inp["l0_filt_w3"]], 1))
    m["fw4"] = np.ascontiguousarray(inp["l0_filt_w4"])
    m["fbias"] = col_layout(inp["l0_filt_bias"])
    sw = np.concatenate([inp["l0_short_w"], inp["l0_short_b"][None, :]], 0)
    m["shortw"] = np.ascontiguousarray(sw.T.reshape(6, 128, 4).transpose(1, 0, 2))
    m.update(host_consts())
    return m


def prep_core(xs, cs):
    m = {}
    m["x_in"] = np.ascontiguousarray(np.stack(xs, 0))
    c = np.stack(cs, 0)
    m["cT"] = np.ascontiguousarray(c.reshape(len(xs), NK, 128).transpose(2, 1, 0))
    return m


ALL_STAGES = ("adaln", "l0_inproj", "attn", "hyfilt", "hyena", "l0_outproj", "ffn0", "rw_norm", "rw_proj",
              "rw_scan0", "rw_scan1", "rw_post", "ffn1", "final")


def kernel(**inputs):
    inp = {k: np.asarray(v) for k, v in inputs.items()}
    xs = [inp["x_prompt"][i] for i in range(inp["x_prompt"].shape[0])] + \
         [inp["x_sample"][i] for i in range(inp["x_sample"].shape[0])]
    cs = [inp["c_prompt"][i] for i in range(inp["c_prompt"].shape[0])] + \
         [inp["c_sample"][i] for i in range(inp["c_sample"].shape[0])]
    nseq = len(xs)
    nb = inp["x_prompt"].shape[0]
    assign = []
    for core in range(NCORES):
        ids = []
        for slot in range(NS):
            sid = core + NCORES * slot
            ids.append(sid if sid < nseq else core)
        assign.append(ids)
    nc, P, C = build_program(ALL_STAGES, ())
    finish_program(P, C)
    shared = prep_shared(inp)
    in_maps = []
    for core in range(NCORES):
        m = dict(shared)
        m.update(prep_core([xs[i] for i in assign[core]], [cs[i] for i in assign[core]]))
        in_maps.append(m)
    res = run_bass_kernel_spmd(nc, in_maps, core_ids=list(range(NCORES)))
    outs = [None] * nseq
    for core in range(NCORES):
        y = np.asarray(res.results[core]["y_out"])
        for slot in range(NS):
            sid = core + NCORES * slot
            if sid < nseq:
                outs[sid] = y[slot]
    y_prompt = np.stack(outs[:nb], 0).astype(np.float32)
    y_sample = np.stack(outs[nb:], 0).astype(np.float32)
    return (y_prompt, y_sample)
```

```python
import contextlib
import math
import numpy as np
import concourse.bass as bass
import concourse.mybir as mybir
from concourse.bass_utils import run_bass_kernel_spmd

F32 = mybir.dt.float32
BF16 = mybir.dt.bfloat16
AF = mybir.ActivationFunctionType
ALU = mybir.AluOpType

D = 1024
L = 4096
NK = 8
BLK = 512
NBLK = L // BLK
NCORES = 8
NS = 3
DFF = 2816
NFC = DFF // 128
RMS_EPS = 1e-6

ENGS = ("pe", "act", "dve", "pool", "sp")
NDMASEM = 6


class Buf:
    __slots__ = ("t", "name", "w", "r", "a")

    def __init__(self, t, name):
        self.t = t
        self.name = name
        self.w = {}
        self.r = {}
        self.a = {}

    def __getitem__(self, idx):
        return self.t[idx]


class Prog:
    def __init__(self, nc):
        self.nc = nc
        self.base = contextlib.ExitStack()
        self.es = self.base
        self.streams = {e: [] for e in ENGS}
        self.cnt = {e: 0 for e in ENGS}
        self.seen = {e: {} for e in ENGS}
        self.sem = {}
        self.dtot = {}
        self.rr = {e: 0 for e in ENGS}
        for e in ENGS:
            self.sem[e] = self.base.enter_context(nc.semaphore("c_" + e))
        for q in ("sp", "act", "pool"):
            for i in range(NDMASEM):
                k = "d_%s%d" % (q, i)
                self.sem[k] = self.base.enter_context(nc.semaphore(k))
                self.dtot[k] = 0
        self.nbuf = 0
        self.ninst = 0

    @contextlib.contextmanager
    def scope(self):
        old = self.es
        es = contextlib.ExitStack()
        self.es = es
        try:
            yield
            self.barrier()
            self.emit()
        finally:
            es.close()
            self.es = old

    def sbuf(self, shape, dt, name=None):
        self.nbuf += 1
        name = (name or "sb") + "_%d" % self.nbuf
        t = self.es.enter_context(self.nc.sbuf_tensor(name, list(shape), dt))
        return Buf(t, name)

    def psum(self, shape, dt=F32, name=None):
        self.nbuf += 1
        name = (name or "ps") + "_%d" % self.nbuf
        t = self.es.enter_context(self.nc.psum_tensor(name, list(shape), dt))
        return Buf(t, name)

    def dram(self, name, shape, dt, kind="Internal"):
        t = self.nc.dram_tensor(name, list(shape), dt, kind=kind)
        return Buf(t.ap(), name)

    def _deps(self, eng, reads, writes, accs=()):
        need = {}
        seen = self.seen[eng]

        def add(k, v):
            if k == eng and eng == "pe":
                return
            if seen.get(k, 0) >= v:
                return
            if need.get(k, 0) < v:
                need[k] = v

        for b in reads:
            for k, v in b.w.items():
                add(k, v)
            for k, v in b.a.items():
                add(k, v)
        for b in writes:
            for k, v in b.w.items():
                add(k, v)
            for k, v in b.a.items():
                add(k, v)
            for k, v in b.r.items():
                add(k, v)
        for b in accs:
            for k, v in b.w.items():
                add(k, v)
            for k, v in b.r.items():
                add(k, v)
        for k, v in need.items():
            seen[k] = v
        return list(need.items())

    @staticmethod
    def _commit(ev, reads, writes, accs):
        k, v = ev
        for b in reads:
            if b.r.get(k, 0) < v:
                b.r[k] = v
        for b in writes:
            b.w.clear()
            b.w[k] = v
            b.r.clear()
            b.a.clear()
        for b in accs:
            if b.a.get(k, 0) < v:
                b.a[k] = v

    def op(self, eng, fn, reads=(), writes=(), accs=()):
        waits = self._deps(eng, reads, writes, accs)
        self.cnt[eng] += 1
        ev = (eng, self.cnt[eng])
        self.streams[eng].append((waits, fn, eng, 1))
        self._commit(ev, reads, writes, accs)
        self.ninst += 1 + len(waits)

    def dma(self, q, out, in_, reads=(), writes=(), accs=(), **kw):
        i = self.rr[q]
        self.rr[q] = (i + 1) % NDMASEM
        k = "d_%s%d" % (q, i)
        waits = self._deps(q, reads, writes, accs)
        prev = self.dtot[k]
        if prev > 0 and self.seen[q].get(k, 0) < prev:
            waits.append((k, prev))
            self.seen[q][k] = prev
        self.dtot[k] = prev + 16
        ev = (k, prev + 16)

        def fn(e, out=out, in_=in_, kw=kw):
            return e.dma_start(out=out, in_=in_, **kw)

        self.streams[q].append((waits, fn, k, 16))
        self._commit(ev, reads, writes, accs)
        self.ninst += 1 + len(waits)

    def barrier(self):
        tot = dict(self.dtot)
        for e in ENGS:
            tot[e] = self.cnt[e]
        for e in ENGS:
            waits = []
            for k, v in tot.items():
                if v > 0 and k != e and self.seen[e].get(k, 0) < v:
                    waits.append((k, v))
                    self.seen[e][k] = v
            if waits:
                self.streams[e].append((waits, None, None, 0))
                self.ninst += len(waits)

    def wait_all(self, eng, bufs):
        waits = self._deps(eng, bufs, ())
        self.streams[eng].append((waits, None, None, 0))

    def emit(self):
        if not any(self.streams.values()):
            return
        nc = self.nc
        engobj = {"pe": "tensor", "act": "scalar", "dve": "vector", "pool": "gpsimd", "sp": "sync"}
        with nc.Block() as block:
            for e in ENGS:
                stream = self.streams[e]

                def body(eng, stream=stream):
                    for waits, fn, sk, inc in stream:
                        for k, v in waits:
                            eng.wait_ge(self.sem[k], v)
                        if fn is not None:
                            fn(eng).then_inc(self.sem[sk], inc)

                getattr(block, engobj[e])(body)
        self.streams = {e: [] for e in ENGS}

    def close(self):
        self.emit()
        self.base.close()


class Rot:
    def __init__(self, bufs):
        self.bufs = bufs
        self.i = 0

    def next(self):
        b = self.bufs[self.i % len(self.bufs)]
        self.i += 1
        return b


class Ctx:
    pass


def mm(P, out_buf, out_ap, lhsT_buf, lhsT_ap, rhs_buf, rhs_ap, start, stop):
    P.op("pe", lambda e: e.matmul(out_ap, lhsT=lhsT_ap, rhs=rhs_ap, start=start, stop=stop),
         reads=[lhsT_buf, rhs_buf], writes=[out_buf])


def load_weight_bf16(P, C, dram_buf, nk, ncols, name):
    wb = P.sbuf([128, nk, ncols], BF16, name)
    grp = max(32, (C.wstage_n // nk) // 32 * 32)
    i = 0
    for c0 in range(0, ncols, grp):
        cw = min(grp, ncols - c0)
        st = C.wstage.next()
        P.dma("sp", st[:, 0:nk * cw].rearrange("p (k c) -> p k c", c=cw),
              dram_buf[:, :, c0:c0 + cw], reads=[dram_buf], writes=[st])
        if i % 2 == 0:
            P.op("dve", lambda e, st=st, c0=c0, cw=cw: e.tensor_copy(
                out=wb[:, :, c0:c0 + cw], in_=st[:, 0:nk * cw].rearrange("p (k c) -> p k c", c=cw)),
                reads=[st], accs=[wb])
        else:
            P.op("act", lambda e, st=st, c0=c0, cw=cw: e.copy(
                out=wb[:, :, c0:c0 + cw], in_=st[:, 0:nk * cw].rearrange("p (k c) -> p k c", c=cw)),
                reads=[st], accs=[wb])
        i += 1
    return wb


def norm_mod(P, C, xT, W, gain, shift, s, out_buf, out_dt_is_bf16=True):
    ssp = C.ps_ss.next()
    for k in range(NK):
        sq = C.nm_sq.next()
        P.op("act", lambda e, sq=sq, k=k: e.activation(out=sq[:, 0:W], in_=xT[:, k, 0:W], func=AF.Square),
             reads=[xT], writes=[sq])
        mm(P, ssp, ssp[:, 0:W], C.ones32, C.ones32[:], sq, sq[:, 0:W], k == 0, k == NK - 1)
    rstd = C.nm_rstd.next()
    P.op("act", lambda e: e.activation(out=rstd[:, 0:W], in_=ssp[:, 0:W], func=AF.Sqrt,
                                       scale=1.0 / D, bias=C.eps_col[:, 0:1]),
         reads=[ssp, C.eps_col], writes=[rstd])
    P.op("dve", lambda e: e.reciprocal(out=rstd[:, 0:W], in_=rstd[:, 0:W]), reads=[rstd], writes=[rstd])
    for k in range(NK):
        if shift is None:
            P.op("dve", lambda e, k=k: e.scalar_tensor_tensor(
                out=out_buf[:, k, 0:W], in0=xT[:, k, 0:W], scalar=gain[:, k, s:s + 1], in1=rstd[:, 0:W],
                op0=ALU.mult, op1=ALU.mult), reads=[xT, gain, rstd], accs=[out_buf])
        else:
            tmp = C.nm_tmp.next()
            P.op("dve", lambda e, k=k, tmp=tmp: e.scalar_tensor_tensor(
                out=tmp[:, 0:W], in0=xT[:, k, 0:W], scalar=gain[:, k, s:s + 1], in1=rstd[:, 0:W],
                op0=ALU.mult, op1=ALU.mult), reads=[xT, gain, rstd], writes=[tmp])
            P.op("act", lambda e, k=k, tmp=tmp: e.activation(
                out=out_buf[:, k, 0:W], in_=tmp[:, 0:W], func=AF.Identity, bias=shift[:, k, s:s + 1], scale=1.0),
                reads=[tmp, shift], accs=[out_buf])


def alloc_norm_scratch(P, C, W=BLK):
    C.nm_sq = Rot([P.sbuf([128, W], F32, "nmsq") for _ in range(2)])
    C.nm_rstd = Rot([P.sbuf([128, W], F32, "nmrs") for _ in range(2)])
    C.nm_tmp = Rot([P.sbuf([128, W], F32, "nmtmp") for _ in range(2)])
    C.ps_ss = Rot([P.psum([128, W], F32, "psss")])


def fence(P, buf):
    return buf


def stage_adaln(P, C):
    with P.scope():
        cT = P.sbuf([128, NK, NS], F32, "cT")
        P.dma("sp", cT[:], C.d_cT[:], reads=[C.d_cT], writes=[cT])
        sc = P.sbuf([128, NK, NS], F32, "silu_c")
        P.op("act", lambda e: e.activation(out=sc[:], in_=cT[:], func=AF.Silu), reads=[cT], writes=[sc])
        wts = Rot([P.sbuf([128, NK, 1024], F32, "adaw") for _ in range(2)])
        pss = Rot([P.psum([128, 512], F32, "adaps") for _ in range(2)])
        adab = P.sbuf([128, 2, 48], F32, "adab")
        P.dma("act", adab[:], C.d_adab[:], reads=[C.d_adab], writes=[adab])
        for l in range(2):
            for j in range(6):
                wt = wts.next()
                P.dma("sp" if j % 2 == 0 else "act", wt[:], C.d_adaw[l][:, :, j * 1024:(j + 1) * 1024],
                      reads=[C.d_adaw[l]], writes=[wt])
                psb = pss.next()
                ps = psb[:, 0:8 * NS].rearrange("p (c s) -> p c s", s=NS)
                for cc in range(8):
                    for k in range(NK):
                        mm(P, psb, ps[:, cc, :], wt, wt[:, k, cc * 128:(cc + 1) * 128], sc, sc[:, k, :],
                           k == 0, k == NK - 1)
                P.op("dve", lambda e, l=l, j=j, ps=ps: e.tensor_tensor(
                    out=C.modT[l][:, j * 8:(j + 1) * 8, :], in0=ps,
                    in1=adab[:, l, j * 8:(j + 1) * 8].unsqueeze(2).to_broadcast([128, 8, NS]), op=ALU.add),
                    reads=[psb, adab], accs=[C.modT[l]])
        nw = P.sbuf([128, 5, NK], F32, "normw")
        P.dma("act", nw[:], C.d_normw[:], reads=[C.d_normw], writes=[nw])
        for l in range(2):
            for which in range(2):
                jsc = 1 + 3 * which
                g = C.gain[l][which]
                P.op("dve", lambda e, l=l, which=which, jsc=jsc, g=g: e.scalar_tensor_tensor(
                    out=g[:], in0=C.modT[l][:, jsc * 8:(jsc + 1) * 8, :], scalar=1.0,
                    in1=nw[:, 2 * l + which, :].unsqueeze(2).to_broadcast([128, NK, NS]),
                    op0=ALU.add, op1=ALU.mult), reads=[C.modT[l], nw], writes=[g])
        P.op("dve", lambda e: e.tensor_copy(
            out=C.gain_fin[:], in_=nw[:, 4, :].unsqueeze(2).to_broadcast([128, NK, NS])),
            reads=[nw], writes=[C.gain_fin])


def mod_vec(C, l, j):
    return C.modT[l]


def stage_l0_inproj(P, C):
    with P.scope():
        C.wstage_n = 2048
        C.wstage = Rot([P.sbuf([128, 2048], F32, "wst") for _ in range(2)])
        w_in = load_weight_bf16(P, C, C.d_w_in, NK, 3072, "w_in")
        alloc_norm_scratch(P, C)
        xins = Rot([P.sbuf([128, 4, D], F32, "xin") for _ in range(2)])
        xTs = Rot([P.sbuf([128, NK, BLK], F32, "xT") for _ in range(2)])
        hTs = Rot([P.sbuf([128, NK, BLK], BF16, "hT") for _ in range(2)])
        pts = Rot([P.psum([128, BLK], F32, "pt") for _ in range(2)])
        pps = Rot([P.psum([128, BLK], F32, "pp") for _ in range(2)])
        pvs = Rot([P.psum([128, 1024], F32, "pv") for _ in range(1)])
        qks = Rot([P.sbuf([128, 12, BLK], BF16, "qk") for _ in range(1)])
        hys = Rot([P.sbuf([128, 6, BLK], F32, "hy") for _ in range(1)])
        vas = []
        for _ in range(2):
            va = P.sbuf([128, 4, 12, 65], BF16, "vaug")
            P.op("pool", lambda e, va=va: e.memset(va[:], 1.0), writes=[va])
            vas.append(va)
        vas = Rot(vas)
        mod = C.modT[0]
        ev = 0
        for s in range(NS):
            for blk in range(NBLK):
                t0 = blk * BLK
                xin = xins.next()
                P.dma("sp", xin[:], C.d_x[s, t0:t0 + BLK, :].rearrange("(j p) f -> p j f", p=128),
                      reads=[C.d_x], writes=[xin])
                xT = xTs.next()
                for k in range(NK):
                    pt = pts.next()
                    for j in range(4):
                        P.op("pe", lambda e, pt=pt, j=j, k=k, xin=xin: e.transpose(
                            pt[:, j * 128:(j + 1) * 128], xin[:, j, k * 128:(k + 1) * 128], C.ident32[:]),
                            reads=[xin, C.ident32], writes=[pt])
                    if k % 2 == 0:
                        P.op("act", lambda e, pt=pt, k=k, xT=xT: e.copy(out=xT[:, k, :], in_=pt[:]),
                             reads=[pt], accs=[xT])
                    else:
                        P.op("dve", lambda e, pt=pt, k=k, xT=xT: e.tensor_copy(out=xT[:, k, :], in_=pt[:]),
                             reads=[pt], accs=[xT])
                P.dma("pool", C.d_xT[0][s][:, :, t0:t0 + BLK], xT[:], reads=[xT], accs=[C.d_xT[0][s]])
                hT = hTs.next()
                norm_mod(P, C, xT, BLK, C.gain[0][0], mod_shift(C, 0, 0), s, hT)
                qk = qks.next()
                for c in range(12):
                    pp = pps.next()
                    for k in range(NK):
                        mm(P, pp, pp[:], w_in, w_in[:, k, c * 128:(c + 1) * 128], hT, hT[:, k, :], k == 0, k == NK - 1)
                    if c % 2 == 0:
                        P.op("act", lambda e, pp=pp, c=c, qk=qk: e.copy(out=qk[:, c, :], in_=pp[:]),
                             reads=[pp], accs=[qk])
                    else:
                        P.op("dve", lambda e, pp=pp, c=c, qk=qk: e.tensor_copy(out=qk[:, c, :], in_=pp[:]),
                             reads=[pp], accs=[qk])
                P.dma("pool", C.d_qkT[s][:, :, t0:t0 + BLK], qk[:], reads=[qk], accs=[C.d_qkT[s]])
                va = vas.next()
                for j in range(4):
                    pv = pvs.next()
                    for k in range(NK):
                        mm(P, pv, pv[:, 0:512], hT, hT[:, k, j * 128:(j + 1) * 128], w_in, w_in[:, k, 1536:2048],
                           k == 0, k == NK - 1)
                    for k in range(NK):
                        mm(P, pv, pv[:, 512:768], hT, hT[:, k, j * 128:(j + 1) * 128], w_in, w_in[:, k, 2048:2304],
                           k == 0, k == NK - 1)
                    P.op("act" if j % 2 == 0 else "dve",
                         (lambda e, pv=pv, j=j, va=va: e.copy(
                             out=va[:, j, :, 0:64], in_=pv[:, 0:768].rearrange("p (h d) -> p h d", d=64)))
                         if j % 2 == 0 else
                         (lambda e, pv=pv, j=j, va=va: e.tensor_copy(
                             out=va[:, j, :, 0:64], in_=pv[:, 0:768].rearrange("p (h d) -> p h d", d=64))),
                         reads=[pv], accs=[va])
                P.dma("pool", C.d_vaug[s][t0:t0 + BLK].rearrange("(j p) h d -> p j h d", p=128), va[:],
                      reads=[va], accs=[C.d_vaug[s]])
                hy = hys.next()
                for c in range(6):
                    pp = pps.next()
                    for k in range(NK):
                        mm(P, pp, pp[:], w_in, w_in[:, k, 2304 + c * 128:2304 + (c + 1) * 128], hT, hT[:, k, :],
                           k == 0, k == NK - 1)
                    if c % 2 == 0:
                        P.op("act", lambda e, pp=pp, c=c, hy=hy: e.copy(out=hy[:, c, :], in_=pp[:]),
                             reads=[pp], accs=[hy])
                    else:
                        P.op("dve", lambda e, pp=pp, c=c, hy=hy: e.tensor_copy(out=hy[:, c, :], in_=pp[:]),
                             reads=[pp], accs=[hy])
                P.dma("pool", C.d_hyT[s][:, :, t0:t0 + BLK], hy[:], reads=[hy], accs=[C.d_hyT[s]])


DILS = (1, 4, 16)
FINAL_SRC = 4
import os
ATT_STOP = int(os.environ.get("ATT_STOP", "9"))
ATT_GROUPS = tuple(int(x) for x in os.environ.get("ATT_GROUPS", "0,1,2").split(","))


def stage_attention(P, C):
    with P.scope():
        tabs = P.sbuf([128, 9, 4, 128], F32, "abias")
        P.dma("sp", tabs[:], C.d_abias[:], reads=[C.d_abias], writes=[tabs])
        qTs = Rot([P.sbuf([128, 2, L], BF16, "qT") for _ in range(2)])
        kTs = Rot([P.sbuf([128, 2, L], BF16, "kT") for _ in range(2)])
        vts = Rot([P.sbuf([128, 4, 65], BF16, "vt") for _ in range(4)])
        pss = Rot([P.psum([128, 512], F32, "pss") for _ in range(4)])
        pos = Rot([P.psum([128, 512], F32, "pso") for _ in range(2)])
        sbs = Rot([P.sbuf([128, 4, 128], F32, "ssb") for _ in range(2)])
        pTs = Rot([P.sbuf([128, 4, 128], BF16, "pT") for _ in range(4)])
        osb = Rot([P.sbuf([128, 4, 65], F32, "osb") for _ in range(3)])
        cnt = 0
        for s in range(NS):
            for g, d in enumerate(DILS):
                if g not in ATT_GROUPS:
                    continue
                qT = qTs.next()
                kT = kTs.next()
                P.dma("sp", qT[:], C.d_qkT[s][:, 2 * g:2 * g + 2, :], reads=[C.d_qkT[s]], writes=[qT])
                P.dma("act", kT[:], C.d_qkT[s][:, 6 + 2 * g:6 + 2 * g + 2, :], reads=[C.d_qkT[s]], writes=[kT])
                n = L // d
                ntile = n // 128
                for r in range(d):
                    def load_v(m):
                        vt = vts.next()
                        if m == 0:
                            ks, nk = 0, 64
                        elif m == ntile:
                            ks, nk = n - 64, 64
                        else:
                            ks, nk = 128 * m - 64, 128
                        t0 = ks * d + r
                        P.dma("sp", vt[0:nk], C.d_vaug[s][t0:t0 + (nk - 1) * d + 1:d, 4 * g:4 * g + 4, :],
                              reads=[C.d_vaug[s]], writes=[vt])
                        return vt, ks, nk
                    vcur = load_v(0)
                    for qt in range(ntile):
                        vnext = load_v(qt + 1)
                        i0 = qt * 128
                        q0 = i0 * d + r
                        qsl = slice(q0, q0 + 127 * d + 1, d)
                        pTl = []
                        if ATT_STOP <= 1:
                            vcur = vnext
                            continue
                        for bi, (vt, ks, nk) in enumerate((vcur, vnext)):
                            if bi == 0:
                                ti = 3 * g + (2 if qt == 0 else 0)
                            else:
                                ti = 3 * g + 1
                            k0 = ks * d + r
                            ksl = slice(k0, k0 + (nk - 1) * d + 1, d)
                            sb = sbs.next()
                            pT = pTs.next()
                            for hh in range(2):
                                ps = pss.next()
                                psv = ps[:, 0:256].rearrange("p (a q) -> p a q", q=128)
                                for hp in range(2):
                                    mm(P, ps, psv[0:nk, hp, :], kT, kT[hh * 64:(hh + 1) * 64, hp, ksl],
                                       qT, qT[hh * 64:(hh + 1) * 64, hp, qsl], True, True)
                                P.op("dve", lambda e, ps=ps, psv=psv, sb=sb, nk=nk, ti=ti, hh=hh: e.scalar_tensor_tensor(
                                    out=sb[0:nk, hh::2, :], in0=psv[0:nk], scalar=0.125, in1=tabs[0:nk, ti, hh::2, :],
                                    op0=ALU.mult, op1=ALU.add), reads=[ps, tabs], accs=[sb])
                            P.op("act", lambda e, sb=sb, pT=pT, nk=nk: e.activation(
                                out=pT[0:nk], in_=sb[0:nk], func=AF.Exp), reads=[sb], writes=[pT])
                            pTl.append((pT, vt, nk))
                        if ATT_STOP <= 2:
                            vcur = vnext
                            continue
                        po = pos.next()
                        pov = po[:, 0:260].rearrange("p (h d) -> p h d", d=65)
                        for h in range(4):
                            for bi, (pT, vt, nk) in enumerate(pTl):
                                mm(P, po, pov[:, h, :], pT, pT[0:nk, h, :], vt, vt[0:nk, h, :], bi == 0, bi == 1)
                        ob = osb.next()
                        if cnt % 2 == 0:
                            P.op("act", lambda e, ob=ob, pov=pov: e.copy(out=ob[:], in_=pov), reads=[po], writes=[ob])
                        else:
                            P.op("dve", lambda e, ob=ob, pov=pov: e.tensor_copy(out=ob[:], in_=pov),
                                 reads=[po], writes=[ob])
                        cnt += 1
                        if ATT_STOP <= 3:
                            vcur = vnext
                            continue
                        P.dma("pool", C.d_oacc[s][g, q0:q0 + 127 * d + 1:d, :, :], ob[:],
                              reads=[ob], accs=[C.d_oacc[s]])
                        vcur = vnext


HY_GRP = 32
TWO_PI = 2.0 * math.pi


def alloc_fft(P, C):
    F = Ctx()
    F.w128 = P.sbuf([64, 256], F32, "w128")
    F.tw = P.sbuf([128, 2, 128], F32, "tw")
    F.bd = P.sbuf([128, 3, 128], F32, "bd")
    F.bdc = P.sbuf([128, 2, 256], F32, "bdc")
    F.twi = P.sbuf([128, 2, 128], F32, "twi")
    F.vinv = P.sbuf([128, 2, 64], F32, "vinv")
    for b, d in ((F.w128, C.d_w128), (F.tw, C.d_tw), (F.bd, C.d_bd), (F.bdc, C.d_bdc), (F.twi, C.d_twi),
                 (F.vinv, C.d_vinv)):
        P.dma("act", b[:], d[:], reads=[d], writes=[b])
    F.psA = Rot([P.psum([128, 512], F32, "psA") for _ in range(2)])
    F.psX = Rot([P.psum([128, 512], F32, "psX") for _ in range(2)])
    F.p1 = Rot([P.sbuf([128, 2, 128], F32, "fp1") for _ in range(2)])
    F.p2 = Rot([P.sbuf([128, 2, 128], F32, "fp2") for _ in range(2)])
    F.b = Rot([P.sbuf([128, 2, 128], F32, "fb") for _ in range(2)])
    return F


def cmul(P, F, src_buf, src_v, tab_buf, tab_r, tab_i, out_buf, out_v):
    p1 = F.p1.next()
    p2 = F.p2.next()
    n = src_v.shape[0]
    P.op("dve", lambda e: e.tensor_tensor(out=p1[0:n], in0=src_v, in1=tab_r.unsqueeze(1).to_broadcast([n, 2, 128]),
                                          op=ALU.mult), reads=[src_buf, tab_buf], writes=[p1])
    P.op("dve", lambda e: e.tensor_tensor(out=p2[0:n], in0=src_v, in1=tab_i.unsqueeze(1).to_broadcast([n, 2, 128]),
                                          op=ALU.mult), reads=[src_buf, tab_buf], writes=[p2])
    P.op("dve", lambda e: e.tensor_tensor(out=out_v[:, 0, :], in0=p1[0:n, 0, :], in1=p2[0:n, 1, :], op=ALU.subtract),
         reads=[p1, p2], accs=[out_buf])
    P.op("dve", lambda e: e.tensor_tensor(out=out_v[:, 1, :], in0=p2[0:n, 0, :], in1=p1[0:n, 1, :], op=ALU.add),
         reads=[p1, p2], accs=[out_buf])


def fft_fwd_pair(P, F, xin_buf, xin_ap):
    psA = F.psA.next()
    mm(P, psA, psA[:, 0:256], xin_buf, xin_ap, F.w128, F.w128[:], True, True)
    b = F.b.next()
    cmul(P, F, psA, psA[:, 0:256].rearrange("p (c k) -> p c k", k=128), F.tw, F.tw[:, 0, :], F.tw[:, 1, :], b, b[:])
    psX = F.psX.next()
    mm(P, psX, psX[:, 0:128], F.bd, F.bd[:, 0, :], b, b[:, 0, :], True, False)
    mm(P, psX, psX[:, 0:128], F.bd, F.bd[:, 2, :], b, b[:, 1, :], False, True)
    mm(P, psX, psX[:, 128:256], F.bd, F.bd[:, 1, :], b, b[:, 0, :], True, False)
    mm(P, psX, psX[:, 128:256], F.bd, F.bd[:, 0, :], b, b[:, 1, :], False, True)
    return psX, psX[:, 0:256].rearrange("p (c k) -> p c k", k=128)


def fft_fwd_multi(P, F, inputs):
    psAs = []
    for buf, ap in inputs:
        psA = F.psA.next()
        mm(P, psA, psA[:, 0:256], buf, ap, F.w128, F.w128[:], True, True)
        psAs.append(psA)
    bs = []
    for psA in psAs:
        b = F.b.next()
        cmul(P, F, psA, psA[:, 0:256].rearrange("p (c k) -> p c k", k=128), F.tw, F.tw[:, 0, :], F.tw[:, 1, :], b, b[:])
        bs.append(b)
    outs = []
    for b in bs:
        psX = F.psX.next()
        mm(P, psX, psX[:, 0:128], F.bd, F.bd[:, 0, :], b, b[:, 0, :], True, False)
        mm(P, psX, psX[:, 0:128], F.bd, F.bd[:, 2, :], b, b[:, 1, :], False, True)
        mm(P, psX, psX[:, 128:256], F.bd, F.bd[:, 1, :], b, b[:, 0, :], True, False)
        mm(P, psX, psX[:, 128:256], F.bd, F.bd[:, 0, :], b, b[:, 1, :], False, True)
        outs.append((psX, psX[:, 0:256].rearrange("p (c k) -> p c k", k=128)))
    return outs


def wrap_pi(P, u, m, n, W):
    for _ in range(2):
        P.op("dve", lambda e: e.tensor_single_scalar(out=m[0:n, 0:W], in_=u[0:n, 0:W], scalar=math.pi, op=ALU.is_gt),
             reads=[u], writes=[m])
        P.op("dve", lambda e: e.scalar_tensor_tensor(out=u[0:n, 0:W], in0=m[0:n, 0:W], scalar=-TWO_PI, in1=u[0:n, 0:W],
                                                     op0=ALU.mult, op1=ALU.add), reads=[m, u], writes=[u])
        P.op("dve", lambda e: e.tensor_single_scalar(out=m[0:n, 0:W], in_=u[0:n, 0:W], scalar=-math.pi, op=ALU.is_lt),
             reads=[u], writes=[m])
        P.op("dve", lambda e: e.scalar_tensor_tensor(out=u[0:n, 0:W], in0=m[0:n, 0:W], scalar=TWO_PI, in1=u[0:n, 0:W],
                                                     op0=ALU.mult, op1=ALU.add), reads=[m, u], writes=[u])


def stage_hyena_filter(P, C):
    with P.scope():
        hcol = P.sbuf([64, 8], F32, "hcol")
        P.dma("sp", hcol[:, 0:4], C.d_hcol[:], reads=[C.d_hcol], writes=[hcol])
        for i in range(3):
            P.op("dve", lambda e, i=i: e.tensor_tensor(out=hcol[:, 4 + i:5 + i], in0=hcol[:, i:i + 1],
                                                       in1=hcol[:, 3:4], op=ALU.mult), reads=[hcol], writes=[hcol])
        w1 = P.sbuf([33, 64], F32, "fw1")
        w23 = P.sbuf([64, 2, 64], F32, "fw23")
        w4 = P.sbuf([64, 512], F32, "fw4")
        fbias = P.sbuf([128, 2], F32, "fbias")
        P.dma("sp", w1[:], C.d_fw1[:], reads=[C.d_fw1], writes=[w1])
        P.dma("sp", w23[:], C.d_fw23[:], reads=[C.d_fw23], writes=[w23])
        P.dma("sp", w4[:], C.d_fw4[:], reads=[C.d_fw4], writes=[w4])
        P.dma("sp", fbias[:], C.d_fbias[:], reads=[C.d_fbias], writes=[fbias])
        hT = P.sbuf([128, 4, L], F32, "filt_hT")
        pos = Rot([P.sbuf([33, BLK], F32, "pos") for _ in range(2)])
        win = Rot([P.sbuf([128, 2, BLK], F32, "win") for _ in range(2)])
        us = Rot([P.sbuf([64, BLK], F32, "fu") for _ in range(3)])
        ms = Rot([P.sbuf([64, BLK], F32, "fm") for _ in range(2)])
        pps = Rot([P.psum([128, BLK], F32, "fpp") for _ in range(2)])
        for blk in range(NBLK):
            t0 = blk * BLK
            po = pos.next()
            wn = win.next()
            P.dma("sp", po[:], C.d_posT[:, t0:t0 + BLK], reads=[C.d_posT], writes=[po])
            P.dma("act", wn[:], C.d_winT[:, :, t0:t0 + BLK], reads=[C.d_winT], writes=[wn])
            prev_buf, prev_ap, kdim = po, po[:], 33
            for layer in range(3):
                pp = pps.next()
                if layer == 0:
                    mm(P, pp, pp[0:64, :], w1, w1[:], prev_buf, prev_ap, True, True)
                else:
                    mm(P, pp, pp[0:64, :], w23, w23[:, layer - 1, :], prev_buf, prev_ap, True, True)
                u = us.next()
                m = ms.next()
                P.op("dve", lambda e, pp=pp, u=u, layer=layer: e.tensor_scalar(
                    out=u[:], in0=pp[0:64, :], scalar1=hcol[:, 3:4], scalar2=hcol[:, 4 + layer:5 + layer],
                    op0=ALU.mult, op1=ALU.add), reads=[pp, hcol], writes=[u])
                wrap_pi(P, u, m, 64, BLK)
                P.op("act", lambda e, u=u: e.activation(out=u[:], in_=u[:], func=AF.Sin), reads=[u], writes=[u])
                prev_buf, prev_ap = u, u[:]
            for c in range(4):
                pp = pps.next()
                mm(P, pp, pp[:], w4, w4[:, c * 128:(c + 1) * 128], prev_buf, prev_ap, True, True)
                P.op("dve", lambda e, pp=pp, c=c, wn=wn, t0=t0: e.tensor_tensor(
                    out=hT[:, c, t0:t0 + BLK], in0=pp[:], in1=wn[:, c % 2, :], op=ALU.mult),
                    reads=[pp, wn], accs=[hT])
        junk = P.sbuf([128, L], F32, "fjunk")
        acc = P.sbuf([128, 8], F32, "facc")
        P.op("pool", lambda e: e.memset(acc[:], 0.0), writes=[acc])
        for c in range(4):
            lo = 0 if c < 2 else 1
            P.op("act", lambda e, c=c, lo=lo: e.activation(out=junk[:, lo:L], in_=hT[:, c, lo:L], func=AF.Abs,
                                                           accum_out=acc[:, c:c + 1]),
                 reads=[hT], writes=[junk, acc])
        P.op("dve", lambda e: e.tensor_tensor(out=acc[:, 4:6], in0=acc[:, 0:2], in1=acc[:, 2:4], op=ALU.add),
             reads=[acc], writes=[acc])
        P.op("dve", lambda e: e.reciprocal(out=acc[:, 6:8], in_=acc[:, 4:6]), reads=[acc], writes=[acc])
        for c in range(4):
            P.op("dve", lambda e, c=c: e.tensor_scalar(
                out=hT[:, c, :], in0=hT[:, c, :], scalar1=acc[:, 6 + c % 2:7 + c % 2], scalar2=None, op0=ALU.mult),
                reads=[hT, acc], writes=[hT])
        for j in range(2):
            P.op("dve", lambda e, j=j: e.tensor_tensor(out=hT[:, j, 0:1], in0=hT[:, j, 0:1], in1=fbias[:, j:j + 1],
                                                       op=ALU.add), reads=[hT, fbias], writes=[hT])
            P.op("dve", lambda e, j=j: e.memset(hT[:, 2 + j, 0:1], 0.0), writes=[hT])
        P.dma("pool", C.d_filtT[:], hT[:], reads=[hT], writes=[C.d_filtT])
    with P.scope():
        F = alloc_fft(P, C)
        xf = Rot([P.sbuf([64, HY_GRP, 64], F32, "xf") for _ in range(2)])
        xb = Rot([P.sbuf([64, HY_GRP, 64], F32, "xb") for _ in range(2)])
        fo = Rot([P.sbuf([128, HY_GRP // 2, 2, 128], F32, "fo") for _ in range(2)])
        for j in range(2):
            for c0 in range(0, 128, HY_GRP):
                a = xf.next()
                b = xb.next()
                P.dma("sp", a[:], C.d_filtT[c0:c0 + HY_GRP, j, :].rearrange("c (a b) -> a c b", b=64),
                      reads=[C.d_filtT], writes=[a])
                P.dma("act", b[:], C.d_filtT[c0:c0 + HY_GRP, 2 + j, :].rearrange("c (a b) -> a c b", b=64),
                      reads=[C.d_filtT], writes=[b])
                o = fo.next()
                for i in range(HY_GRP // 2):
                    (psf, vf), (psb, vb) = fft_fwd_multi(P, F, [
                        (a, a[:, 2 * i:2 * i + 2, :].rearrange("p c n -> p (c n)")),
                        (b, b[:, 2 * i:2 * i + 2, :].rearrange("p c n -> p (c n)"))])
                    P.op("act", lambda e, o=o, i=i, vb=vb: e.copy(out=o[:, i], in_=vb), reads=[psb], accs=[o])
                    P.op("dve", lambda e, o=o, i=i, vf=vf: e.tensor_tensor(out=o[:, i, 0, :], in0=vf[:, 0, :],
                                                                           in1=o[:, i, 0, :], op=ALU.add),
                         reads=[psf, o], accs=[o])
                    P.op("dve", lambda e, o=o, i=i, vf=vf: e.tensor_tensor(out=o[:, i, 1, :], in0=vf[:, 1, :],
                                                                           in1=o[:, i, 1, :], op=ALU.subtract),
                         reads=[psf, o], accs=[o])
                pr0 = (j * 128 + c0) // 2
                P.dma("pool", C.d_F[:, pr0:pr0 + HY_GRP // 2], o[:], reads=[o], accs=[C.d_F])


def dwconv3(P, src, dst, W, wt, c):
    P.op("act", lambda e: e.activation(out=dst[:, 0:W], in_=src[:, 0:W], func=AF.Identity,
                                       scale=wt[:, c, 1:2], bias=wt[:, c, 3:4]), reads=[src, wt], writes=[dst])
    P.op("dve", lambda e: e.scalar_tensor_tensor(out=dst[:, 1:W], in0=src[:, 0:W - 1], scalar=wt[:, c, 0:1],
                                                 in1=dst[:, 1:W], op0=ALU.mult, op1=ALU.add),
         reads=[src, wt, dst], writes=[dst])
    P.op("dve", lambda e: e.scalar_tensor_tensor(out=dst[:, 0:W - 1], in0=src[:, 1:W], scalar=wt[:, c, 2:3],
                                                 in1=dst[:, 0:W - 1], op0=ALU.mult, op1=ALU.add),
         reads=[src, wt, dst], writes=[dst])


def stage_hyena(P, C):
    with P.scope():
        swt = P.sbuf([128, 6, 4], F32, "shortw")
        P.dma("sp", swt[:], C.d_shortw[:], reads=[C.d_shortw], writes=[swt])
        srcs = Rot([P.sbuf([128, L], F32, "hysrc") for _ in range(3)])
        dsts = Rot([P.sbuf([128, L], F32, "hydst") for _ in range(4)])
        for s in range(NS):
            for j in range(2):
                res = []
                for part in range(3):
                    c = 2 * part + j
                    src = srcs.next()
                    P.dma("sp" if part != 1 else "act", src[:], C.d_hyT[s][:, c, :], reads=[C.d_hyT[s]], writes=[src])
                    dst = dsts.next()
                    dwconv3(P, src, dst, L, swt, c)
                    res.append(dst)
                x0, x1, v = res
                P.op("dve", lambda e, x1=x1, v=v: e.tensor_tensor(out=v[:], in0=v[:], in1=x1[:], op=ALU.mult),
                     reads=[v, x1], writes=[v])
                P.dma("pool", C.d_zT[s][:, j, :], v[:], reads=[v], accs=[C.d_zT[s]])
                P.dma("pool", C.d_x0T[s][:, j, :], x0[:], reads=[x0], accs=[C.d_x0T[s]])
    with P.scope():
        F = alloc_fft(P, C)
        psC = Rot([P.psum([128, 512], F32, "psC") for _ in range(2)])
        psY = Rot([P.psum([128, 512], F32, "psY") for _ in range(2)])
        zin = Rot([P.sbuf([64, HY_GRP, 64], F32, "zin") for _ in range(2)])
        x0in = Rot([P.sbuf([64, HY_GRP, 64], F32, "x0in") for _ in range(2)])
        oin = Rot([P.sbuf([64, HY_GRP, 64], F32, "oin") for _ in range(2)])
        fsp = Rot([P.sbuf([128, HY_GRP // 2, 2, 128], F32, "fsp") for _ in range(2)])
        ys = Rot([P.sbuf([128, 2, 128], F32, "fy") for _ in range(2)])
        ds = Rot([P.sbuf([128, 2, 128], F32, "fd") for _ in range(2)])
        for s in range(NS):
            for j in range(2):
                for c0 in range(0, 128, HY_GRP):
                    zi = zin.next()
                    xi = x0in.next()
                    fs = fsp.next()
                    oi = oin.next()
                    P.dma("sp", zi[:], C.d_zT[s][c0:c0 + HY_GRP, j, :].rearrange("c (a b) -> a c b", b=64),
                          reads=[C.d_zT[s]], writes=[zi])
                    P.dma("act", xi[:], C.d_x0T[s][c0:c0 + HY_GRP, j, :].rearrange("c (a b) -> a c b", b=64),
                          reads=[C.d_x0T[s]], writes=[xi])
                    pr0 = (j * 128 + c0) // 2
                    P.dma("sp", fs[:], C.d_F[:, pr0:pr0 + HY_GRP // 2], reads=[C.d_F], writes=[fs])
                    for i0 in range(0, HY_GRP // 2, 2):
                        pr = (i0, i0 + 1)
                        st = {}
                        for i in pr:
                            psA = F.psA.next()
                            mm(P, psA, psA[:, 0:256], zi, zi[:, 2 * i:2 * i + 2, :].rearrange("p c n -> p (c n)"),
                               F.w128, F.w128[:], True, True)
                            st[i] = dict(psA=psA)
                        for i in pr:
                            b = F.b.next()
                            psA = st[i]["psA"]
                            cmul(P, F, psA, psA[:, 0:256].rearrange("p (c k) -> p c k", k=128), F.tw, F.tw[:, 0, :],
                                 F.tw[:, 1, :], b, b[:])
                            st[i]["b"] = b
                        for i in pr:
                            b = st[i]["b"]
                            psX = F.psX.next()
                            mm(P, psX, psX[:, 0:128], F.bd, F.bd[:, 0, :], b, b[:, 0, :], True, False)
                            mm(P, psX, psX[:, 0:128], F.bd, F.bd[:, 2, :], b, b[:, 1, :], False, True)
                            mm(P, psX, psX[:, 128:256], F.bd, F.bd[:, 1, :], b, b[:, 0, :], True, False)
                            mm(P, psX, psX[:, 128:256], F.bd, F.bd[:, 0, :], b, b[:, 1, :], False, True)
                            st[i]["psX"] = psX
                        for i in pr:
                            psX = st[i]["psX"]
                            y = ys.next()
                            cmul(P, F, psX, psX[:, 0:256].rearrange("p (c k) -> p c k", k=128), fs, fs[:, i, 0, :],
                                 fs[:, i, 1, :], y, y[:])
                            st[i]["y"] = y
                        for i in pr:
                            y = st[i]["y"]
                            pc = psC.next()
                            mm(P, pc, pc[:, 0:256], y, y[:, 0, :], F.bdc, F.bdc[:, 0, :], True, False)
                            mm(P, pc, pc[:, 0:256], y, y[:, 1, :], F.bdc, F.bdc[:, 1, :], False, True)
                            st[i]["pc"] = pc
                        for i in pr:
                            pc = st[i]["pc"]
                            dd = ds.next()
                            cmul(P, F, pc, pc[:, 0:256].rearrange("p (c k) -> p c k", k=128), F.twi, F.twi[:, 0, :],
                                 F.twi[:, 1, :], dd, dd[:])
                            st[i]["dd"] = dd
                        for i in pr:
                            dd = st[i]["dd"]
                            py = psY.next()
                            mm(P, py, py[0:64, 0:128], F.vinv, F.vinv[:, 0, :], dd, dd[:, 0, :], True, False)
                            mm(P, py, py[0:64, 0:128], F.vinv, F.vinv[:, 1, :], dd, dd[:, 1, :], False, True)
                            st[i]["py"] = py
                        for i in pr:
                            py = st[i]["py"]
                            P.op("dve", lambda e, py=py, oi=oi, xi=xi, i=i: e.tensor_tensor(
                                out=oi[:, 2 * i:2 * i + 2, :].rearrange("p c n -> p (c n)"), in0=py[0:64, 0:128],
                                in1=xi[:, 2 * i:2 * i + 2, :].rearrange("p c n -> p (c n)"), op=ALU.mult),
                                reads=[py, xi], accs=[oi])
                    P.dma("pool", C.d_hyoT[s][c0:c0 + HY_GRP, j, :].rearrange("c (a b) -> a c b", b=64), oi[:],
                          reads=[oi], accs=[C.d_hyoT[s]])


def stage_l0_outproj(P, C):
    with P.scope():
        C.wstage_n = 2048
        C.wstage = Rot([P.sbuf([128, 2048], F32, "wst") for _ in range(2)])
        w_out = load_weight_bf16(P, C, C.d_w_out, 4, D, "w_out")
        identb = P.sbuf([128, 128], BF16, "identb")
        P.op("dve", lambda e: e.tensor_copy(out=identb[:], in_=C.ident32[:]), reads=[C.ident32], writes=[identb])
        alloc_norm_scratch(P, C)
        oas = Rot([P.sbuf([128, 4, 3, 260], F32, "oa") for _ in range(2)])
        o2s = Rot([P.sbuf([128, 4, 260], F32, "o2") for _ in range(2)])
        rds = Rot([P.sbuf([128, 4, 4], F32, "rden") for _ in range(2)])
        abs_ = Rot([P.sbuf([128, 4, 4, 64], BF16, "attnb") for _ in range(2)])
        hyl = Rot([P.sbuf([128, 2, BLK], F32, "hyl") for _ in range(2)])
        mixs = Rot([P.sbuf([128, 4, BLK], BF16, "mixT") for _ in range(2)])
        xrs = Rot([P.sbuf([128, NK, BLK], F32, "xr") for _ in range(2)])
        x1s = Rot([P.sbuf([128, NK, BLK], F32, "x1T") for _ in range(2)])
        h2s = Rot([P.sbuf([128, NK, BLK], BF16, "h2T") for _ in range(2)])
        ptb = Rot([P.psum([128, BLK], BF16, "ptb") for _ in range(2)])
        pps = Rot([P.psum([128, BLK], F32, "pp") for _ in range(2)])
        g1 = C.gatev[0][0]
        for s in range(NS):
            for blk in range(NBLK):
                t0 = blk * BLK
                oa = oas.next()
                for g in range(3):
                    P.dma("sp" if g != 1 else "act", oa[:, :, g, :],
                          C.d_oacc[s][g, t0:t0 + BLK].rearrange("(j p) h d -> p j (h d)", p=128),
                          reads=[C.d_oacc[s]], accs=[oa])
                hy = hyl.next()
                P.dma("act", hy[:], C.d_hyoT[s][:, :, t0:t0 + BLK], reads=[C.d_hyoT[s]], writes=[hy])
                xr = xrs.next()
                P.dma("sp", xr[:], C.d_xT[0][s][:, :, t0:t0 + BLK], reads=[C.d_xT[0][s]], writes=[xr])
                o2 = o2s.next()
                P.op("dve", lambda e, oa=oa, o2=o2: e.tensor_tensor(out=o2[:], in0=oa[:, :, 0, :], in1=oa[:, :, 1, :],
                                                                    op=ALU.add), reads=[oa], writes=[o2])
                P.op("dve", lambda e, oa=oa, o2=o2: e.tensor_tensor(out=o2[:], in0=o2[:], in1=oa[:, :, 2, :],
                                                                    op=ALU.add), reads=[oa, o2], writes=[o2])
                o2v = o2[:].rearrange("p j (h d) -> p j h d", d=65)
                rd = rds.next()
                P.op("dve", lambda e, o2v=o2v, rd=rd: e.reciprocal(out=rd[:], in_=o2v[:, :, :, 64]),
                     reads=[o2], writes=[rd])
                ab = abs_.next()
                P.op("dve", lambda e, o2v=o2v, rd=rd, ab=ab: e.tensor_tensor(
                    out=ab[:], in0=o2v[:, :, :, 0:64], in1=rd[:].unsqueeze(3).to_broadcast([128, 4, 4, 64]),
                    op=ALU.mult), reads=[o2, rd], writes=[ab])
                mix = mixs.next()
                for c in range(2):
                    pt = ptb.next()
                    for j in range(4):
                        P.op("pe", lambda e, pt=pt, j=j, c=c, ab=ab: e.transpose(
                            pt[:, j * 128:(j + 1) * 128],
                            ab[:, j, 2 * c:2 * c + 2, :].rearrange("p h d -> p (h d)"), identb[:]),
                            reads=[ab, identb], writes=[pt])
                    P.op("act", lambda e, pt=pt, c=c, mix=mix: e.copy(out=mix[:, c, :], in_=pt[:]),
                         reads=[pt], accs=[mix])
                P.op("act", lambda e, hy=hy, mix=mix: e.copy(out=mix[:, 2:4, :], in_=hy[:]),
                     reads=[hy], accs=[mix])
                x1 = x1s.next()
                for c in range(NK):
                    pp = pps.next()
                    for k in range(4):
                        mm(P, pp, pp[:], w_out, w_out[:, k, c * 128:(c + 1) * 128], mix, mix[:, k, :], k == 0, k == 3)
                    P.op("dve", lambda e, pp=pp, c=c, x1=x1, xr=xr, s=s: e.scalar_tensor_tensor(
                        out=x1[:, c, :], in0=pp[:], scalar=g1[:, c, s:s + 1], in1=xr[:, c, :],
                        op0=ALU.mult, op1=ALU.add), reads=[pp, g1, xr], accs=[x1])
                P.dma("pool", C.d_xT[1][s][:, :, t0:t0 + BLK], x1[:], reads=[x1], accs=[C.d_xT[1][s]])
                h2 = h2s.next()
                norm_mod(P, C, x1, BLK, C.gain[0][1], C.shiftv[0][1], s, h2)
                P.dma("pool", C.d_h2T[0][s][:, :, t0:t0 + BLK], h2[:], reads=[h2], accs=[C.d_h2T[0][s]])


GELU_C = 0.044715
GELU_S = 2.0 * math.sqrt(2.0 / math.pi)


def stage_ffn(P, C, l, xin_idx, xout_idx):
    with P.scope():
        C.wstage_n = 1024
        C.wstage = Rot([P.sbuf([128, 1024], F32, "wst") for _ in range(2)])
        w_up = load_weight_bf16(P, C, C.d_ffn_up[l], NK, 2 * DFF, "w_up")
        w_dn = load_weight_bf16(P, C, C.d_ffn_dn[l], NFC, D, "w_dn")
        cw = P.sbuf([128, NFC, 4], F32, "convw")
        P.dma("sp", cw[:], C.d_ffn_cw[l][:], reads=[C.d_ffn_cw[l]], writes=[cw])
        hhs = Rot([P.sbuf([128, NK, BLK + 2], BF16, "hh") for _ in range(2)])
        actT = P.sbuf([128, NFC, BLK], BF16, "actT")
        xcs = Rot([P.sbuf([128, BLK], F32, "xc") for _ in range(2)])
        xos = Rot([P.sbuf([128, BLK], F32, "xo") for _ in range(2)])
        tmp = {n: Rot([P.sbuf([128, BLK], F32, n) for _ in range(2)]) for n in ("cv", "sq")}
        pas = Rot([P.psum([128, BLK], F32, "pa") for _ in range(2)])
        phs = Rot([P.psum([128, BLK], F32, "ph") for _ in range(1)])
        pgs = Rot([P.psum([128, BLK], F32, "pg") for _ in range(2)])
        pos = Rot([P.psum([128, BLK], F32, "po") for _ in range(2)])
        g2 = C.gatev[l][1]
        for s in range(NS):
            for blk in range(NBLK):
                t0 = blk * BLK
                hh = hhs.next()
                lo = max(t0 - 1, 0)
                hi = min(t0 + BLK + 1, L)
                c_lo = lo - (t0 - 1)
                P.dma("sp", hh[:, :, c_lo:c_lo + (hi - lo)], C.d_h2T[l][s][:, :, lo:hi], reads=[C.d_h2T[l][s]],
                      writes=[hh])
                if blk == 0:
                    P.op("pool", lambda e, hh=hh: e.memset(hh[:, :, 0:1], 0.0), accs=[hh])
                if blk == NBLK - 1:
                    P.op("pool", lambda e, hh=hh: e.memset(hh[:, :, BLK + 1:BLK + 2], 0.0), accs=[hh])
                for c in range(NFC):
                    pa = pas.next()
                    ph = phs.next()
                    pg = pgs.next()
                    for k in range(NK):
                        mm(P, pa, pa[:], w_up, w_up[:, k, c * 128:(c + 1) * 128], hh, hh[:, k, 1:BLK + 1],
                           k == 0, k == NK - 1)
                    for k in range(NK):
                        mm(P, ph, ph[:, 0:2], w_up, w_up[:, k, c * 128:(c + 1) * 128], hh, hh[:, k, 0:BLK + 2:BLK + 1],
                           k == 0, k == NK - 1)
                    for k in range(NK):
                        mm(P, pg, pg[:], w_up, w_up[:, k, DFF + c * 128:DFF + (c + 1) * 128], hh, hh[:, k, 1:BLK + 1],
                           k == 0, k == NK - 1)
                    cv = tmp["cv"].next()
                    P.op("act", lambda e, pa=pa, cv=cv, c=c: e.activation(
                        out=cv[:], in_=pa[:], func=AF.Identity, scale=cw[:, c, 1:2], bias=cw[:, c, 3:4]),
                        reads=[pa, cw], writes=[cv])
                    P.op("dve", lambda e, pa=pa, cv=cv, c=c: e.scalar_tensor_tensor(
                        out=cv[:, 1:BLK], in0=pa[:, 0:BLK - 1], scalar=cw[:, c, 0:1], in1=cv[:, 1:BLK],
                        op0=ALU.mult, op1=ALU.add), reads=[pa, cw, cv], writes=[cv])
                    P.op("dve", lambda e, pa=pa, cv=cv, c=c: e.scalar_tensor_tensor(
                        out=cv[:, 0:BLK - 1], in0=pa[:, 1:BLK], scalar=cw[:, c, 2:3], in1=cv[:, 0:BLK - 1],
                        op0=ALU.mult, op1=ALU.add), reads=[pa, cw, cv], writes=[cv])
                    P.op("dve", lambda e, ph=ph, cv=cv, c=c: e.scalar_tensor_tensor(
                        out=cv[:, 0:1], in0=ph[:, 0:1], scalar=cw[:, c, 0:1], in1=cv[:, 0:1],
                        op0=ALU.mult, op1=ALU.add), reads=[ph, cw, cv], writes=[cv])
                    P.op("dve", lambda e, ph=ph, cv=cv, c=c: e.scalar_tensor_tensor(
                        out=cv[:, BLK - 1:BLK], in0=ph[:, 1:2], scalar=cw[:, c, 2:3], in1=cv[:, BLK - 1:BLK],
                        op0=ALU.mult, op1=ALU.add), reads=[ph, cw, cv], writes=[cv])
                    tt = tmp["sq"].next()
                    P.op("act", lambda e, cv=cv, tt=tt: e.activation(out=tt[:], in_=cv[:], func=AF.Gelu_apprx_tanh),
                         reads=[cv], writes=[tt])
                    P.op("dve", lambda e, tt=tt, pg=pg, c=c: e.tensor_tensor(out=actT[:, c, :], in0=pg[:], in1=tt[:],
                                                                             op=ALU.mult),
                         reads=[tt, pg], accs=[actT])
                for c in range(NK):
                    xc = xcs.next()
                    P.dma("act", xc[:], C.d_xT[xin_idx][s][:, c, t0:t0 + BLK], reads=[C.d_xT[xin_idx][s]], writes=[xc])
                    po = pos.next()
                    for k in range(NFC):
                        mm(P, po, po[:], w_dn, w_dn[:, k, c * 128:(c + 1) * 128], actT, actT[:, k, :],
                           k == 0, k == NFC - 1)
                    xo = xos.next()
                    P.op("dve", lambda e, po=po, c=c, xo=xo, xc=xc, s=s: e.scalar_tensor_tensor(
                        out=xo[:], in0=po[:], scalar=g2[:, c, s:s + 1], in1=xc[:], op0=ALU.mult, op1=ALU.add),
                        reads=[po, g2, xc], writes=[xo])
                    P.dma("pool", C.d_xT[xout_idx][s][:, c, t0:t0 + BLK], xo[:], reads=[xo],
                          accs=[C.d_xT[xout_idx][s]])


def stage_final(P, C, xin_idx):
    with P.scope():
        alloc_norm_scratch(P, C)
        xrs = Rot([P.sbuf([128, NK, BLK], F32, "xr") for _ in range(2)])
        yTs = Rot([P.sbuf([128, NK, BLK], F32, "yT") for _ in range(2)])
        yts = Rot([P.sbuf([128, 4, D], F32, "ytok") for _ in range(2)])
        pts = Rot([P.psum([128, 1024], F32, "pt2") for _ in range(2)])
        cnt = 0
        for s in range(NS):
            for blk in range(NBLK):
                t0 = blk * BLK
                xr = xrs.next()
                P.dma("sp", xr[:], C.d_xT[xin_idx][s][:, :, t0:t0 + BLK], reads=[C.d_xT[xin_idx][s]], writes=[xr])
                yT = yTs.next()
                norm_mod(P, C, xr, BLK, C.gain_fin, None, s, yT)
                yt = yts.next()
                for j in range(4):
                    pt = pts.next()
                    for c in range(NK):
                        P.op("pe", lambda e, pt=pt, j=j, c=c, yT=yT: e.transpose(
                            pt[:, c * 128:(c + 1) * 128], yT[:, c, j * 128:(j + 1) * 128], C.ident32[:]),
                            reads=[yT, C.ident32], writes=[pt])
                    if cnt % 2 == 0:
                        P.op("act", lambda e, pt=pt, yt=yt, j=j: e.copy(out=yt[:, j, :], in_=pt[:]),
                             reads=[pt], accs=[yt])
                    else:
                        P.op("dve", lambda e, pt=pt, yt=yt, j=j: e.tensor_copy(out=yt[:, j, :], in_=pt[:]),
                             reads=[pt], accs=[yt])
                    cnt += 1
                P.dma("pool", C.d_y[s, t0:t0 + BLK, :].rearrange("(j p) f -> p j f", p=128), yt[:],
                      reads=[yt], accs=[C.d_y])


DECAY_C = -math.exp(-0.5)
R1_STOP = int(os.environ.get("R1_STOP", "9"))
R1_SUB = int(os.environ.get("R1_SUB", "99"))


def stage_rwkv_norm(P, C):
    with P.scope():
        alloc_norm_scratch(P, C)
        xrs = Rot([P.sbuf([128, NK, BLK], F32, "xr") for _ in range(2)])
        hs = Rot([P.sbuf([128, NK, BLK], F32, "h1") for _ in range(2)])
        for s in range(NS):
            for blk in range(NBLK):
                t0 = blk * BLK
                xr = xrs.next()
                P.dma("sp", xr[:], C.d_xT[2][s][:, :, t0:t0 + BLK], reads=[C.d_xT[2][s]], writes=[xr])
                h = hs.next()
                norm_mod(P, C, xr, BLK, C.gain[1][0], C.shiftv[1][0], s, h)
                P.dma("pool", C.d_h1T[s][:, :, t0:t0 + BLK], h[:], reads=[h], accs=[C.d_h1T[s]])


def load_weight_pair(P, C, dram_buf, nk, c_lo, ncols, name, cols, mu_i):
    wb = P.sbuf([128, nk, ncols], BF16, name)
    ws = P.sbuf([128, nk, ncols], BF16, name + "s")
    grp = max(32, (C.wstage_n // nk) // 32 * 32)
    for c0 in range(0, ncols, grp):
        cw = min(grp, ncols - c0)
        st = C.wstage.next()
        stv = st[:, 0:nk * cw].rearrange("p (k c) -> p k c", c=cw)
        P.dma("sp", stv, dram_buf[:, :, c_lo + c0:c_lo + c0 + cw], reads=[dram_buf], writes=[st])
        P.op("act", lambda e, stv=stv, c0=c0, cw=cw: e.copy(out=wb[:, :, c0:c0 + cw], in_=stv), reads=[st], accs=[wb])
        P.op("dve", lambda e, stv=stv, c0=c0, cw=cw: e.tensor_tensor(
            out=ws[:, :, c0:c0 + cw], in0=stv, in1=cols[:, mu_i, :].unsqueeze(2).to_broadcast([128, nk, cw]),
            op=ALU.mult), reads=[st, cols], accs=[ws])
    return wb, ws


def stage_rwkv_proj(P, C):
    with P.scope():
        C.wstage_n = 512
        C.wstage = Rot([P.sbuf([128, 512], F32, "wst") for _ in range(2)])
        cols = P.sbuf([128, 14, NK], F32, "rwcols")
        P.dma("sp", cols[:], C.d_rwcols[:], reads=[C.d_rwcols], writes=[cols])
        w_r = load_weight_pair(P, C, C.d_w_r, NK, 0, D, "w_r", cols, 0)
        w_k = load_weight_pair(P, C, C.d_w_k, NK, 0, D, "w_k", cols, 2)
        w_v = load_weight_pair(P, C, C.d_w_v, NK, 0, D, "w_v", cols, 3)
        w_g1 = load_weight_pair(P, C, C.d_g1, NK, 0, 256, "w_g1", cols, 5)
        w_l1w = load_weight_pair(P, C, C.d_lora1, NK, 0, 128, "w_l1w", cols, 1)
        w_l1a = load_weight_pair(P, C, C.d_lora1, NK, 128, 128, "w_l1a", cols, 4)
        w_g2 = load_weight_bf16(P, C, C.d_g2, 2, D, "w_g2")
        w_l2 = load_weight_bf16(P, C, C.d_lora2, 4, D, "w_l2")
        bones = P.sbuf([128, 128], F32, "bones")
        P.dma("sp", bones[:], C.d_bones[:], reads=[C.d_bones], writes=[bones])
        hls = Rot([P.sbuf([128, NK, BLK + 2], F32, "hl") for _ in range(1)])
        xxh = P.sbuf([128, 4, BLK], F32, "xxh")
        hb = P.sbuf([128, NK, BLK], BF16, "hb")
        xb = P.sbuf([128, NK, BLK], BF16, "xb")
        pps = Rot([P.psum([128, BLK], F32, "pp") for _ in range(3)])
        pvs = Rot([P.psum([128, 1024], F32, "pv") for _ in range(1)])
        pls = Rot([P.psum([128, BLK], F32, "pl") for _ in range(3)])
        ob16 = Rot([P.sbuf([128, BLK], BF16, "ob16") for _ in range(6)])
        of32 = Rot([P.sbuf([128, BLK], F32, "of32") for _ in range(4)])
        kf = Rot([P.sbuf([128, BLK], F32, "kf") for _ in range(3)])
        kkf = Rot([P.sbuf([128, BLK], F32, "kkf") for _ in range(2)])
        vts = Rot([P.sbuf([128, D], BF16, "vtok") for _ in range(2)])
        sgs = Rot([P.sbuf([128, 2, BLK], BF16, "sg") for _ in range(1)])
        lts = Rot([P.sbuf([64, 4, BLK], BF16, "lt") for _ in range(1)])

        def evac(ps_ap, ps_buf, dst_ap, dst_buf):
            P.op("act", lambda e: e.copy(out=dst_ap, in_=ps_ap), reads=[ps_buf], writes=[dst_buf])

        def proj(ps_buf, ps_ap, wpair, csl):
            wb, ws = wpair
            for k in range(NK):
                mm(P, ps_buf, ps_ap, wb, wb[:, k, csl], hb, hb[:, k, :], k == 0, False)
            for k in range(NK):
                mm(P, ps_buf, ps_ap, ws, ws[:, k, csl], xb, xb[:, k, :], False, k == NK - 1)

        for s in range(NS):
            for blk in range(NBLK):
                t0 = blk * BLK
                hl = hls.next()
                lo = max(t0 - 1, 0)
                hi = min(t0 + BLK + 1, L)
                c_lo = lo - (t0 - 1)
                P.dma("sp", hl[:, :, c_lo:c_lo + (hi - lo)], C.d_h1T[s][:, :, lo:hi], reads=[C.d_h1T[s]], writes=[hl])
                if blk == 0:
                    P.op("dve", lambda e, hl=hl: e.memset(hl[:, :, 0:1], 0.0), accs=[hl])
                if blk == NBLK - 1:
                    P.op("dve", lambda e, hl=hl: e.memset(hl[:, :, BLK + 1:BLK + 2], 0.0), accs=[hl])
                P.op("act", lambda e, hl=hl: e.copy(out=hb[:], in_=hl[:, :, 1:BLK + 1]), reads=[hl], writes=[hb])
                for half in range(2):
                    ks = slice(4 * half, 4 * half + 4)
                    P.op("dve", lambda e, hl=hl, ks=ks: e.tensor_tensor(
                        out=xxh[:], in0=hl[:, ks, 0:BLK], in1=hl[:, ks, 2:BLK + 2], op=ALU.add),
                        reads=[hl], writes=[xxh])
                    P.op("dve", lambda e, hl=hl, ks=ks: e.scalar_tensor_tensor(
                        out=xb[:, ks, :], in0=xxh[:], scalar=0.5, in1=hl[:, ks, 1:BLK + 1], op0=ALU.mult,
                        op1=ALU.subtract), reads=[hl, xxh], accs=[xb])
                if R1_STOP <= 1:
                    continue
                for j in range(4):
                    pv = pvs.next()
                    jsl = slice(j * 128, (j + 1) * 128)
                    for half in range(2):
                        hsl = slice(half * 512, (half + 1) * 512)
                        for k in range(NK):
                            mm(P, pv, pv[:, hsl], hb, hb[:, k, jsl], w_v[0], w_v[0][:, k, hsl], k == 0, False)
                        for k in range(NK):
                            mm(P, pv, pv[:, hsl], xb, xb[:, k, jsl], w_v[1], w_v[1][:, k, hsl], False, k == NK - 1)
                    vt = vts.next()
                    evac(pv[:], pv, vt[:], vt)
                    P.dma("pool", C.d_vtok[s][t0 + j * 128:t0 + (j + 1) * 128, :], vt[:], reads=[vt],
                          accs=[C.d_vtok[s]])
                if R1_STOP <= 2:
                    continue
                sg = sgs.next()
                for cc in range(2):
                    pl = pls.next()
                    proj(pl, pl[:], w_g1, slice(cc * 128, (cc + 1) * 128))
                    P.op("act", lambda e, pl=pl, cc=cc, sg=sg: e.activation(
                        out=sg[:, cc, :], in_=pl[:], func=AF.Sigmoid), reads=[pl], accs=[sg])
                lt = lts.next()
                for q in range(4):
                    pl = pls.next()
                    proj(pl, pl[0:64, :], w_l1w if q < 2 else w_l1a, slice((q % 2) * 64, (q % 2) * 64 + 64))
                    if q < 2:
                        P.op("act", lambda e, pl=pl, q=q, lt=lt: e.activation(out=lt[:, q, :], in_=pl[0:64, :],
                                                                              func=AF.Tanh), reads=[pl], accs=[lt])
                    else:
                        P.op("act", lambda e, pl=pl, q=q, lt=lt: e.copy(out=lt[:, q, :], in_=pl[0:64, :]),
                             reads=[pl], accs=[lt])
                if R1_STOP <= 3:
                    continue
                def phase_a(c):
                    csl = slice(c * 128, (c + 1) * 128)
                    pp = pps.next()
                    proj(pp, pp[:], w_r, csl)
                    o = ob16.next()
                    evac(pp[:], pp, o[:], o)
                    P.dma("pool", C.d_rT[s][:, c, t0:t0 + BLK], o[:], reads=[o], accs=[C.d_rT[s]])
                    pp = pps.next()
                    proj(pp, pp[:], w_v, csl)
                    o = ob16.next()
                    evac(pp[:], pp, o[:], o)
                    P.dma("pool", C.d_vT[s][:, c, t0:t0 + BLK], o[:], reads=[o], accs=[C.d_vT[s]])
                    pp = pps.next()
                    mm(P, pp, pp[:], w_g2, w_g2[:, 0, csl], sg, sg[:, 0, :], True, False)
                    mm(P, pp, pp[:], w_g2, w_g2[:, 1, csl], sg, sg[:, 1, :], False, True)
                    o = ob16.next()
                    evac(pp[:], pp, o[:], o)
                    P.dma("pool", C.d_gT[s][:, c, t0:t0 + BLK], o[:], reads=[o], accs=[C.d_gT[s]])
                    pp = pps.next()
                    proj(pp, pp[:], w_k, csl)
                    kk_ = kf.next()
                    P.op("act", lambda e, pp=pp, kk_=kk_: e.copy(out=kk_[:], in_=pp[:]), reads=[pp], writes=[kk_])
                    return kk_

                def phase_b(c, kk_):
                    csl = slice(c * 128, (c + 1) * 128)
                    kq = kkf.next()
                    P.op("dve", lambda e, kk_=kk_, kq=kq, c=c: e.tensor_scalar(
                        out=kq[:], in0=kk_[:], scalar1=cols[:, 10, c:c + 1], scalar2=None, op0=ALU.mult),
                        reads=[kk_, cols], writes=[kq])
                    sq = of32.next()
                    P.op("act", lambda e, kq=kq, sq=sq: e.activation(out=sq[:], in_=kq[:], func=AF.Square),
                         reads=[kq], writes=[sq])
                    pl = pls.next()
                    mm(P, pl, pl[:], bones, bones[:], sq, sq[:], True, True)
                    rn = of32.next()
                    P.op("act", lambda e, pl=pl, rn=rn: e.activation(out=rn[:], in_=pl[:], func=AF.Sqrt),
                         reads=[pl], writes=[rn])
                    P.op("dve", lambda e, rn=rn: e.tensor_scalar_max(out=rn[:], in0=rn[:], scalar1=1e-12),
                         reads=[rn], writes=[rn])
                    P.op("dve", lambda e, rn=rn: e.reciprocal(out=rn[:], in_=rn[:]), reads=[rn], writes=[rn])
                    P.op("dve", lambda e, kq=kq, rn=rn: e.tensor_tensor(out=kq[:], in0=kq[:], in1=rn[:], op=ALU.mult),
                         reads=[kq, rn], writes=[kq])
                    o = ob16.next()
                    P.op("dve", lambda e, kq=kq, o=o: e.tensor_copy(out=o[:], in_=kq[:]), reads=[kq], writes=[o])
                    P.dma("pool", C.d_kkT[s][:, c, t0:t0 + BLK], o[:], reads=[o], accs=[C.d_kkT[s]])
                    for dd in range(2):
                        pl = pls.next()
                        mm(P, pl, pl[:], w_l2, w_l2[0:64, dd, csl], lt, lt[:, dd, :], True, True)
                        lw = of32.next()
                        P.op("act", lambda e, pl=pl, lw=lw, dd=dd, c=c: e.activation(
                            out=lw[:], in_=pl[:], func=AF.Sigmoid, bias=cols[:, 6 + dd, c:c + 1], scale=1.0),
                            reads=[pl, cols], writes=[lw])
                        P.dma("pool", C.d_lwT[dd][s][:, c, t0:t0 + BLK], lw[:], reads=[lw], accs=[C.d_lwT[dd][s]])
                        pl = pls.next()
                        mm(P, pl, pl[:], w_l2, w_l2[0:64, 2 + dd, csl], lt, lt[:, 2 + dd, :], True, True)
                        aa = of32.next()
                        P.op("act", lambda e, pl=pl, aa=aa, dd=dd, c=c: e.activation(
                            out=aa[:], in_=pl[:], func=AF.Sigmoid, bias=cols[:, 8 + dd, c:c + 1], scale=1.0),
                            reads=[pl, cols], writes=[aa])
                        o = ob16.next()
                        P.op("dve", lambda e, kq=kq, aa=aa, o=o: e.tensor_tensor(out=o[:], in0=kq[:], in1=aa[:],
                                                                                 op=ALU.mult),
                             reads=[kq, aa], writes=[o])
                        P.dma("pool", C.d_bT[dd][s][:, c, t0:t0 + BLK], o[:], reads=[o], accs=[C.d_bT[dd][s]])
                        P.op("dve", lambda e, aa=aa, c=c: e.tensor_scalar(
                            out=aa[:], in0=aa[:], scalar1=-1.0, scalar2=cols[:, 11, c:c + 1], op0=ALU.add,
                            op1=ALU.mult), reads=[aa, cols], writes=[aa])
                        o = ob16.next()
                        P.op("dve", lambda e, aa=aa, kk_=kk_, o=o: e.scalar_tensor_tensor(
                            out=o[:], in0=aa[:], scalar=1.0, in1=kk_[:], op0=ALU.add, op1=ALU.mult),
                            reads=[aa, kk_], writes=[o])
                        P.dma("pool", C.d_kdT[dd][s][:, c, t0:t0 + BLK], o[:], reads=[o], accs=[C.d_kdT[dd][s]])

                prev = None
                for c in range(NK):
                    kk_c = phase_a(c)
                    if prev is not None:
                        phase_b(*prev)
                    prev = (c, kk_c)
                phase_b(*prev)


SBLK = 128
NSB = L // SBLK
CPB = SBLK // 64
SC_LIMIT = int(os.environ.get("SC_LIMIT", "999"))
CAST_MOD = int(os.environ.get("CAST_MOD", "4"))


def stage_rwkv_scan(P, C, dd):
    with P.scope():
        NF = 16 * SBLK
        msk = P.sbuf([64, 192], F32, "scmask")
        P.dma("sp", msk[:], C.d_scanmask[:, dd, :], reads=[C.d_scanmask], writes=[msk])
        rmask = P.sbuf([64, NF], F32, "rmask")
        P.dma("act", rmask[:], C.d_rmask[:, 0:NF], reads=[C.d_rmask], writes=[rmask])
        idb = P.sbuf([64, 64], BF16, "idb")
        P.op("dve", lambda e: e.tensor_copy(out=idb[:], in_=C.ident32[0:64, 0:64]), reads=[C.ident32], writes=[idb])
        Rr = P.sbuf([64, 16, SBLK], BF16, "scR")
        KD = P.sbuf([64, 16, SBLK], BF16, "scKD")
        Bb = P.sbuf([64, 16, SBLK], BF16, "scB")
        KK = P.sbuf([64, 16, SBLK], BF16, "scKK")
        fA = P.sbuf([64, 16, SBLK], F32, "scA")
        fB = P.sbuf([64, 16, SBLK], F32, "scBf")
        fC = P.sbuf([64, 16, SBLK], F32, "scC")
        BS = []
        for _ in range(2):
            b_ = Ctx()
            b_.AR = P.sbuf([64, 16, CPB, 128], BF16, "scAR")
            b_.KT = P.sbuf([64, 16, SBLK], BF16, "scKT")
            b_.BT = P.sbuf([64, 16, SBLK], BF16, "scBT")
            b_.Vt = P.sbuf([64, CPB, D], BF16, "scV")
            b_.Yo = P.sbuf([64, CPB, D], F32, "scY")
            b_.PC = P.sbuf([64, 16, CPB], F32, "scPC")
            BS.append(b_)
        Sf = P.sbuf([64, 16, 64], F32, "scSf")
        Sb = P.sbuf([64, 16, 64], BF16, "scSb")
        MNk = [[P.sbuf([64, 4, 128], BF16, "MNk") for _ in range(4)] for _ in range(2)]
        MNb = [[P.sbuf([64, 4, 128], BF16, "MNb") for _ in range(4)] for _ in range(2)]
        NT0 = [[P.sbuf([64, 4, 64], BF16, "NT0") for _ in range(4)] for _ in range(2)]
        KTt = [[P.sbuf([64, 4, 2, 64], BF16, "KTt") for _ in range(4)] for _ in range(2)]
        Nl = [[[P.sbuf([64, 4, 2, 64], BF16, "Nl") for _ in range(5)] for _ in range(4)] for _ in range(2)]
        Xb = [P.sbuf([64, 4, 64], BF16, "Xb") for _ in range(4)]
        tmpS = [P.sbuf([64, 4, 64], F32, "tmpS") for _ in range(2)]
        psMNk = P.psum([64, 512], F32, "psMNk")
        psMNb = P.psum([64, 512], F32, "psMNb")
        psN = Rot([P.psum([64, 512], F32, "psN") for _ in range(2)])
        psXb = [P.psum([64, 512], F32, "psX") for _ in range(2)]
        psYS = P.psum([64, 512], F32, "psYS")
        psT = P.psum([64, 512], F32, "psT")
        v4 = lambda b, w: b[:, 0:4 * w].rearrange("p (h t) -> p h t", t=w)
        v42 = lambda b: b[:, 0:512].rearrange("p (h a t) -> p h a t", a=2, t=64)
        xview = lambda g: psXb[g // 2][:, (g % 2) * 256:(g % 2) * 256 + 256].rearrange("p (h t) -> p h t", t=64)
        ecnt = [0]

        def cast(src_buf, src_ap, dst_buf, dst_ap):
            ecnt[0] += 1
            if ecnt[0] % CAST_MOD != 0:
                P.op("act", lambda e: e.copy(out=dst_ap, in_=src_ap), reads=[src_buf], writes=[dst_buf])
            else:
                P.op("dve", lambda e: e.tensor_copy(out=dst_ap, in_=src_ap), reads=[src_buf], writes=[dst_buf])

        def prep(s, blk, bs):
            t0 = blk * SBLK
            tsl_all = slice(t0, t0 + SBLK)
            for h2 in range(2):
                prt = slice(h2 * 64, (h2 + 1) * 64)
                P.dma("sp", fA[:, h2::2, :], C.d_lwT[dd][s][prt, :, tsl_all], reads=[C.d_lwT[dd][s]], accs=[fA])
                P.dma("sp", KK[:, h2::2, :], C.d_kkT[s][prt, :, tsl_all], reads=[C.d_kkT[s]], accs=[KK])
                P.dma("sp", Rr[:, h2::2, :], C.d_rT[s][prt, :, tsl_all], reads=[C.d_rT[s]], accs=[Rr])
                P.dma("sp", KD[:, h2::2, :], C.d_kdT[dd][s][prt, :, tsl_all], reads=[C.d_kdT[dd][s]], accs=[KD])
                P.dma("sp", Bb[:, h2::2, :], C.d_bT[dd][s][prt, :, tsl_all], reads=[C.d_bT[dd][s]], accs=[Bb])
            P.dma("sp", bs.Vt[:], C.d_vtok[s][tsl_all, :].rearrange("(c i) f -> i c f", i=64),
                  reads=[C.d_vtok[s]], writes=[bs.Vt])
            yield
            fl = lambda b: b[:].rearrange("p h t -> p (h t)")
            c4 = lambda b: b[:].rearrange("p h (c t) -> p (h c) t", t=64)
            c5 = lambda b: b[:].rearrange("p h (c t) -> p h c t", t=64)
            P.op("dve", lambda e: e.tensor_tensor_scan(out=fl(fB), data0=rmask[:], data1=fl(fA), initial=0.0,
                                                       op0=ALU.mult, op1=ALU.add), reads=[rmask, fA], writes=[fB])
            yield
            if dd == 1:
                P.op("dve", lambda e: e.tensor_tensor(out=fl(fC), in0=fl(fA), in1=fl(fB), op=ALU.subtract),
                     reads=[fA, fB], writes=[fC])
                yield
                P.op("dve", lambda e: e.tensor_tensor(
                    out=c4(fB), in0=c4(fC), in1=c4(fB)[:, :, 63:64].to_broadcast([64, 16 * CPB, 64]), op=ALU.add),
                    reads=[fC, fB], writes=[fB])
                yield
            P.op("dve", lambda e: e.tensor_tensor(out=fl(fA), in0=fl(fB), in1=fl(fA), op=ALU.subtract),
                 reads=[fA, fB], writes=[fA])
            yield
            P.op("act", lambda e: e.activation(out=fl(fA), in_=fl(fA), func=AF.Exp, scale=DECAY_C),
                 reads=[fA], writes=[fA])
            P.op("act", lambda e: e.activation(out=fl(fC), in_=fl(fB), func=AF.Exp, scale=DECAY_C),
                 reads=[fB], writes=[fC])
            yield
            P.op("act", lambda e: e.activation(out=fl(fB), in_=fl(fB), func=AF.Exp, scale=-DECAY_C),
                 reads=[fB], writes=[fB])
            pcol = 63 if dd == 0 else 0
            P.op("act", lambda e: e.copy(out=bs.PC[:], in_=c5(fC)[:, :, :, pcol]), reads=[fC], writes=[bs.PC])
            yield
            P.op("dve", lambda e: e.scalar_tensor_tensor(out=bs.AR[:, :, :, 0:64], in0=c5(KK), scalar=-1.0,
                                                         in1=c5(fA), op0=ALU.mult, op1=ALU.mult),
                 reads=[KK, fA], accs=[bs.AR])
            yield
            P.op("dve", lambda e: e.tensor_tensor(out=bs.AR[:, :, :, 64:128], in0=c5(Rr), in1=c5(fC), op=ALU.mult),
                 reads=[Rr, fC], accs=[bs.AR])
            yield
            P.op("dve", lambda e: e.tensor_tensor(out=bs.KT[:], in0=KD[:], in1=fB[:], op=ALU.mult),
                 reads=[KD, fB], writes=[bs.KT])
            yield
            P.op("dve", lambda e: e.tensor_tensor(out=bs.BT[:], in0=Bb[:], in1=fB[:], op=ALU.mult),
                 reads=[Bb, fB], writes=[bs.BT])

        def s1(bs, c, par):
            tsl = slice(c * 64, (c + 1) * 64)
            AR, KT, BT = bs.AR, bs.KT, bs.BT
            for g in range(4):
                for hi in range(4):
                    h = 4 * g + hi
                    mm(P, psMNk, v4(psMNk, 128)[:, hi, :], KT, KT[:, h, tsl], AR, AR[:, h, c, :], True, True)
                P.op("dve", lambda e, g=g: e.tensor_tensor(
                    out=MNk[par][g][:], in0=v4(psMNk, 128), in1=msk[:, 0:128].unsqueeze(1).to_broadcast([64, 4, 128]),
                    op=ALU.mult), reads=[psMNk, msk], writes=[MNk[par][g]])
                for hi in range(4):
                    h = 4 * g + hi
                    mm(P, psMNb, v4(psMNb, 128)[:, hi, :], BT, BT[:, h, tsl], AR, AR[:, h, c, :], True, True)
                P.op("dve", lambda e, g=g: e.tensor_tensor(
                    out=MNb[par][g][:], in0=v4(psMNb, 128), in1=msk[:, 0:128].unsqueeze(1).to_broadcast([64, 4, 128]),
                    op=ALU.mult), reads=[psMNb, msk], writes=[MNb[par][g]])
                pn = psN.next()
                for hi in range(4):
                    h = 4 * g + hi
                    mm(P, pn, v4(pn, 64)[:, hi, :], AR, AR[:, h, c, 0:64], BT, BT[:, h, tsl], True, True)
                P.op("dve", lambda e, g=g, pn=pn: e.tensor_tensor(
                    out=NT0[par][g][:], in0=v4(pn, 64), in1=msk[:, 128:192].unsqueeze(1).to_broadcast([64, 4, 64]),
                    op=ALU.mult), reads=[pn, msk], writes=[NT0[par][g]])
                for hi in range(4):
                    h = 4 * g + hi
                    mm(P, psT, v42(psT)[:, hi, 0, :], KT, KT[:, h, tsl], idb, idb[:], True, True)
                    mm(P, psT, v42(psT)[:, hi, 1, :], BT, BT[:, h, tsl], idb, idb[:], True, True)
                P.op("act", lambda e, g=g: e.copy(out=KTt[par][g][:], in_=v42(psT)), reads=[psT],
                     writes=[KTt[par][g]])

        def level_ops(par, g, j):
            if j == 0:
                return (MNb[par][g], (lambda hi: MNb[par][g][:, hi, 0:64]), NT0[par][g], (lambda hi: NT0[par][g][:, hi, :]))
            t = Nl[par][g][j - 1]
            return (t, (lambda hi: t[:, hi, 0, :]), t, (lambda hi: t[:, hi, 1, :]))

        def square(par, j):
            for g in range(4):
                nbuf, nap, tbuf, tap = level_ops(par, g, j)
                pn = psN.next()
                for hi in range(4):
                    mm(P, pn, v42(pn)[:, hi, 0, :], tbuf, tap(hi), nbuf, nap(hi), True, True)
                    if j < 4:
                        mm(P, pn, v42(pn)[:, hi, 1, :], nbuf, nap(hi), tbuf, tap(hi), True, True)
                dst = Nl[par][g][j]
                if j < 4:
                    cast(pn, v42(pn), dst, dst[:])
                else:
                    cast(pn, v42(pn)[:, :, 0, :], dst, dst[:, :, 0, :])

        def g_step(bs, c, par):
            AR, Vt = bs.AR, bs.Vt
            for g in range(4):
                xv = xview(g)
                pb_ = psXb[g // 2]
                for hi in range(4):
                    h = 4 * g + hi
                    first = (g % 2 == 0 and hi == 0)
                    P.op("pe", lambda e, xv=xv, hi=hi, h=h, first=first: e.matmul(
                        xv[:, hi, :], lhsT=AR[:, h, c, 0:64], rhs=Sb[:, h, :], start=first, stop=False,
                        skip_group_check=True), reads=[AR, Sb], writes=[pb_])
                    P.op("pe", lambda e, xv=xv, hi=hi, h=h, g=g: e.matmul(
                        xv[:, hi, :], lhsT=MNk[par][g][:, hi, 0:64], rhs=Vt[:, c, h * 64:(h + 1) * 64], start=False,
                        stop=False, skip_group_check=True), reads=[MNk[par][g], Vt], writes=[pb_])
                cast(pb_, xv, Xb[g], Xb[g][:])

        def apply(par, j):
            for g in range(4):
                xv = xview(g)
                pb_ = psXb[g // 2]
                nbuf, nap, _, _ = level_ops(par, g, j)
                for hi in range(4):
                    P.op("pe", lambda e, xv=xv, hi=hi, g=g, nap=nap: e.matmul(
                        xv[:, hi, :], lhsT=nap(hi), rhs=Xb[g][:, hi, :], start=False, stop=(j == 5),
                        skip_group_check=True), reads=[nbuf, Xb[g]], writes=[pb_])
                cast(pb_, xv, Xb[g], Xb[g][:])

        def ys_step(bs, c, par):
            AR, Vt, Yo = bs.AR, bs.Vt, bs.Yo
            for g in range(4):
                pys = v42(psYS)
                for hi in range(4):
                    h = 4 * g + hi
                    vv = Vt[:, c, h * 64:(h + 1) * 64]
                    mm(P, psYS, pys[:, hi, 0, :], AR, AR[:, h, c, 64:128], Sb, Sb[:, h, :], True, False)
                    mm(P, psYS, pys[:, hi, 0, :], MNk[par][g], MNk[par][g][:, hi, 64:128], Vt, vv, False, False)
                    mm(P, psYS, pys[:, hi, 0, :], MNb[par][g], MNb[par][g][:, hi, 64:128], Xb[g], Xb[g][:, hi, :],
                       False, True)
                    mm(P, psYS, pys[:, hi, 1, :], KTt[par][g], KTt[par][g][:, hi, 0, :], Vt, vv, True, False)
                    mm(P, psYS, pys[:, hi, 1, :], KTt[par][g], KTt[par][g][:, hi, 1, :], Xb[g], Xb[g][:, hi, :],
                       False, True)
                ts_ = tmpS[g % 2]
                P.op("dve", lambda e, g=g, pys=pys, ts_=ts_: e.tensor_tensor(
                    out=ts_[:], in0=pys[:, :, 1, :], in1=Sf[:, 4 * g:4 * g + 4, :], op=ALU.add),
                    reads=[psYS, Sf], writes=[ts_])
                pcb = bs.PC[:, 4 * g:4 * g + 4, c:c + 1].to_broadcast([64, 4, 64])
                P.op("dve", lambda e, g=g, ts_=ts_, pcb=pcb: e.tensor_tensor(
                    out=Sb[:, 4 * g:4 * g + 4, :], in0=ts_[:], in1=pcb, op=ALU.mult),
                    reads=[ts_, bs.PC], accs=[Sb])
                P.op("dve", lambda e, g=g, pys=pys, c=c: e.tensor_copy(
                    out=Yo[:, c, g * 256:(g + 1) * 256].rearrange("p (h v) -> p h v", v=64), in_=pys[:, :, 0, :]),
                    reads=[psYS], accs=[Yo])
                P.op("dve", lambda e, g=g, ts_=ts_, pcb=pcb: e.tensor_tensor(
                    out=Sf[:, 4 * g:4 * g + 4, :], in0=ts_[:], in1=pcb, op=ALU.mult),
                    reads=[ts_, bs.PC], accs=[Sf])

        for s in range(NS):
            P.op("dve", lambda e: e.memset(Sf[:], 0.0), writes=[Sf])
            P.op("dve", lambda e: e.memset(Sb[:], 0.0), writes=[Sb])
            blocks = list(range(NSB) if dd == 0 else range(NSB - 1, -1, -1))
            seq = []
            for bp, blk in enumerate(blocks):
                for c in (range(CPB) if dd == 0 else range(CPB - 1, -1, -1)):
                    seq.append((bp, blk, c))
            seq = seq[:SC_LIMIT]
            for _ in prep(s, seq[0][1], BS[0]):
                pass
            s1(BS[0], seq[0][2], 0)
            for j in range(5):
                square(0, j)
            pending = None
            for n, (bp, blk, c) in enumerate(seq):
                par = n % 2
                bs = BS[bp % 2]
                nxt = seq[n + 1] if n + 1 < len(seq) else None
                first_of_block = (n == 0) or (seq[n - 1][0] != bp)
                if first_of_block and CPB > 1:
                    later = [q for q in seq[n + 1:] if q[0] == bp + 1]
                    if later:
                        pending = prep(s, later[0][1], BS[(bp + 1) % 2])
                if nxt is not None:
                    nbs = BS[nxt[0] % 2]
                    if nxt[0] != bp:
                        if pending is not None:
                            for _ in pending:
                                pass
                            pending = None
                        elif CPB == 1:
                            for _ in prep(s, nxt[1], nbs):
                                pass
                    s1(nbs, nxt[2], 1 - par)
                g_step(bs, c, par)
                for j in range(6):
                    apply(par, j)
                    if nxt is not None and j < 5:
                        square(1 - par, j)
                    if pending is not None:
                        for _ in range(2):
                            try:
                                next(pending)
                            except StopIteration:
                                pending = None
                                break
                ys_step(bs, c, par)
                last_of_block = (nxt is None) or (nxt[0] != bp)
                if last_of_block:
                    t0 = blk * SBLK
                    P.dma("pool", C.d_ytok[dd][s][t0:t0 + SBLK, :].rearrange("(c i) f -> i c f", i=64), bs.Yo[:],
                          reads=[bs.Yo], accs=[C.d_ytok[dd][s]])


GN_EPS = 64e-5


def stage_rwkv_post(P, C):
    with P.scope():
        C.wstage_n = 1024
        C.wstage = Rot([P.sbuf([128, 1024], F32, "wst") for _ in range(2)])
        w_o = load_weight_bf16(P, C, C.d_w_o, NK, D, "w_o")
        cols = P.sbuf([128, 14, NK], F32, "rwcols")
        P.dma("sp", cols[:], C.d_rwcols[:], reads=[C.d_rwcols], writes=[cols])
        lnb = P.sbuf([128, NK], F32, "lnb")
        P.dma("sp", lnb[:], C.d_lnb[:], reads=[C.d_lnb], writes=[lnb])
        bones = P.sbuf([128, 128], F32, "bones")
        P.dma("sp", bones[:], C.d_bones[:], reads=[C.d_bones], writes=[bones])
        alloc_norm_scratch(P, C)
        yf = P.sbuf([128, 4, D], F32, "yf")
        yb = P.sbuf([128, 4, D], F32, "yb")
        st = P.sbuf([128, 6, 64], F32, "gnst")
        ynT = P.sbuf([128, NK, BLK], F32, "ynT")
        rB = P.sbuf([128, NK, BLK], BF16, "rB")
        k0B = P.sbuf([128, NK, BLK], BF16, "k0B")
        k1B = P.sbuf([128, NK, BLK], BF16, "k1B")
        vB = P.sbuf([128, NK, BLK], BF16, "vB")
        gB = P.sbuf([128, NK, BLK], BF16, "gB")
        xr = P.sbuf([128, NK, BLK], F32, "xr")
        outT = P.sbuf([128, NK, BLK], BF16, "outT")
        x3 = P.sbuf([128, NK, BLK], F32, "x3T")
        h2 = P.sbuf([128, NK, BLK], BF16, "h2T")
        kms = Rot([P.sbuf([128, BLK], F32, "km") for _ in range(2)])
        qs = Rot([P.sbuf([128, BLK], F32, "qq") for _ in range(2)])
        bns = Rot([P.sbuf([128, BLK], F32, "bn") for _ in range(2)])
        pts = Rot([P.psum([128, BLK], F32, "pt") for _ in range(2)])
        pbs = Rot([P.psum([128, BLK], F32, "pb") for _ in range(2)])
        pps = Rot([P.psum([128, BLK], F32, "pp") for _ in range(2)])
        g1 = C.gatev[1][0]
        for s in range(NS):
            for blk in range(NBLK):
                t0 = blk * BLK
                tsl = slice(t0, t0 + BLK)
                P.dma("sp", yf[:], C.d_ytok[0][s][tsl, :].rearrange("(j p) f -> p j f", p=128),
                      reads=[C.d_ytok[0][s]], writes=[yf])
                P.dma("act", yb[:], C.d_ytok[1][s][tsl, :].rearrange("(j p) f -> p j f", p=128),
                      reads=[C.d_ytok[1][s]], writes=[yb])
                P.dma("sp", rB[:], C.d_rT[s][:, :, tsl], reads=[C.d_rT[s]], writes=[rB])
                P.dma("act", k0B[:], C.d_kdT[0][s][:, :, tsl], reads=[C.d_kdT[0][s]], writes=[k0B])
                P.dma("sp", k1B[:], C.d_kdT[1][s][:, :, tsl], reads=[C.d_kdT[1][s]], writes=[k1B])
                P.dma("act", vB[:], C.d_vT[s][:, :, tsl], reads=[C.d_vT[s]], writes=[vB])
                P.dma("sp", gB[:], C.d_gT[s][:, :, tsl], reads=[C.d_gT[s]], writes=[gB])
                P.dma("act", xr[:], C.d_xT[2][s][:, :, tsl], reads=[C.d_xT[2][s]], writes=[xr])
                yv = yf[:].rearrange("p j (h v) -> p (j h) v", v=64)
                ybv = yb[:].rearrange("p j (h v) -> p (j h) v", v=64)
                P.op("dve", lambda e: e.tensor_tensor(out=yf[:], in0=yf[:], in1=yb[:], op=ALU.add),
                     reads=[yf, yb], writes=[yf])
                P.op("dve", lambda e: e.tensor_reduce(out=st[:, 0, :], in_=yv, axis=mybir.AxisListType.X, op=ALU.add),
                     reads=[yf], writes=[st])
                P.op("act", lambda e: e.activation(out=yb[:], in_=yf[:], func=AF.Square), reads=[yf], writes=[yb])
                P.op("dve", lambda e: e.tensor_reduce(out=st[:, 1, :], in_=ybv, axis=mybir.AxisListType.X, op=ALU.add),
                     reads=[yb], writes=[st])
                P.op("dve", lambda e: e.tensor_scalar(out=st[:, 2, :], in0=st[:, 0, :], scalar1=1.0 / 64, scalar2=None,
                                                      op0=ALU.mult), reads=[st], writes=[st])
                P.op("dve", lambda e: e.tensor_tensor(out=st[:, 3, :], in0=st[:, 2, :], in1=st[:, 2, :], op=ALU.mult),
                     reads=[st], writes=[st])
                P.op("dve", lambda e: e.scalar_tensor_tensor(out=st[:, 4, :], in0=st[:, 1, :], scalar=1.0 / 64,
                                                             in1=st[:, 3, :], op0=ALU.mult, op1=ALU.subtract),
                     reads=[st], writes=[st])
                P.op("dve", lambda e: e.tensor_scalar_add(out=st[:, 4, :], in0=st[:, 4, :], scalar1=GN_EPS),
                     reads=[st], writes=[st])
                P.op("act", lambda e: e.activation(out=st[:, 5, :], in_=st[:, 4, :], func=AF.Sqrt),
                     reads=[st], writes=[st])
                P.op("dve", lambda e: e.reciprocal(out=st[:, 5, :], in_=st[:, 5, :]), reads=[st], writes=[st])
                P.op("dve", lambda e: e.tensor_tensor(out=yv, in0=yv, in1=st[:, 2, :].unsqueeze(2).to_broadcast(
                    [128, 64, 64]), op=ALU.subtract), reads=[yf, st], writes=[yf])
                P.op("dve", lambda e: e.tensor_tensor(out=yv, in0=yv, in1=st[:, 5, :].unsqueeze(2).to_broadcast(
                    [128, 64, 64]), op=ALU.mult), reads=[yf, st], writes=[yf])
                for k in range(NK):
                    pt = pts.next()
                    for j in range(4):
                        P.op("pe", lambda e, pt=pt, j=j, k=k: e.transpose(
                            pt[:, j * 128:(j + 1) * 128], yf[:, j, k * 128:(k + 1) * 128], C.ident32[:]),
                            reads=[yf, C.ident32], writes=[pt])
                    P.op("act", lambda e, pt=pt, k=k: e.activation(
                        out=ynT[:, k, :], in_=pt[:], func=AF.Identity, scale=cols[:, 13, k:k + 1],
                        bias=lnb[:, k:k + 1]), reads=[pt, cols, lnb], accs=[ynT])
                    km = kms.next()
                    P.op("dve", lambda e, km=km, k=k: e.tensor_tensor(out=km[:], in0=k0B[:, k, :], in1=k1B[:, k, :],
                                                                       op=ALU.add), reads=[k0B, k1B], writes=[km])
                    q = qs.next()
                    P.op("dve", lambda e, km=km, q=q, k=k: e.scalar_tensor_tensor(
                        out=q[:], in0=rB[:, k, :], scalar=cols[:, 12, k:k + 1], in1=km[:], op0=ALU.mult, op1=ALU.mult),
                        reads=[rB, cols, km], writes=[q])
                    pb = pbs.next()
                    mm(P, pb, pb[:], bones, bones[:], q, q[:], True, True)
                    bn = bns.next()
                    P.op("dve", lambda e, pb=pb, bn=bn, k=k: e.scalar_tensor_tensor(
                        out=bn[:], in0=pb[:], scalar=0.5, in1=vB[:, k, :], op0=ALU.mult, op1=ALU.mult),
                        reads=[pb, vB], writes=[bn])
                    P.op("dve", lambda e, bn=bn, k=k: e.tensor_tensor(out=bn[:], in0=bn[:], in1=ynT[:, k, :],
                                                                       op=ALU.add), reads=[bn, ynT], writes=[bn])
                    P.op("dve", lambda e, bn=bn, k=k: e.tensor_tensor(out=outT[:, k, :], in0=bn[:], in1=gB[:, k, :],
                                                                       op=ALU.mult), reads=[bn, gB], accs=[outT])
                for c in range(NK):
                    pp = pps.next()
                    for k in range(NK):
                        mm(P, pp, pp[:], w_o, w_o[:, k, c * 128:(c + 1) * 128], outT, outT[:, k, :], k == 0, k == NK - 1)
                    P.op("dve", lambda e, pp=pp, c=c, s=s: e.scalar_tensor_tensor(
                        out=x3[:, c, :], in0=pp[:], scalar=g1[:, c, s:s + 1], in1=xr[:, c, :],
                        op0=ALU.mult, op1=ALU.add), reads=[pp, g1, xr], accs=[x3])
                P.dma("pool", C.d_xT[3][s][:, :, tsl], x3[:], reads=[x3], accs=[C.d_xT[3][s]])
                norm_mod(P, C, x3, BLK, C.gain[1][1], C.shiftv[1][1], s, h2)
                P.dma("pool", C.d_h2T[1][s][:, :, tsl], h2[:], reads=[h2], accs=[C.d_h2T[1][s]])

class _ModView:
    def __init__(self, buf, j):
        self.buf = buf
        self.j = j

    @property
    def w(self):
        return self.buf.w

    @property
    def r(self):
        return self.buf.r

    @property
    def a(self):
        return self.buf.a

    def __getitem__(self, idx):
        p, k, s = idx
        return self.buf[p, self.j * 8 + k, s]


def mod_shift(C, l, which):
    return C.shiftv[l][which]


def host_consts():
    c = {}
    c["ident32"] = np.eye(128, dtype=np.float32)
    c["ones32"] = np.ones((128, 128), dtype=np.float32)
    bo = np.zeros((128, 128), np.float32)
    bo[0:64, 0:64] = 1.0
    bo[64:128, 64:128] = 1.0
    c["c_bones"] = bo
    ii = np.arange(64)[:, None]
    tt = np.arange(64)[None, :]
    sm = np.zeros((64, 2, 192), np.float32)
    sm[:, 0, 0:64] = (ii < tt)
    sm[:, 0, 64:128] = (ii <= tt)
    sm[:, 0, 128:192] = (tt < ii)
    sm[:, 1, 0:64] = (ii > tt)
    sm[:, 1, 64:128] = (ii >= tt)
    sm[:, 1, 128:192] = (tt > ii)
    c["c_scanmask"] = sm
    rm = np.ones((64, 16 * 256), np.float32)
    rm[:, ::64] = 0.0
    c["c_rmask"] = rm
    slopes = np.exp2(-8.0 * (np.arange(12, dtype=np.float32) + 1.0) / 12).astype(np.float32).reshape(3, 4)
    tab = np.zeros((128, 9, 4, 128), np.float32)
    kk = np.arange(128)[:, None]
    qq = np.arange(128)[None, :]
    NEG = -30000.0
    for g, d in enumerate((1, 4, 16)):
        for h in range(4):
            sl = slopes[g, h] * d
            lo = np.where(kk >= qq, -sl * np.abs(kk - qq - 64), NEG)
            up = np.where(kk <= qq, -sl * np.abs(kk - qq + 64), NEG)
            tab[:, 3 * g + 0, h, :] = lo
            tab[:, 3 * g + 1, h, :] = up
            tab[0:64, 3 * g + 2, h, :] = lo[64:128]
    c["abias"] = tab.astype(np.float32)
    n1 = np.arange(64, dtype=np.float64)[:, None]
    k1 = np.arange(128, dtype=np.float64)[None, :]
    ang = 2 * np.pi * n1 * k1 / 128.0
    c["c_w128"] = np.concatenate([np.cos(ang), -np.sin(ang)], 1).astype(np.float32)
    n2 = np.tile(np.arange(64, dtype=np.float64), 2)[:, None]
    ang = 2 * np.pi * n2 * k1 / 8192.0
    c["c_tw"] = np.stack([np.cos(ang), -np.sin(ang)], 1).astype(np.float32)
    a64 = 2 * np.pi * np.outer(np.arange(64), np.arange(64)) / 64.0
    def bdiag(m):
        z = np.zeros((128, 128))
        z[0:64, 0:64] = m
        z[64:128, 64:128] = m
        return z
    bdr, bdi = bdiag(np.cos(a64)), bdiag(-np.sin(a64))
    c["c_bd"] = np.stack([bdr, bdi, -bdi], 1).astype(np.float32)
    cr, ci = bdiag(np.cos(a64)), bdiag(np.sin(a64))
    c["c_bdc"] = np.stack([np.concatenate([cr, ci], 1), np.concatenate([-ci, cr], 1)], 1).astype(np.float32)
    kk1 = np.arange(128, dtype=np.float64)[:, None]
    nn2 = np.tile(np.arange(64, dtype=np.float64), 2)[None, :]
    ang = 2 * np.pi * kk1 * nn2 / 8192.0
    c["c_twi"] = np.stack([np.cos(ang), np.sin(ang)], 1).astype(np.float32)
    ang = 2 * np.pi * np.outer(np.arange(128), np.arange(64)) / 128.0
    c["c_vinv"] = np.stack([np.cos(ang) / 8192.0, -np.sin(ang) / 8192.0], 1).astype(np.float32)
    t = np.linspace(0.0, 1.0, L, dtype=np.float32)[:, None]
    w = (2.0 * np.float32(math.pi) * np.arange(L, dtype=np.float32)[:, None] / np.float32(L)).astype(np.float32)
    f = np.linspace(1e-4, 15, 16, dtype=np.float32)[None, :]
    z = (f * w).astype(np.float32)
    pos = np.concatenate([t, np.cos(z), -np.sin(z)], -1).astype(np.float32)
    c["c_posT"] = np.ascontiguousarray(pos.T)
    min_decay = math.log(1e-2) / 1.5
    max_decay = math.log(1e-2) / 0.3
    deltas = np.linspace(min_decay, max_decay, 256, dtype=np.float32)[None, :]
    win = np.exp(-t * np.abs(deltas)).astype(np.float32)
    c["c_winT"] = np.ascontiguousarray(win.T.reshape(2, 128, L).transpose(1, 0, 2))
    return c


def build_program(stages, dbg=()):
    nc = bass.Bass("TRN2", target_bir_lowering=False)
    P = Prog(nc)
    C = Ctx()
    C.dbg = {}
    din = lambda name, shape, dt=F32: P.dram(name, shape, dt, kind="ExternalInput")
    C.d_x = din("x_in", [NS, L, D])
    C.d_cT = din("cT", [128, NK, NS])
    C.d_adaw = [din("ada_w%d" % l, [128, NK, 6 * D]) for l in range(2)]
    C.d_adab = din("ada_b", [128, 2, 48])
    C.d_normw = din("normw", [128, 5, NK])
    C.d_w_in = din("w_in", [128, NK, 3072])
    C.d_abias = din("abias", [128, 9, 4, 128])
    C.d_w128 = din("c_w128", [64, 256])
    C.d_tw = din("c_tw", [128, 2, 128])
    C.d_bd = din("c_bd", [128, 3, 128])
    C.d_bdc = din("c_bdc", [128, 2, 256])
    C.d_twi = din("c_twi", [128, 2, 128])
    C.d_vinv = din("c_vinv", [128, 2, 64])
    C.d_posT = din("c_posT", [33, L])
    C.d_winT = din("c_winT", [128, 2, L])
    C.d_hcol = din("hcol", [64, 4])
    C.d_fw1 = din("fw1", [33, 64])
    C.d_fw23 = din("fw23", [64, 2, 64])
    C.d_fw4 = din("fw4", [64, 512])
    C.d_fbias = din("fbias", [128, 2])
    C.d_shortw = din("shortw", [128, 6, 4])
    C.d_w_out = din("w_out", [128, 4, D])
    C.d_ffn_up = [din("ffn_up%d" % l, [128, NK, 2 * DFF]) for l in range(2)]
    C.d_ffn_dn = [din("ffn_dn%d" % l, [128, NFC, D]) for l in range(2)]
    C.d_ffn_cw = [din("ffn_cw%d" % l, [128, NFC, 4]) for l in range(2)]
    C.d_w_r = din("w_r", [128, NK, D])
    C.d_w_k = din("w_k", [128, NK, D])
    C.d_w_v = din("w_v", [128, NK, D])
    C.d_w_o = din("w_o", [128, NK, D])
    C.d_g1 = din("rw_g1", [128, NK, 256])
    C.d_g2 = din("rw_g2", [128, 2, D])
    C.d_lora1 = din("rw_lora1", [128, NK, 256])
    C.d_lora2 = din("rw_lora2", [128, 4, D])
    C.d_rwcols = din("rw_cols", [128, 14, NK])
    C.d_bones = din("c_bones", [128, 128])
    C.d_lnb = din("rw_lnb", [128, NK])
    C.d_scanmask = din("c_scanmask", [64, 2, 192])
    C.d_rmask = din("c_rmask", [64, 16 * 256])
    C.d_ident32 = din("ident32", [128, 128])
    C.d_ones32 = din("ones32", [128, 128])
    C.d_y = P.dram("y_out", [NS, L, D], F32, kind="ExternalOutput")
    outs = []

    def scratch(name, shape, dt):
        if name in dbg:
            b = P.dram(name, shape, dt, kind="ExternalOutput")
            outs.append(b)
            return b
        return P.dram(name, shape, dt)

    C.d_xT = [[scratch("xT%d_%d" % (i, s), [128, NK, L], F32) for s in range(NS)] for i in range(5)]
    C.d_qkT = [scratch("qkT_%d" % s, [128, 12, L], BF16) for s in range(NS)]
    C.d_vaug = [scratch("vaug_%d" % s, [L, 12, 65], BF16) for s in range(NS)]
    C.d_hyT = [scratch("hyT_%d" % s, [128, 6, L], F32) for s in range(NS)]
    C.d_oacc = [scratch("oacc_%d" % s, [3, L, 4, 65], F32) for s in range(NS)]
    C.d_h2T = [[scratch("h2T%d_%d" % (l, s), [128, NK, L], BF16) for s in range(NS)] for l in range(2)]
    C.d_h1T = [scratch("h1T_%d" % s, [128, NK, L], F32) for s in range(NS)]
    C.d_vtok = [scratch("vtok_%d" % s, [L, D], BF16) for s in range(NS)]
    C.d_rT = [scratch("rT_%d" % s, [128, NK, L], BF16) for s in range(NS)]
    C.d_vT = [scratch("vT_%d" % s, [128, NK, L], BF16) for s in range(NS)]
    C.d_gT = [scratch("gT_%d" % s, [128, NK, L], BF16) for s in range(NS)]
    C.d_kkT = [scratch("kkT_%d" % s, [128, NK, L], BF16) for s in range(NS)]
    C.d_lwT = [[scratch("lwT%d_%d" % (dd, s), [128, NK, L], F32) for s in range(NS)] for dd in range(2)]
    C.d_bT = [[scratch("bT%d_%d" % (dd, s), [128, NK, L], BF16) for s in range(NS)] for dd in range(2)]
    C.d_kdT = [[scratch("kdT%d_%d" % (dd, s), [128, NK, L], BF16) for s in range(NS)] for dd in range(2)]
    C.d_ytok = [[scratch("ytok%d_%d" % (dd, s), [L, D], F32) for s in range(NS)] for dd in range(2)]
    C.d_filtT = scratch("filtT", [128, 4, L], F32)
    C.d_F = scratch("Fspec", [128, 128, 2, 128], F32)
    C.d_zT = [scratch("zT_%d" % s, [128, 2, L], F32) for s in range(NS)]
    C.d_x0T = [scratch("x0T_%d" % s, [128, 2, L], F32) for s in range(NS)]
    C.d_hyoT = [scratch("hyoT_%d" % s, [128, 2, L], F32) for s in range(NS)]
    C.ident32 = P.sbuf([128, 128], F32, "ident32")
    C.ones32 = P.sbuf([128, 128], F32, "ones32")
    C.eps_col = P.sbuf([128, 1], F32, "eps")
    P.dma("sp", C.ident32[:], C.d_ident32[:], reads=[C.d_ident32], writes=[C.ident32])
    P.dma("sp", C.ones32[:], C.d_ones32[:], reads=[C.d_ones32], writes=[C.ones32])
    P.op("pool", lambda e: e.memset(C.eps_col[:], RMS_EPS), writes=[C.eps_col])
    C.modT = [P.sbuf([128, 48, NS], F32, "modT%d" % l) for l in range(2)]
    C.gain = [[P.sbuf([128, NK, NS], F32, "gain%d%d" % (l, w)) for w in range(2)] for l in range(2)]
    C.gain_fin = P.sbuf([128, NK, NS], F32, "gainf")
    C.shiftv = [[_ModView(C.modT[l], 0), _ModView(C.modT[l], 3)] for l in range(2)]
    C.gatev = [[_ModView(C.modT[l], 2), _ModView(C.modT[l], 5)] for l in range(2)]

    def dbg_out(name, src_buf, src_ap, shape, dt=F32):
        if name in dbg:
            o = P.dram("dbg_" + name, shape, dt, kind="ExternalOutput")
            P.dma("sp", o[:], src_ap, reads=[src_buf], writes=[o])
            outs.append(o)

    C.dbg_out = dbg_out
    if "adaln" in stages:
        stage_adaln(P, C)
        dbg_out("modT0", C.modT[0], C.modT[0][:], [128, 48, NS])
        dbg_out("gain00", C.gain[0][0], C.gain[0][0][:], [128, NK, NS])
    if "l0_inproj" in stages:
        stage_l0_inproj(P, C)
    if "attn" in stages:
        stage_attention(P, C)
    if "hyfilt" in stages:
        stage_hyena_filter(P, C)
    if "hyena" in stages:
        stage_hyena(P, C)
    if "l0_outproj" in stages:
        stage_l0_outproj(P, C)
    if "ffn0" in stages:
        stage_ffn(P, C, 0, 1, 2)
    if "rw_norm" in stages:
        stage_rwkv_norm(P, C)
    if "rw_proj" in stages:
        stage_rwkv_proj(P, C)
    if "rw_scan0" in stages:
        stage_rwkv_scan(P, C, 0)
    if "rw_scan1" in stages:
        stage_rwkv_scan(P, C, 1)
    if "rw_post" in stages:
        stage_rwkv_post(P, C)
    if "ffn1" in stages:
        stage_ffn(P, C, 1, 3, 4)
    if "final" in stages:
        stage_final(P, C, FINAL_SRC)
    outs.append(C.d_y)
    final = [o for o in outs if o is not None]
    C.final_bufs = final
    return nc, P, C


def finish_program(P, C, extra=()):
    bufs = list(C.final_bufs) + list(extra)
    P.wait_all("sp", bufs)
    P.close()


def arrange_w(w, nk=None):
    K, N = w.shape
    nk = K // 128
    return np.ascontiguousarray(w.reshape(nk, 128, N).transpose(1, 0, 2))


def col_layout(v):
    return np.ascontiguousarray(v.reshape(-1, 128).T)


def prep_shared(inp):
    m = {}
    m["ada_w0"] = arrange_w(inp["l0_ada_w"])
    m["ada_w1"] = arrange_w(inp["l1_ada_w"])
    m["ada_b"] = np.ascontiguousarray(np.stack([col_layout(inp["l0_ada_b"]), col_layout(inp["l1_ada_b"])], 1))
    m["normw"] = np.ascontiguousarray(np.stack([col_layout(inp[k]) for k in
                                               ("l0_norm1", "l0_norm2", "l1_norm1", "l1_norm2", "final_norm")], 1))
    m["w_in"] = arrange_w(inp["l0_w_in"])
    m["w_out"] = arrange_w(inp["l0_w_out"])
    ffn = [(inp["l0_ffn_up"], inp["l0_ffn_down"], inp["l0_ffn_conv_w"], inp["l0_ffn_conv_b"]),
           (inp["l1_ffn_up"], inp["l1_ffn_down"], inp["l1_ffn_conv_w"], inp["l1_ffn_conv_b"])]
    for l in range(2):
        up, dn, cw_, cb_ = ffn[l]
        m["ffn_up%d" % l] = arrange_w(up)
        m["ffn_dn%d" % l] = arrange_w(dn)
        cwb = np.concatenate([cw_, cb_[None, :]], 0)
        m["ffn_cw%d" % l] = np.ascontiguousarray(cwb.T.reshape(NFC, 128, 4).transpose(1, 0, 2))
    for nm in ("w_r", "w_k", "w_v", "w_o"):
        m[nm] = arrange_w(inp["l1_" + nm])
    g1p = np.zeros((D, 256), np.float32)
    g1p[:, :160] = inp["l1_g1"]
    m["rw_g1"] = arrange_w(g1p)
    g2 = np.zeros((256, D), np.float32)
    g2[:160] = inp["l1_g2"]
    m["rw_g2"] = arrange_w(g2)
    m["rw_lora1"] = arrange_w(np.concatenate([inp["l1_w1"][0], inp["l1_w1"][1], inp["l1_a1"][0], inp["l1_a1"][1]], 1))
    l2 = np.zeros((128, 4, D), np.float32)
    l2[:64, 0] = inp["l1_w2"][0]
    l2[:64, 1] = inp["l1_w2"][1]
    l2[:64, 2] = inp["l1_a2"][0]
    l2[:64, 3] = inp["l1_a2"][1]
    m["rw_lora2"] = l2
    cl = [col_layout(inp["l1_mu"][i]) for i in range(6)]
    cl += [col_layout(inp["l1_w0"][0]), col_layout(inp["l1_w0"][1]), col_layout(inp["l1_a0"][0]),
           col_layout(inp["l1_a0"][1]), col_layout(inp["l1_k_k"]), col_layout(inp["l1_k_a"]),
           col_layout(inp["l1_r_k"].reshape(-1)), col_layout(inp["l1_ln_w"])]
    m["rw_cols"] = np.ascontiguousarray(np.stack(cl, 1))
    m["rw_lnb"] = col_layout(inp["l1_ln_b"])
    m["hcol"] = np.ascontiguousarray(np.stack([inp["l0_filt_b1"], inp["l0_filt_b2"], inp["l0_filt_b3"],
                                               inp["l0_filt_freq"]], 1))
    m["fw1"] = np.ascontiguousarray(inp["l0_filt_w1"])
    m["fw23"] = np.ascontiguousarray(np.stack([inp["l0_filt_w2"], inp["l0_filt_w3"]], 1))
    m["fw4"] = np.ascontiguousarray(inp["l0_filt_w4"])
    m["fbias"] = col_layout(inp["l0_filt_bias"])
    sw = np.concatenate([inp["l0_short_w"], inp["l0_short_b"][None, :]], 0)
    m["shortw"] = np.ascontiguousarray(sw.T.reshape(6, 128, 4).transpose(1, 0, 2))
    m.update(host_consts())
    return m


def prep_core(xs, cs):
    m = {}
    m["x_in"] = np.ascontiguousarray(np.stack(xs, 0))
    c = np.stack(cs, 0)
    m["cT"] = np.ascontiguousarray(c.reshape(len(xs), NK, 128).transpose(2, 1, 0))
    return m


ALL_STAGES = ("adaln", "l0_inproj", "attn", "hyfilt", "hyena", "l0_outproj", "ffn0", "rw_norm", "rw_proj",
              "rw_scan0", "rw_scan1", "rw_post", "ffn1", "final")


def kernel(**inputs):
    inp = {k: np.asarray(v) for k, v in inputs.items()}
    xs = [inp["x_prompt"][i] for i in range(inp["x_prompt"].shape[0])] + \
         [inp["x_sample"][i] for i in range(inp["x_sample"].shape[0])]
    cs = [inp["c_prompt"][i] for i in range(inp["c_prompt"].shape[0])] + \
         [inp["c_sample"][i] for i in range(inp["c_sample"].shape[0])]
    nseq = len(xs)
    nb = inp["x_prompt"].shape[0]
    assign = []
    for core in range(NCORES):
        ids = []
        for slot in range(NS):
            sid = core + NCORES * slot
            ids.append(sid if sid < nseq else core)
        assign.append(ids)
    nc, P, C = build_program(ALL_STAGES, ())
    finish_program(P, C)
    shared = prep_shared(inp)
    in_maps = []
    for core in range(NCORES):
        m = dict(shared)
        m.update(prep_core([xs[i] for i in assign[core]], [cs[i] for i in assign[core]]))
        in_maps.append(m)
    res = run_bass_kernel_spmd(nc, in_maps, core_ids=list(range(NCORES)))
    outs = [None] * nseq
    for core in range(NCORES):
        y = np.asarray(res.results[core]["y_out"])
        for slot in range(NS):
            sid = core + NCORES * slot
            if sid < nseq:
                outs[sid] = y[slot]
    y_prompt = np.stack(outs[:nb], 0).astype(np.float32)
    y_sample = np.stack(outs[nb:], 0).astype(np.float32)
    return (y_prompt, y_sample)
```

```python
import contextlib
import math
import numpy as np
import concourse.bass as bass
import concourse.mybir as mybir
from concourse.bass_utils import run_bass_kernel_spmd

F32 = mybir.dt.float32
BF16 = mybir.dt.bfloat16
AF = mybir.ActivationFunctionType
ALU = mybir.AluOpType

D = 1024
L = 4096
NK = 8
BLK = 512
NBLK = L // BLK
NCORES = 8
NS = 3
DFF = 2816
NFC = DFF // 128
RMS_EPS = 1e-6

ENGS = ("pe", "act", "dve", "pool", "sp")
NDMASEM = 6


class Buf:
    __slots__ = ("t", "name", "w", "r", "a")

    def __init__(self, t, name):
        self.t = t
        self.name = name
        self.w = {}
        self.r = {}
        self.a = {}

    def __getitem__(self, idx):
        return self.t[idx]


class Prog:
    def __init__(self, nc):
        self.nc = nc
        self.base = contextlib.ExitStack()
        self.es = self.base
        self.streams = {e: [] for e in ENGS}
        self.cnt = {e: 0 for e in ENGS}
        self.seen = {e: {} for e in ENGS}
        self.sem = {}
        self.dtot = {}
        self.rr = {e: 0 for e in ENGS}
        for e in ENGS:
            self.sem[e] = self.base.enter_context(nc.semaphore("c_" + e))
        for q in ("sp", "act", "pool"):
            for i in range(NDMASEM):
                k = "d_%s%d" % (q, i)
                self.sem[k] = self.base.enter_context(nc.semaphore(k))
                self.dtot[k] = 0
        self.nbuf = 0
        self.ninst = 0

    @contextlib.contextmanager
    def scope(self):
        old = self.es
        es = contextlib.ExitStack()
        self.es = es
        try:
            yield
            self.barrier()
            self.emit()
        finally:
            es.close()
            self.es = old

    def sbuf(self, shape, dt, name=None):
        self.nbuf += 1
        name = (name or "sb") + "_%d" % self.nbuf
        t = self.es.enter_context(self.nc.sbuf_tensor(name, list(shape), dt))
        return Buf(t, name)

    def psum(self, shape, dt=F32, name=None):
        self.nbuf += 1
        name = (name or "ps") + "_%d" % self.nbuf
        t = self.es.enter_context(self.nc.psum_tensor(name, list(shape), dt))
        return Buf(t, name)

    def dram(self, name, shape, dt, kind="Internal"):
        t = self.nc.dram_tensor(name, list(shape), dt, kind=kind)
        return Buf(t.ap(), name)

    def _deps(self, eng, reads, writes, accs=()):
        need = {}
        seen = self.seen[eng]

        def add(k, v):
            if k == eng and eng == "pe":
                return
            if seen.get(k, 0) >= v:
                return
            if need.get(k, 0) < v:
                need[k] = v

        for b in reads:
            for k, v in b.w.items():
                add(k, v)
            for k, v in b.a.items():
                add(k, v)
        for b in writes:
            for k, v in b.w.items():
                add(k, v)
            for k, v in b.a.items():
                add(k, v)
            for k, v in b.r.items():
                add(k, v)
        for b in accs:
            for k, v in b.w.items():
                add(k, v)
            for k, v in b.r.items():
                add(k, v)
        for k, v in need.items():
            seen[k] = v
        return list(need.items())

    @staticmethod
    def _commit(ev, reads, writes, accs):
        k, v = ev
        for b in reads:
            if b.r.get(k, 0) < v:
                b.r[k] = v
        for b in writes:
            b.w.clear()
            b.w[k] = v
            b.r.clear()
            b.a.clear()
        for b in accs:
            if b.a.get(k, 0) < v:
                b.a[k] = v

    def op(self, eng, fn, reads=(), writes=(), accs=()):
        waits = self._deps(eng, reads, writes, accs)
        self.cnt[eng] += 1
        ev = (eng, self.cnt[eng])
        self.streams[eng].append((waits, fn, eng, 1))
        self._commit(ev, reads, writes, accs)
        self.ninst += 1 + len(waits)

    def dma(self, q, out, in_, reads=(), writes=(), accs=(), **kw):
        i = self.rr[q]
        self.rr[q] = (i + 1) % NDMASEM
        k = "d_%s%d" % (q, i)
        waits = self._deps(q, reads, writes, accs)
        prev = self.dtot[k]
        if prev > 0 and self.seen[q].get(k, 0) < prev:
            waits.append((k, prev))
            self.seen[q][k] = prev
        self.dtot[k] = prev + 16
        ev = (k, prev + 16)

        def fn(e, out=out, in_=in_, kw=kw):
            return e.dma_start(out=out, in_=in_, **kw)

        self.streams[q].append((waits, fn, k, 16))
        self._commit(ev, reads, writes, accs)
        self.ninst += 1 + len(waits)

    def barrier(self):
        tot = dict(self.dtot)
        for e in ENGS:
            tot[e] = self.cnt[e]
        for e in ENGS:
            waits = []
            for k, v in tot.items():
                if v > 0 and k != e and self.seen[e].get(k, 0) < v:
                    waits.append((k, v))
                    self.seen[e][k] = v
            if waits:
                self.streams[e].append((waits, None, None, 0))
                self.ninst += len(waits)

    def wait_all(self, eng, bufs):
        waits = self._deps(eng, bufs, ())
        self.streams[eng].append((waits, None, None, 0))

    def emit(self):
        if not any(self.streams.values()):
            return
        nc = self.nc
        engobj = {"pe": "tensor", "act": "scalar", "dve": "vector", "pool": "gpsimd", "sp": "sync"}
        with nc.Block() as block:
            for e in ENGS:
                stream = self.streams[e]

                def body(eng, stream=stream):
                    for waits, fn, sk, inc in stream:
                        for k, v in waits:
                            eng.wait_ge(self.sem[k], v)
                        if fn is not None:
                            fn(eng).then_inc(self.sem[sk], inc)

                getattr(block, engobj[e])(body)
        self.streams = {e: [] for e in ENGS}

    def close(self):
        self.emit()
        self.base.close()


class Rot:
    def __init__(self, bufs):
        self.bufs = bufs
        self.i = 0

    def next(self):
        b = self.bufs[self.i % len(self.bufs)]
        self.i += 1
        return b


class Ctx:
    pass


def mm(P, out_buf, out_ap, lhsT_buf, lhsT_ap, rhs_buf, rhs_ap, start, stop):
    P.op("pe", lambda e: e.matmul(out_ap, lhsT=lhsT_ap, rhs=rhs_ap, start=start, stop=stop),
         reads=[lhsT_buf, rhs_buf], writes=[out_buf])


def load_weight_bf16(P, C, dram_buf, nk, ncols, name):
    wb = P.sbuf([128, nk, ncols], BF16, name)
    grp = max(32, (C.wstage_n // nk) // 32 * 32)
    i = 0
    for c0 in range(0, ncols, grp):
        cw = min(grp, ncols - c0)
        st = C.wstage.next()
        P.dma("sp", st[:, 0:nk * cw].rearrange("p (k c) -> p k c", c=cw),
              dram_buf[:, :, c0:c0 + cw], reads=[dram_buf], writes=[st])
        if i % 2 == 0:
            P.op("dve", lambda e, st=st, c0=c0, cw=cw: e.tensor_copy(
                out=wb[:, :, c0:c0 + cw], in_=st[:, 0:nk * cw].rearrange("p (k c) -> p k c", c=cw)),
                reads=[st], accs=[wb])
        else:
            P.op("act", lambda e, st=st, c0=c0, cw=cw: e.copy(
                out=wb[:, :, c0:c0 + cw], in_=st[:, 0:nk * cw].rearrange("p (k c) -> p k c", c=cw)),
                reads=[st], accs=[wb])
        i += 1
    return wb


def norm_mod(P, C, xT, W, gain, shift, s, out_buf, out_dt_is_bf16=True):
    ssp = C.ps_ss.next()
    for k in range(NK):
        sq = C.nm_sq.next()
        P.op("act", lambda e, sq=sq, k=k: e.activation(out=sq[:, 0:W], in_=xT[:, k, 0:W], func=AF.Square),
             reads=[xT], writes=[sq])
        mm(P, ssp, ssp[:, 0:W], C.ones32, C.ones32[:], sq, sq[:, 0:W], k == 0, k == NK - 1)
    rstd = C.nm_rstd.next()
    P.op("act", lambda e: e.activation(out=rstd[:, 0:W], in_=ssp[:, 0:W], func=AF.Sqrt,
                                       scale=1.0 / D, bias=C.eps_col[:, 0:1]),
         reads=[ssp, C.eps_col], writes=[rstd])
    P.op("dve", lambda e: e.reciprocal(out=rstd[:, 0:W], in_=rstd[:, 0:W]), reads=[rstd], writes=[rstd])
    for k in range(NK):
        if shift is None:
            P.op("dve", lambda e, k=k: e.scalar_tensor_tensor(
                out=out_buf[:, k, 0:W], in0=xT[:, k, 0:W], scalar=gain[:, k, s:s + 1], in1=rstd[:, 0:W],
                op0=ALU.mult, op1=ALU.mult), reads=[xT, gain, rstd], accs=[out_buf])
        else:
            tmp = C.nm_tmp.next()
            P.op("dve", lambda e, k=k, tmp=tmp: e.scalar_tensor_tensor(
                out=tmp[:, 0:W], in0=xT[:, k, 0:W], scalar=gain[:, k, s:s + 1], in1=rstd[:, 0:W],
                op0=ALU.mult, op1=ALU.mult), reads=[xT, gain, rstd], writes=[tmp])
            P.op("act", lambda e, k=k, tmp=tmp: e.activation(
                out=out_buf[:, k, 0:W], in_=tmp[:, 0:W], func=AF.Identity, bias=shift[:, k, s:s + 1], scale=1.0),
                reads=[tmp, shift], accs=[out_buf])


def alloc_norm_scratch(P, C, W=BLK):
    C.nm_sq = Rot([P.sbuf([128, W], F32, "nmsq") for _ in range(2)])
    C.nm_rstd = Rot([P.sbuf([128, W], F32, "nmrs") for _ in range(2)])
    C.nm_tmp = Rot([P.sbuf([128, W], F32, "nmtmp") for _ in range(2)])
    C.ps_ss = Rot([P.psum([128, W], F32, "psss")])


def fence(P, buf):
    return buf


def stage_adaln(P, C):
    with P.scope():
        cT = P.sbuf([128, NK, NS], F32, "cT")
        P.dma("sp", cT[:], C.d_cT[:], reads=[C.d_cT], writes=[cT])
        sc = P.sbuf([128, NK, NS], F32, "silu_c")
        P.op("act", lambda e: e.activation(out=sc[:], in_=cT[:], func=AF.Silu), reads=[cT], writes=[sc])
        wts = Rot([P.sbuf([128, NK, 1024], F32, "adaw") for _ in range(2)])
        pss = Rot([P.psum([128, 512], F32, "adaps") for _ in range(2)])
        adab = P.sbuf([128, 2, 48], F32, "adab")
        P.dma("sp", adab[:], C.d_adab[:], reads=[C.d_adab], writes=[adab])
        for l in range(2):
            for j in range(6):
                wt = wts.next()
                P.dma("sp" if j % 2 == 0 else "act", wt[:], C.d_adaw[l][:, :, j * 1024:(j + 1) * 1024],
                      reads=[C.d_adaw[l]], writes=[wt])
                psb = pss.next()
                ps = psb[:, 0:8 * NS].rearrange("p (c s) -> p c s", s=NS)
                for cc in range(8):
                    for k in range(NK):
                        mm(P, psb, ps[:, cc, :], wt, wt[:, k, cc * 128:(cc + 1) * 128], sc, sc[:, k, :],
                           k == 0, k == NK - 1)
                P.op("dve", lambda e, l=l, j=j, ps=ps: e.tensor_tensor(
                    out=C.modT[l][:, j * 8:(j + 1) * 8, :], in0=ps,
                    in1=adab[:, l, j * 8:(j + 1) * 8].unsqueeze(2).to_broadcast([128, 8, NS]), op=ALU.add),
                    reads=[psb, adab], accs=[C.modT[l]])
        nw = P.sbuf([128, 5, NK], F32, "normw")
        P.dma("sp", nw[:], C.d_normw[:], reads=[C.d_normw], writes=[nw])
        for l in range(2):
            for which in range(2):
                jsc = 1 + 3 * which
                g = C.gain[l][which]
                P.op("dve", lambda e, l=l, which=which, jsc=jsc, g=g: e.scalar_tensor_tensor(
                    out=g[:], in0=C.modT[l][:, jsc * 8:(jsc + 1) * 8, :], scalar=1.0,
                    in1=nw[:, 2 * l + which, :].unsqueeze(2).to_broadcast([128, NK, NS]),
                    op0=ALU.add, op1=ALU.mult), reads=[C.modT[l], nw], writes=[g])
        P.op("dve", lambda e: e.tensor_copy(
            out=C.gain_fin[:], in_=nw[:, 4, :].unsqueeze(2).to_broadcast([128, NK, NS])),
            reads=[nw], writes=[C.gain_fin])


def mod_vec(C, l, j):
    return C.modT[l]


def stage_l0_inproj(P, C):
    with P.scope():
        C.wstage_n = 2048
        C.wstage = Rot([P.sbuf([128, 2048], F32, "wst") for _ in range(2)])
        w_in = load_weight_bf16(P, C, C.d_w_in, NK, 3072, "w_in")
        alloc_norm_scratch(P, C)
        xins = Rot([P.sbuf([128, 4, D], F32, "xin") for _ in range(2)])
        xTs = Rot([P.sbuf([128, NK, BLK], F32, "xT") for _ in range(2)])
        hTs = Rot([P.sbuf([128, NK, BLK], BF16, "hT") for _ in range(2)])
        pts = Rot([P.psum([128, BLK], F32, "pt") for _ in range(2)])
        pps = Rot([P.psum([128, BLK], F32, "pp") for _ in range(2)])
        pvs = Rot([P.psum([128, 1024], F32, "pv") for _ in range(1)])
        qks = Rot([P.sbuf([128, 12, BLK], BF16, "qk") for _ in range(1)])
        hys = Rot([P.sbuf([128, 6, BLK], F32, "hy") for _ in range(1)])
        vas = []
        for _ in range(2):
            va = P.sbuf([128, 4, 12, 65], BF16, "vaug")
            P.op("pool", lambda e, va=va: e.memset(va[:], 1.0), writes=[va])
            vas.append(va)
        vas = Rot(vas)
        mod = C.modT[0]
        ev = 0
        for s in range(NS):
            for blk in range(NBLK):
                t0 = blk * BLK
                xin = xins.next()
                P.dma("sp", xin[:], C.d_x[s, t0:t0 + BLK, :].rearrange("(j p) f -> p j f", p=128),
                      reads=[C.d_x], writes=[xin])
                xT = xTs.next()
                for k in range(NK):
                    pt = pts.next()
                    for j in range(4):
                        P.op("pe", lambda e, pt=pt, j=j, k=k, xin=xin: e.transpose(
                            pt[:, j * 128:(j + 1) * 128], xin[:, j, k * 128:(k + 1) * 128], C.ident32[:]),
                            reads=[xin, C.ident32], writes=[pt])
                    if k % 2 == 0:
                        P.op("act", lambda e, pt=pt, k=k, xT=xT: e.copy(out=xT[:, k, :], in_=pt[:]),
                             reads=[pt], accs=[xT])
                    else:
                        P.op("dve", lambda e, pt=pt, k=k, xT=xT: e.tensor_copy(out=xT[:, k, :], in_=pt[:]),
                             reads=[pt], accs=[xT])
                P.dma("pool", C.d_xT[0][s][:, :, t0:t0 + BLK], xT[:], reads=[xT], accs=[C.d_xT[0][s]])
                hT = hTs.next()
                norm_mod(P, C, xT, BLK, C.gain[0][0], mod_shift(C, 0, 0), s, hT)
                qk = qks.next()
                for c in range(12):
                    pp = pps.next()
                    for k in range(NK):
                        mm(P, pp, pp[:], w_in, w_in[:, k, c * 128:(c + 1) * 128], hT, hT[:, k, :], k == 0, k == NK - 1)
                    if c % 2 == 0:
                        P.op("act", lambda e, pp=pp, c=c, qk=qk: e.copy(out=qk[:, c, :], in_=pp[:]),
                             reads=[pp], accs=[qk])
                    else:
                        P.op("dve", lambda e, pp=pp, c=c, qk=qk: e.tensor_copy(out=qk[:, c, :], in_=pp[:]),
                             reads=[pp], accs=[qk])
                P.dma("pool", C.d_qkT[s][:, :, t0:t0 + BLK], qk[:], reads=[qk], accs=[C.d_qkT[s]])
                va = vas.next()
                for j in range(4):
                    pv = pvs.next()
                    for k in range(NK):
                        mm(P, pv, pv[:, 0:512], hT, hT[:, k, j * 128:(j + 1) * 128], w_in, w_in[:, k, 1536:2048],
                           k == 0, k == NK - 1)
                    for k in range(NK):
                        mm(P, pv, pv[:, 512:768], hT, hT[:, k, j * 128:(j + 1) * 128], w_in, w_in[:, k, 2048:2304],
                           k == 0, k == NK - 1)
                    P.op("act" if j % 2 == 0 else "dve",
                         (lambda e, pv=pv, j=j, va=va: e.copy(
                             out=va[:, j, :, 0:64], in_=pv[:, 0:768].rearrange("p (h d) -> p h d", d=64)))
                         if j % 2 == 0 else
                         (lambda e, pv=pv, j=j, va=va: e.tensor_copy(
                             out=va[:, j, :, 0:64], in_=pv[:, 0:768].rearrange("p (h d) -> p h d", d=64))),
                         reads=[pv], accs=[va])
                P.dma("pool", C.d_vaug[s][t0:t0 + BLK].rearrange("(j p) h d -> p j h d", p=128), va[:],
                      reads=[va], accs=[C.d_vaug[s]])
                hy = hys.next()
                for c in range(6):
                    pp = pps.next()
                    for k in range(NK):
                        mm(P, pp, pp[:], w_in, w_in[:, k, 2304 + c * 128:2304 + (c + 1) * 128], hT, hT[:, k, :],
                           k == 0, k == NK - 1)
                    if c % 2 == 0:
                        P.op("act", lambda e, pp=pp, c=c, hy=hy: e.copy(out=hy[:, c, :], in_=pp[:]),
                             reads=[pp], accs=[hy])
                    else:
                        P.op("dve", lambda e, pp=pp, c=c, hy=hy: e.tensor_copy(out=hy[:, c, :], in_=pp[:]),
                             reads=[pp], accs=[hy])
                P.dma("pool", C.d_hyT[s][:, :, t0:t0 + BLK], hy[:], reads=[hy], accs=[C.d_hyT[s]])


DILS = (1, 4, 16)
FINAL_SRC = 4
import os
ATT_STOP = int(os.environ.get("ATT_STOP", "9"))
ATT_GROUPS = tuple(int(x) for x in os.environ.get("ATT_GROUPS", "0,1,2").split(","))


def stage_attention(P, C):
    with P.scope():
        tabs = P.sbuf([128, 9, 4, 128], F32, "abias")
        P.dma("sp", tabs[:], C.d_abias[:], reads=[C.d_abias], writes=[tabs])
        qTs = Rot([P.sbuf([128, 2, L], BF16, "qT") for _ in range(2)])
        kTs = Rot([P.sbuf([128, 2, L], BF16, "kT") for _ in range(2)])
        vts = Rot([P.sbuf([128, 4, 65], BF16, "vt") for _ in range(4)])
        pss = Rot([P.psum([128, 512], F32, "pss") for _ in range(4)])
        pos = Rot([P.psum([128, 512], F32, "pso") for _ in range(2)])
        sbs = Rot([P.sbuf([128, 4, 128], F32, "ssb") for _ in range(2)])
        pTs = Rot([P.sbuf([128, 4, 128], BF16, "pT") for _ in range(4)])
        osb = Rot([P.sbuf([128, 4, 65], F32, "osb") for _ in range(3)])
        cnt = 0
        for s in range(NS):
            for g, d in enumerate(DILS):
                if g not in ATT_GROUPS:
                    continue
                qT = qTs.next()
                kT = kTs.next()
                P.dma("sp", qT[:], C.d_qkT[s][:, 2 * g:2 * g + 2, :], reads=[C.d_qkT[s]], writes=[qT])
                P.dma("sp", kT[:], C.d_qkT[s][:, 6 + 2 * g:6 + 2 * g + 2, :], reads=[C.d_qkT[s]], writes=[kT])
                n = L // d
                ntile = n // 128
                for r in range(d):
                    def load_v(m):
                        vt = vts.next()
                        if m == 0:
                            ks, nk = 0, 64
                        elif m == ntile:
                            ks, nk = n - 64, 64
                        else:
                            ks, nk = 128 * m - 64, 128
                        t0 = ks * d + r
                        P.dma("sp", vt[0:nk], C.d_vaug[s][t0:t0 + (nk - 1) * d + 1:d, 4 * g:4 * g + 4, :],
                              reads=[C.d_vaug[s]], writes=[vt])
                        return vt, ks, nk
                    vcur = load_v(0)
                    for qt in range(ntile):
                        vnext = load_v(qt + 1)
                        i0 = qt * 128
                        q0 = i0 * d + r
                        qsl = slice(q0, q0 + 127 * d + 1, d)
                        pTl = []
                        if ATT_STOP <= 1:
                            vcur = vnext
                            continue
                        for bi, (vt, ks, nk) in enumerate((vcur, vnext)):
                            if bi == 0:
                                ti = 3 * g + (2 if qt == 0 else 0)
                            else:
                                ti = 3 * g + 1
                            k0 = ks * d + r
                            ksl = slice(k0, k0 + (nk - 1) * d + 1, d)
                            sb = sbs.next()
                            pT = pTs.next()
                            for hh in range(2):
                                ps = pss.next()
                                psv = ps[:, 0:256].rearrange("p (a q) -> p a q", q=128)
                                for hp in range(2):
                                    mm(P, ps, psv[0:nk, hp, :], kT, kT[hh * 64:(hh + 1) * 64, hp, ksl],
                                       qT, qT[hh * 64:(hh + 1) * 64, hp, qsl], True, True)
                                P.op("dve", lambda e, ps=ps, psv=psv, sb=sb, nk=nk, ti=ti, hh=hh: e.scalar_tensor_tensor(
                                    out=sb[0:nk, hh::2, :], in0=psv[0:nk], scalar=0.125, in1=tabs[0:nk, ti, hh::2, :],
                                    op0=ALU.mult, op1=ALU.add), reads=[ps, tabs], accs=[sb])
                            P.op("act", lambda e, sb=sb, pT=pT, nk=nk: e.activation(
                                out=pT[0:nk], in_=sb[0:nk], func=AF.Exp), reads=[sb], writes=[pT])
                            pTl.append((pT, vt, nk))
                        if ATT_STOP <= 2:
                            vcur = vnext
                            continue
                        po = pos.next()
                        pov = po[:, 0:260].rearrange("p (h d) -> p h d", d=65)
                        for h in range(4):
                            for bi, (pT, vt, nk) in enumerate(pTl):
                                mm(P, po, pov[:, h, :], pT, pT[0:nk, h, :], vt, vt[0:nk, h, :], bi == 0, bi == 1)
                        ob = osb.next()
                        if cnt % 2 == 0:
                            P.op("act", lambda e, ob=ob, pov=pov: e.copy(out=ob[:], in_=pov), reads=[po], writes=[ob])
                        else:
                            P.op("dve", lambda e, ob=ob, pov=pov: e.tensor_copy(out=ob[:], in_=pov),
                                 reads=[po], writes=[ob])
                        cnt += 1
                        if ATT_STOP <= 3:
                            vcur = vnext
                            continue
                        P.dma("pool", C.d_oacc[s][g, q0:q0 + 127 * d + 1:d, :, :], ob[:],
                              reads=[ob], accs=[C.d_oacc[s]])
                        vcur = vnext


HY_GRP = 32
TWO_PI = 2.0 * math.pi


def alloc_fft(P, C):
    F = Ctx()
    F.w128 = P.sbuf([64, 256], F32, "w128")
    F.tw = P.sbuf([128, 2, 128], F32, "tw")
    F.bd = P.sbuf([128, 3, 128], F32, "bd")
    F.bdc = P.sbuf([128, 2, 256], F32, "bdc")
    F.twi = P.sbuf([128, 2, 128], F32, "twi")
    F.vinv = P.sbuf([128, 2, 64], F32, "vinv")
    for b, d in ((F.w128, C.d_w128), (F.tw, C.d_tw), (F.bd, C.d_bd), (F.bdc, C.d_bdc), (F.twi, C.d_twi),
                 (F.vinv, C.d_vinv)):
        P.dma("sp", b[:], d[:], reads=[d], writes=[b])
    F.psA = Rot([P.psum([128, 512], F32, "psA") for _ in range(2)])
    F.psX = Rot([P.psum([128, 512], F32, "psX") for _ in range(2)])
    F.p1 = Rot([P.sbuf([128, 2, 128], F32, "fp1") for _ in range(2)])
    F.p2 = Rot([P.sbuf([128, 2, 128], F32, "fp2") for _ in range(2)])
    F.b = Rot([P.sbuf([128, 2, 128], F32, "fb") for _ in range(2)])
    return F


def cmul(P, F, src_buf, src_v, tab_buf, tab_r, tab_i, out_buf, out_v):
    p1 = F.p1.next()
    p2 = F.p2.next()
    n = src_v.shape[0]
    P.op("dve", lambda e: e.tensor_tensor(out=p1[0:n], in0=src_v, in1=tab_r.unsqueeze(1).to_broadcast([n, 2, 128]),
                                          op=ALU.mult), reads=[src_buf, tab_buf], writes=[p1])
    P.op("dve", lambda e: e.tensor_tensor(out=p2[0:n], in0=src_v, in1=tab_i.unsqueeze(1).to_broadcast([n, 2, 128]),
                                          op=ALU.mult), reads=[src_buf, tab_buf], writes=[p2])
    P.op("dve", lambda e: e.tensor_tensor(out=out_v[:, 0, :], in0=p1[0:n, 0, :], in1=p2[0:n, 1, :], op=ALU.subtract),
         reads=[p1, p2], accs=[out_buf])
    P.op("dve", lambda e: e.tensor_tensor(out=out_v[:, 1, :], in0=p2[0:n, 0, :], in1=p1[0:n, 1, :], op=ALU.add),
         reads=[p1, p2], accs=[out_buf])


def fft_fwd_pair(P, F, xin_buf, xin_ap):
    psA = F.psA.next()
    mm(P, psA, psA[:, 0:256], xin_buf, xin_ap, F.w128, F.w128[:], True, True)
    b = F.b.next()
    cmul(P, F, psA, psA[:, 0:256].rearrange("p (c k) -> p c k", k=128), F.tw, F.tw[:, 0, :], F.tw[:, 1, :], b, b[:])
    psX = F.psX.next()
    mm(P, psX, psX[:, 0:128], F.bd, F.bd[:, 0, :], b, b[:, 0, :], True, False)
    mm(P, psX, psX[:, 0:128], F.bd, F.bd[:, 2, :], b, b[:, 1, :], False, True)
    mm(P, psX, psX[:, 128:256], F.bd, F.bd[:, 1, :], b, b[:, 0, :], True, False)
    mm(P, psX, psX[:, 128:256], F.bd, F.bd[:, 0, :], b, b[:, 1, :], False, True)
    return psX, psX[:, 0:256].rearrange("p (c k) -> p c k", k=128)


def fft_fwd_multi(P, F, inputs):
    psAs = []
    for buf, ap in inputs:
        psA = F.psA.next()
        mm(P, psA, psA[:, 0:256], buf, ap, F.w128, F.w128[:], True, True)
        psAs.append(psA)
    bs = []
    for psA in psAs:
        b = F.b.next()
        cmul(P, F, psA, psA[:, 0:256].rearrange("p (c k) -> p c k", k=128), F.tw, F.tw[:, 0, :], F.tw[:, 1, :], b, b[:])
        bs.append(b)
    outs = []
    for b in bs:
        psX = F.psX.next()
        mm(P, psX, psX[:, 0:128], F.bd, F.bd[:, 0, :], b, b[:, 0, :], True, False)
        mm(P, psX, psX[:, 0:128], F.bd, F.bd[:, 2, :], b, b[:, 1, :], False, True)
        mm(P, psX, psX[:, 128:256], F.bd, F.bd[:, 1, :], b, b[:, 0, :], True, False)
        mm(P, psX, psX[:, 128:256], F.bd, F.bd[:, 0, :], b, b[:, 1, :], False, True)
        outs.append((psX, psX[:, 0:256].rearrange("p (c k) -> p c k", k=128)))
    return outs


def wrap_pi(P, u, m, n, W):
    for _ in range(2):
        P.op("dve", lambda e: e.tensor_single_scalar(out=m[0:n, 0:W], in_=u[0:n, 0:W], scalar=math.pi, op=ALU.is_gt),
             reads=[u], writes=[m])
        P.op("dve", lambda e: e.scalar_tensor_tensor(out=u[0:n, 0:W], in0=m[0:n, 0:W], scalar=-TWO_PI, in1=u[0:n, 0:W],
                                                     op0=ALU.mult, op1=ALU.add), reads=[m, u], writes=[u])
        P.op("dve", lambda e: e.tensor_single_scalar(out=m[0:n, 0:W], in_=u[0:n, 0:W], scalar=-math.pi, op=ALU.is_lt),
             reads=[u], writes=[m])
        P.op("dve", lambda e: e.scalar_tensor_tensor(out=u[0:n, 0:W], in0=m[0:n, 0:W], scalar=TWO_PI, in1=u[0:n, 0:W],
                                                     op0=ALU.mult, op1=ALU.add), reads=[m, u], writes=[u])


def stage_hyena_filter(P, C):
    with P.scope():
        hcol = P.sbuf([64, 8], F32, "hcol")
        P.dma("sp", hcol[:, 0:4], C.d_hcol[:], reads=[C.d_hcol], writes=[hcol])
        for i in range(3):
            P.op("dve", lambda e, i=i: e.tensor_tensor(out=hcol[:, 4 + i:5 + i], in0=hcol[:, i:i + 1],
                                                       in1=hcol[:, 3:4], op=ALU.mult), reads=[hcol], writes=[hcol])
        w1 = P.sbuf([33, 64], F32, "fw1")
        w23 = P.sbuf([64, 2, 64], F32, "fw23")
        w4 = P.sbuf([64, 512], F32, "fw4")
        fbias = P.sbuf([128, 2], F32, "fbias")
        P.dma("sp", w1[:], C.d_fw1[:], reads=[C.d_fw1], writes=[w1])
        P.dma("sp", w23[:], C.d_fw23[:], reads=[C.d_fw23], writes=[w23])
        P.dma("sp", w4[:], C.d_fw4[:], reads=[C.d_fw4], writes=[w4])
        P.dma("sp", fbias[:], C.d_fbias[:], reads=[C.d_fbias], writes=[fbias])
        hT = P.sbuf([128, 4, L], F32, "filt_hT")
        pos = Rot([P.sbuf([33, BLK], F32, "pos") for _ in range(2)])
        win = Rot([P.sbuf([128, 2, BLK], F32, "win") for _ in range(2)])
        us = Rot([P.sbuf([64, BLK], F32, "fu") for _ in range(3)])
        ms = Rot([P.sbuf([64, BLK], F32, "fm") for _ in range(2)])
        pps = Rot([P.psum([128, BLK], F32, "fpp") for _ in range(2)])
        for blk in range(NBLK):
            t0 = blk * BLK
            po = pos.next()
            wn = win.next()
            P.dma("sp", po[:], C.d_posT[:, t0:t0 + BLK], reads=[C.d_posT], writes=[po])
            P.dma("sp", wn[:], C.d_winT[:, :, t0:t0 + BLK], reads=[C.d_winT], writes=[wn])
            prev_buf, prev_ap, kdim = po, po[:], 33
            for layer in range(3):
                pp = pps.next()
                if layer == 0:
                    mm(P, pp, pp[0:64, :], w1, w1[:], prev_buf, prev_ap, True, True)
                else:
                    mm(P, pp, pp[0:64, :], w23, w23[:, layer - 1, :], prev_buf, prev_ap, True, True)
                u = us.next()
                m = ms.next()
                P.op("dve", lambda e, pp=pp, u=u, layer=layer: e.tensor_scalar(
                    out=u[:], in0=pp[0:64, :], scalar1=hcol[:, 3:4], scalar2=hcol[:, 4 + layer:5 + layer],
                    op0=ALU.mult, op1=ALU.add), reads=[pp, hcol], writes=[u])
                wrap_pi(P, u, m, 64, BLK)
                P.op("act", lambda e, u=u: e.activation(out=u[:], in_=u[:], func=AF.Sin), reads=[u], writes=[u])
                prev_buf, prev_ap = u, u[:]
            for c in range(4):
                pp = pps.next()
                mm(P, pp, pp[:], w4, w4[:, c * 128:(c + 1) * 128], prev_buf, prev_ap, True, True)
                P.op("dve", lambda e, pp=pp, c=c, wn=wn, t0=t0: e.tensor_tensor(
                    out=hT[:, c, t0:t0 + BLK], in0=pp[:], in1=wn[:, c % 2, :], op=ALU.mult),
                    reads=[pp, wn], accs=[hT])
        junk = P.sbuf([128, L], F32, "fjunk")
        acc = P.sbuf([128, 8], F32, "facc")
        P.op("pool", lambda e: e.memset(acc[:], 0.0), writes=[acc])
        for c in range(4):
            lo = 0 if c < 2 else 1
            P.op("act", lambda e, c=c, lo=lo: e.activation(out=junk[:, lo:L], in_=hT[:, c, lo:L], func=AF.Abs,
                                                           accum_out=acc[:, c:c + 1]),
                 reads=[hT], writes=[junk, acc])
        P.op("dve", lambda e: e.tensor_tensor(out=acc[:, 4:6], in0=acc[:, 0:2], in1=acc[:, 2:4], op=ALU.add),
             reads=[acc], writes=[acc])
        P.op("dve", lambda e: e.reciprocal(out=acc[:, 6:8], in_=acc[:, 4:6]), reads=[acc], writes=[acc])
        for c in range(4):
            P.op("dve", lambda e, c=c: e.tensor_scalar(
                out=hT[:, c, :], in0=hT[:, c, :], scalar1=acc[:, 6 + c % 2:7 + c % 2], scalar2=None, op0=ALU.mult),
                reads=[hT, acc], writes=[hT])
        for j in range(2):
            P.op("dve", lambda e, j=j: e.tensor_tensor(out=hT[:, j, 0:1], in0=hT[:, j, 0:1], in1=fbias[:, j:j + 1],
                                                       op=ALU.add), reads=[hT, fbias], writes=[hT])
            P.op("dve", lambda e, j=j: e.memset(hT[:, 2 + j, 0:1], 0.0), writes=[hT])
        P.dma("pool", C.d_filtT[:], hT[:], reads=[hT], writes=[C.d_filtT])
    with P.scope():
        F = alloc_fft(P, C)
        xf = Rot([P.sbuf([64, HY_GRP, 64], F32, "xf") for _ in range(2)])
        xb = Rot([P.sbuf([64, HY_GRP, 64], F32, "xb") for _ in range(2)])
        fo = Rot([P.sbuf([128, HY_GRP // 2, 2, 128], F32, "fo") for _ in range(2)])
        for j in range(2):
            for c0 in range(0, 128, HY_GRP):
                a = xf.next()
                b = xb.next()
                P.dma("sp", a[:], C.d_filtT[c0:c0 + HY_GRP, j, :].rearrange("c (a b) -> a c b", b=64),
                      reads=[C.d_filtT], writes=[a])
                P.dma("sp", b[:], C.d_filtT[c0:c0 + HY_GRP, 2 + j, :].rearrange("c (a b) -> a c b", b=64),
                      reads=[C.d_filtT], writes=[b])
                o = fo.next()
                for i in range(HY_GRP // 2):
                    (psf, vf), (psb, vb) = fft_fwd_multi(P, F, [
                        (a, a[:, 2 * i:2 * i + 2, :].rearrange("p c n -> p (c n)")),
                        (b, b[:, 2 * i:2 * i + 2, :].rearrange("p c n -> p (c n)"))])
                    P.op("act", lambda e, o=o, i=i, vb=vb: e.copy(out=o[:, i], in_=vb), reads=[psb], accs=[o])
                    P.op("dve", lambda e, o=o, i=i, vf=vf: e.tensor_tensor(out=o[:, i, 0, :], in0=vf[:, 0, :],
                                                                           in1=o[:, i, 0, :], op=ALU.add),
                         reads=[psf, o], accs=[o])
                    P.op("dve", lambda e, o=o, i=i, vf=vf: e.tensor_tensor(out=o[:, i, 1, :], in0=vf[:, 1, :],
                                                                           in1=o[:, i, 1, :], op=ALU.subtract),
                         reads=[psf, o], accs=[o])
                pr0 = (j * 128 + c0) // 2
                P.dma("pool", C.d_F[:, pr0:pr0 + HY_GRP // 2], o[:], reads=[o], accs=[C.d_F])


def dwconv3(P, src, dst, W, wt, c):
    P.op("act", lambda e: e.activation(out=dst[:, 0:W], in_=src[:, 0:W], func=AF.Identity,
                                       scale=wt[:, c, 1:2], bias=wt[:, c, 3:4]), reads=[src, wt], writes=[dst])
    P.op("dve", lambda e: e.scalar_tensor_tensor(out=dst[:, 1:W], in0=src[:, 0:W - 1], scalar=wt[:, c, 0:1],
                                                 in1=dst[:, 1:W], op0=ALU.mult, op1=ALU.add),
         reads=[src, wt, dst], writes=[dst])
    P.op("dve", lambda e: e.scalar_tensor_tensor(out=dst[:, 0:W - 1], in0=src[:, 1:W], scalar=wt[:, c, 2:3],
                                                 in1=dst[:, 0:W - 1], op0=ALU.mult, op1=ALU.add),
         reads=[src, wt, dst], writes=[dst])


def stage_hyena(P, C):
    with P.scope():
        swt = P.sbuf([128, 6, 4], F32, "shortw")
        P.dma("sp", swt[:], C.d_shortw[:], reads=[C.d_shortw], writes=[swt])
        srcs = Rot([P.sbuf([128, L], F32, "hysrc") for _ in range(3)])
        dsts = Rot([P.sbuf([128, L], F32, "hydst") for _ in range(4)])
        for s in range(NS):
            for j in range(2):
                res = []
                for part in range(3):
                    c = 2 * part + j
                    src = srcs.next()
                    P.dma("sp" if part != 1 else "act", src[:], C.d_hyT[s][:, c, :], reads=[C.d_hyT[s]], writes=[src])
                    dst = dsts.next()
                    dwconv3(P, src, dst, L, swt, c)
                    res.append(dst)
                x0, x1, v = res
                P.op("dve", lambda e, x1=x1, v=v: e.tensor_tensor(out=v[:], in0=v[:], in1=x1[:], op=ALU.mult),
                     reads=[v, x1], writes=[v])
                P.dma("pool", C.d_zT[s][:, j, :], v[:], reads=[v], accs=[C.d_zT[s]])
                P.dma("pool", C.d_x0T[s][:, j, :], x0[:], reads=[x0], accs=[C.d_x0T[s]])
    with P.scope():
        F = alloc_fft(P, C)
        psC = Rot([P.psum([128, 512], F32, "psC") for _ in range(2)])
        psY = Rot([P.psum([128, 512], F32, "psY") for _ in range(2)])
        zin = Rot([P.sbuf([64, HY_GRP, 64], F32, "zin") for _ in range(2)])
        x0in = Rot([P.sbuf([64, HY_GRP, 64], F32, "x0in") for _ in range(2)])
        oin = Rot([P.sbuf([64, HY_GRP, 64], F32, "oin") for _ in range(2)])
        fsp = Rot([P.sbuf([128, HY_GRP // 2, 2, 128], F32, "fsp") for _ in range(2)])
        ys = Rot([P.sbuf([128, 2, 128], F32, "fy") for _ in range(2)])
        ds = Rot([P.sbuf([128, 2, 128], F32, "fd") for _ in range(2)])
        for s in range(NS):
            for j in range(2):
                for c0 in range(0, 128, HY_GRP):
                    zi = zin.next()
                    xi = x0in.next()
                    fs = fsp.next()
                    oi = oin.next()
                    P.dma("sp", zi[:], C.d_zT[s][c0:c0 + HY_GRP, j, :].rearrange("c (a b) -> a c b", b=64),
                          reads=[C.d_zT[s]], writes=[zi])
                    P.dma("sp", xi[:], C.d_x0T[s][c0:c0 + HY_GRP, j, :].rearrange("c (a b) -> a c b", b=64),
                          reads=[C.d_x0T[s]], writes=[xi])
                    pr0 = (j * 128 + c0) // 2
                    P.dma("sp", fs[:], C.d_F[:, pr0:pr0 + HY_GRP // 2], reads=[C.d_F], writes=[fs])
                    for i0 in range(0, HY_GRP // 2, 2):
                        pr = (i0, i0 + 1)
                        st = {}
                        for i in pr:
                            psA = F.psA.next()
                            mm(P, psA, psA[:, 0:256], zi, zi[:, 2 * i:2 * i + 2, :].rearrange("p c n -> p (c n)"),
                               F.w128, F.w128[:], True, True)
                            st[i] = dict(psA=psA)
                        for i in pr:
                            b = F.b.next()
                            psA = st[i]["psA"]
                            cmul(P, F, psA, psA[:, 0:256].rearrange("p (c k) -> p c k", k=128), F.tw, F.tw[:, 0, :],
                                 F.tw[:, 1, :], b, b[:])
                            st[i]["b"] = b
                        for i in pr:
                            b = st[i]["b"]
                            psX = F.psX.next()
                            mm(P, psX, psX[:, 0:128], F.bd, F.bd[:, 0, :], b, b[:, 0, :], True, False)
                            mm(P, psX, psX[:, 0:128], F.bd, F.bd[:, 2, :], b, b[:, 1, :], False, True)
                            mm(P, psX, psX[:, 128:256], F.bd, F.bd[:, 1, :], b, b[:, 0, :], True, False)
                            mm(P, psX, psX[:, 128:256], F.bd, F.bd[:, 0, :], b, b[:, 1, :], False, True)
                            st[i]["psX"] = psX
                        for i in pr:
                            psX = st[i]["psX"]
                            y = ys.next()
                            cmul(P, F, psX, psX[:, 0:256].rearrange("p (c k) -> p c k", k=128), fs, fs[:, i, 0, :],
                                 fs[:, i, 1, :], y, y[:])
                            st[i]["y"] = y
                        for i in pr:
                            y = st[i]["y"]
                            pc = psC.next()
                            mm(P, pc, pc[:, 0:256], y, y[:, 0, :], F.bdc, F.bdc[:, 0, :], True, False)
                            mm(P, pc, pc[:, 0:256], y, y[:, 1, :], F.bdc, F.bdc[:, 1, :], False, True)
                            st[i]["pc"] = pc
                        for i in pr:
                            pc = st[i]["pc"]
                            dd = ds.next()
                            cmul(P, F, pc, pc[:, 0:256].rearrange("p (c k) -> p c k", k=128), F.twi, F.twi[:, 0, :],
                                 F.twi[:, 1, :], dd, dd[:])
                            st[i]["dd"] = dd
                        for i in pr:
                            dd = st[i]["dd"]
                            py = psY.next()
                            mm(P, py, py[0:64, 0:128], F.vinv, F.vinv[:, 0, :], dd, dd[:, 0, :], True, False)
                            mm(P, py, py[0:64, 0:128], F.vinv, F.vinv[:, 1, :], dd, dd[:, 1, :], False, True)
                            st[i]["py"] = py
                        for i in pr:
                            py = st[i]["py"]
                            P.op("dve", lambda e, py=py, oi=oi, xi=xi, i=i: e.tensor_tensor(
                                out=oi[:, 2 * i:2 * i + 2, :].rearrange("p c n -> p (c n)"), in0=py[0:64, 0:128],
                                in1=xi[:, 2 * i:2 * i + 2, :].rearrange("p c n -> p (c n)"), op=ALU.mult),
                                reads=[py, xi], accs=[oi])
                    P.dma("pool", C.d_hyoT[s][c0:c0 + HY_GRP, j, :].rearrange("c (a b) -> a c b", b=64), oi[:],
                          reads=[oi], accs=[C.d_hyoT[s]])


def stage_l0_outproj(P, C):
    with P.scope():
        C.wstage_n = 2048
        C.wstage = Rot([P.sbuf([128, 2048], F32, "wst") for _ in range(2)])
        w_out = load_weight_bf16(P, C, C.d_w_out, 4, D, "w_out")
        identb = P.sbuf([128, 128], BF16, "identb")
        P.op("dve", lambda e: e.tensor_copy(out=identb[:], in_=C.ident32[:]), reads=[C.ident32], writes=[identb])
        alloc_norm_scratch(P, C)
        oas = Rot([P.sbuf([128, 4, 3, 260], F32, "oa") for _ in range(2)])
        o2s = Rot([P.sbuf([128, 4, 260], F32, "o2") for _ in range(2)])
        rds = Rot([P.sbuf([128, 4, 4], F32, "rden") for _ in range(2)])
        abs_ = Rot([P.sbuf([128, 4, 4, 64], BF16, "attnb") for _ in range(2)])
        hyl = Rot([P.sbuf([128, 2, BLK], F32, "hyl") for _ in range(2)])
        mixs = Rot([P.sbuf([128, 4, BLK], BF16, "mixT") for _ in range(2)])
        xrs = Rot([P.sbuf([128, NK, BLK], F32, "xr") for _ in range(2)])
        x1s = Rot([P.sbuf([128, NK, BLK], F32, "x1T") for _ in range(2)])
        h2s = Rot([P.sbuf([128, NK, BLK], BF16, "h2T") for _ in range(2)])
        ptb = Rot([P.psum([128, BLK], BF16, "ptb") for _ in range(2)])
        pps = Rot([P.psum([128, BLK], F32, "pp") for _ in range(2)])
        g1 = C.gatev[0][0]
        for s in range(NS):
            for blk in range(NBLK):
                t0 = blk * BLK
                oa = oas.next()
                for g in range(3):
                    P.dma("sp" if g != 1 else "act", oa[:, :, g, :],
                          C.d_oacc[s][g, t0:t0 + BLK].rearrange("(j p) h d -> p j (h d)", p=128),
                          reads=[C.d_oacc[s]], accs=[oa])
                hy = hyl.next()
                P.dma("sp", hy[:], C.d_hyoT[s][:, :, t0:t0 + BLK], reads=[C.d_hyoT[s]], writes=[hy])
                xr = xrs.next()
                P.dma("sp", xr[:], C.d_xT[0][s][:, :, t0:t0 + BLK], reads=[C.d_xT[0][s]], writes=[xr])
                o2 = o2s.next()
                P.op("dve", lambda e, oa=oa, o2=o2: e.tensor_tensor(out=o2[:], in0=oa[:, :, 0, :], in1=oa[:, :, 1, :],
                                                                    op=ALU.add), reads=[oa], writes=[o2])
                P.op("dve", lambda e, oa=oa, o2=o2: e.tensor_tensor(out=o2[:], in0=o2[:], in1=oa[:, :, 2, :],
                                                                    op=ALU.add), reads=[oa, o2], writes=[o2])
                o2v = o2[:].rearrange("p j (h d) -> p j h d", d=65)
                rd = rds.next()
                P.op("dve", lambda e, o2v=o2v, rd=rd: e.reciprocal(out=rd[:], in_=o2v[:, :, :, 64]),
                     reads=[o2], writes=[rd])
                ab = abs_.next()
                P.op("dve", lambda e, o2v=o2v, rd=rd, ab=ab: e.tensor_tensor(
                    out=ab[:], in0=o2v[:, :, :, 0:64], in1=rd[:].unsqueeze(3).to_broadcast([128, 4, 4, 64]),
                    op=ALU.mult), reads=[o2, rd], writes=[ab])
                mix = mixs.next()
                for c in range(2):
                    pt = ptb.next()
                    for j in range(4):
                        P.op("pe", lambda e, pt=pt, j=j, c=c, ab=ab: e.transpose(
                            pt[:, j * 128:(j + 1) * 128],
                            ab[:, j, 2 * c:2 * c + 2, :].rearrange("p h d -> p (h d)"), identb[:]),
                            reads=[ab, identb], writes=[pt])
                    P.op("act", lambda e, pt=pt, c=c, mix=mix: e.copy(out=mix[:, c, :], in_=pt[:]),
                         reads=[pt], accs=[mix])
                P.op("act", lambda e, hy=hy, mix=mix: e.copy(out=mix[:, 2:4, :], in_=hy[:]),
                     reads=[hy], accs=[mix])
                x1 = x1s.next()
                for c in range(NK):
                    pp = pps.next()
                    for k in range(4):
                        mm(P, pp, pp[:], w_out, w_out[:, k, c * 128:(c + 1) * 128], mix, mix[:, k, :], k == 0, k == 3)
                    P.op("dve", lambda e, pp=pp, c=c, x1=x1, xr=xr, s=s: e.scalar_tensor_tensor(
                        out=x1[:, c, :], in0=pp[:], scalar=g1[:, c, s:s + 1], in1=xr[:, c, :],
                        op0=ALU.mult, op1=ALU.add), reads=[pp, g1, xr], accs=[x1])
                P.dma("pool", C.d_xT[1][s][:, :, t0:t0 + BLK], x1[:], reads=[x1], accs=[C.d_xT[1][s]])
                h2 = h2s.next()
                norm_mod(P, C, x1, BLK, C.gain[0][1], C.shiftv[0][1], s, h2)
                P.dma("pool", C.d_h2T[0][s][:, :, t0:t0 + BLK], h2[:], reads=[h2], accs=[C.d_h2T[0][s]])


GELU_C = 0.044715
GELU_S = 2.0 * math.sqrt(2.0 / math.pi)


def stage_ffn(P, C, l, xin_idx, xout_idx):
    with P.scope():
        C.wstage_n = 1024
        C.wstage = Rot([P.sbuf([128, 1024], F32, "wst") for _ in range(2)])
        w_up = load_weight_bf16(P, C, C.d_ffn_up[l], NK, 2 * DFF, "w_up")
        w_dn = load_weight_bf16(P, C, C.d_ffn_dn[l], NFC, D, "w_dn")
        cw = P.sbuf([128, NFC, 4], F32, "convw")
        P.dma("sp", cw[:], C.d_ffn_cw[l][:], reads=[C.d_ffn_cw[l]], writes=[cw])
        hhs = Rot([P.sbuf([128, NK, BLK + 2], BF16, "hh") for _ in range(2)])
        actT = P.sbuf([128, NFC, BLK], BF16, "actT")
        xcs = Rot([P.sbuf([128, BLK], F32, "xc") for _ in range(2)])
        xos = Rot([P.sbuf([128, BLK], F32, "xo") for _ in range(2)])
        tmp = {n: Rot([P.sbuf([128, BLK], F32, n) for _ in range(2)]) for n in ("cv", "sq")}
        pas = Rot([P.psum([128, BLK], F32, "pa") for _ in range(2)])
        phs = Rot([P.psum([128, BLK], F32, "ph") for _ in range(1)])
        pgs = Rot([P.psum([128, BLK], F32, "pg") for _ in range(2)])
        pos = Rot([P.psum([128, BLK], F32, "po") for _ in range(2)])
        g2 = C.gatev[l][1]
        for s in range(NS):
            for blk in range(NBLK):
                t0 = blk * BLK
                hh = hhs.next()
                lo = max(t0 - 1, 0)
                hi = min(t0 + BLK + 1, L)
                c_lo = lo - (t0 - 1)
                P.dma("sp", hh[:, :, c_lo:c_lo + (hi - lo)], C.d_h2T[l][s][:, :, lo:hi], reads=[C.d_h2T[l][s]],
                      writes=[hh])
                if blk == 0:
                    P.op("pool", lambda e, hh=hh: e.memset(hh[:, :, 0:1], 0.0), accs=[hh])
                if blk == NBLK - 1:
                    P.op("pool", lambda e, hh=hh: e.memset(hh[:, :, BLK + 1:BLK + 2], 0.0), accs=[hh])
                for c in range(NFC):
                    pa = pas.next()
                    ph = phs.next()
                    pg = pgs.next()
                    for k in range(NK):
                        mm(P, pa, pa[:], w_up, w_up[:, k, c * 128:(c + 1) * 128], hh, hh[:, k, 1:BLK + 1],
                           k == 0, k == NK - 1)
                    for k in range(NK):
                        mm(P, ph, ph[:, 0:2], w_up, w_up[:, k, c * 128:(c + 1) * 128], hh, hh[:, k, 0:BLK + 2:BLK + 1],
                           k == 0, k == NK - 1)
                    for k in range(NK):
                        mm(P, pg, pg[:], w_up, w_up[:, k, DFF + c * 128:DFF + (c + 1) * 128], hh, hh[:, k, 1:BLK + 1],
                           k == 0, k == NK - 1)
                    cv = tmp["cv"].next()
                    P.op("act", lambda e, pa=pa, cv=cv, c=c: e.activation(
                        out=cv[:], in_=pa[:], func=AF.Identity, scale=cw[:, c, 1:2], bias=cw[:, c, 3:4]),
                        reads=[pa, cw], writes=[cv])
                    P.op("dve", lambda e, pa=pa, cv=cv, c=c: e.scalar_tensor_tensor(
                        out=cv[:, 1:BLK], in0=pa[:, 0:BLK - 1], scalar=cw[:, c, 0:1], in1=cv[:, 1:BLK],
                        op0=ALU.mult, op1=ALU.add), reads=[pa, cw, cv], writes=[cv])
                    P.op("dve", lambda e, pa=pa, cv=cv, c=c: e.scalar_tensor_tensor(
                        out=cv[:, 0:BLK - 1], in0=pa[:, 1:BLK], scalar=cw[:, c, 2:3], in1=cv[:, 0:BLK - 1],
                        op0=ALU.mult, op1=ALU.add), reads=[pa, cw, cv], writes=[cv])
                    P.op("dve", lambda e, ph=ph, cv=cv, c=c: e.scalar_tensor_tensor(
                        out=cv[:, 0:1], in0=ph[:, 0:1], scalar=cw[:, c, 0:1], in1=cv[:, 0:1],
                        op0=ALU.mult, op1=ALU.add), reads=[ph, cw, cv], writes=[cv])
                    P.op("dve", lambda e, ph=ph, cv=cv, c=c: e.scalar_tensor_tensor(
                        out=cv[:, BLK - 1:BLK], in0=ph[:, 1:2], scalar=cw[:, c, 2:3], in1=cv[:, BLK - 1:BLK],
                        op0=ALU.mult, op1=ALU.add), reads=[ph, cw, cv], writes=[cv])
                    tt = tmp["sq"].next()
                    P.op("act", lambda e, cv=cv, tt=tt: e.activation(out=tt[:], in_=cv[:], func=AF.Gelu_apprx_tanh),
                         reads=[cv], writes=[tt])
                    P.op("dve", lambda e, tt=tt, pg=pg, c=c: e.tensor_tensor(out=actT[:, c, :], in0=pg[:], in1=tt[:],
                                                                             op=ALU.mult),
                         reads=[tt, pg], accs=[actT])
                for c in range(NK):
                    xc = xcs.next()
                    P.dma("sp", xc[:], C.d_xT[xin_idx][s][:, c, t0:t0 + BLK], reads=[C.d_xT[xin_idx][s]], writes=[xc])
                    po = pos.next()
                    for k in range(NFC):
                        mm(P, po, po[:], w_dn, w_dn[:, k, c * 128:(c + 1) * 128], actT, actT[:, k, :],
                           k == 0, k == NFC - 1)
                    xo = xos.next()
                    P.op("dve", lambda e, po=po, c=c, xo=xo, xc=xc, s=s: e.scalar_tensor_tensor(
                        out=xo[:], in0=po[:], scalar=g2[:, c, s:s + 1], in1=xc[:], op0=ALU.mult, op1=ALU.add),
                        reads=[po, g2, xc], writes=[xo])
                    P.dma("pool", C.d_xT[xout_idx][s][:, c, t0:t0 + BLK], xo[:], reads=[xo],
                          accs=[C.d_xT[xout_idx][s]])


def stage_final(P, C, xin_idx):
    with P.scope():
        alloc_norm_scratch(P, C)
        xrs = Rot([P.sbuf([128, NK, BLK], F32, "xr") for _ in range(2)])
        yTs = Rot([P.sbuf([128, NK, BLK], F32, "yT") for _ in range(2)])
        yts = Rot([P.sbuf([128, 4, D], F32, "ytok") for _ in range(2)])
        pts = Rot([P.psum([128, 1024], F32, "pt2") for _ in range(2)])
        cnt = 0
        for s in range(NS):
            for blk in range(NBLK):
                t0 = blk * BLK
                xr = xrs.next()
                P.dma("sp", xr[:], C.d_xT[xin_idx][s][:, :, t0:t0 + BLK], reads=[C.d_xT[xin_idx][s]], writes=[xr])
                yT = yTs.next()
                norm_mod(P, C, xr, BLK, C.gain_fin, None, s, yT)
                yt = yts.next()
                for j in range(4):
                    pt = pts.next()
                    for c in range(NK):
                        P.op("pe", lambda e, pt=pt, j=j, c=c, yT=yT: e.transpose(
                            pt[:, c * 128:(c + 1) * 128], yT[:, c, j * 128:(j + 1) * 128], C.ident32[:]),
                            reads=[yT, C.ident32], writes=[pt])
                    if cnt % 2 == 0:
                        P.op("act", lambda e, pt=pt, yt=yt, j=j: e.copy(out=yt[:, j, :], in_=pt[:]),
                             reads=[pt], accs=[yt])
                    else:
                        P.op("dve", lambda e, pt=pt, yt=yt, j=j: e.tensor_copy(out=yt[:, j, :], in_=pt[:]),
                             reads=[pt], accs=[yt])
                    cnt += 1
                P.dma("pool", C.d_y[s, t0:t0 + BLK, :].rearrange("(j p) f -> p j f", p=128), yt[:],
                      reads=[yt], accs=[C.d_y])


DECAY_C = -math.exp(-0.5)
R1_STOP = int(os.environ.get("R1_STOP", "9"))
R1_SUB = int(os.environ.get("R1_SUB", "99"))


def stage_rwkv_norm(P, C):
    with P.scope():
        alloc_norm_scratch(P, C)
        xrs = Rot([P.sbuf([128, NK, BLK], F32, "xr") for _ in range(2)])
        hs = Rot([P.sbuf([128, NK, BLK], F32, "h1") for _ in range(2)])
        for s in range(NS):
            for blk in range(NBLK):
                t0 = blk * BLK
                xr = xrs.next()
                P.dma("sp", xr[:], C.d_xT[2][s][:, :, t0:t0 + BLK], reads=[C.d_xT[2][s]], writes=[xr])
                h = hs.next()
                norm_mod(P, C, xr, BLK, C.gain[1][0], C.shiftv[1][0], s, h)
                P.dma("pool", C.d_h1T[s][:, :, t0:t0 + BLK], h[:], reads=[h], accs=[C.d_h1T[s]])


def load_weight_pair(P, C, dram_buf, nk, c_lo, ncols, name, cols, mu_i):
    wb = P.sbuf([128, nk, ncols], BF16, name)
    ws = P.sbuf([128, nk, ncols], BF16, name + "s")
    grp = max(32, (C.wstage_n // nk) // 32 * 32)
    for c0 in range(0, ncols, grp):
        cw = min(grp, ncols - c0)
        st = C.wstage.next()
        stv = st[:, 0:nk * cw].rearrange("p (k c) -> p k c", c=cw)
        P.dma("sp", stv, dram_buf[:, :, c_lo + c0:c_lo + c0 + cw], reads=[dram_buf], writes=[st])
        P.op("act", lambda e, stv=stv, c0=c0, cw=cw: e.copy(out=wb[:, :, c0:c0 + cw], in_=stv), reads=[st], accs=[wb])
        P.op("dve", lambda e, stv=stv, c0=c0, cw=cw: e.tensor_tensor(
            out=ws[:, :, c0:c0 + cw], in0=stv, in1=cols[:, mu_i, :].unsqueeze(2).to_broadcast([128, nk, cw]),
            op=ALU.mult), reads=[st, cols], accs=[ws])
    return wb, ws


def stage_rwkv_proj(P, C):
    with P.scope():
        C.wstage_n = 512
        C.wstage = Rot([P.sbuf([128, 512], F32, "wst") for _ in range(2)])
        cols = P.sbuf([128, 14, NK], F32, "rwcols")
        P.dma("sp", cols[:], C.d_rwcols[:], reads=[C.d_rwcols], writes=[cols])
        w_r = load_weight_pair(P, C, C.d_w_r, NK, 0, D, "w_r", cols, 0)
        w_k = load_weight_pair(P, C, C.d_w_k, NK, 0, D, "w_k", cols, 2)
        w_v = load_weight_pair(P, C, C.d_w_v, NK, 0, D, "w_v", cols, 3)
        w_g1 = load_weight_pair(P, C, C.d_g1, NK, 0, 256, "w_g1", cols, 5)
        w_l1w = load_weight_pair(P, C, C.d_lora1, NK, 0, 128, "w_l1w", cols, 1)
        w_l1a = load_weight_pair(P, C, C.d_lora1, NK, 128, 128, "w_l1a", cols, 4)
        w_g2 = load_weight_bf16(P, C, C.d_g2, 2, D, "w_g2")
        w_l2 = load_weight_bf16(P, C, C.d_lora2, 4, D, "w_l2")
        bones = P.sbuf([128, 128], F32, "bones")
        P.dma("sp", bones[:], C.d_bones[:], reads=[C.d_bones], writes=[bones])
        hls = Rot([P.sbuf([128, NK, BLK + 2], F32, "hl") for _ in range(1)])
        xxh = P.sbuf([128, 4, BLK], F32, "xxh")
        hb = P.sbuf([128, NK, BLK], BF16, "hb")
        xb = P.sbuf([128, NK, BLK], BF16, "xb")
        pps = Rot([P.psum([128, BLK], F32, "pp") for _ in range(3)])
        pvs = Rot([P.psum([128, 1024], F32, "pv") for _ in range(1)])
        pls = Rot([P.psum([128, BLK], F32, "pl") for _ in range(3)])
        ob16 = Rot([P.sbuf([128, BLK], BF16, "ob16") for _ in range(6)])
        of32 = Rot([P.sbuf([128, BLK], F32, "of32") for _ in range(4)])
        kf = Rot([P.sbuf([128, BLK], F32, "kf") for _ in range(3)])
        kkf = Rot([P.sbuf([128, BLK], F32, "kkf") for _ in range(2)])
        vts = Rot([P.sbuf([128, D], BF16, "vtok") for _ in range(2)])
        sgs = Rot([P.sbuf([128, 2, BLK], BF16, "sg") for _ in range(1)])
        lts = Rot([P.sbuf([64, 4, BLK], BF16, "lt") for _ in range(1)])

        def evac(ps_ap, ps_buf, dst_ap, dst_buf):
            P.op("act", lambda e: e.copy(out=dst_ap, in_=ps_ap), reads=[ps_buf], writes=[dst_buf])

        def proj(ps_buf, ps_ap, wpair, csl):
            wb, ws = wpair
            for k in range(NK):
                mm(P, ps_buf, ps_ap, wb, wb[:, k, csl], hb, hb[:, k, :], k == 0, False)
            for k in range(NK):
                mm(P, ps_buf, ps_ap, ws, ws[:, k, csl], xb, xb[:, k, :], False, k == NK - 1)

        for s in range(NS):
            for blk in range(NBLK):
                t0 = blk * BLK
                hl = hls.next()
                lo = max(t0 - 1, 0)
                hi = min(t0 + BLK + 1, L)
                c_lo = lo - (t0 - 1)
                P.dma("sp", hl[:, :, c_lo:c_lo + (hi - lo)], C.d_h1T[s][:, :, lo:hi], reads=[C.d_h1T[s]], writes=[hl])
                if blk == 0:
                    P.op("dve", lambda e, hl=hl: e.memset(hl[:, :, 0:1], 0.0), accs=[hl])
                if blk == NBLK - 1:
                    P.op("dve", lambda e, hl=hl: e.memset(hl[:, :, BLK + 1:BLK + 2], 0.0), accs=[hl])
                P.op("act", lambda e, hl=hl: e.copy(out=hb[:], in_=hl[:, :, 1:BLK + 1]), reads=[hl], writes=[hb])
                for half in range(2):
                    ks = slice(4 * half, 4 * half + 4)
                    P.op("dve", lambda e, hl=hl, ks=ks: e.tensor_tensor(
                        out=xxh[:], in0=hl[:, ks, 0:BLK], in1=hl[:, ks, 2:BLK + 2], op=ALU.add),
                        reads=[hl], writes=[xxh])
                    P.op("dve", lambda e, hl=hl, ks=ks: e.scalar_tensor_tensor(
                        out=xb[:, ks, :], in0=xxh[:], scalar=0.5, in1=hl[:, ks, 1:BLK + 1], op0=ALU.mult,
                        op1=ALU.subtract), reads=[hl, xxh], accs=[xb])
                if R1_STOP <= 1:
                    continue
                for j in range(4):
                    pv = pvs.next()
                    jsl = slice(j * 128, (j + 1) * 128)
                    for half in range(2):
                        hsl = slice(half * 512, (half + 1) * 512)
                        for k in range(NK):
                            mm(P, pv, pv[:, hsl], hb, hb[:, k, jsl], w_v[0], w_v[0][:, k, hsl], k == 0, False)
                        for k in range(NK):
                            mm(P, pv, pv[:, hsl], xb, xb[:, k, jsl], w_v[1], w_v[1][:, k, hsl], False, k == NK - 1)
                    vt = vts.next()
                    evac(pv[:], pv, vt[:], vt)
                    P.dma("pool", C.d_vtok[s][t0 + j * 128:t0 + (j + 1) * 128, :], vt[:], reads=[vt],
                          accs=[C.d_vtok[s]])
                if R1_STOP <= 2:
                    continue
                sg = sgs.next()
                for cc in range(2):
                    pl = pls.next()
                    proj(pl, pl[:], w_g1, slice(cc * 128, (cc + 1) * 128))
                    P.op("act", lambda e, pl=pl, cc=cc, sg=sg: e.activation(
                        out=sg[:, cc, :], in_=pl[:], func=AF.Sigmoid), reads=[pl], accs=[sg])
                lt = lts.next()
                for q in range(4):
                    pl = pls.next()
                    proj(pl, pl[0:64, :], w_l1w if q < 2 else w_l1a, slice((q % 2) * 64, (q % 2) * 64 + 64))
                    if q < 2:
                        P.op("act", lambda e, pl=pl, q=q, lt=lt: e.activation(out=lt[:, q, :], in_=pl[0:64, :],
                                                                              func=AF.Tanh), reads=[pl], accs=[lt])
                    else:
                        P.op("act", lambda e, pl=pl, q=q, lt=lt: e.copy(out=lt[:, q, :], in_=pl[0:64, :]),
                             reads=[pl], accs=[lt])
                if R1_STOP <= 3:
                    continue
                def phase_a(c):
                    csl = slice(c * 128, (c + 1) * 128)
                    pp = pps.next()
                    proj(pp, pp[:], w_r, csl)
                    o = ob16.next()
                    evac(pp[:], pp, o[:], o)
                    P.dma("pool", C.d_rT[s][:, c, t0:t0 + BLK], o[:], reads=[o], accs=[C.d_rT[s]])
                    pp = pps.next()
                    proj(pp, pp[:], w_v, csl)
                    o = ob16.next()
                    evac(pp[:], pp, o[:], o)
                    P.dma("pool", C.d_vT[s][:, c, t0:t0 + BLK], o[:], reads=[o], accs=[C.d_vT[s]])
                    pp = pps.next()
                    mm(P, pp, pp[:], w_g2, w_g2[:, 0, csl], sg, sg[:, 0, :], True, False)
                    mm(P, pp, pp[:], w_g2, w_g2[:, 1, csl], sg, sg[:, 1, :], False, True)
                    o = ob16.next()
                    evac(pp[:], pp, o[:], o)
                    P.dma("pool", C.d_gT[s][:, c, t0:t0 + BLK], o[:], reads=[o], accs=[C.d_gT[s]])
                    pp = pps.next()
                    proj(pp, pp[:], w_k, csl)
                    kk_ = kf.next()
                    P.op("act", lambda e, pp=pp, kk_=kk_: e.copy(out=kk_[:], in_=pp[:]), reads=[pp], writes=[kk_])
                    return kk_

                def phase_b(c, kk_):
                    csl = slice(c * 128, (c + 1) * 128)
                    kq = kkf.next()
                    P.op("dve", lambda e, kk_=kk_, kq=kq, c=c: e.tensor_scalar(
                        out=kq[:], in0=kk_[:], scalar1=cols[:, 10, c:c + 1], scalar2=None, op0=ALU.mult),
                        reads=[kk_, cols], writes=[kq])
                    sq = of32.next()
                    P.op("act", lambda e, kq=kq, sq=sq: e.activation(out=sq[:], in_=kq[:], func=AF.Square),
                         reads=[kq], writes=[sq])
                    pl = pls.next()
                    mm(P, pl, pl[:], bones, bones[:], sq, sq[:], True, True)
                    rn = of32.next()
                    P.op("act", lambda e, pl=pl, rn=rn: e.activation(out=rn[:], in_=pl[:], func=AF.Sqrt),
                         reads=[pl], writes=[rn])
                    P.op("dve", lambda e, rn=rn: e.tensor_scalar_max(out=rn[:], in0=rn[:], scalar1=1e-12),
                         reads=[rn], writes=[rn])
                    P.op("dve", lambda e, rn=rn: e.reciprocal(out=rn[:], in_=rn[:]), reads=[rn], writes=[rn])
                    P.op("dve", lambda e, kq=kq, rn=rn: e.tensor_tensor(out=kq[:], in0=kq[:], in1=rn[:], op=ALU.mult),
                         reads=[kq, rn], writes=[kq])
                    o = ob16.next()
                    P.op("dve", lambda e, kq=kq, o=o: e.tensor_copy(out=o[:], in_=kq[:]), reads=[kq], writes=[o])
                    P.dma("pool", C.d_kkT[s][:, c, t0:t0 + BLK], o[:], reads=[o], accs=[C.d_kkT[s]])
                    for dd in range(2):
                        pl = pls.next()
                        mm(P, pl, pl[:], w_l2, w_l2[0:64, dd, csl], lt, lt[:, dd, :], True, True)
                        lw = of32.next()
                        P.op("act", lambda e, pl=pl, lw=lw, dd=dd, c=c: e.activation(
                            out=lw[:], in_=pl[:], func=AF.Sigmoid, bias=cols[:, 6 + dd, c:c + 1], scale=1.0),
                            reads=[pl, cols], writes=[lw])
                        P.dma("pool", C.d_lwT[dd][s][:, c, t0:t0 + BLK], lw[:], reads=[lw], accs=[C.d_lwT[dd][s]])
                        pl = pls.next()
                        mm(P, pl, pl[:], w_l2, w_l2[0:64, 2 + dd, csl], lt, lt[:, 2 + dd, :], True, True)
                        aa = of32.next()
                        P.op("act", lambda e, pl=pl, aa=aa, dd=dd, c=c: e.activation(
                            out=aa[:], in_=pl[:], func=AF.Sigmoid, bias=cols[:, 8 + dd, c:c + 1], scale=1.0),
                            reads=[pl, cols], writes=[aa])
                        o = ob16.next()
                        P.op("dve", lambda e, kq=kq, aa=aa, o=o: e.tensor_tensor(out=o[:], in0=kq[:], in1=aa[:],
                                                                                 op=ALU.mult),
                             reads=[kq, aa], writes=[o])
                        P.dma("pool", C.d_bT[dd][s][:, c, t0:t0 + BLK], o[:], reads=[o], accs=[C.d_bT[dd][s]])
                        P.op("dve", lambda e, aa=aa, c=c: e.tensor_scalar(
                            out=aa[:], in0=aa[:], scalar1=-1.0, scalar2=cols[:, 11, c:c + 1], op0=ALU.add,
                            op1=ALU.mult), reads=[aa, cols], writes=[aa])
                        o = ob16.next()
                        P.op("dve", lambda e, aa=aa, kk_=kk_, o=o: e.scalar_tensor_tensor(
                            out=o[:], in0=aa[:], scalar=1.0, in1=kk_[:], op0=ALU.add, op1=ALU.mult),
                            reads=[aa, kk_], writes=[o])
                        P.dma("pool", C.d_kdT[dd][s][:, c, t0:t0 + BLK], o[:], reads=[o], accs=[C.d_kdT[dd][s]])

                prev = None
                for c in range(NK):
                    kk_c = phase_a(c)
                    if prev is not None:
                        phase_b(*prev)
                    prev = (c, kk_c)
                phase_b(*prev)


SBLK = 128
NSB = L // SBLK
CPB = SBLK // 64
SC_LIMIT = int(os.environ.get("SC_LIMIT", "999"))
CAST_MOD = int(os.environ.get("CAST_MOD", "4"))


def stage_rwkv_scan(P, C, dd):
    with P.scope():
        NF = 16 * SBLK
        msk = P.sbuf([64, 192], F32, "scmask")
        P.dma("sp", msk[:], C.d_scanmask[:, dd, :], reads=[C.d_scanmask], writes=[msk])
        rmask = P.sbuf([64, NF], F32, "rmask")
        P.dma("sp", rmask[:], C.d_rmask[:, 0:NF], reads=[C.d_rmask], writes=[rmask])
        idb = P.sbuf([64, 64], BF16, "idb")
        P.op("dve", lambda e: e.tensor_copy(out=idb[:], in_=C.ident32[0:64, 0:64]), reads=[C.ident32], writes=[idb])
        Rr = P.sbuf([64, 16, SBLK], BF16, "scR")
        KD = P.sbuf([64, 16, SBLK], BF16, "scKD")
        Bb = P.sbuf([64, 16, SBLK], BF16, "scB")
        KK = P.sbuf([64, 16, SBLK], BF16, "scKK")
        fA = P.sbuf([64, 16, SBLK], F32, "scA")
        fB = P.sbuf([64, 16, SBLK], F32, "scBf")
        fC = P.sbuf([64, 16, SBLK], F32, "scC")
        BS = []
        for _ in range(2):
            b_ = Ctx()
            b_.AR = P.sbuf([64, 16, CPB, 128], BF16, "scAR")
            b_.KT = P.sbuf([64, 16, SBLK], BF16, "scKT")
            b_.BT = P.sbuf([64, 16, SBLK], BF16, "scBT")
            b_.Vt = P.sbuf([64, CPB, D], BF16, "scV")
            b_.Yo = P.sbuf([64, CPB, D], F32, "scY")
            b_.PC = P.sbuf([64, 16, CPB], F32, "scPC")
            BS.append(b_)
        Sf = P.sbuf([64, 16, 64], F32, "scSf")
        Sb = P.sbuf([64, 16, 64], BF16, "scSb")
        MNk = [[P.sbuf([64, 4, 128], BF16, "MNk") for _ in range(4)] for _ in range(2)]
        MNb = [[P.sbuf([64, 4, 128], BF16, "MNb") for _ in range(4)] for _ in range(2)]
        NT0 = [[P.sbuf([64, 4, 64], BF16, "NT0") for _ in range(4)] for _ in range(2)]
        KTt = [[P.sbuf([64, 4, 2, 64], BF16, "KTt") for _ in range(4)] for _ in range(2)]
        Nl = [[[P.sbuf([64, 4, 2, 64], BF16, "Nl") for _ in range(5)] for _ in range(4)] for _ in range(2)]
        Xb = [P.sbuf([64, 4, 64], BF16, "Xb") for _ in range(4)]
        tmpS = [P.sbuf([64, 4, 64], F32, "tmpS") for _ in range(2)]
        psMNk = P.psum([64, 512], F32, "psMNk")
        psMNb = P.psum([64, 512], F32, "psMNb")
        psN = Rot([P.psum([64, 512], F32, "psN") for _ in range(2)])
        psXb = [P.psum([64, 512], F32, "psX") for _ in range(2)]
        psYS = P.psum([64, 512], F32, "psYS")
        psT = P.psum([64, 512], F32, "psT")
        v4 = lambda b, w: b[:, 0:4 * w].rearrange("p (h t) -> p h t", t=w)
        v42 = lambda b: b[:, 0:512].rearrange("p (h a t) -> p h a t", a=2, t=64)
        xview = lambda g: psXb[g // 2][:, (g % 2) * 256:(g % 2) * 256 + 256].rearrange("p (h t) -> p h t", t=64)
        ecnt = [0]

        def cast(src_buf, src_ap, dst_buf, dst_ap):
            ecnt[0] += 1
            if ecnt[0] % CAST_MOD != 0:
                P.op("act", lambda e: e.copy(out=dst_ap, in_=src_ap), reads=[src_buf], writes=[dst_buf])
            else:
                P.op("dve", lambda e: e.tensor_copy(out=dst_ap, in_=src_ap), reads=[src_buf], writes=[dst_buf])

        def prep(s, blk, bs):
            t0 = blk * SBLK
            tsl_all = slice(t0, t0 + SBLK)
            for h2 in range(2):
                prt = slice(h2 * 64, (h2 + 1) * 64)
                P.dma("sp", fA[:, h2::2, :], C.d_lwT[dd][s][prt, :, tsl_all], reads=[C.d_lwT[dd][s]], accs=[fA])
                P.dma("sp", KK[:, h2::2, :], C.d_kkT[s][prt, :, tsl_all], reads=[C.d_kkT[s]], accs=[KK])
                P.dma("sp", Rr[:, h2::2, :], C.d_rT[s][prt, :, tsl_all], reads=[C.d_rT[s]], accs=[Rr])
                P.dma("sp", KD[:, h2::2, :], C.d_kdT[dd][s][prt, :, tsl_all], reads=[C.d_kdT[dd][s]], accs=[KD])
                P.dma("sp", Bb[:, h2::2, :], C.d_bT[dd][s][prt, :, tsl_all], reads=[C.d_bT[dd][s]], accs=[Bb])
            P.dma("sp", bs.Vt[:], C.d_vtok[s][tsl_all, :].rearrange("(c i) f -> i c f", i=64),
                  reads=[C.d_vtok[s]], writes=[bs.Vt])
            yield
            fl = lambda b: b[:].rearrange("p h t -> p (h t)")
            c4 = lambda b: b[:].rearrange("p h (c t) -> p (h c) t", t=64)
            c5 = lambda b: b[:].rearrange("p h (c t) -> p h c t", t=64)
            P.op("dve", lambda e: e.tensor_tensor_scan(out=fl(fB), data0=rmask[:], data1=fl(fA), initial=0.0,
                                                       op0=ALU.mult, op1=ALU.add), reads=[rmask, fA], writes=[fB])
            yield
            if dd == 1:
                P.op("dve", lambda e: e.tensor_tensor(out=fl(fC), in0=fl(fA), in1=fl(fB), op=ALU.subtract),
                     reads=[fA, fB], writes=[fC])
                yield
                P.op("dve", lambda e: e.tensor_tensor(
                    out=c4(fB), in0=c4(fC), in1=c4(fB)[:, :, 63:64].to_broadcast([64, 16 * CPB, 64]), op=ALU.add),
                    reads=[fC, fB], writes=[fB])
                yield
            P.op("dve", lambda e: e.tensor_tensor(out=fl(fA), in0=fl(fB), in1=fl(fA), op=ALU.subtract),
                 reads=[fA, fB], writes=[fA])
            yield
            P.op("act", lambda e: e.activation(out=fl(fA), in_=fl(fA), func=AF.Exp, scale=DECAY_C),
                 reads=[fA], writes=[fA])
            P.op("act", lambda e: e.activation(out=fl(fC), in_=fl(fB), func=AF.Exp, scale=DECAY_C),
                 reads=[fB], writes=[fC])
            yield
            P.op("act", lambda e: e.activation(out=fl(fB), in_=fl(fB), func=AF.Exp, scale=-DECAY_C),
                 reads=[fB], writes=[fB])
            pcol = 63 if dd == 0 else 0
            P.op("act", lambda e: e.copy(out=bs.PC[:], in_=c5(fC)[:, :, :, pcol]), reads=[fC], writes=[bs.PC])
            yield
            P.op("dve", lambda e: e.scalar_tensor_tensor(out=bs.AR[:, :, :, 0:64], in0=c5(KK), scalar=-1.0,
                                                         in1=c5(fA), op0=ALU.mult, op1=ALU.mult),
                 reads=[KK, fA], accs=[bs.AR])
            yield
            P.op("dve", lambda e: e.tensor_tensor(out=bs.AR[:, :, :, 64:128], in0=c5(Rr), in1=c5(fC), op=ALU.mult),
                 reads=[Rr, fC], accs=[bs.AR])
            yield
            P.op("dve", lambda e: e.tensor_tensor(out=bs.KT[:], in0=KD[:], in1=fB[:], op=ALU.mult),
                 reads=[KD, fB], writes=[bs.KT])
            yield
            P.op("dve", lambda e: e.tensor_tensor(out=bs.BT[:], in0=Bb[:], in1=fB[:], op=ALU.mult),
                 reads=[Bb, fB], writes=[bs.BT])

        def s1(bs, c, par):
            tsl = slice(c * 64, (c + 1) * 64)
            AR, KT, BT = bs.AR, bs.KT, bs.BT
            for g in range(4):
                for hi in range(4):
                    h = 4 * g + hi
                    mm(P, psMNk, v4(psMNk, 128)[:, hi, :], KT, KT[:, h, tsl], AR, AR[:, h, c, :], True, True)
                P.op("dve", lambda e, g=g: e.tensor_tensor(
                    out=MNk[par][g][:], in0=v4(psMNk, 128), in1=msk[:, 0:128].unsqueeze(1).to_broadcast([64, 4, 128]),
                    op=ALU.mult), reads=[psMNk, msk], writes=[MNk[par][g]])
                for hi in range(4):
                    h = 4 * g + hi
                    mm(P, psMNb, v4(psMNb, 128)[:, hi, :], BT, BT[:, h, tsl], AR, AR[:, h, c, :], True, True)
                P.op("dve", lambda e, g=g: e.tensor_tensor(
                    out=MNb[par][g][:], in0=v4(psMNb, 128), in1=msk[:, 0:128].unsqueeze(1).to_broadcast([64, 4, 128]),
                    op=ALU.mult), reads=[psMNb, msk], writes=[MNb[par][g]])
                pn = psN.next()
                for hi in range(4):
                    h = 4 * g + hi
                    mm(P, pn, v4(pn, 64)[:, hi, :], AR, AR[:, h, c, 0:64], BT, BT[:, h, tsl], True, True)
                P.op("dve", lambda e, g=g, pn=pn: e.tensor_tensor(
                    out=NT0[par][g][:], in0=v4(pn, 64), in1=msk[:, 128:192].unsqueeze(1).to_broadcast([64, 4, 64]),
                    op=ALU.mult), reads=[pn, msk], writes=[NT0[par][g]])
                for hi in range(4):
                    h = 4 * g + hi
                    mm(P, psT, v42(psT)[:, hi, 0, :], KT, KT[:, h, tsl], idb, idb[:], True, True)
                    mm(P, psT, v42(psT)[:, hi, 1, :], BT, BT[:, h, tsl], idb, idb[:], True, True)
                P.op("act", lambda e, g=g: e.copy(out=KTt[par][g][:], in_=v42(psT)), reads=[psT],
                     writes=[KTt[par][g]])

        def level_ops(par, g, j):
            if j == 0:
                return (MNb[par][g], (lambda hi: MNb[par][g][:, hi, 0:64]), NT0[par][g], (lambda hi: NT0[par][g][:, hi, :]))
            t = Nl[par][g][j - 1]
            return (t, (lambda hi: t[:, hi, 0, :]), t, (lambda hi: t[:, hi, 1, :]))

        def square(par, j):
            for g in range(4):
                nbuf, nap, tbuf, tap = level_ops(par, g, j)
                pn = psN.next()
                for hi in range(4):
                    mm(P, pn, v42(pn)[:, hi, 0, :], tbuf, tap(hi), nbuf, nap(hi), True, True)
                    if j < 4:
                        mm(P, pn, v42(pn)[:, hi, 1, :], nbuf, nap(hi), tbuf, tap(hi), True, True)
                dst = Nl[par][g][j]
                if j < 4:
                    cast(pn, v42(pn), dst, dst[:])
                else:
                    cast(pn, v42(pn)[:, :, 0, :], dst, dst[:, :, 0, :])

        def g_step(bs, c, par):
            AR, Vt = bs.AR, bs.Vt
            for g in range(4):
                xv = xview(g)
                pb_ = psXb[g // 2]
                for hi in range(4):
                    h = 4 * g + hi
                    first = (g % 2 == 0 and hi == 0)
                    P.op("pe", lambda e, xv=xv, hi=hi, h=h, first=first: e.matmul(
                        xv[:, hi, :], lhsT=AR[:, h, c, 0:64], rhs=Sb[:, h, :], start=first, stop=False,
                        skip_group_check=True), reads=[AR, Sb], writes=[pb_])
                    P.op("pe", lambda e, xv=xv, hi=hi, h=h, g=g: e.matmul(
                        xv[:, hi, :], lhsT=MNk[par][g][:, hi, 0:64], rhs=Vt[:, c, h * 64:(h + 1) * 64], start=False,
                        stop=False, skip_group_check=True), reads=[MNk[par][g], Vt], writes=[pb_])
                cast(pb_, xv, Xb[g], Xb[g][:])

        def apply(par, j):
            for g in range(4):
                xv = xview(g)
                pb_ = psXb[g // 2]
                nbuf, nap, _, _ = level_ops(par, g, j)
                for hi in range(4):
                    P.op("pe", lambda e, xv=xv, hi=hi, g=g, nap=nap: e.matmul(
                        xv[:, hi, :], lhsT=nap(hi), rhs=Xb[g][:, hi, :], start=False, stop=(j == 5),
                        skip_group_check=True), reads=[nbuf, Xb[g]], writes=[pb_])
                cast(pb_, xv, Xb[g], Xb[g][:])

        def ys_step(bs, c, par):
            AR, Vt, Yo = bs.AR, bs.Vt, bs.Yo
            for g in range(4):
                pys = v42(psYS)
                for hi in range(4):
                    h = 4 * g + hi
                    vv = Vt[:, c, h * 64:(h + 1) * 64]
                    mm(P, psYS, pys[:, hi, 0, :], AR, AR[:, h, c, 64:128], Sb, Sb[:, h, :], True, False)
                    mm(P, psYS, pys[:, hi, 0, :], MNk[par][g], MNk[par][g][:, hi, 64:128], Vt, vv, False, False)
                    mm(P, psYS, pys[:, hi, 0, :], MNb[par][g], MNb[par][g][:, hi, 64:128], Xb[g], Xb[g][:, hi, :],
                       False, True)
                    mm(P, psYS, pys[:, hi, 1, :], KTt[par][g], KTt[par][g][:, hi, 0, :], Vt, vv, True, False)
                    mm(P, psYS, pys[:, hi, 1, :], KTt[par][g], KTt[par][g][:, hi, 1, :], Xb[g], Xb[g][:, hi, :],
                       False, True)
                ts_ = tmpS[g % 2]
                P.op("dve", lambda e, g=g, pys=pys, ts_=ts_: e.tensor_tensor(
                    out=ts_[:], in0=pys[:, :, 1, :], in1=Sf[:, 4 * g:4 * g + 4, :], op=ALU.add),
                    reads=[psYS, Sf], writes=[ts_])
                pcb = bs.PC[:, 4 * g:4 * g + 4, c:c + 1].to_broadcast([64, 4, 64])
                P.op("dve", lambda e, g=g, ts_=ts_, pcb=pcb: e.tensor_tensor(
                    out=Sb[:, 4 * g:4 * g + 4, :], in0=ts_[:], in1=pcb, op=ALU.mult),
                    reads=[ts_, bs.PC], accs=[Sb])
                P.op("dve", lambda e, g=g, pys=pys, c=c: e.tensor_copy(
                    out=Yo[:, c, g * 256:(g + 1) * 256].rearrange("p (h v) -> p h v", v=64), in_=pys[:, :, 0, :]),
                    reads=[psYS], accs=[Yo])
                P.op("dve", lambda e, g=g, ts_=ts_, pcb=pcb: e.tensor_tensor(
                    out=Sf[:, 4 * g:4 * g + 4, :], in0=ts_[:], in1=pcb, op=ALU.mult),
                    reads=[ts_, bs.PC], accs=[Sf])

        for s in range(NS):
            P.op("dve", lambda e: e.memset(Sf[:], 0.0), writes=[Sf])
            P.op("dve", lambda e: e.memset(Sb[:], 0.0), writes=[Sb])
            blocks = list(range(NSB) if dd == 0 else range(NSB - 1, -1, -1))
            seq = []
            for bp, blk in enumerate(blocks):
                for c in (range(CPB) if dd == 0 else range(CPB - 1, -1, -1)):
                    seq.append((bp, blk, c))
            seq = seq[:SC_LIMIT]
            for _ in prep(s, seq[0][1], BS[0]):
                pass
            s1(BS[0], seq[0][2], 0)
            for j in range(5):
                square(0, j)
            pending = None
            for n, (bp, blk, c) in enumerate(seq):
                par = n % 2
                bs = BS[bp % 2]
                nxt = seq[n + 1] if n + 1 < len(seq) else None
                first_of_block = (n == 0) or (seq[n - 1][0] != bp)
                if first_of_block and CPB > 1:
                    later = [q for q in seq[n + 1:] if q[0] == bp + 1]
                    if later:
                        pending = prep(s, later[0][1], BS[(bp + 1) % 2])
                if nxt is not None:
                    nbs = BS[nxt[0] % 2]
                    if nxt[0] != bp:
                        if pending is not None:
                            for _ in pending:
                                pass
                            pending = None
                        elif CPB == 1:
                            for _ in prep(s, nxt[1], nbs):
                                pass
                    s1(nbs, nxt[2], 1 - par)
                g_step(bs, c, par)
                for j in range(6):
                    apply(par, j)
                    if nxt is not None and j < 5:
                        square(1 - par, j)
                    if pending is not None:
                        for _ in range(2):
                            try:
                                next(pending)
                            except StopIteration:
                                pending = None
                                break
                ys_step(bs, c, par)
                last_of_block = (nxt is None) or (nxt[0] != bp)
                if last_of_block:
                    t0 = blk * SBLK
                    P.dma("pool", C.d_ytok[dd][s][t0:t0 + SBLK, :].rearrange("(c i) f -> i c f", i=64), bs.Yo[:],
                          reads=[bs.Yo], accs=[C.d_ytok[dd][s]])


GN_EPS = 64e-5


def stage_rwkv_post(P, C):
    with P.scope():
        C.wstage_n = 1024
        C.wstage = Rot([P.sbuf([128, 1024], F32, "wst") for _ in range(2)])
        w_o = load_weight_bf16(P, C, C.d_w_o, NK, D, "w_o")
        cols = P.sbuf([128, 14, NK], F32, "rwcols")
        P.dma("sp", cols[:], C.d_rwcols[:], reads=[C.d_rwcols], writes=[cols])
        lnb = P.sbuf([128, NK], F32, "lnb")
        P.dma("sp", lnb[:], C.d_lnb[:], reads=[C.d_lnb], writes=[lnb])
        bones = P.sbuf([128, 128], F32, "bones")
        P.dma("sp", bones[:], C.d_bones[:], reads=[C.d_bones], writes=[bones])
        alloc_norm_scratch(P, C)
        yf = P.sbuf([128, 4, D], F32, "yf")
        yb = P.sbuf([128, 4, D], F32, "yb")
        st = P.sbuf([128, 6, 64], F32, "gnst")
        ynT = P.sbuf([128, NK, BLK], F32, "ynT")
        rB = P.sbuf([128, NK, BLK], BF16, "rB")
        k0B = P.sbuf([128, NK, BLK], BF16, "k0B")
        k1B = P.sbuf([128, NK, BLK], BF16, "k1B")
        vB = P.sbuf([128, NK, BLK], BF16, "vB")
        gB = P.sbuf([128, NK, BLK], BF16, "gB")
        xr = P.sbuf([128, NK, BLK], F32, "xr")
        outT = P.sbuf([128, NK, BLK], BF16, "outT")
        x3 = P.sbuf([128, NK, BLK], F32, "x3T")
        h2 = P.sbuf([128, NK, BLK], BF16, "h2T")
        kms = Rot([P.sbuf([128, BLK], F32, "km") for _ in range(2)])
        qs = Rot([P.sbuf([128, BLK], F32, "qq") for _ in range(2)])
        bns = Rot([P.sbuf([128, BLK], F32, "bn") for _ in range(2)])
        pts = Rot([P.psum([128, BLK], F32, "pt") for _ in range(2)])
        pbs = Rot([P.psum([128, BLK], F32, "pb") for _ in range(2)])
        pps = Rot([P.psum([128, BLK], F32, "pp") for _ in range(2)])
        g1 = C.gatev[1][0]
        for s in range(NS):
            for blk in range(NBLK):
                t0 = blk * BLK
                tsl = slice(t0, t0 + BLK)
                P.dma("sp", yf[:], C.d_ytok[0][s][tsl, :].rearrange("(j p) f -> p j f", p=128),
                      reads=[C.d_ytok[0][s]], writes=[yf])
                P.dma("sp", yb[:], C.d_ytok[1][s][tsl, :].rearrange("(j p) f -> p j f", p=128),
                      reads=[C.d_ytok[1][s]], writes=[yb])
                P.dma("sp", rB[:], C.d_rT[s][:, :, tsl], reads=[C.d_rT[s]], writes=[rB])
                P.dma("sp", k0B[:], C.d_kdT[0][s][:, :, tsl], reads=[C.d_kdT[0][s]], writes=[k0B])
                P.dma("sp", k1B[:], C.d_kdT[1][s][:, :, tsl], reads=[C.d_kdT[1][s]], writes=[k1B])
                P.dma("sp", vB[:], C.d_vT[s][:, :, tsl], reads=[C.d_vT[s]], writes=[vB])
                P.dma("sp", gB[:], C.d_gT[s][:, :, tsl], reads=[C.d_gT[s]], writes=[gB])
                P.dma("sp", xr[:], C.d_xT[2][s][:, :, tsl], reads=[C.d_xT[2][s]], writes=[xr])
                yv = yf[:].rearrange("p j (h v) -> p (j h) v", v=64)
                ybv = yb[:].rearrange("p j (h v) -> p (j h) v", v=64)
                P.op("dve", lambda e: e.tensor_tensor(out=yf[:], in0=yf[:], in1=yb[:], op=ALU.add),
                     reads=[yf, yb], writes=[yf])
                P.op("dve", lambda e: e.tensor_reduce(out=st[:, 0, :], in_=yv, axis=mybir.AxisListType.X, op=ALU.add),
                     reads=[yf], writes=[st])
                P.op("act", lambda e: e.activation(out=yb[:], in_=yf[:], func=AF.Square), reads=[yf], writes=[yb])
                P.op("dve", lambda e: e.tensor_reduce(out=st[:, 1, :], in_=ybv, axis=mybir.AxisListType.X, op=ALU.add),
                     reads=[yb], writes=[st])
                P.op("dve", lambda e: e.tensor_scalar(out=st[:, 2, :], in0=st[:, 0, :], scalar1=1.0 / 64, scalar2=None,
                                                      op0=ALU.mult), reads=[st], writes=[st])
                P.op("dve", lambda e: e.tensor_tensor(out=st[:, 3, :], in0=st[:, 2, :], in1=st[:, 2, :], op=ALU.mult),
                     reads=[st], writes=[st])
                P.op("dve", lambda e: e.scalar_tensor_tensor(out=st[:, 4, :], in0=st[:, 1, :], scalar=1.0 / 64,
                                                             in1=st[:, 3, :], op0=ALU.mult, op1=ALU.subtract),
                     reads=[st], writes=[st])
                P.op("dve", lambda e: e.tensor_scalar_add(out=st[:, 4, :], in0=st[:, 4, :], scalar1=GN_EPS),
                     reads=[st], writes=[st])
                P.op("act", lambda e: e.activation(out=st[:, 5, :], in_=st[:, 4, :], func=AF.Sqrt),
                     reads=[st], writes=[st])
                P.op("dve", lambda e: e.reciprocal(out=st[:, 5, :], in_=st[:, 5, :]), reads=[st], writes=[st])
                P.op("dve", lambda e: e.tensor_tensor(out=yv, in0=yv, in1=st[:, 2, :].unsqueeze(2).to_broadcast(
                    [128, 64, 64]), op=ALU.subtract), reads=[yf, st], writes=[yf])
                P.op("dve", lambda e: e.tensor_tensor(out=yv, in0=yv, in1=st[:, 5, :].unsqueeze(2).to_broadcast(
                    [128, 64, 64]), op=ALU.mult), reads=[yf, st], writes=[yf])
                for k in range(NK):
                    pt = pts.next()
                    for j in range(4):
                        P.op("pe", lambda e, pt=pt, j=j, k=k: e.transpose(
                            pt[:, j * 128:(j + 1) * 128], yf[:, j, k * 128:(k + 1) * 128], C.ident32[:]),
                            reads=[yf, C.ident32], writes=[pt])
                    P.op("act", lambda e, pt=pt, k=k: e.activation(
                        out=ynT[:, k, :], in_=pt[:], func=AF.Identity, scale=cols[:, 13, k:k + 1],
                        bias=lnb[:, k:k + 1]), reads=[pt, cols, lnb], accs=[ynT])
                    km = kms.next()
                    P.op("dve", lambda e, km=km, k=k: e.tensor_tensor(out=km[:], in0=k0B[:, k, :], in1=k1B[:, k, :],
                                                                       op=ALU.add), reads=[k0B, k1B], writes=[km])
                    q = qs.next()
                    P.op("dve", lambda e, km=km, q=q, k=k: e.scalar_tensor_tensor(
                        out=q[:], in0=rB[:, k, :], scalar=cols[:, 12, k:k + 1], in1=km[:], op0=ALU.mult, op1=ALU.mult),
                        reads=[rB, cols, km], writes=[q])
                    pb = pbs.next()
                    mm(P, pb, pb[:], bones, bones[:], q, q[:], True, True)
                    bn = bns.next()
                    P.op("dve", lambda e, pb=pb, bn=bn, k=k: e.scalar_tensor_tensor(
                        out=bn[:], in0=pb[:], scalar=0.5, in1=vB[:, k, :], op0=ALU.mult, op1=ALU.mult),
                        reads=[pb, vB], writes=[bn])
                    P.op("dve", lambda e, bn=bn, k=k: e.tensor_tensor(out=bn[:], in0=bn[:], in1=ynT[:, k, :],
                                                                       op=ALU.add), reads=[bn, ynT], writes=[bn])
                    P.op("dve", lambda e, bn=bn, k=k: e.tensor_tensor(out=outT[:, k, :], in0=bn[:], in1=gB[:, k, :],
                                                                       op=ALU.mult), reads=[bn, gB], accs=[outT])
                for c in range(NK):
                    pp = pps.next()
                    for k in range(NK):
                        mm(P, pp, pp[:], w_o, w_o[:, k, c * 128:(c + 1) * 128], outT, outT[:, k, :], k == 0, k == NK - 1)
                    P.op("dve", lambda e, pp=pp, c=c, s=s: e.scalar_tensor_tensor(
                        out=x3[:, c, :], in0=pp[:], scalar=g1[:, c, s:s + 1], in1=xr[:, c, :],
                        op0=ALU.mult, op1=ALU.add), reads=[pp, g1, xr], accs=[x3])
                P.dma("pool", C.d_xT[3][s][:, :, tsl], x3[:], reads=[x3], accs=[C.d_xT[3][s]])
                norm_mod(P, C, x3, BLK, C.gain[1][1], C.shiftv[1][1], s, h2)
                P.dma("pool", C.d_h2T[1][s][:, :, tsl], h2[:], reads=[h2], accs=[C.d_h2T[1][s]])

class _ModView:
    def __init__(self, buf, j):
        self.buf = buf
        self.j = j

    @property
    def w(self):
        return self.buf.w

    @property
    def r(self):
        return self.buf.r

    @property
    def a(self):
        return self.buf.a

    def __getitem__(self, idx):
        p, k, s = idx
        return self.buf[p, self.j * 8 + k, s]


def mod_shift(C, l, which):
    return C.shiftv[l][which]


def host_consts():
    c = {}
    c["ident32"] = np.eye(128, dtype=np.float32)
    c["ones32"] = np.ones((128, 128), dtype=np.float32)
    bo = np.zeros((128, 128), np.float32)
    bo[0:64, 0:64] = 1.0
    bo[64:128, 64:128] = 1.0
    c["c_bones"] = bo
    ii = np.arange(64)[:, None]
    tt = np.arange(64)[None, :]
    sm = np.zeros((64, 2, 192), np.float32)
    sm[:, 0, 0:64] = (ii < tt)
    sm[:, 0, 64:128] = (ii <= tt)
    sm[:, 0, 128:192] = (tt < ii)
    sm[:, 1, 0:64] = (ii > tt)
    sm[:, 1, 64:128] = (ii >= tt)
    sm[:, 1, 128:192] = (tt > ii)
    c["c_scanmask"] = sm
    rm = np.ones((64, 16 * 256), np.float32)
    rm[:, ::64] = 0.0
    c["c_rmask"] = rm
    slopes = np.exp2(-8.0 * (np.arange(12, dtype=np.float32) + 1.0) / 12).astype(np.float32).reshape(3, 4)
    tab = np.zeros((128, 9, 4, 128), np.float32)
    kk = np.arange(128)[:, None]
    qq = np.arange(128)[None, :]
    NEG = -30000.0
    for g, d in enumerate((1, 4, 16)):
        for h in range(4):
            sl = slopes[g, h] * d
            lo = np.where(kk >= qq, -sl * np.abs(kk - qq - 64), NEG)
            up = np.where(kk <= qq, -sl * np.abs(kk - qq + 64), NEG)
            tab[:, 3 * g + 0, h, :] = lo
            tab[:, 3 * g + 1, h, :] = up
            tab[0:64, 3 * g + 2, h, :] = lo[64:128]
    c["abias"] = tab.astype(np.float32)
    n1 = np.arange(64, dtype=np.float64)[:, None]
    k1 = np.arange(128, dtype=np.float64)[None, :]
    ang = 2 * np.pi * n1 * k1 / 128.0
    c["c_w128"] = np.concatenate([np.cos(ang), -np.sin(ang)], 1).astype(np.float32)
    n2 = np.tile(np.arange(64, dtype=np.float64), 2)[:, None]
    ang = 2 * np.pi * n2 * k1 / 8192.0
    c["c_tw"] = np.stack([np.cos(ang), -np.sin(ang)], 1).astype(np.float32)
    a64 = 2 * np.pi * np.outer(np.arange(64), np.arange(64)) / 64.0
    def bdiag(m):
        z = np.zeros((128, 128))
        z[0:64, 0:64] = m
        z[64:128, 64:128] = m
        return z
    bdr, bdi = bdiag(np.cos(a64)), bdiag(-np.sin(a64))
    c["c_bd"] = np.stack([bdr, bdi, -bdi], 1).astype(np.float32)
    cr, ci = bdiag(np.cos(a64)), bdiag(np.sin(a64))
    c["c_bdc"] = np.stack([np.concatenate([cr, ci], 1), np.concatenate([-ci, cr], 1)], 1).astype(np.float32)
    kk1 = np.arange(128, dtype=np.float64)[:, None]
    nn2 = np.tile(np.arange(64, dtype=np.float64), 2)[None, :]
    ang = 2 * np.pi * kk1 * nn2 / 8192.0
    c["c_twi"] = np.stack([np.cos(ang), np.sin(ang)], 1).astype(np.float32)
    ang = 2 * np.pi * np.outer(np.arange(128), np.arange(64)) / 128.0
    c["c_vinv"] = np.stack([np.cos(ang) / 8192.0, -np.sin(ang) / 8192.0], 1).astype(np.float32)
    t = np.linspace(0.0, 1.0, L, dtype=np.float32)[:, None]
    w = (2.0 * np.float32(math.pi) * np.arange(L, dtype=np.float32)[:, None] / np.float32(L)).astype(np.float32)
    f = np.linspace(1e-4, 15, 16, dtype=np.float32)[None, :]
    z = (f * w).astype(np.float32)
    pos = np.concatenate([t, np.cos(z), -np.sin(z)], -1).astype(np.float32)
    c["c_posT"] = np.ascontiguousarray(pos.T)
    min_decay = math.log(1e-2) / 1.5
    max_decay = math.log(1e-2) / 0.3
    deltas = np.linspace(min_decay, max_decay, 256, dtype=np.float32)[None, :]
    win = np.exp(-t * np.abs(deltas)).astype(np.float32)
    c["c_winT"] = np.ascontiguousarray(win.T.reshape(2, 128, L).transpose(1, 0, 2))
    return c


def build_program(stages, dbg=()):
    nc = bass.Bass("TRN2", target_bir_lowering=False)
    P = Prog(nc)
    C = Ctx()
    C.dbg = {}
    din = lambda name, shape, dt=F32: P.dram(name, shape, dt, kind="ExternalInput")
    C.d_x = din("x_in", [NS, L, D])
    C.d_cT = din("cT", [128, NK, NS])
    C.d_adaw = [din("ada_w%d" % l, [128, NK, 6 * D]) for l in range(2)]
    C.d_adab = din("ada_b", [128, 2, 48])
    C.d_normw = din("normw", [128, 5, NK])
    C.d_w_in = din("w_in", [128, NK, 3072])
    C.d_abias = din("abias", [128, 9, 4, 128])
    C.d_w128 = din("c_w128", [64, 256])
    C.d_tw = din("c_tw", [128, 2, 128])
    C.d_bd = din("c_bd", [128, 3, 128])
    C.d_bdc = din("c_bdc", [128, 2, 256])
    C.d_twi = din("c_twi", [128, 2, 128])
    C.d_vinv = din("c_vinv", [128, 2, 64])
    C.d_posT = din("c_posT", [33, L])
    C.d_winT = din("c_winT", [128, 2, L])
    C.d_hcol = din("hcol", [64, 4])
    C.d_fw1 = din("fw1", [33, 64])
    C.d_fw23 = din("fw23", [64, 2, 64])
    C.d_fw4 = din("fw4", [64, 512])
    C.d_fbias = din("fbias", [128, 2])
    C.d_shortw = din("shortw", [128, 6, 4])
    C.d_w_out = din("w_out", [128, 4, D])
    C.d_ffn_up = [din("ffn_up%d" % l, [128, NK, 2 * DFF]) for l in range(2)]
    C.d_ffn_dn = [din("ffn_dn%d" % l, [128, NFC, D]) for l in range(2)]
    C.d_ffn_cw = [din("ffn_cw%d" % l, [128, NFC, 4]) for l in range(2)]
    C.d_w_r = din("w_r", [128, NK, D])
    C.d_w_k = din("w_k", [128, NK, D])
    C.d_w_v = din("w_v", [128, NK, D])
    C.d_w_o = din("w_o", [128, NK, D])
    C.d_g1 = din("rw_g1", [128, NK, 256])
    C.d_g2 = din("rw_g2", [128, 2, D])
    C.d_lora1 = din("rw_lora1", [128, NK, 256])
    C.d_lora2 = din("rw_lora2", [128, 4, D])
    C.d_rwcols = din("rw_cols", [128, 14, NK])
    C.d_bones = din("c_bones", [128, 128])
    C.d_lnb = din("rw_lnb", [128, NK])
    C.d_scanmask = din("c_scanmask", [64, 2, 192])
    C.d_rmask = din("c_rmask", [64, 16 * 256])
    C.d_ident32 = din("ident32", [128, 128])
    C.d_ones32 = din("ones32", [128, 128])
    C.d_y = P.dram("y_out", [NS, L, D], F32, kind="ExternalOutput")
    outs = []

    def scratch(name, shape, dt):
        if name in dbg:
            b = P.dram(name, shape, dt, kind="ExternalOutput")
            outs.append(b)
            return b
        return P.dram(name, shape, dt)

    C.d_xT = [[scratch("xT%d_%d" % (i, s), [128, NK, L], F32) for s in range(NS)] for i in range(5)]
    C.d_qkT = [scratch("qkT_%d" % s, [128, 12, L], BF16) for s in range(NS)]
    C.d_vaug = [scratch("vaug_%d" % s, [L, 12, 65], BF16) for s in range(NS)]
    C.d_hyT = [scratch("hyT_%d" % s, [128, 6, L], F32) for s in range(NS)]
    C.d_oacc = [scratch("oacc_%d" % s, [3, L, 4, 65], F32) for s in range(NS)]
    C.d_h2T = [[scratch("h2T%d_%d" % (l, s), [128, NK, L], BF16) for s in range(NS)] for l in range(2)]
    C.d_h1T = [scratch("h1T_%d" % s, [128, NK, L], F32) for s in range(NS)]
    C.d_vtok = [scratch("vtok_%d" % s, [L, D], BF16) for s in range(NS)]
    C.d_rT = [scratch("rT_%d" % s, [128, NK, L], BF16) for s in range(NS)]
    C.d_vT = [scratch("vT_%d" % s, [128, NK, L], BF16) for s in range(NS)]
    C.d_gT = [scratch("gT_%d" % s, [128, NK, L], BF16) for s in range(NS)]
    C.d_kkT = [scratch("kkT_%d" % s, [128, NK, L], BF16) for s in range(NS)]
    C.d_lwT = [[scratch("lwT%d_%d" % (dd, s), [128, NK, L], F32) for s in range(NS)] for dd in range(2)]
    C.d_bT = [[scratch("bT%d_%d" % (dd, s), [128, NK, L], BF16) for s in range(NS)] for dd in range(2)]
    C.d_kdT = [[scratch("kdT%d_%d" % (dd, s), [128, NK, L], BF16) for s in range(NS)] for dd in range(2)]
    C.d_ytok = [[scratch("ytok%d_%d" % (dd, s), [L, D], F32) for s in range(NS)] for dd in range(2)]
    C.d_filtT = scratch("filtT", [128, 4, L], F32)
    C.d_F = scratch("Fspec", [128, 128, 2, 128], F32)
    C.d_zT = [scratch("zT_%d" % s, [128, 2, L], F32) for s in range(NS)]
    C.d_x0T = [scratch("x0T_%d" % s, [128, 2, L], F32) for s in range(NS)]
    C.d_hyoT = [scratch("hyoT_%d" % s, [128, 2, L], F32) for s in range(NS)]
    C.ident32 = P.sbuf([128, 128], F32, "ident32")
    C.ones32 = P.sbuf([128, 128], F32, "ones32")
    C.eps_col = P.sbuf([128, 1], F32, "eps")
    P.dma("sp", C.ident32[:], C.d_ident32[:], reads=[C.d_ident32], writes=[C.ident32])
    P.dma("sp", C.ones32[:], C.d_ones32[:], reads=[C.d_ones32], writes=[C.ones32])
    P.op("pool", lambda e: e.memset(C.eps_col[:], RMS_EPS), writes=[C.eps_col])
    C.modT = [P.sbuf([128, 48, NS], F32, "modT%d" % l) for l in range(2)]
    C.gain = [[P.sbuf([128, NK, NS], F32, "gain%d%d" % (l, w)) for w in range(2)] for l in range(2)]
    C.gain_fin = P.sbuf([128, NK, NS], F32, "gainf")
    C.shiftv = [[_ModView(C.modT[l], 0), _ModView(C.modT[l], 3)] for l in range(2)]
    C.gatev = [[_ModView(C.modT[l], 2), _ModView(C.modT[l], 5)] for l in range(2)]

    def dbg_out(name, src_buf, src_ap, shape, dt=F32):
        if name in dbg:
            o = P.dram("dbg_" + name, shape, dt, kind="ExternalOutput")
            P.dma("sp", o[:], src_ap, reads=[src_buf], writes=[o])
            outs.append(o)

    C.dbg_out = dbg_out
    if "adaln" in stages:
        stage_adaln(P, C)
        dbg_out("modT0", C.modT[0], C.modT[0][:], [128, 48, NS])
        dbg_out("gain00", C.gain[0][0], C.gain[0][0][:], [128, NK, NS])
    if "l0_inproj" in stages:
        stage_l0_inproj(P, C)
    if "attn" in stages:
        stage_attention(P, C)
    if "hyfilt" in stages:
        stage_hyena_filter(P, C)
    if "hyena" in stages:
        stage_hyena(P, C)
    if "l0_outproj" in stages:
        stage_l0_outproj(P, C)
    if "ffn0" in stages:
        stage_ffn(P, C, 0, 1, 2)
    if "rw_norm" in stages:
        stage_rwkv_norm(P, C)
    if "rw_proj" in stages:
        stage_rwkv_proj(P, C)
    if "rw_scan0" in stages:
        stage_rwkv_scan(P, C, 0)
    if "rw_scan1" in stages:
        stage_rwkv_scan(P, C, 1)
    if "rw_post" in stages:
        stage_rwkv_post(P, C)
    if "ffn1" in stages:
        stage_ffn(P, C, 1, 3, 4)
    if "final" in stages:
        stage_final(P, C, FINAL_SRC)
    outs.append(C.d_y)
    final = [o for o in outs if o is not None]
    C.final_bufs = final
    return nc, P, C


def finish_program(P, C, extra=()):
    bufs = list(C.final_bufs) + list(extra)
    P.wait_all("sp", bufs)
    P.close()


def arrange_w(w, nk=None):
    K, N = w.shape
    nk = K // 128
    return np.ascontiguousarray(w.reshape(nk, 128, N).transpose(1, 0, 2))


def col_layout(v):
    return np.ascontiguousarray(v.reshape(-1, 128).T)


def prep_shared(inp):
    m = {}
    m["ada_w0"] = arrange_w(inp["l0_ada_w"])
    m["ada_w1"] = arrange_w(inp["l1_ada_w"])
    m["ada_b"] = np.ascontiguousarray(np.stack([col_layout(inp["l0_ada_b"]), col_layout(inp["l1_ada_b"])], 1))
    m["normw"] = np.ascontiguousarray(np.stack([col_layout(inp[k]) for k in
                                               ("l0_norm1", "l0_norm2", "l1_norm1", "l1_norm2", "final_norm")], 1))
    m["w_in"] = arrange_w(inp["l0_w_in"])
    m["w_out"] = arrange_w(inp["l0_w_out"])
    ffn = [(inp["l0_ffn_up"], inp["l0_ffn_down"], inp["l0_ffn_conv_w"], inp["l0_ffn_conv_b"]),
           (inp["l1_ffn_up"], inp["l1_ffn_down"], inp["l1_ffn_conv_w"], inp["l1_ffn_conv_b"])]
    for l in range(2):
        up, dn, cw_, cb_ = ffn[l]
        m["ffn_up%d" % l] = arrange_w(up)
        m["ffn_dn%d" % l] = arrange_w(dn)
        cwb = np.concatenate([cw_, cb_[None, :]], 0)
        m["ffn_cw%d" % l] = np.ascontiguousarray(cwb.T.reshape(NFC, 128, 4).transpose(1, 0, 2))
    for nm in ("w_r", "w_k", "w_v", "w_o"):
        m[nm] = arrange_w(inp["l1_" + nm])
    g1p = np.zeros((D, 256), np.float32)
    g1p[:, :160] = inp["l1_g1"]
    m["rw_g1"] = arrange_w(g1p)
    g2 = np.zeros((256, D), np.float32)
    g2[:160] = inp["l1_g2"]
    m["rw_g2"] = arrange_w(g2)
    m["rw_lora1"] = arrange_w(np.concatenate([inp["l1_w1"][0], inp["l1_w1"][1], inp["l1_a1"][0], inp["l1_a1"][1]], 1))
    l2 = np.zeros((128, 4, D), np.float32)
    l2[:64, 0] = inp["l1_w2"][0]
    l2[:64, 1] = inp["l1_w2"][1]
    l2[:64, 2] = inp["l1_a2"][0]
    l2[:64, 3] = inp["l1_a2"][1]
    m["rw_lora2"] = l2
    cl = [col_layout(inp["l1_mu"][i]) for i in range(6)]
    cl += [col_layout(inp["l1_w0"][0]), col_layout(inp["l1_w0"][1]), col_layout(inp["l1_a0"][0]),
           col_layout(inp["l1_a0"][1]), col_layout(inp["l1_k_k"]), col_layout(inp["l1_k_a"]),
           col_layout(inp["l1_r_k"].reshape(-1)), col_layout(inp["l1_ln_w"])]
    m["rw_cols"] = np.ascontiguousarray(np.stack(cl, 1))
    m["rw_lnb"] = col_layout(inp["l1_ln_b"])
    m["hcol"] = np.ascontiguousarray(np.stack([inp["l0_filt_b1"], inp["l0_filt_b2"], inp["l0_filt_b3"],
                                               inp["l0_filt_freq"]], 1))
    m["fw1"] = np.ascontiguousarray(inp["l0_filt_w1"])
    m["fw23"] = np.ascontiguousarray(np.stack([inp["l0_filt_w2"], inp["l0_filt_w3"]], 1))
    m["fw4"] = np.ascontiguousarray(inp["l0_filt_w4"])
    m["fbias"] = col_layout(inp["l0_filt_bias"])
    sw = np.concatenate([inp["l0_short_w"], inp["l0_short_b"][None, :]], 0)
    m["shortw"] = np.ascontiguousarray(sw.T.reshape(6, 128, 4).transpose(1, 0, 2))
    m.update(host_consts())
    return m


def prep_core(xs, cs):
    m = {}
    m["x_in"] = np.ascontiguousarray(np.stack(xs, 0))
    c = np.stack(cs, 0)
    m["cT"] = np.ascontiguousarray(c.reshape(len(xs), NK, 128).transpose(2, 1, 0))
    return m


ALL_STAGES = ("adaln", "l0_inproj", "attn", "hyfilt", "hyena", "l0_outproj", "ffn0", "rw_norm", "rw_proj",
              "rw_scan0", "rw_scan1", "rw_post", "ffn1", "final")


def kernel(**inputs):
    inp = {k: np.asarray(v) for k, v in inputs.items()}
    xs = [inp["x_prompt"][i] for i in range(inp["x_prompt"].shape[0])] + \
         [inp["x_sample"][i] for i in range(inp["x_sample"].shape[0])]
    cs = [inp["c_prompt"][i] for i in range(inp["c_prompt"].shape[0])] + \
         [inp["c_sample"][i] for i in range(inp["c_sample"].shape[0])]
    nseq = len(xs)
    nb = inp["x_prompt"].shape[0]
    assign = []
    for core in range(NCORES):
        ids = []
        for slot in range(NS):
            sid = core + NCORES * slot
            ids.append(sid if sid < nseq else core)
        assign.append(ids)
    nc, P, C = build_program(ALL_STAGES, ())
    finish_program(P, C)
    shared = prep_shared(inp)
    in_maps = []
    for core in range(NCORES):
        m = dict(shared)
        m.update(prep_core([xs[i] for i in assign[core]], [cs[i] for i in assign[core]]))
        in_maps.append(m)
    res = run_bass_kernel_spmd(nc, in_maps, core_ids=list(range(NCORES)))
    outs = [None] * nseq
    for core in range(NCORES):
        y = np.asarray(res.results[core]["y_out"])
        for slot in range(NS):
            sid = core + NCORES * slot
            if sid < nseq:
                outs[sid] = y[slot]
    y_prompt = np.stack(outs[:nb], 0).astype(np.float32)
    y_sample = np.stack(outs[nb:], 0).astype(np.float32)
    return (y_prompt, y_sample)
```

```python
import contextlib
import math
import numpy as np
import concourse.bass as bass
import concourse.mybir as mybir
from concourse.bass_utils import run_bass_kernel_spmd

F32 = mybir.dt.float32
BF16 = mybir.dt.bfloat16
AF = mybir.ActivationFunctionType
ALU = mybir.AluOpType

D = 1024
L = 4096
NK = 8
BLK = 512
NBLK = L // BLK
NCORES = 8
NS = 3
DFF = 2816
NFC = DFF // 128
RMS_EPS = 1e-6

ENGS = ("pe", "act", "dve", "pool", "sp")
NDMASEM = 6


class Buf:
    __slots__ = ("t", "name", "w", "r", "a")

    def __init__(self, t, name):
        self.t = t
        self.name = name
        self.w = {}
        self.r = {}
        self.a = {}

    def __getitem__(self, idx):
        return self.t[idx]


class Prog:
    def __init__(self, nc):
        self.nc = nc
        self.base = contextlib.ExitStack()
        self.es = self.base
        self.streams = {e: [] for e in ENGS}
        self.cnt = {e: 0 for e in ENGS}
        self.seen = {e: {} for e in ENGS}
        self.sem = {}
        self.dtot = {}
        self.rr = {e: 0 for e in ENGS}
        for e in ENGS:
            self.sem[e] = self.base.enter_context(nc.semaphore("c_" + e))
        for q in ("sp", "act", "pool"):
            for i in range(NDMASEM):
                k = "d_%s%d" % (q, i)
                self.sem[k] = self.base.enter_context(nc.semaphore(k))
                self.dtot[k] = 0
        self.nbuf = 0
        self.ninst = 0

    @contextlib.contextmanager
    def scope(self):
        old = self.es
        es = contextlib.ExitStack()
        self.es = es
        try:
            yield
            self.barrier()
            self.emit()
        finally:
            es.close()
            self.es = old

    def sbuf(self, shape, dt, name=None):
        self.nbuf += 1
        name = (name or "sb") + "_%d" % self.nbuf
        t = self.es.enter_context(self.nc.sbuf_tensor(name, list(shape), dt))
        return Buf(t, name)

    def psum(self, shape, dt=F32, name=None):
        self.nbuf += 1
        name = (name or "ps") + "_%d" % self.nbuf
        t = self.es.enter_context(self.nc.psum_tensor(name, list(shape), dt))
        return Buf(t, name)

    def dram(self, name, shape, dt, kind="Internal"):
        t = self.nc.dram_tensor(name, list(shape), dt, kind=kind)
        return Buf(t.ap(), name)

    def _deps(self, eng, reads, writes, accs=()):
        need = {}
        seen = self.seen[eng]

        def add(k, v):
            if k == eng and eng == "pe":
                return
            if seen.get(k, 0) >= v:
                return
            if need.get(k, 0) < v:
                need[k] = v

        for b in reads:
            for k, v in b.w.items():
                add(k, v)
            for k, v in b.a.items():
                add(k, v)
        for b in writes:
            for k, v in b.w.items():
                add(k, v)
            for k, v in b.a.items():
                add(k, v)
            for k, v in b.r.items():
                add(k, v)
        for b in accs:
            for k, v in b.w.items():
                add(k, v)
            for k, v in b.r.items():
                add(k, v)
        for k, v in need.items():
            seen[k] = v
        return list(need.items())

    @staticmethod
    def _commit(ev, reads, writes, accs):
        k, v = ev
        for b in reads:
            if b.r.get(k, 0) < v:
                b.r[k] = v
        for b in writes:
            b.w.clear()
            b.w[k] = v
            b.r.clear()
            b.a.clear()
        for b in accs:
            if b.a.get(k, 0) < v:
                b.a[k] = v

    def op(self, eng, fn, reads=(), writes=(), accs=()):
        waits = self._deps(eng, reads, writes, accs)
        self.cnt[eng] += 1
        ev = (eng, self.cnt[eng])
        self.streams[eng].append((waits, fn, eng, 1))
        self._commit(ev, reads, writes, accs)
        self.ninst += 1 + len(waits)

    def dma(self, q, out, in_, reads=(), writes=(), accs=(), **kw):
        i = self.rr[q]
        self.rr[q] = (i + 1) % NDMASEM
        k = "d_%s%d" % (q, i)
        waits = self._deps(q, reads, writes, accs)
        prev = self.dtot[k]
        if prev > 0 and self.seen[q].get(k, 0) < prev:
            waits.append((k, prev))
            self.seen[q][k] = prev
        self.dtot[k] = prev + 16
        ev = (k, prev + 16)

        def fn(e, out=out, in_=in_, kw=kw):
            return e.dma_start(out=out, in_=in_, **kw)

        self.streams[q].append((waits, fn, k, 16))
        self._commit(ev, reads, writes, accs)
        self.ninst += 1 + len(waits)

    def barrier(self):
        tot = dict(self.dtot)
        for e in ENGS:
            tot[e] = self.cnt[e]
        for e in ENGS:
            waits = []
            for k, v in tot.items():
                if v > 0 and k != e and self.seen[e].get(k, 0) < v:
                    waits.append((k, v))
                    self.seen[e][k] = v
            if waits:
                self.streams[e].append((waits, None, None, 0))
                self.ninst += len(waits)

    def wait_all(self, eng, bufs):
        waits = self._deps(eng, bufs, ())
        self.streams[eng].append((waits, None, None, 0))

    def emit(self):
        if not any(self.streams.values()):
            return
        nc = self.nc
        engobj = {"pe": "tensor", "act": "scalar", "dve": "vector", "pool": "gpsimd", "sp": "sync"}
        with nc.Block() as block:
            for e in ENGS:
                stream = self.streams[e]

                def body(eng, stream=stream):
                    for waits, fn, sk, inc in stream:
                        for k, v in waits:
                            eng.wait_ge(self.sem[k], v)
                        if fn is not None:
                            fn(eng).then_inc(self.sem[sk], inc)

                getattr(block, engobj[e])(body)
        self.streams = {e: [] for e in ENGS}

    def close(self):
        self.emit()
        self.base.close()


class Rot:
    def __init__(self, bufs):
        self.bufs = bufs
        self.i = 0

    def next(self):
        b = self.bufs[self.i % len(self.bufs)]
        self.i += 1
        return b


class Ctx:
    pass


def mm(P, out_buf, out_ap, lhsT_buf, lhsT_ap, rhs_buf, rhs_ap, start, stop):
    P.op("pe", lambda e: e.matmul(out_ap, lhsT=lhsT_ap, rhs=rhs_ap, start=start, stop=stop),
         reads=[lhsT_buf, rhs_buf], writes=[out_buf])


def load_weight_bf16(P, C, dram_buf, nk, ncols, name):
    wb = P.sbuf([128, nk, ncols], BF16, name)
    grp = max(32, (C.wstage_n // nk) // 32 * 32)
    i = 0
    for c0 in range(0, ncols, grp):
        cw = min(grp, ncols - c0)
        st = C.wstage.next()
        P.dma("sp", st[:, 0:nk * cw].rearrange("p (k c) -> p k c", c=cw),
              dram_buf[:, :, c0:c0 + cw], reads=[dram_buf], writes=[st])
        if i % 2 == 0:
            P.op("dve", lambda e, st=st, c0=c0, cw=cw: e.tensor_copy(
                out=wb[:, :, c0:c0 + cw], in_=st[:, 0:nk * cw].rearrange("p (k c) -> p k c", c=cw)),
                reads=[st], accs=[wb])
        else:
            P.op("act", lambda e, st=st, c0=c0, cw=cw: e.copy(
                out=wb[:, :, c0:c0 + cw], in_=st[:, 0:nk * cw].rearrange("p (k c) -> p k c", c=cw)),
                reads=[st], accs=[wb])
        i += 1
    return wb


def norm_mod(P, C, xT, W, gain, shift, s, out_buf, out_dt_is_bf16=True):
    ssp = C.ps_ss.next()
    for k in range(NK):
        sq = C.nm_sq.next()
        P.op("act", lambda e, sq=sq, k=k: e.activation(out=sq[:, 0:W], in_=xT[:, k, 0:W], func=AF.Square),
             reads=[xT], writes=[sq])
        mm(P, ssp, ssp[:, 0:W], C.ones32, C.ones32[:], sq, sq[:, 0:W], k == 0, k == NK - 1)
    rstd = C.nm_rstd.next()
    P.op("act", lambda e: e.activation(out=rstd[:, 0:W], in_=ssp[:, 0:W], func=AF.Sqrt,
                                       scale=1.0 / D, bias=C.eps_col[:, 0:1]),
         reads=[ssp, C.eps_col], writes=[rstd])
    P.op("dve", lambda e: e.reciprocal(out=rstd[:, 0:W], in_=rstd[:, 0:W]), reads=[rstd], writes=[rstd])
    for k in range(NK):
        if shift is None:
            P.op("dve", lambda e, k=k: e.scalar_tensor_tensor(
                out=out_buf[:, k, 0:W], in0=xT[:, k, 0:W], scalar=gain[:, k, s:s + 1], in1=rstd[:, 0:W],
                op0=ALU.mult, op1=ALU.mult), reads=[xT, gain, rstd], accs=[out_buf])
        else:
            tmp = C.nm_tmp.next()
            P.op("dve", lambda e, k=k, tmp=tmp: e.scalar_tensor_tensor(
                out=tmp[:, 0:W], in0=xT[:, k, 0:W], scalar=gain[:, k, s:s + 1], in1=rstd[:, 0:W],
                op0=ALU.mult, op1=ALU.mult), reads=[xT, gain, rstd], writes=[tmp])
            P.op("act", lambda e, k=k, tmp=tmp: e.activation(
                out=out_buf[:, k, 0:W], in_=tmp[:, 0:W], func=AF.Identity, bias=shift[:, k, s:s + 1], scale=1.0),
                reads=[tmp, shift], accs=[out_buf])


def alloc_norm_scratch(P, C, W=BLK):
    C.nm_sq = Rot([P.sbuf([128, W], F32, "nmsq") for _ in range(2)])
    C.nm_rstd = Rot([P.sbuf([128, W], F32, "nmrs") for _ in range(2)])
    C.nm_tmp = Rot([P.sbuf([128, W], F32, "nmtmp") for _ in range(2)])
    C.ps_ss = Rot([P.psum([128, W], F32, "psss")])


def fence(P, buf):
    return buf


def stage_adaln(P, C):
    with P.scope():
        cT = P.sbuf([128, NK, NS], F32, "cT")
        P.dma("sp", cT[:], C.d_cT[:], reads=[C.d_cT], writes=[cT])
        sc = P.sbuf([128, NK, NS], F32, "silu_c")
        P.op("act", lambda e: e.activation(out=sc[:], in_=cT[:], func=AF.Silu), reads=[cT], writes=[sc])
        wts = Rot([P.sbuf([128, NK, 1024], F32, "adaw") for _ in range(2)])
        pss = Rot([P.psum([128, 512], F32, "adaps") for _ in range(2)])
        adab = P.sbuf([128, 2, 48], F32, "adab")
        P.dma("sp", adab[:], C.d_adab[:], reads=[C.d_adab], writes=[adab])
        for l in range(2):
            for j in range(6):
                wt = wts.next()
                P.dma("sp" if j % 2 == 0 else "act", wt[:], C.d_adaw[l][:, :, j * 1024:(j + 1) * 1024],
                      reads=[C.d_adaw[l]], writes=[wt])
                psb = pss.next()
                ps = psb[:, 0:8 * NS].rearrange("p (c s) -> p c s", s=NS)
                for cc in range(8):
                    for k in range(NK):
                        mm(P, psb, ps[:, cc, :], wt, wt[:, k, cc * 128:(cc + 1) * 128], sc, sc[:, k, :],
                           k == 0, k == NK - 1)
                P.op("dve", lambda e, l=l, j=j, ps=ps: e.tensor_tensor(
                    out=C.modT[l][:, j * 8:(j + 1) * 8, :], in0=ps,
                    in1=adab[:, l, j * 8:(j + 1) * 8].unsqueeze(2).to_broadcast([128, 8, NS]), op=ALU.add),
                    reads=[psb, adab], accs=[C.modT[l]])
        nw = P.sbuf([128, 5, NK], F32, "normw")
        P.dma("sp", nw[:], C.d_normw[:], reads=[C.d_normw], writes=[nw])
        for l in range(2):
            for which in range(2):
                jsc = 1 + 3 * which
                g = C.gain[l][which]
                P.op("dve", lambda e, l=l, which=which, jsc=jsc, g=g: e.scalar_tensor_tensor(
                    out=g[:], in0=C.modT[l][:, jsc * 8:(jsc + 1) * 8, :], scalar=1.0,
                    in1=nw[:, 2 * l + which, :].unsqueeze(2).to_broadcast([128, NK, NS]),
                    op0=ALU.add, op1=ALU.mult), reads=[C.modT[l], nw], writes=[g])
        P.op("dve", lambda e: e.tensor_copy(
            out=C.gain_fin[:], in_=nw[:, 4, :].unsqueeze(2).to_broadcast([128, NK, NS])),
            reads=[nw], writes=[C.gain_fin])


def mod_vec(C, l, j):
    return C.modT[l]


def stage_l0_inproj(P, C):
    with P.scope():
        C.wstage_n = 2048
        C.wstage = Rot([P.sbuf([128, 2048], F32, "wst") for _ in range(2)])
        w_in = load_weight_bf16(P, C, C.d_w_in, NK, 3072, "w_in")
        alloc_norm_scratch(P, C)
        xins = Rot([P.sbuf([128, 4, D], F32, "xin") for _ in range(2)])
        xTs = Rot([P.sbuf([128, NK, BLK], F32, "xT") for _ in range(2)])
        hTs = Rot([P.sbuf([128, NK, BLK], BF16, "hT") for _ in range(2)])
        pts = Rot([P.psum([128, BLK], F32, "pt") for _ in range(2)])
        pps = Rot([P.psum([128, BLK], F32, "pp") for _ in range(2)])
        pvs = Rot([P.psum([128, 1024], F32, "pv") for _ in range(1)])
        qks = Rot([P.sbuf([128, 12, BLK], BF16, "qk") for _ in range(1)])
        hys = Rot([P.sbuf([128, 6, BLK], F32, "hy") for _ in range(1)])
        vas = []
        for _ in range(2):
            va = P.sbuf([128, 4, 12, 65], BF16, "vaug")
            P.op("pool", lambda e, va=va: e.memset(va[:], 1.0), writes=[va])
            vas.append(va)
        vas = Rot(vas)
        mod = C.modT[0]
        ev = 0
        for s in range(NS):
            for blk in range(NBLK):
                t0 = blk * BLK
                xin = xins.next()
                P.dma("sp", xin[:], C.d_x[s, t0:t0 + BLK, :].rearrange("(j p) f -> p j f", p=128),
                      reads=[C.d_x], writes=[xin])
                xT = xTs.next()
                for k in range(NK):
                    pt = pts.next()
                    for j in range(4):
                        P.op("pe", lambda e, pt=pt, j=j, k=k, xin=xin: e.transpose(
                            pt[:, j * 128:(j + 1) * 128], xin[:, j, k * 128:(k + 1) * 128], C.ident32[:]),
                            reads=[xin, C.ident32], writes=[pt])
                    if k % 2 == 0:
                        P.op("act", lambda e, pt=pt, k=k, xT=xT: e.copy(out=xT[:, k, :], in_=pt[:]),
                             reads=[pt], accs=[xT])
                    else:
                        P.op("dve", lambda e, pt=pt, k=k, xT=xT: e.tensor_copy(out=xT[:, k, :], in_=pt[:]),
                             reads=[pt], accs=[xT])
                P.dma("pool", C.d_xT[0][s][:, :, t0:t0 + BLK], xT[:], reads=[xT], accs=[C.d_xT[0][s]])
                hT = hTs.next()
                norm_mod(P, C, xT, BLK, C.gain[0][0], mod_shift(C, 0, 0), s, hT)
                qk = qks.next()
                for c in range(12):
                    pp = pps.next()
                    for k in range(NK):
                        mm(P, pp, pp[:], w_in, w_in[:, k, c * 128:(c + 1) * 128], hT, hT[:, k, :], k == 0, k == NK - 1)
                    if c % 2 == 0:
                        P.op("act", lambda e, pp=pp, c=c, qk=qk: e.copy(out=qk[:, c, :], in_=pp[:]),
                             reads=[pp], accs=[qk])
                    else:
                        P.op("dve", lambda e, pp=pp, c=c, qk=qk: e.tensor_copy(out=qk[:, c, :], in_=pp[:]),
                             reads=[pp], accs=[qk])
                P.dma("pool", C.d_qkT[s][:, :, t0:t0 + BLK], qk[:], reads=[qk], accs=[C.d_qkT[s]])
                va = vas.next()
                for j in range(4):
                    pv = pvs.next()
                    for k in range(NK):
                        mm(P, pv, pv[:, 0:512], hT, hT[:, k, j * 128:(j + 1) * 128], w_in, w_in[:, k, 1536:2048],
                           k == 0, k == NK - 1)
                    for k in range(NK):
                        mm(P, pv, pv[:, 512:768], hT, hT[:, k, j * 128:(j + 1) * 128], w_in, w_in[:, k, 2048:2304],
                           k == 0, k == NK - 1)
                    P.op("act" if j % 2 == 0 else "dve",
                         (lambda e, pv=pv, j=j, va=va: e.copy(
                             out=va[:, j, :, 0:64], in_=pv[:, 0:768].rearrange("p (h d) -> p h d", d=64)))
                         if j % 2 == 0 else
                         (lambda e, pv=pv, j=j, va=va: e.tensor_copy(
                             out=va[:, j, :, 0:64], in_=pv[:, 0:768].rearrange("p (h d) -> p h d", d=64))),
                         reads=[pv], accs=[va])
                P.dma("pool", C.d_vaug[s][t0:t0 + BLK].rearrange("(j p) h d -> p j h d", p=128), va[:],
                      reads=[va], accs=[C.d_vaug[s]])
                hy = hys.next()
                for c in range(6):
                    pp = pps.next()
                    for k in range(NK):
                        mm(P, pp, pp[:], w_in, w_in[:, k, 2304 + c * 128:2304 + (c + 1) * 128], hT, hT[:, k, :],
                           k == 0, k == NK - 1)
                    if c % 2 == 0:
                        P.op("act", lambda e, pp=pp, c=c, hy=hy: e.copy(out=hy[:, c, :], in_=pp[:]),
                             reads=[pp], accs=[hy])
                    else:
                        P.op("dve", lambda e, pp=pp, c=c, hy=hy: e.tensor_copy(out=hy[:, c, :], in_=pp[:]),
                             reads=[pp], accs=[hy])
                P.dma("pool", C.d_hyT[s][:, :, t0:t0 + BLK], hy[:], reads=[hy], accs=[C.d_hyT[s]])


DILS = (1, 4, 16)
FINAL_SRC = 4
import os
ATT_STOP = int(os.environ.get("ATT_STOP", "9"))
ATT_GROUPS = tuple(int(x) for x in os.environ.get("ATT_GROUPS", "0,1,2").split(","))


def stage_attention(P, C):
    with P.scope():
        tabs = P.sbuf([128, 9, 4, 128], F32, "abias")
        P.dma("sp", tabs[:], C.d_abias[:], reads=[C.d_abias], writes=[tabs])
        qTs = Rot([P.sbuf([128, 2, L], BF16, "qT") for _ in range(2)])
        kTs = Rot([P.sbuf([128, 2, L], BF16, "kT") for _ in range(2)])
        vts = Rot([P.sbuf([128, 4, 65], BF16, "vt") for _ in range(4)])
        pss = Rot([P.psum([128, 512], F32, "pss") for _ in range(4)])
        pos = Rot([P.psum([128, 512], F32, "pso") for _ in range(2)])
        sbs = Rot([P.sbuf([128, 4, 128], F32, "ssb") for _ in range(2)])
        pTs = Rot([P.sbuf([128, 4, 128], BF16, "pT") for _ in range(4)])
        osb = Rot([P.sbuf([128, 4, 65], F32, "osb") for _ in range(3)])
        cnt = 0
        for s in range(NS):
            for g, d in enumerate(DILS):
                if g not in ATT_GROUPS:
                    continue
                qT = qTs.next()
                kT = kTs.next()
                P.dma("sp", qT[:], C.d_qkT[s][:, 2 * g:2 * g + 2, :], reads=[C.d_qkT[s]], writes=[qT])
                P.dma("sp", kT[:], C.d_qkT[s][:, 6 + 2 * g:6 + 2 * g + 2, :], reads=[C.d_qkT[s]], writes=[kT])
                n = L // d
                ntile = n // 128
                for r in range(d):
                    def load_v(m):
                        vt = vts.next()
                        if m == 0:
                            ks, nk = 0, 64
                        elif m == ntile:
                            ks, nk = n - 64, 64
                        else:
                            ks, nk = 128 * m - 64, 128
                        t0 = ks * d + r
                        P.dma("sp", vt[0:nk], C.d_vaug[s][t0:t0 + (nk - 1) * d + 1:d, 4 * g:4 * g + 4, :],
                              reads=[C.d_vaug[s]], writes=[vt])
                        return vt, ks, nk
                    vcur = load_v(0)
                    for qt in range(ntile):
                        vnext = load_v(qt + 1)
                        i0 = qt * 128
                        q0 = i0 * d + r
                        qsl = slice(q0, q0 + 127 * d + 1, d)
                        pTl = []
                        if ATT_STOP <= 1:
                            vcur = vnext
                            continue
                        for bi, (vt, ks, nk) in enumerate((vcur, vnext)):
                            if bi == 0:
                                ti = 3 * g + (2 if qt == 0 else 0)
                            else:
                                ti = 3 * g + 1
                            k0 = ks * d + r
                            ksl = slice(k0, k0 + (nk - 1) * d + 1, d)
                            sb = sbs.next()
                            pT = pTs.next()
                            for hh in range(2):
                                ps = pss.next()
                                psv = ps[:, 0:256].rearrange("p (a q) -> p a q", q=128)
                                for hp in range(2):
                                    mm(P, ps, psv[0:nk, hp, :], kT, kT[hh * 64:(hh + 1) * 64, hp, ksl],
                                       qT, qT[hh * 64:(hh + 1) * 64, hp, qsl], True, True)
                                P.op("dve", lambda e, ps=ps, psv=psv, sb=sb, nk=nk, ti=ti, hh=hh: e.scalar_tensor_tensor(
                                    out=sb[0:nk, hh::2, :], in0=psv[0:nk], scalar=0.125, in1=tabs[0:nk, ti, hh::2, :],
                                    op0=ALU.mult, op1=ALU.add), reads=[ps, tabs], accs=[sb])
                            P.op("act", lambda e, sb=sb, pT=pT, nk=nk: e.activation(
                                out=pT[0:nk], in_=sb[0:nk], func=AF.Exp), reads=[sb], writes=[pT])
                            pTl.append((pT, vt, nk))
                        if ATT_STOP <= 2:
                            vcur = vnext
                            continue
                        po = pos.next()
                        pov = po[:, 0:260].rearrange("p (h d) -> p h d", d=65)
                        for h in range(4):
                            for bi, (pT, vt, nk) in enumerate(pTl):
                                mm(P, po, pov[:, h, :], pT, pT[0:nk, h, :], vt, vt[0:nk, h, :], bi == 0, bi == 1)
                        ob = osb.next()
                        if cnt % 2 == 0:
                            P.op("act", lambda e, ob=ob, pov=pov: e.copy(out=ob[:], in_=pov), reads=[po], writes=[ob])
                        else:
                            P.op("dve", lambda e, ob=ob, pov=pov: e.tensor_copy(out=ob[:], in_=pov),
                                 reads=[po], writes=[ob])
                        cnt += 1
                        if ATT_STOP <= 3:
                            vcur = vnext
                            continue
                        P.dma("pool", C.d_oacc[s][g, q0:q0 + 127 * d + 1:d, :, :], ob[:],
                              reads=[ob], accs=[C.d_oacc[s]])
                        vcur = vnext


HY_GRP = 32
TWO_PI = 2.0 * math.pi


def alloc_fft(P, C):
    F = Ctx()
    F.w128 = P.sbuf([64, 256], F32, "w128")
    F.tw = P.sbuf([128, 2, 128], F32, "tw")
    F.bd = P.sbuf([128, 3, 128], F32, "bd")
    F.bdc = P.sbuf([128, 2, 256], F32, "bdc")
    F.twi = P.sbuf([128, 2, 128], F32, "twi")
    F.vinv = P.sbuf([128, 2, 64], F32, "vinv")
    for b, d in ((F.w128, C.d_w128), (F.tw, C.d_tw), (F.bd, C.d_bd), (F.bdc, C.d_bdc), (F.twi, C.d_twi),
                 (F.vinv, C.d_vinv)):
        P.dma("sp", b[:], d[:], reads=[d], writes=[b])
    F.psA = Rot([P.psum([128, 512], F32, "psA") for _ in range(2)])
    F.psX = Rot([P.psum([128, 512], F32, "psX") for _ in range(2)])
    F.p1 = Rot([P.sbuf([128, 2, 128], F32, "fp1") for _ in range(2)])
    F.p2 = Rot([P.sbuf([128, 2, 128], F32, "fp2") for _ in range(2)])
    F.b = Rot([P.sbuf([128, 2, 128], F32, "fb") for _ in range(2)])
    return F


def cmul(P, F, src_buf, src_v, tab_buf, tab_r, tab_i, out_buf, out_v):
    p1 = F.p1.next()
    p2 = F.p2.next()
    n = src_v.shape[0]
    P.op("dve", lambda e: e.tensor_tensor(out=p1[0:n], in0=src_v, in1=tab_r.unsqueeze(1).to_broadcast([n, 2, 128]),
                                          op=ALU.mult), reads=[src_buf, tab_buf], writes=[p1])
    P.op("dve", lambda e: e.tensor_tensor(out=p2[0:n], in0=src_v, in1=tab_i.unsqueeze(1).to_broadcast([n, 2, 128]),
                                          op=ALU.mult), reads=[src_buf, tab_buf], writes=[p2])
    P.op("dve", lambda e: e.tensor_tensor(out=out_v[:, 0, :], in0=p1[0:n, 0, :], in1=p2[0:n, 1, :], op=ALU.subtract),
         reads=[p1, p2], accs=[out_buf])
    P.op("dve", lambda e: e.tensor_tensor(out=out_v[:, 1, :], in0=p2[0:n, 0, :], in1=p1[0:n, 1, :], op=ALU.add),
         reads=[p1, p2], accs=[out_buf])


def fft_fwd_pair(P, F, xin_buf, xin_ap):
    psA = F.psA.next()
    mm(P, psA, psA[:, 0:256], xin_buf, xin_ap, F.w128, F.w128[:], True, True)
    b = F.b.next()
    cmul(P, F, psA, psA[:, 0:256].rearrange("p (c k) -> p c k", k=128), F.tw, F.tw[:, 0, :], F.tw[:, 1, :], b, b[:])
    psX = F.psX.next()
    mm(P, psX, psX[:, 0:128], F.bd, F.bd[:, 0, :], b, b[:, 0, :], True, False)
    mm(P, psX, psX[:, 0:128], F.bd, F.bd[:, 2, :], b, b[:, 1, :], False, True)
    mm(P, psX, psX[:, 128:256], F.bd, F.bd[:, 1, :], b, b[:, 0, :], True, False)
    mm(P, psX, psX[:, 128:256], F.bd, F.bd[:, 0, :], b, b[:, 1, :], False, True)
    return psX, psX[:, 0:256].rearrange("p (c k) -> p c k", k=128)


def fft_fwd_multi(P, F, inputs):
    psAs = []
    for buf, ap in inputs:
        psA = F.psA.next()
        mm(P, psA, psA[:, 0:256], buf, ap, F.w128, F.w128[:], True, True)
        psAs.append(psA)
    bs = []
    for psA in psAs:
        b = F.b.next()
        cmul(P, F, psA, psA[:, 0:256].rearrange("p (c k) -> p c k", k=128), F.tw, F.tw[:, 0, :], F.tw[:, 1, :], b, b[:])
        bs.append(b)
    outs = []
    for b in bs:
        psX = F.psX.next()
        mm(P, psX, psX[:, 0:128], F.bd, F.bd[:, 0, :], b, b[:, 0, :], True, False)
        mm(P, psX, psX[:, 0:128], F.bd, F.bd[:, 2, :], b, b[:, 1, :], False, True)
        mm(P, psX, psX[:, 128:256], F.bd, F.bd[:, 1, :], b, b[:, 0, :], True, False)
        mm(P, psX, psX[:, 128:256], F.bd, F.bd[:, 0, :], b, b[:, 1, :], False, True)
        outs.append((psX, psX[:, 0:256].rearrange("p (c k) -> p c k", k=128)))
    return outs


def wrap_pi(P, u, m, n, W):
    for _ in range(2):
        P.op("dve", lambda e: e.tensor_single_scalar(out=m[0:n, 0:W], in_=u[0:n, 0:W], scalar=math.pi, op=ALU.is_gt),
             reads=[u], writes=[m])
        P.op("dve", lambda e: e.scalar_tensor_tensor(out=u[0:n, 0:W], in0=m[0:n, 0:W], scalar=-TWO_PI, in1=u[0:n, 0:W],
                                                     op0=ALU.mult, op1=ALU.add), reads=[m, u], writes=[u])
        P.op("dve", lambda e: e.tensor_single_scalar(out=m[0:n, 0:W], in_=u[0:n, 0:W], scalar=-math.pi, op=ALU.is_lt),
             reads=[u], writes=[m])
        P.op("dve", lambda e: e.scalar_tensor_tensor(out=u[0:n, 0:W], in0=m[0:n, 0:W], scalar=TWO_PI, in1=u[0:n, 0:W],
                                                     op0=ALU.mult, op1=ALU.add), reads=[m, u], writes=[u])


def stage_hyena_filter(P, C):
    with P.scope():
        hcol = P.sbuf([64, 8], F32, "hcol")
        P.dma("sp", hcol[:, 0:4], C.d_hcol[:], reads=[C.d_hcol], writes=[hcol])
        for i in range(3):
            P.op("dve", lambda e, i=i: e.tensor_tensor(out=hcol[:, 4 + i:5 + i], in0=hcol[:, i:i + 1],
                                                       in1=hcol[:, 3:4], op=ALU.mult), reads=[hcol], writes=[hcol])
        w1 = P.sbuf([33, 64], F32, "fw1")
        w23 = P.sbuf([64, 2, 64], F32, "fw23")
        w4 = P.sbuf([64, 512], F32, "fw4")
        fbias = P.sbuf([128, 2], F32, "fbias")
        P.dma("sp", w1[:], C.d_fw1[:], reads=[C.d_fw1], writes=[w1])
        P.dma("sp", w23[:], C.d_fw23[:], reads=[C.d_fw23], writes=[w23])
        P.dma("sp", w4[:], C.d_fw4[:], reads=[C.d_fw4], writes=[w4])
        P.dma("sp", fbias[:], C.d_fbias[:], reads=[C.d_fbias], writes=[fbias])
        hT = P.sbuf([128, 4, L], F32, "filt_hT")
        pos = Rot([P.sbuf([33, BLK], F32, "pos") for _ in range(2)])
        win = Rot([P.sbuf([128, 2, BLK], F32, "win") for _ in range(2)])
        us = Rot([P.sbuf([64, BLK], F32, "fu") for _ in range(3)])
        ms = Rot([P.sbuf([64, BLK], F32, "fm") for _ in range(2)])
        pps = Rot([P.psum([128, BLK], F32, "fpp") for _ in range(2)])
        for blk in range(NBLK):
            t0 = blk * BLK
            po = pos.next()
            wn = win.next()
            P.dma("sp", po[:], C.d_posT[:, t0:t0 + BLK], reads=[C.d_posT], writes=[po])
            P.dma("sp", wn[:], C.d_winT[:, :, t0:t0 + BLK], reads=[C.d_winT], writes=[wn])
            prev_buf, prev_ap, kdim = po, po[:], 33
            for layer in range(3):
                pp = pps.next()
                if layer == 0:
                    mm(P, pp, pp[0:64, :], w1, w1[:], prev_buf, prev_ap, True, True)
                else:
                    mm(P, pp, pp[0:64, :], w23, w23[:, layer - 1, :], prev_buf, prev_ap, True, True)
                u = us.next()
                m = ms.next()
                P.op("dve", lambda e, pp=pp, u=u, layer=layer: e.tensor_scalar(
                    out=u[:], in0=pp[0:64, :], scalar1=hcol[:, 3:4], scalar2=hcol[:, 4 + layer:5 + layer],
                    op0=ALU.mult, op1=ALU.add), reads=[pp, hcol], writes=[u])
                wrap_pi(P, u, m, 64, BLK)
                P.op("act", lambda e, u=u: e.activation(out=u[:], in_=u[:], func=AF.Sin), reads=[u], writes=[u])
                prev_buf, prev_ap = u, u[:]
            for c in range(4):
                pp = pps.next()
                mm(P, pp, pp[:], w4, w4[:, c * 128:(c + 1) * 128], prev_buf, prev_ap, True, True)
                P.op("dve", lambda e, pp=pp, c=c, wn=wn, t0=t0: e.tensor_tensor(
                    out=hT[:, c, t0:t0 + BLK], in0=pp[:], in1=wn[:, c % 2, :], op=ALU.mult),
                    reads=[pp, wn], accs=[hT])
        junk = P.sbuf([128, L], F32, "fjunk")
        acc = P.sbuf([128, 8], F32, "facc")
        P.op("pool", lambda e: e.memset(acc[:], 0.0), writes=[acc])
        for c in range(4):
            lo = 0 if c < 2 else 1
            P.op("act", lambda e, c=c, lo=lo: e.activation(out=junk[:, lo:L], in_=hT[:, c, lo:L], func=AF.Abs,
                                                           accum_out=acc[:, c:c + 1]),
                 reads=[hT], writes=[junk, acc])
        P.op("dve", lambda e: e.tensor_tensor(out=acc[:, 4:6], in0=acc[:, 0:2], in1=acc[:, 2:4], op=ALU.add),
             reads=[acc], writes=[acc])
        P.op("dve", lambda e: e.reciprocal(out=acc[:, 6:8], in_=acc[:, 4:6]), reads=[acc], writes=[acc])
        for c in range(4):
            P.op("dve", lambda e, c=c: e.tensor_scalar(
                out=hT[:, c, :], in0=hT[:, c, :], scalar1=acc[:, 6 + c % 2:7 + c % 2], scalar2=None, op0=ALU.mult),
                reads=[hT, acc], writes=[hT])
        for j in range(2):
            P.op("dve", lambda e, j=j: e.tensor_tensor(out=hT[:, j, 0:1], in0=hT[:, j, 0:1], in1=fbias[:, j:j + 1],
                                                       op=ALU.add), reads=[hT, fbias], writes=[hT])
            P.op("dve", lambda e, j=j: e.memset(hT[:, 2 + j, 0:1], 0.0), writes=[hT])
        P.dma("pool", C.d_filtT[:], hT[:], reads=[hT], writes=[C.d_filtT])
    with P.scope():
        F = alloc_fft(P, C)
        xf = Rot([P.sbuf([64, HY_GRP, 64], F32, "xf") for _ in range(2)])
        xb = Rot([P.sbuf([64, HY_GRP, 64], F32, "xb") for _ in range(2)])
        fo = Rot([P.sbuf([128, HY_GRP // 2, 2, 128], F32, "fo") for _ in range(2)])
        for j in range(2):
            for c0 in range(0, 128, HY_GRP):
                a = xf.next()
                b = xb.next()
                P.dma("sp", a[:], C.d_filtT[c0:c0 + HY_GRP, j, :].rearrange("c (a b) -> a c b", b=64),
                      reads=[C.d_filtT], writes=[a])
                P.dma("sp", b[:], C.d_filtT[c0:c0 + HY_GRP, 2 + j, :].rearrange("c (a b) -> a c b", b=64),
                      reads=[C.d_filtT], writes=[b])
                o = fo.next()
                for i in range(HY_GRP // 2):
                    (psf, vf), (psb, vb) = fft_fwd_multi(P, F, [
                        (a, a[:, 2 * i:2 * i + 2, :].rearrange("p c n -> p (c n)")),
                        (b, b[:, 2 * i:2 * i + 2, :].rearrange("p c n -> p (c n)"))])
                    P.op("act", lambda e, o=o, i=i, vb=vb: e.copy(out=o[:, i], in_=vb), reads=[psb], accs=[o])
                    P.op("dve", lambda e, o=o, i=i, vf=vf: e.tensor_tensor(out=o[:, i, 0, :], in0=vf[:, 0, :],
                                                                           in1=o[:, i, 0, :], op=ALU.add),
                         reads=[psf, o], accs=[o])
                    P.op("dve", lambda e, o=o, i=i, vf=vf: e.tensor_tensor(out=o[:, i, 1, :], in0=vf[:, 1, :],
                                                                           in1=o[:, i, 1, :], op=ALU.subtract),
                         reads=[psf, o], accs=[o])
                pr0 = (j * 128 + c0) // 2
                P.dma("pool", C.d_F[:, pr0:pr0 + HY_GRP // 2], o[:], reads=[o], accs=[C.d_F])


def dwconv3(P, src, dst, W, wt, c):
    P.op("act", lambda e: e.activation(out=dst[:, 0:W], in_=src[:, 0:W], func=AF.Identity,
                                       scale=wt[:, c, 1:2], bias=wt[:, c, 3:4]), reads=[src, wt], writes=[dst])
    P.op("dve", lambda e: e.scalar_tensor_tensor(out=dst[:, 1:W], in0=src[:, 0:W - 1], scalar=wt[:, c, 0:1],
                                                 in1=dst[:, 1:W], op0=ALU.mult, op1=ALU.add),
         reads=[src, wt, dst], writes=[dst])
    P.op("dve", lambda e: e.scalar_tensor_tensor(out=dst[:, 0:W - 1], in0=src[:, 1:W], scalar=wt[:, c, 2:3],
                                                 in1=dst[:, 0:W - 1], op0=ALU.mult, op1=ALU.add),
         reads=[src, wt, dst], writes=[dst])


def stage_hyena(P, C):
    with P.scope():
        swt = P.sbuf([128, 6, 4], F32, "shortw")
        P.dma("sp", swt[:], C.d_shortw[:], reads=[C.d_shortw], writes=[swt])
        srcs = Rot([P.sbuf([128, L], F32, "hysrc") for _ in range(3)])
        dsts = Rot([P.sbuf([128, L], F32, "hydst") for _ in range(4)])
        for s in range(NS):
            for j in range(2):
                res = []
                for part in range(3):
                    c = 2 * part + j
                    src = srcs.next()
                    P.dma("sp" if part != 1 else "act", src[:], C.d_hyT[s][:, c, :], reads=[C.d_hyT[s]], writes=[src])
                    dst = dsts.next()
                    dwconv3(P, src, dst, L, swt, c)
                    res.append(dst)
                x0, x1, v = res
                P.op("dve", lambda e, x1=x1, v=v: e.tensor_tensor(out=v[:], in0=v[:], in1=x1[:], op=ALU.mult),
                     reads=[v, x1], writes=[v])
                P.dma("pool", C.d_zT[s][:, j, :], v[:], reads=[v], accs=[C.d_zT[s]])
                P.dma("pool", C.d_x0T[s][:, j, :], x0[:], reads=[x0], accs=[C.d_x0T[s]])
    with P.scope():
        F = alloc_fft(P, C)
        psC = Rot([P.psum([128, 512], F32, "psC") for _ in range(2)])
        psY = Rot([P.psum([128, 512], F32, "psY") for _ in range(2)])
        zin = Rot([P.sbuf([64, HY_GRP, 64], F32, "zin") for _ in range(2)])
        x0in = Rot([P.sbuf([64, HY_GRP, 64], F32, "x0in") for _ in range(2)])
        oin = Rot([P.sbuf([64, HY_GRP, 64], F32, "oin") for _ in range(2)])
        fsp = Rot([P.sbuf([128, HY_GRP // 2, 2, 128], F32, "fsp") for _ in range(2)])
        ys = Rot([P.sbuf([128, 2, 128], F32, "fy") for _ in range(2)])
        ds = Rot([P.sbuf([128, 2, 128], F32, "fd") for _ in range(2)])
        for s in range(NS):
            for j in range(2):
                for c0 in range(0, 128, HY_GRP):
                    zi = zin.next()
                    xi = x0in.next()
                    fs = fsp.next()
                    oi = oin.next()
                    P.dma("sp", zi[:], C.d_zT[s][c0:c0 + HY_GRP, j, :].rearrange("c (a b) -> a c b", b=64),
                          reads=[C.d_zT[s]], writes=[zi])
                    P.dma("sp", xi[:], C.d_x0T[s][c0:c0 + HY_GRP, j, :].rearrange("c (a b) -> a c b", b=64),
                          reads=[C.d_x0T[s]], writes=[xi])
                    pr0 = (j * 128 + c0) // 2
                    P.dma("sp", fs[:], C.d_F[:, pr0:pr0 + HY_GRP // 2], reads=[C.d_F], writes=[fs])
                    for i0 in range(0, HY_GRP // 2, 2):
                        pr = (i0, i0 + 1)
                        st = {}
                        for i in pr:
                            psA = F.psA.next()
                            mm(P, psA, psA[:, 0:256], zi, zi[:, 2 * i:2 * i + 2, :].rearrange("p c n -> p (c n)"),
                               F.w128, F.w128[:], True, True)
                            st[i] = dict(psA=psA)
                        for i in pr:
                            b = F.b.next()
                            psA = st[i]["psA"]
                            cmul(P, F, psA, psA[:, 0:256].rearrange("p (c k) -> p c k", k=128), F.tw, F.tw[:, 0, :],
                                 F.tw[:, 1, :], b, b[:])
                            st[i]["b"] = b
                        for i in pr:
                            b = st[i]["b"]
                            psX = F.psX.next()
                            mm(P, psX, psX[:, 0:128], F.bd, F.bd[:, 0, :], b, b[:, 0, :], True, False)
                            mm(P, psX, psX[:, 0:128], F.bd, F.bd[:, 2, :], b, b[:, 1, :], False, True)
                            mm(P, psX, psX[:, 128:256], F.bd, F.bd[:, 1, :], b, b[:, 0, :], True, False)
                            mm(P, psX, psX[:, 128:256], F.bd, F.bd[:, 0, :], b, b[:, 1, :], False, True)
                            st[i]["psX"] = psX
                        for i in pr:
                            psX = st[i]["psX"]
                            y = ys.next()
                            cmul(P, F, psX, psX[:, 0:256].rearrange("p (c k) -> p c k", k=128), fs, fs[:, i, 0, :],
                                 fs[:, i, 1, :], y, y[:])
                            st[i]["y"] = y
                        for i in pr:
                            y = st[i]["y"]
                            pc = psC.next()
                            mm(P, pc, pc[:, 0:256], y, y[:, 0, :], F.bdc, F.bdc[:, 0, :], True, False)
                            mm(P, pc, pc[:, 0:256], y, y[:, 1, :], F.bdc, F.bdc[:, 1, :], False, True)
                            st[i]["pc"] = pc
                        for i in pr:
                            pc = st[i]["pc"]
                            dd = ds.next()
                            cmul(P, F, pc, pc[:, 0:256].rearrange("p (c k) -> p c k", k=128), F.twi, F.twi[:, 0, :],
                                 F.twi[:, 1, :], dd, dd[:])
                            st[i]["dd"] = dd
                        for i in pr:
                            dd = st[i]["dd"]
                            py = psY.next()
                            mm(P, py, py[0:64, 0:128], F.vinv, F.vinv[:, 0, :], dd, dd[:, 0, :], True, False)
                            mm(P, py, py[0:64, 0:128], F.vinv, F.vinv[:, 1, :], dd, dd[:, 1, :], False, True)
                            st[i]["py"] = py
                        for i in pr:
                            py = st[i]["py"]
                            P.op("dve", lambda e, py=py, oi=oi, xi=xi, i=i: e.tensor_tensor(
                                out=oi[:, 2 * i:2 * i + 2, :].rearrange("p c n -> p (c n)"), in0=py[0:64, 0:128],
                                in1=xi[:, 2 * i:2 * i + 2, :].rearrange("p c n -> p (c n)"), op=ALU.mult),
                                reads=[py, xi], accs=[oi])
                    P.dma("pool", C.d_hyoT[s][c0:c0 + HY_GRP, j, :].rearrange("c (a b) -> a c b", b=64), oi[:],
                          reads=[oi], accs=[C.d_hyoT[s]])


def stage_l0_outproj(P, C):
    with P.scope():
        C.wstage_n = 2048
        C.wstage = Rot([P.sbuf([128, 2048], F32, "wst") for _ in range(2)])
        w_out = load_weight_bf16(P, C, C.d_w_out, 4, D, "w_out")
        identb = P.sbuf([128, 128], BF16, "identb")
        P.op("dve", lambda e: e.tensor_copy(out=identb[:], in_=C.ident32[:]), reads=[C.ident32], writes=[identb])
        alloc_norm_scratch(P, C)
        oas = Rot([P.sbuf([128, 4, 3, 260], F32, "oa") for _ in range(2)])
        o2s = Rot([P.sbuf([128, 4, 260], F32, "o2") for _ in range(2)])
        rds = Rot([P.sbuf([128, 4, 4], F32, "rden") for _ in range(2)])
        abs_ = Rot([P.sbuf([128, 4, 4, 64], BF16, "attnb") for _ in range(2)])
        hyl = Rot([P.sbuf([128, 2, BLK], F32, "hyl") for _ in range(2)])
        mixs = Rot([P.sbuf([128, 4, BLK], BF16, "mixT") for _ in range(2)])
        xrs = Rot([P.sbuf([128, NK, BLK], F32, "xr") for _ in range(2)])
        x1s = Rot([P.sbuf([128, NK, BLK], F32, "x1T") for _ in range(2)])
        h2s = Rot([P.sbuf([128, NK, BLK], BF16, "h2T") for _ in range(2)])
        ptb = Rot([P.psum([128, BLK], BF16, "ptb") for _ in range(2)])
        pps = Rot([P.psum([128, BLK], F32, "pp") for _ in range(2)])
        g1 = C.gatev[0][0]
        for s in range(NS):
            for blk in range(NBLK):
                t0 = blk * BLK
                oa = oas.next()
                for g in range(3):
                    P.dma("sp" if g != 1 else "act", oa[:, :, g, :],
                          C.d_oacc[s][g, t0:t0 + BLK].rearrange("(j p) h d -> p j (h d)", p=128),
                          reads=[C.d_oacc[s]], accs=[oa])
                hy = hyl.next()
                P.dma("sp", hy[:], C.d_hyoT[s][:, :, t0:t0 + BLK], reads=[C.d_hyoT[s]], writes=[hy])
                xr = xrs.next()
                P.dma("sp", xr[:], C.d_xT[0][s][:, :, t0:t0 + BLK], reads=[C.d_xT[0][s]], writes=[xr])
                o2 = o2s.next()
                P.op("dve", lambda e, oa=oa, o2=o2: e.tensor_tensor(out=o2[:], in0=oa[:, :, 0, :], in1=oa[:, :, 1, :],
                                                                    op=ALU.add), reads=[oa], writes=[o2])
                P.op("dve", lambda e, oa=oa, o2=o2: e.tensor_tensor(out=o2[:], in0=o2[:], in1=oa[:, :, 2, :],
                                                                    op=ALU.add), reads=[oa, o2], writes=[o2])
                o2v = o2[:].rearrange("p j (h d) -> p j h d", d=65)
                rd = rds.next()
                P.op("dve", lambda e, o2v=o2v, rd=rd: e.reciprocal(out=rd[:], in_=o2v[:, :, :, 64]),
                     reads=[o2], writes=[rd])
                ab = abs_.next()
                P.op("dve", lambda e, o2v=o2v, rd=rd, ab=ab: e.tensor_tensor(
                    out=ab[:], in0=o2v[:, :, :, 0:64], in1=rd[:].unsqueeze(3).to_broadcast([128, 4, 4, 64]),
                    op=ALU.mult), reads=[o2, rd], writes=[ab])
                mix = mixs.next()
                for c in range(2):
                    pt = ptb.next()
                    for j in range(4):
                        P.op("pe", lambda e, pt=pt, j=j, c=c, ab=ab: e.transpose(
                            pt[:, j * 128:(j + 1) * 128],
                            ab[:, j, 2 * c:2 * c + 2, :].rearrange("p h d -> p (h d)"), identb[:]),
                            reads=[ab, identb], writes=[pt])
                    P.op("act", lambda e, pt=pt, c=c, mix=mix: e.copy(out=mix[:, c, :], in_=pt[:]),
                         reads=[pt], accs=[mix])
                P.op("act", lambda e, hy=hy, mix=mix: e.copy(out=mix[:, 2:4, :], in_=hy[:]),
                     reads=[hy], accs=[mix])
                x1 = x1s.next()
                for c in range(NK):
                    pp = pps.next()
                    for k in range(4):
                        mm(P, pp, pp[:], w_out, w_out[:, k, c * 128:(c + 1) * 128], mix, mix[:, k, :], k == 0, k == 3)
                    P.op("dve", lambda e, pp=pp, c=c, x1=x1, xr=xr, s=s: e.scalar_tensor_tensor(
                        out=x1[:, c, :], in0=pp[:], scalar=g1[:, c, s:s + 1], in1=xr[:, c, :],
                        op0=ALU.mult, op1=ALU.add), reads=[pp, g1, xr], accs=[x1])
                P.dma("pool", C.d_xT[1][s][:, :, t0:t0 + BLK], x1[:], reads=[x1], accs=[C.d_xT[1][s]])
                h2 = h2s.next()
                norm_mod(P, C, x1, BLK, C.gain[0][1], C.shiftv[0][1], s, h2)
                P.dma("pool", C.d_h2T[0][s][:, :, t0:t0 + BLK], h2[:], reads=[h2], accs=[C.d_h2T[0][s]])


GELU_C = 0.044715
GELU_S = 2.0 * math.sqrt(2.0 / math.pi)


def stage_ffn(P, C, l, xin_idx, xout_idx):
    with P.scope():
        C.wstage_n = 1024
        C.wstage = Rot([P.sbuf([128, 1024], F32, "wst") for _ in range(2)])
        w_up = load_weight_bf16(P, C, C.d_ffn_up[l], NK, 2 * DFF, "w_up")
        w_dn = load_weight_bf16(P, C, C.d_ffn_dn[l], NFC, D, "w_dn")
        cw = P.sbuf([128, NFC, 4], F32, "convw")
        P.dma("sp", cw[:], C.d_ffn_cw[l][:], reads=[C.d_ffn_cw[l]], writes=[cw])
        hhs = Rot([P.sbuf([128, NK, BLK + 2], BF16, "hh") for _ in range(2)])
        actT = P.sbuf([128, NFC, BLK], BF16, "actT")
        xcs = Rot([P.sbuf([128, BLK], F32, "xc") for _ in range(2)])
        xos = Rot([P.sbuf([128, BLK], F32, "xo") for _ in range(2)])
        tmp = {n: Rot([P.sbuf([128, BLK], F32, n) for _ in range(2)]) for n in ("cv", "sq")}
        pas = Rot([P.psum([128, BLK], F32, "pa") for _ in range(2)])
        phs = Rot([P.psum([128, BLK], F32, "ph") for _ in range(1)])
        pgs = Rot([P.psum([128, BLK], F32, "pg") for _ in range(2)])
        pos = Rot([P.psum([128, BLK], F32, "po") for _ in range(2)])
        g2 = C.gatev[l][1]
        for s in range(NS):
            for blk in range(NBLK):
                t0 = blk * BLK
                hh = hhs.next()
                lo = max(t0 - 1, 0)
                hi = min(t0 + BLK + 1, L)
                c_lo = lo - (t0 - 1)
                P.dma("sp", hh[:, :, c_lo:c_lo + (hi - lo)], C.d_h2T[l][s][:, :, lo:hi], reads=[C.d_h2T[l][s]],
                      writes=[hh])
                if blk == 0:
                    P.op("pool", lambda e, hh=hh: e.memset(hh[:, :, 0:1], 0.0), accs=[hh])
                if blk == NBLK - 1:
                    P.op("pool", lambda e, hh=hh: e.memset(hh[:, :, BLK + 1:BLK + 2], 0.0), accs=[hh])
                for c in range(NFC):
                    pa = pas.next()
                    ph = phs.next()
                    pg = pgs.next()
                    for k in range(NK):
                        mm(P, pa, pa[:], w_up, w_up[:, k, c * 128:(c + 1) * 128], hh, hh[:, k, 1:BLK + 1],
                           k == 0, k == NK - 1)
                    for k in range(NK):
                        mm(P, ph, ph[:, 0:2], w_up, w_up[:, k, c * 128:(c + 1) * 128], hh, hh[:, k, 0:BLK + 2:BLK + 1],
                           k == 0, k == NK - 1)
                    for k in range(NK):
                        mm(P, pg, pg[:], w_up, w_up[:, k, DFF + c * 128:DFF + (c + 1) * 128], hh, hh[:, k, 1:BLK + 1],
                           k == 0, k == NK - 1)
                    cv = tmp["cv"].next()
                    P.op("act", lambda e, pa=pa, cv=cv, c=c: e.activation(
                        out=cv[:], in_=pa[:], func=AF.Identity, scale=cw[:, c, 1:2], bias=cw[:, c, 3:4]),
                        reads=[pa, cw], writes=[cv])
                    P.op("dve", lambda e, pa=pa, cv=cv, c=c: e.scalar_tensor_tensor(
                        out=cv[:, 1:BLK], in0=pa[:, 0:BLK - 1], scalar=cw[:, c, 0:1], in1=cv[:, 1:BLK],
                        op0=ALU.mult, op1=ALU.add), reads=[pa, cw, cv], writes=[cv])
                    P.op("dve", lambda e, pa=pa, cv=cv, c=c: e.scalar_tensor_tensor(
                        out=cv[:, 0:BLK - 1], in0=pa[:, 1:BLK], scalar=cw[:, c, 2:3], in1=cv[:, 0:BLK - 1],
                        op0=ALU.mult, op1=ALU.add), reads=[pa, cw, cv], writes=[cv])
                    P.op("dve", lambda e, ph=ph, cv=cv, c=c: e.scalar_tensor_tensor(
                        out=cv[:, 0:1], in0=ph[:, 0:1], scalar=cw[:, c, 0:1], in1=cv[:, 0:1],
                        op0=ALU.mult, op1=ALU.add), reads=[ph, cw, cv], writes=[cv])
                    P.op("dve", lambda e, ph=ph, cv=cv, c=c: e.scalar_tensor_tensor(
                        out=cv[:, BLK - 1:BLK], in0=ph[:, 1:2], scalar=cw[:, c, 2:3], in1=cv[:, BLK - 1:BLK],
                        op0=ALU.mult, op1=ALU.add), reads=[ph, cw, cv], writes=[cv])
                    tt = tmp["sq"].next()
                    P.op("act", lambda e, cv=cv, tt=tt: e.activation(out=tt[:], in_=cv[:], func=AF.Gelu_apprx_tanh),
                         reads=[cv], writes=[tt])
                    P.op("dve", lambda e, tt=tt, pg=pg, c=c: e.tensor_tensor(out=actT[:, c, :], in0=pg[:], in1=tt[:],
                                                                             op=ALU.mult),
                         reads=[tt, pg], accs=[actT])
                for c in range(NK):
                    xc = xcs.next()
                    P.dma("sp", xc[:], C.d_xT[xin_idx][s][:, c, t0:t0 + BLK], reads=[C.d_xT[xin_idx][s]], writes=[xc])
                    po = pos.next()
                    for k in range(NFC):
                        mm(P, po, po[:], w_dn, w_dn[:, k, c * 128:(c + 1) * 128], actT, actT[:, k, :],
                           k == 0, k == NFC - 1)
                    xo = xos.next()
                    P.op("dve", lambda e, po=po, c=c, xo=xo, xc=xc, s=s: e.scalar_tensor_tensor(
                        out=xo[:], in0=po[:], scalar=g2[:, c, s:s + 1], in1=xc[:], op0=ALU.mult, op1=ALU.add),
                        reads=[po, g2, xc], writes=[xo])
                    P.dma("pool", C.d_xT[xout_idx][s][:, c, t0:t0 + BLK], xo[:], reads=[xo],
                          accs=[C.d_xT[xout_idx][s]])


def stage_final(P, C, xin_idx):
    with P.scope():
        alloc_norm_scratch(P, C)
        xrs = Rot([P.sbuf([128, NK, BLK], F32, "xr") for _ in range(2)])
        yTs = Rot([P.sbuf([128, NK, BLK], F32, "yT") for _ in range(2)])
        yts = Rot([P.sbuf([128, 4, D], F32, "ytok") for _ in range(2)])
        pts = Rot([P.psum([128, 1024], F32, "pt2") for _ in range(2)])
        cnt = 0
        for s in range(NS):
            for blk in range(NBLK):
                t0 = blk * BLK
                xr = xrs.next()
                P.dma("sp", xr[:], C.d_xT[xin_idx][s][:, :, t0:t0 + BLK], reads=[C.d_xT[xin_idx][s]], writes=[xr])
                yT = yTs.next()
                norm_mod(P, C, xr, BLK, C.gain_fin, None, s, yT)
                yt = yts.next()
                for j in range(4):
                    pt = pts.next()
                    for c in range(NK):
                        P.op("pe", lambda e, pt=pt, j=j, c=c, yT=yT: e.transpose(
                            pt[:, c * 128:(c + 1) * 128], yT[:, c, j * 128:(j + 1) * 128], C.ident32[:]),
                            reads=[yT, C.ident32], writes=[pt])
                    if cnt % 2 == 0:
                        P.op("act", lambda e, pt=pt, yt=yt, j=j: e.copy(out=yt[:, j, :], in_=pt[:]),
                             reads=[pt], accs=[yt])
                    else:
                        P.op("dve", lambda e, pt=pt, yt=yt, j=j: e.tensor_copy(out=yt[:, j, :], in_=pt[:]),
                             reads=[pt], accs=[yt])
                    cnt += 1
                P.dma("pool", C.d_y[s, t0:t0 + BLK, :].rearrange("(j p) f -> p j f", p=128), yt[:],
                      reads=[yt], accs=[C.d_y])


DECAY_C = -math.exp(-0.5)
R1_STOP = int(os.environ.get("R1_STOP", "9"))
R1_SUB = int(os.environ.get("R1_SUB", "99"))


def stage_rwkv_norm(P, C):
    with P.scope():
        alloc_norm_scratch(P, C)
        xrs = Rot([P.sbuf([128, NK, BLK], F32, "xr") for _ in range(2)])
        hs = Rot([P.sbuf([128, NK, BLK], F32, "h1") for _ in range(2)])
        for s in range(NS):
            for blk in range(NBLK):
                t0 = blk * BLK
                xr = xrs.next()
                P.dma("sp", xr[:], C.d_xT[2][s][:, :, t0:t0 + BLK], reads=[C.d_xT[2][s]], writes=[xr])
                h = hs.next()
                norm_mod(P, C, xr, BLK, C.gain[1][0], C.shiftv[1][0], s, h)
                P.dma("pool", C.d_h1T[s][:, :, t0:t0 + BLK], h[:], reads=[h], accs=[C.d_h1T[s]])


def load_weight_pair(P, C, dram_buf, nk, c_lo, ncols, name, cols, mu_i):
    wb = P.sbuf([128, nk, ncols], BF16, name)
    ws = P.sbuf([128, nk, ncols], BF16, name + "s")
    grp = max(32, (C.wstage_n // nk) // 32 * 32)
    for c0 in range(0, ncols, grp):
        cw = min(grp, ncols - c0)
        st = C.wstage.next()
        stv = st[:, 0:nk * cw].rearrange("p (k c) -> p k c", c=cw)
        P.dma("sp", stv, dram_buf[:, :, c_lo + c0:c_lo + c0 + cw], reads=[dram_buf], writes=[st])
        P.op("act", lambda e, stv=stv, c0=c0, cw=cw: e.copy(out=wb[:, :, c0:c0 + cw], in_=stv), reads=[st], accs=[wb])
        P.op("dve", lambda e, stv=stv, c0=c0, cw=cw: e.tensor_tensor(
            out=ws[:, :, c0:c0 + cw], in0=stv, in1=cols[:, mu_i, :].unsqueeze(2).to_broadcast([128, nk, cw]),
            op=ALU.mult), reads=[st, cols], accs=[ws])
    return wb, ws


def stage_rwkv_proj(P, C):
    with P.scope():
        C.wstage_n = 512
        C.wstage = Rot([P.sbuf([128, 512], F32, "wst") for _ in range(2)])
        cols = P.sbuf([128, 14, NK], F32, "rwcols")
        P.dma("sp", cols[:], C.d_rwcols[:], reads=[C.d_rwcols], writes=[cols])
        w_r = load_weight_pair(P, C, C.d_w_r, NK, 0, D, "w_r", cols, 0)
        w_k = load_weight_pair(P, C, C.d_w_k, NK, 0, D, "w_k", cols, 2)
        w_v = load_weight_pair(P, C, C.d_w_v, NK, 0, D, "w_v", cols, 3)
        w_g1 = load_weight_pair(P, C, C.d_g1, NK, 0, 256, "w_g1", cols, 5)
        w_l1w = load_weight_pair(P, C, C.d_lora1, NK, 0, 128, "w_l1w", cols, 1)
        w_l1a = load_weight_pair(P, C, C.d_lora1, NK, 128, 128, "w_l1a", cols, 4)
        w_g2 = load_weight_bf16(P, C, C.d_g2, 2, D, "w_g2")
        w_l2 = load_weight_bf16(P, C, C.d_lora2, 4, D, "w_l2")
        bones = P.sbuf([128, 128], F32, "bones")
        P.dma("sp", bones[:], C.d_bones[:], reads=[C.d_bones], writes=[bones])
        hls = Rot([P.sbuf([128, NK, BLK + 2], F32, "hl") for _ in range(1)])
        xxh = P.sbuf([128, 4, BLK], F32, "xxh")
        hb = P.sbuf([128, NK, BLK], BF16, "hb")
        xb = P.sbuf([128, NK, BLK], BF16, "xb")
        pps = Rot([P.psum([128, BLK], F32, "pp") for _ in range(3)])
        pvs = Rot([P.psum([128, 1024], F32, "pv") for _ in range(1)])
        pls = Rot([P.psum([128, BLK], F32, "pl") for _ in range(3)])
        ob16 = Rot([P.sbuf([128, BLK], BF16, "ob16") for _ in range(6)])
        of32 = Rot([P.sbuf([128, BLK], F32, "of32") for _ in range(4)])
        kf = Rot([P.sbuf([128, BLK], F32, "kf") for _ in range(3)])
        kkf = Rot([P.sbuf([128, BLK], F32, "kkf") for _ in range(2)])
        vts = Rot([P.sbuf([128, D], BF16, "vtok") for _ in range(2)])
        sgs = Rot([P.sbuf([128, 2, BLK], BF16, "sg") for _ in range(1)])
        lts = Rot([P.sbuf([64, 4, BLK], BF16, "lt") for _ in range(1)])

        def evac(ps_ap, ps_buf, dst_ap, dst_buf):
            P.op("act", lambda e: e.copy(out=dst_ap, in_=ps_ap), reads=[ps_buf], writes=[dst_buf])

        def proj(ps_buf, ps_ap, wpair, csl):
            wb, ws = wpair
            for k in range(NK):
                mm(P, ps_buf, ps_ap, wb, wb[:, k, csl], hb, hb[:, k, :], k == 0, False)
            for k in range(NK):
                mm(P, ps_buf, ps_ap, ws, ws[:, k, csl], xb, xb[:, k, :], False, k == NK - 1)

        for s in range(NS):
            for blk in range(NBLK):
                t0 = blk * BLK
                hl = hls.next()
                lo = max(t0 - 1, 0)
                hi = min(t0 + BLK + 1, L)
                c_lo = lo - (t0 - 1)
                P.dma("sp", hl[:, :, c_lo:c_lo + (hi - lo)], C.d_h1T[s][:, :, lo:hi], reads=[C.d_h1T[s]], writes=[hl])
                if blk == 0:
                    P.op("dve", lambda e, hl=hl: e.memset(hl[:, :, 0:1], 0.0), accs=[hl])
                if blk == NBLK - 1:
                    P.op("dve", lambda e, hl=hl: e.memset(hl[:, :, BLK + 1:BLK + 2], 0.0), accs=[hl])
                P.op("act", lambda e, hl=hl: e.copy(out=hb[:], in_=hl[:, :, 1:BLK + 1]), reads=[hl], writes=[hb])
                for half in range(2):
                    ks = slice(4 * half, 4 * half + 4)
                    P.op("dve", lambda e, hl=hl, ks=ks: e.tensor_tensor(
                        out=xxh[:], in0=hl[:, ks, 0:BLK], in1=hl[:, ks, 2:BLK + 2], op=ALU.add),
                        reads=[hl], writes=[xxh])
                    P.op("dve", lambda e, hl=hl, ks=ks: e.scalar_tensor_tensor(
                        out=xb[:, ks, :], in0=xxh[:], scalar=0.5, in1=hl[:, ks, 1:BLK + 1], op0=ALU.mult,
                        op1=ALU.subtract), reads=[hl, xxh], accs=[xb])
                if R1_STOP <= 1:
                    continue
                for j in range(4):
                    pv = pvs.next()
                    jsl = slice(j * 128, (j + 1) * 128)
                    for half in range(2):
                        hsl = slice(half * 512, (half + 1) * 512)
                        for k in range(NK):
                            mm(P, pv, pv[:, hsl], hb, hb[:, k, jsl], w_v[0], w_v[0][:, k, hsl], k == 0, False)
                        for k in range(NK):
                            mm(P, pv, pv[:, hsl], xb, xb[:, k, jsl], w_v[1], w_v[1][:, k, hsl], False, k == NK - 1)
                    vt = vts.next()
                    evac(pv[:], pv, vt[:], vt)
                    P.dma("pool", C.d_vtok[s][t0 + j * 128:t0 + (j + 1) * 128, :], vt[:], reads=[vt],
                          accs=[C.d_vtok[s]])
                if R1_STOP <= 2:
                    continue
                sg = sgs.next()
                for cc in range(2):
                    pl = pls.next()
                    proj(pl, pl[:], w_g1, slice(cc * 128, (cc + 1) * 128))
                    P.op("act", lambda e, pl=pl, cc=cc, sg=sg: e.activation(
                        out=sg[:, cc, :], in_=pl[:], func=AF.Sigmoid), reads=[pl], accs=[sg])
                lt = lts.next()
                for q in range(4):
                    pl = pls.next()
                    proj(pl, pl[0:64, :], w_l1w if q < 2 else w_l1a, slice((q % 2) * 64, (q % 2) * 64 + 64))
                    if q < 2:
                        P.op("act", lambda e, pl=pl, q=q, lt=lt: e.activation(out=lt[:, q, :], in_=pl[0:64, :],
                                                                              func=AF.Tanh), reads=[pl], accs=[lt])
                    else:
                        P.op("act", lambda e, pl=pl, q=q, lt=lt: e.copy(out=lt[:, q, :], in_=pl[0:64, :]),
                             reads=[pl], accs=[lt])
                if R1_STOP <= 3:
                    continue
                def phase_a(c):
                    csl = slice(c * 128, (c + 1) * 128)
                    pp = pps.next()
                    proj(pp, pp[:], w_r, csl)
                    o = ob16.next()
                    evac(pp[:], pp, o[:], o)
                    P.dma("pool", C.d_rT[s][:, c, t0:t0 + BLK], o[:], reads=[o], accs=[C.d_rT[s]])
                    pp = pps.next()
                    proj(pp, pp[:], w_v, csl)
                    o = ob16.next()
                    evac(pp[:], pp, o[:], o)
                    P.dma("pool", C.d_vT[s][:, c, t0:t0 + BLK], o[:], reads=[o], accs=[C.d_vT[s]])
                    pp = pps.next()
                    mm(P, pp, pp[:], w_g2, w_g2[:, 0, csl], sg, sg[:, 0, :], True, False)
                    mm(P, pp, pp[:], w_g2, w_g2[:, 1, csl], sg, sg[:, 1, :], False, True)
                    o = ob16.next()
                    evac(pp[:], pp, o[:], o)
                    P.dma("pool", C.d_gT[s][:, c, t0:t0 + BLK], o[:], reads=[o], accs=[C.d_gT[s]])
                    pp = pps.next()
                    proj(pp, pp[:], w_k, csl)
                    kk_ = kf.next()
                    P.op("act", lambda e, pp=pp, kk_=kk_: e.copy(out=kk_[:], in_=pp[:]), reads=[pp], writes=[kk_])
                    return kk_

                def phase_b(c, kk_):
                    csl = slice(c * 128, (c + 1) * 128)
                    kq = kkf.next()
                    P.op("dve", lambda e, kk_=kk_, kq=kq, c=c: e.tensor_scalar(
                        out=kq[:], in0=kk_[:], scalar1=cols[:, 10, c:c + 1], scalar2=None, op0=ALU.mult),
                        reads=[kk_, cols], writes=[kq])
                    sq = of32.next()
                    P.op("act", lambda e, kq=kq, sq=sq: e.activation(out=sq[:], in_=kq[:], func=AF.Square),
                         reads=[kq], writes=[sq])
                    pl = pls.next()
                    mm(P, pl, pl[:], bones, bones[:], sq, sq[:], True, True)
                    rn = of32.next()
                    P.op("act", lambda e, pl=pl, rn=rn: e.activation(out=rn[:], in_=pl[:], func=AF.Sqrt),
                         reads=[pl], writes=[rn])
                    P.op("dve", lambda e, rn=rn: e.tensor_scalar_max(out=rn[:], in0=rn[:], scalar1=1e-12),
                         reads=[rn], writes=[rn])
                    P.op("dve", lambda e, rn=rn: e.reciprocal(out=rn[:], in_=rn[:]), reads=[rn], writes=[rn])
                    P.op("dve", lambda e, kq=kq, rn=rn: e.tensor_tensor(out=kq[:], in0=kq[:], in1=rn[:], op=ALU.mult),
                         reads=[kq, rn], writes=[kq])
                    o = ob16.next()
                    P.op("dve", lambda e, kq=kq, o=o: e.tensor_copy(out=o[:], in_=kq[:]), reads=[kq], writes=[o])
                    P.dma("pool", C.d_kkT[s][:, c, t0:t0 + BLK], o[:], reads=[o], accs=[C.d_kkT[s]])
                    for dd in range(2):
                        pl = pls.next()
                        mm(P, pl, pl[:], w_l2, w_l2[0:64, dd, csl], lt, lt[:, dd, :], True, True)
                        lw = of32.next()
                        P.op("act", lambda e, pl=pl, lw=lw, dd=dd, c=c: e.activation(
                            out=lw[:], in_=pl[:], func=AF.Sigmoid, bias=cols[:, 6 + dd, c:c + 1], scale=1.0),
                            reads=[pl, cols], writes=[lw])
                        P.dma("pool", C.d_lwT[dd][s][:, c, t0:t0 + BLK], lw[:], reads=[lw], accs=[C.d_lwT[dd][s]])
                        pl = pls.next()
                        mm(P, pl, pl[:], w_l2, w_l2[0:64, 2 + dd, csl], lt, lt[:, 2 + dd, :], True, True)
                        aa = of32.next()
                        P.op("act", lambda e, pl=pl, aa=aa, dd=dd, c=c: e.activation(
                            out=aa[:], in_=pl[:], func=AF.Sigmoid, bias=cols[:, 8 + dd, c:c + 1], scale=1.0),
                            reads=[pl, cols], writes=[aa])
                        o = ob16.next()
                        P.op("dve", lambda e, kq=kq, aa=aa, o=o: e.tensor_tensor(out=o[:], in0=kq[:], in1=aa[:],
                                                                                 op=ALU.mult),
                             reads=[kq, aa], writes=[o])
                        P.dma("pool", C.d_bT[dd][s][:, c, t0:t0 + BLK], o[:], reads=[o], accs=[C.d_bT[dd][s]])
                        P.op("dve", lambda e, aa=aa, c=c: e.tensor_scalar(
                            out=aa[:], in0=aa[:], scalar1=-1.0, scalar2=cols[:, 11, c:c + 1], op0=ALU.add,
                            op1=ALU.mult), reads=[aa, cols], writes=[aa])
                        o = ob16.next()
                        P.op("dve", lambda e, aa=aa, kk_=kk_, o=o: e.scalar_tensor_tensor(
                            out=o[:], in0=aa[:], scalar=1.0, in1=kk_[:], op0=ALU.add, op1=ALU.mult),
                            reads=[aa, kk_], writes=[o])
                        P.dma("pool", C.d_kdT[dd][s][:, c, t0:t0 + BLK], o[:], reads=[o], accs=[C.d_kdT[dd][s]])

                prev = None
                for c in range(NK):
                    kk_c = phase_a(c)
                    if prev is not None:
                        phase_b(*prev)
                    prev = (c, kk_c)
                phase_b(*prev)


SBLK = 128
NSB = L // SBLK
CPB = SBLK // 64
SC_LIMIT = int(os.environ.get("SC_LIMIT", "999"))
CAST_MOD = int(os.environ.get("CAST_MOD", "4"))


def stage_rwkv_scan(P, C, dd):
    with P.scope():
        NF = 16 * SBLK
        msk = P.sbuf([64, 192], F32, "scmask")
        P.dma("sp", msk[:], C.d_scanmask[:, dd, :], reads=[C.d_scanmask], writes=[msk])
        rmask = P.sbuf([64, NF], F32, "rmask")
        P.dma("sp", rmask[:], C.d_rmask[:, 0:NF], reads=[C.d_rmask], writes=[rmask])
        idb = P.sbuf([64, 64], BF16, "idb")
        P.op("dve", lambda e: e.tensor_copy(out=idb[:], in_=C.ident32[0:64, 0:64]), reads=[C.ident32], writes=[idb])
        Rr = P.sbuf([64, 16, SBLK], BF16, "scR")
        KD = P.sbuf([64, 16, SBLK], BF16, "scKD")
        Bb = P.sbuf([64, 16, SBLK], BF16, "scB")
        KK = P.sbuf([64, 16, SBLK], BF16, "scKK")
        fA = P.sbuf([64, 16, SBLK], F32, "scA")
        fB = P.sbuf([64, 16, SBLK], F32, "scBf")
        fC = P.sbuf([64, 16, SBLK], F32, "scC")
        BS = []
        for _ in range(2):
            b_ = Ctx()
            b_.AR = P.sbuf([64, 16, CPB, 128], BF16, "scAR")
            b_.KT = P.sbuf([64, 16, SBLK], BF16, "scKT")
            b_.BT = P.sbuf([64, 16, SBLK], BF16, "scBT")
            b_.Vt = P.sbuf([64, CPB, D], BF16, "scV")
            b_.Yo = P.sbuf([64, CPB, D], F32, "scY")
            b_.PC = P.sbuf([64, 16, CPB], F32, "scPC")
            BS.append(b_)
        Sf = P.sbuf([64, 16, 64], F32, "scSf")
        Sb = P.sbuf([64, 16, 64], BF16, "scSb")
        MNk = [[P.sbuf([64, 4, 128], BF16, "MNk") for _ in range(4)] for _ in range(2)]
        MNb = [[P.sbuf([64, 4, 128], BF16, "MNb") for _ in range(4)] for _ in range(2)]
        NT0 = [[P.sbuf([64, 4, 64], BF16, "NT0") for _ in range(4)] for _ in range(2)]
        KTt = [[P.sbuf([64, 4, 2, 64], BF16, "KTt") for _ in range(4)] for _ in range(2)]
        Nl = [[[P.sbuf([64, 4, 2, 64], BF16, "Nl") for _ in range(5)] for _ in range(4)] for _ in range(2)]
        Xb = [P.sbuf([64, 4, 64], BF16, "Xb") for _ in range(4)]
        tmpS = [P.sbuf([64, 4, 64], F32, "tmpS") for _ in range(2)]
        psMNk = P.psum([64, 512], F32, "psMNk")
        psMNb = P.psum([64, 512], F32, "psMNb")
        psN = Rot([P.psum([64, 512], F32, "psN") for _ in range(2)])
        psXb = [P.psum([64, 512], F32, "psX") for _ in range(2)]
        psYS = P.psum([64, 512], F32, "psYS")
        psT = P.psum([64, 512], F32, "psT")
        v4 = lambda b, w: b[:, 0:4 * w].rearrange("p (h t) -> p h t", t=w)
        v42 = lambda b: b[:, 0:512].rearrange("p (h a t) -> p h a t", a=2, t=64)
        xview = lambda g: psXb[g // 2][:, (g % 2) * 256:(g % 2) * 256 + 256].rearrange("p (h t) -> p h t", t=64)
        ecnt = [0]

        def cast(src_buf, src_ap, dst_buf, dst_ap):
            ecnt[0] += 1
            if ecnt[0] % CAST_MOD != 0:
                P.op("act", lambda e: e.copy(out=dst_ap, in_=src_ap), reads=[src_buf], writes=[dst_buf])
            else:
                P.op("dve", lambda e: e.tensor_copy(out=dst_ap, in_=src_ap), reads=[src_buf], writes=[dst_buf])

        def prep(s, blk, bs):
            t0 = blk * SBLK
            tsl_all = slice(t0, t0 + SBLK)
            for h2 in range(2):
                prt = slice(h2 * 64, (h2 + 1) * 64)
                P.dma("sp", fA[:, h2::2, :], C.d_lwT[dd][s][prt, :, tsl_all], reads=[C.d_lwT[dd][s]], accs=[fA])
                P.dma("sp", KK[:, h2::2, :], C.d_kkT[s][prt, :, tsl_all], reads=[C.d_kkT[s]], accs=[KK])
                P.dma("sp", Rr[:, h2::2, :], C.d_rT[s][prt, :, tsl_all], reads=[C.d_rT[s]], accs=[Rr])
                P.dma("sp", KD[:, h2::2, :], C.d_kdT[dd][s][prt, :, tsl_all], reads=[C.d_kdT[dd][s]], accs=[KD])
                P.dma("sp", Bb[:, h2::2, :], C.d_bT[dd][s][prt, :, tsl_all], reads=[C.d_bT[dd][s]], accs=[Bb])
            P.dma("sp", bs.Vt[:], C.d_vtok[s][tsl_all, :].rearrange("(c i) f -> i c f", i=64),
                  reads=[C.d_vtok[s]], writes=[bs.Vt])
            yield
            fl = lambda b: b[:].rearrange("p h t -> p (h t)")
            c4 = lambda b: b[:].rearrange("p h (c t) -> p (h c) t", t=64)
            c5 = lambda b: b[:].rearrange("p h (c t) -> p h c t", t=64)
            P.op("dve", lambda e: e.tensor_tensor_scan(out=fl(fB), data0=rmask[:], data1=fl(fA), initial=0.0,
                                                       op0=ALU.mult, op1=ALU.add), reads=[rmask, fA], writes=[fB])
            yield
            if dd == 1:
                P.op("dve", lambda e: e.tensor_tensor(out=fl(fC), in0=fl(fA), in1=fl(fB), op=ALU.subtract),
                     reads=[fA, fB], writes=[fC])
                yield
                P.op("dve", lambda e: e.tensor_tensor(
                    out=c4(fB), in0=c4(fC), in1=c4(fB)[:, :, 63:64].to_broadcast([64, 16 * CPB, 64]), op=ALU.add),
                    reads=[fC, fB], writes=[fB])
                yield
            P.op("dve", lambda e: e.tensor_tensor(out=fl(fA), in0=fl(fB), in1=fl(fA), op=ALU.subtract),
                 reads=[fA, fB], writes=[fA])
            yield
            P.op("act", lambda e: e.activation(out=fl(fA), in_=fl(fA), func=AF.Exp, scale=DECAY_C),
                 reads=[fA], writes=[fA])
            P.op("act", lambda e: e.activation(out=fl(fC), in_=fl(fB), func=AF.Exp, scale=DECAY_C),
                 reads=[fB], writes=[fC])
            yield
            P.op("act", lambda e: e.activation(out=fl(fB), in_=fl(fB), func=AF.Exp, scale=-DECAY_C),
                 reads=[fB], writes=[fB])
            pcol = 63 if dd == 0 else 0
            P.op("act", lambda e: e.copy(out=bs.PC[:], in_=c5(fC)[:, :, :, pcol]), reads=[fC], writes=[bs.PC])
            yield
            P.op("dve", lambda e: e.scalar_tensor_tensor(out=bs.AR[:, :, :, 0:64], in0=c5(KK), scalar=-1.0,
                                                         in1=c5(fA), op0=ALU.mult, op1=ALU.mult),
                 reads=[KK, fA], accs=[bs.AR])
            yield
            P.op("dve", lambda e: e.tensor_tensor(out=bs.AR[:, :, :, 64:128], in0=c5(Rr), in1=c5(fC), op=ALU.mult),
                 reads=[Rr, fC], accs=[bs.AR])
            yield
            P.op("dve", lambda e: e.tensor_tensor(out=bs.KT[:], in0=KD[:], in1=fB[:], op=ALU.mult),
                 reads=[KD, fB], writes=[bs.KT])
            yield
            P.op("dve", lambda e: e.tensor_tensor(out=bs.BT[:], in0=Bb[:], in1=fB[:], op=ALU.mult),
                 reads=[Bb, fB], writes=[bs.BT])

        def s1(bs, c, par):
            tsl = slice(c * 64, (c + 1) * 64)
            AR, KT, BT = bs.AR, bs.KT, bs.BT
            for g in range(4):
                for hi in range(4):
                    h = 4 * g + hi
                    mm(P, psMNk, v4(psMNk, 128)[:, hi, :], KT, KT[:, h, tsl], AR, AR[:, h, c, :], True, True)
                P.op("dve", lambda e, g=g: e.tensor_tensor(
                    out=MNk[par][g][:], in0=v4(psMNk, 128), in1=msk[:, 0:128].unsqueeze(1).to_broadcast([64, 4, 128]),
                    op=ALU.mult), reads=[psMNk, msk], writes=[MNk[par][g]])
                for hi in range(4):
                    h = 4 * g + hi
                    mm(P, psMNb, v4(psMNb, 128)[:, hi, :], BT, BT[:, h, tsl], AR, AR[:, h, c, :], True, True)
                P.op("dve", lambda e, g=g: e.tensor_tensor(
                    out=MNb[par][g][:], in0=v4(psMNb, 128), in1=msk[:, 0:128].unsqueeze(1).to_broadcast([64, 4, 128]),
                    op=ALU.mult), reads=[psMNb, msk], writes=[MNb[par][g]])
                pn = psN.next()
                for hi in range(4):
                    h = 4 * g + hi
                    mm(P, pn, v4(pn, 64)[:, hi, :], AR, AR[:, h, c, 0:64], BT, BT[:, h, tsl], True, True)
                P.op("dve", lambda e, g=g, pn=pn: e.tensor_tensor(
                    out=NT0[par][g][:], in0=v4(pn, 64), in1=msk[:, 128:192].unsqueeze(1).to_broadcast([64, 4, 64]),
                    op=ALU.mult), reads=[pn, msk], writes=[NT0[par][g]])
                for hi in range(4):
                    h = 4 * g + hi
                    mm(P, psT, v42(psT)[:, hi, 0, :], KT, KT[:, h, tsl], idb, idb[:], True, True)
                    mm(P, psT, v42(psT)[:, hi, 1, :], BT, BT[:, h, tsl], idb, idb[:], True, True)
                P.op("act", lambda e, g=g: e.copy(out=KTt[par][g][:], in_=v42(psT)), reads=[psT],
                     writes=[KTt[par][g]])

        def level_ops(par, g, j):
            if j == 0:
                return (MNb[par][g], (lambda hi: MNb[par][g][:, hi, 0:64]), NT0[par][g], (lambda hi: NT0[par][g][:, hi, :]))
            t = Nl[par][g][j - 1]
            return (t, (lambda hi: t[:, hi, 0, :]), t, (lambda hi: t[:, hi, 1, :]))

        def square(par, j):
            for g in range(4):
                nbuf, nap, tbuf, tap = level_ops(par, g, j)
                pn = psN.next()
                for hi in range(4):
                    mm(P, pn, v42(pn)[:, hi, 0, :], tbuf, tap(hi), nbuf, nap(hi), True, True)
                    if j < 4:
                        mm(P, pn, v42(pn)[:, hi, 1, :], nbuf, nap(hi), tbuf, tap(hi), True, True)
                dst = Nl[par][g][j]
                if j < 4:
                    cast(pn, v42(pn), dst, dst[:])
                else:
                    cast(pn, v42(pn)[:, :, 0, :], dst, dst[:, :, 0, :])

        def g_step(bs, c, par):
            AR, Vt = bs.AR, bs.Vt
            for g in range(4):
                xv = xview(g)
                pb_ = psXb[g // 2]
                for hi in range(4):
                    h = 4 * g + hi
                    first = (g % 2 == 0 and hi == 0)
                    P.op("pe", lambda e, xv=xv, hi=hi, h=h, first=first: e.matmul(
                        xv[:, hi, :], lhsT=AR[:, h, c, 0:64], rhs=Sb[:, h, :], start=first, stop=False,
                        skip_group_check=True), reads=[AR, Sb], writes=[pb_])
                    P.op("pe", lambda e, xv=xv, hi=hi, h=h, g=g: e.matmul(
                        xv[:, hi, :], lhsT=MNk[par][g][:, hi, 0:64], rhs=Vt[:, c, h * 64:(h + 1) * 64], start=False,
                        stop=False, skip_group_check=True), reads=[MNk[par][g], Vt], writes=[pb_])
                cast(pb_, xv, Xb[g], Xb[g][:])

        def apply(par, j):
            for g in range(4):
                xv = xview(g)
                pb_ = psXb[g // 2]
                nbuf, nap, _, _ = level_ops(par, g, j)
                for hi in range(4):
                    P.op("pe", lambda e, xv=xv, hi=hi, g=g, nap=nap: e.matmul(
                        xv[:, hi, :], lhsT=nap(hi), rhs=Xb[g][:, hi, :], start=False, stop=(j == 5),
                        skip_group_check=True), reads=[nbuf, Xb[g]], writes=[pb_])
                cast(pb_, xv, Xb[g], Xb[g][:])

        def ys_step(bs, c, par):
            AR, Vt, Yo = bs.AR, bs.Vt, bs.Yo
            for g in range(4):
                pys = v42(psYS)
                for hi in range(4):
                    h = 4 * g + hi
                    vv = Vt[:, c, h * 64:(h + 1) * 64]
                    mm(P, psYS, pys[:, hi, 0, :], AR, AR[:, h, c, 64:128], Sb, Sb[:, h, :], True, False)
                    mm(P, psYS, pys[:, hi, 0, :], MNk[par][g], MNk[par][g][:, hi, 64:128], Vt, vv, False, False)
                    mm(P, psYS, pys[:, hi, 0, :], MNb[par][g], MNb[par][g][:, hi, 64:128], Xb[g], Xb[g][:, hi, :],
                       False, True)
                    mm(P, psYS, pys[:, hi, 1, :], KTt[par][g], KTt[par][g][:, hi, 0, :], Vt, vv, True, False)
                    mm(P, psYS, pys[:, hi, 1, :], KTt[par][g], KTt[par][g][:, hi, 1, :], Xb[g], Xb[g][:, hi, :],
                       False, True)
                ts_ = tmpS[g % 2]
                P.op("dve", lambda e, g=g, pys=pys, ts_=ts_: e.tensor_tensor(
                    out=ts_[:], in0=pys[:, :, 1, :], in1=Sf[:, 4 * g:4 * g + 4, :], op=ALU.add),
                    reads=[psYS, Sf], writes=[ts_])
                pcb = bs.PC[:, 4 * g:4 * g + 4, c:c + 1].to_broadcast([64, 4, 64])
                P.op("dve", lambda e, g=g, pys=pys, c=c: e.tensor_copy(
                    out=Yo[:, c, g * 256:(g + 1) * 256].rearrange("p (h v) -> p h v", v=64), in_=pys[:, :, 0, :]),
                    reads=[psYS], accs=[Yo])
                P.op("dve", lambda e, g=g, ts_=ts_, pcb=pcb: e.tensor_tensor(
                    out=Sb[:, 4 * g:4 * g + 4, :], in0=ts_[:], in1=pcb, op=ALU.mult),
                    reads=[ts_, bs.PC], accs=[Sb])
                P.op("dve", lambda e, g=g, ts_=ts_, pcb=pcb: e.tensor_tensor(
                    out=Sf[:, 4 * g:4 * g + 4, :], in0=ts_[:], in1=pcb, op=ALU.mult),
                    reads=[ts_, bs.PC], accs=[Sf])

        for s in range(NS):
            P.op("dve", lambda e: e.memset(Sf[:], 0.0), writes=[Sf])
            P.op("dve", lambda e: e.memset(Sb[:], 0.0), writes=[Sb])
            blocks = list(range(NSB) if dd == 0 else range(NSB - 1, -1, -1))
            seq = []
            for bp, blk in enumerate(blocks):
                for c in (range(CPB) if dd == 0 else range(CPB - 1, -1, -1)):
                    seq.append((bp, blk, c))
            seq = seq[:SC_LIMIT]
            for _ in prep(s, seq[0][1], BS[0]):
                pass
            s1(BS[0], seq[0][2], 0)
            for j in range(5):
                square(0, j)
            pending = None
            for n, (bp, blk, c) in enumerate(seq):
                par = n % 2
                bs = BS[bp % 2]
                nxt = seq[n + 1] if n + 1 < len(seq) else None
                first_of_block = (n == 0) or (seq[n - 1][0] != bp)
                if first_of_block and CPB > 1:
                    later = [q for q in seq[n + 1:] if q[0] == bp + 1]
                    if later:
                        pending = prep(s, later[0][1], BS[(bp + 1) % 2])
                if nxt is not None:
                    nbs = BS[nxt[0] % 2]
                    if nxt[0] != bp:
                        if pending is not None:
                            for _ in pending:
                                pass
                            pending = None
                        elif CPB == 1:
                            for _ in prep(s, nxt[1], nbs):
                                pass
                    s1(nbs, nxt[2], 1 - par)
                g_step(bs, c, par)
                for j in range(6):
                    apply(par, j)
                    if nxt is not None and j < 5:
                        square(1 - par, j)
                    if pending is not None:
                        for _ in range(2):
                            try:
                                next(pending)
                            except StopIteration:
                                pending = None
                                break
                ys_step(bs, c, par)
                last_of_block = (nxt is None) or (nxt[0] != bp)
                if last_of_block:
                    t0 = blk * SBLK
                    P.dma("pool", C.d_ytok[dd][s][t0:t0 + SBLK, :].rearrange("(c i) f -> i c f", i=64), bs.Yo[:],
                          reads=[bs.Yo], accs=[C.d_ytok[dd][s]])


GN_EPS = 64e-5


def stage_rwkv_post(P, C):
    with P.scope():
        C.wstage_n = 1024
        C.wstage = Rot([P.sbuf([128, 1024], F32, "wst") for _ in range(2)])
        w_o = load_weight_bf16(P, C, C.d_w_o, NK, D, "w_o")
        cols = P.sbuf([128, 14, NK], F32, "rwcols")
        P.dma("sp", cols[:], C.d_rwcols[:], reads=[C.d_rwcols], writes=[cols])
        lnb = P.sbuf([128, NK], F32, "lnb")
        P.dma("sp", lnb[:], C.d_lnb[:], reads=[C.d_lnb], writes=[lnb])
        bones = P.sbuf([128, 128], F32, "bones")
        P.dma("sp", bones[:], C.d_bones[:], reads=[C.d_bones], writes=[bones])
        alloc_norm_scratch(P, C)
        yf = P.sbuf([128, 4, D], F32, "yf")
        yb = P.sbuf([128, 4, D], F32, "yb")
        st = P.sbuf([128, 6, 64], F32, "gnst")
        ynT = P.sbuf([128, NK, BLK], F32, "ynT")
        rB = P.sbuf([128, NK, BLK], BF16, "rB")
        k0B = P.sbuf([128, NK, BLK], BF16, "k0B")
        k1B = P.sbuf([128, NK, BLK], BF16, "k1B")
        vB = P.sbuf([128, NK, BLK], BF16, "vB")
        gB = P.sbuf([128, NK, BLK], BF16, "gB")
        xr = P.sbuf([128, NK, BLK], F32, "xr")
        outT = P.sbuf([128, NK, BLK], BF16, "outT")
        x3 = P.sbuf([128, NK, BLK], F32, "x3T")
        h2 = P.sbuf([128, NK, BLK], BF16, "h2T")
        kms = Rot([P.sbuf([128, BLK], F32, "km") for _ in range(2)])
        qs = Rot([P.sbuf([128, BLK], F32, "qq") for _ in range(2)])
        bns = Rot([P.sbuf([128, BLK], F32, "bn") for _ in range(2)])
        pts = Rot([P.psum([128, BLK], F32, "pt") for _ in range(2)])
        pbs = Rot([P.psum([128, BLK], F32, "pb") for _ in range(2)])
        pps = Rot([P.psum([128, BLK], F32, "pp") for _ in range(2)])
        g1 = C.gatev[1][0]
        for s in range(NS):
            for blk in range(NBLK):
                t0 = blk * BLK
                tsl = slice(t0, t0 + BLK)
                P.dma("sp", yf[:], C.d_ytok[0][s][tsl, :].rearrange("(j p) f -> p j f", p=128),
                      reads=[C.d_ytok[0][s]], writes=[yf])
                P.dma("sp", yb[:], C.d_ytok[1][s][tsl, :].rearrange("(j p) f -> p j f", p=128),
                      reads=[C.d_ytok[1][s]], writes=[yb])
                P.dma("sp", rB[:], C.d_rT[s][:, :, tsl], reads=[C.d_rT[s]], writes=[rB])
                P.dma("sp", k0B[:], C.d_kdT[0][s][:, :, tsl], reads=[C.d_kdT[0][s]], writes=[k0B])
                P.dma("sp", k1B[:], C.d_kdT[1][s][:, :, tsl], reads=[C.d_kdT[1][s]], writes=[k1B])
                P.dma("sp", vB[:], C.d_vT[s][:, :, tsl], reads=[C.d_vT[s]], writes=[vB])
                P.dma("sp", gB[:], C.d_gT[s][:, :, tsl], reads=[C.d_gT[s]], writes=[gB])
                P.dma("sp", xr[:], C.d_xT[2][s][:, :, tsl], reads=[C.d_xT[2][s]], writes=[xr])
                yv = yf[:].rearrange("p j (h v) -> p (j h) v", v=64)
                ybv = yb[:].rearrange("p j (h v) -> p (j h) v", v=64)
                P.op("dve", lambda e: e.tensor_tensor(out=yf[:], in0=yf[:], in1=yb[:], op=ALU.add),
                     reads=[yf, yb], writes=[yf])
                P.op("dve", lambda e: e.tensor_reduce(out=st[:, 0, :], in_=yv, axis=mybir.AxisListType.X, op=ALU.add),
                     reads=[yf], writes=[st])
                P.op("act", lambda e: e.activation(out=yb[:], in_=yf[:], func=AF.Square), reads=[yf], writes=[yb])
                P.op("dve", lambda e: e.tensor_reduce(out=st[:, 1, :], in_=ybv, axis=mybir.AxisListType.X, op=ALU.add),
                     reads=[yb], writes=[st])
                P.op("dve", lambda e: e.tensor_scalar(out=st[:, 2, :], in0=st[:, 0, :], scalar1=1.0 / 64, scalar2=None,
                                                      op0=ALU.mult), reads=[st], writes=[st])
                P.op("dve", lambda e: e.tensor_tensor(out=st[:, 3, :], in0=st[:, 2, :], in1=st[:, 2, :], op=ALU.mult),
                     reads=[st], writes=[st])
                P.op("dve", lambda e: e.scalar_tensor_tensor(out=st[:, 4, :], in0=st[:, 1, :], scalar=1.0 / 64,
                                                             in1=st[:, 3, :], op0=ALU.mult, op1=ALU.subtract),
                     reads=[st], writes=[st])
                P.op("dve", lambda e: e.tensor_scalar_add(out=st[:, 4, :], in0=st[:, 4, :], scalar1=GN_EPS),
                     reads=[st], writes=[st])
                P.op("act", lambda e: e.activation(out=st[:, 5, :], in_=st[:, 4, :], func=AF.Sqrt),
                     reads=[st], writes=[st])
                P.op("dve", lambda e: e.reciprocal(out=st[:, 5, :], in_=st[:, 5, :]), reads=[st], writes=[st])
                P.op("dve", lambda e: e.tensor_tensor(out=yv, in0=yv, in1=st[:, 2, :].unsqueeze(2).to_broadcast(
                    [128, 64, 64]), op=ALU.subtract), reads=[yf, st], writes=[yf])
                P.op("dve", lambda e: e.tensor_tensor(out=yv, in0=yv, in1=st[:, 5, :].unsqueeze(2).to_broadcast(
                    [128, 64, 64]), op=ALU.mult), reads=[yf, st], writes=[yf])
                for k in range(NK):
                    pt = pts.next()
                    for j in range(4):
                        P.op("pe", lambda e, pt=pt, j=j, k=k: e.transpose(
                            pt[:, j * 128:(j + 1) * 128], yf[:, j, k * 128:(k + 1) * 128], C.ident32[:]),
                            reads=[yf, C.ident32], writes=[pt])
                    P.op("act", lambda e, pt=pt, k=k: e.activation(
                        out=ynT[:, k, :], in_=pt[:], func=AF.Identity, scale=cols[:, 13, k:k + 1],
                        bias=lnb[:, k:k + 1]), reads=[pt, cols, lnb], accs=[ynT])
                    km = kms.next()
                    P.op("dve", lambda e, km=km, k=k: e.tensor_tensor(out=km[:], in0=k0B[:, k, :], in1=k1B[:, k, :],
                                                                       op=ALU.add), reads=[k0B, k1B], writes=[km])
                    q = qs.next()
                    P.op("dve", lambda e, km=km, q=q, k=k: e.scalar_tensor_tensor(
                        out=q[:], in0=rB[:, k, :], scalar=cols[:, 12, k:k + 1], in1=km[:], op0=ALU.mult, op1=ALU.mult),
                        reads=[rB, cols, km], writes=[q])
                    pb = pbs.next()
                    mm(P, pb, pb[:], bones, bones[:], q, q[:], True, True)
                    bn = bns.next()
                    P.op("dve", lambda e, pb=pb, bn=bn, k=k: e.scalar_tensor_tensor(
                        out=bn[:], in0=pb[:], scalar=0.5, in1=vB[:, k, :], op0=ALU.mult, op1=ALU.mult),
                        reads=[pb, vB], writes=[bn])
                    P.op("dve", lambda e, bn=bn, k=k: e.tensor_tensor(out=bn[:], in0=bn[:], in1=ynT[:, k, :],
                                                                       op=ALU.add), reads=[bn, ynT], writes=[bn])
                    P.op("dve", lambda e, bn=bn, k=k: e.tensor_tensor(out=outT[:, k, :], in0=bn[:], in1=gB[:, k, :],
                                                                       op=ALU.mult), reads=[bn, gB], accs=[outT])
                for c in range(NK):
                    pp = pps.next()
                    for k in range(NK):
                        mm(P, pp, pp[:], w_o, w_o[:, k, c * 128:(c + 1) * 128], outT, outT[:, k, :], k == 0, k == NK - 1)
                    P.op("dve", lambda e, pp=pp, c=c, s=s: e.scalar_tensor_tensor(
                        out=x3[:, c, :], in0=pp[:], scalar=g1[:, c, s:s + 1], in1=xr[:, c, :],
                        op0=ALU.mult, op1=ALU.add), reads=[pp, g1, xr], accs=[x3])
                P.dma("pool", C.d_xT[3][s][:, :, tsl], x3[:], reads=[x3], accs=[C.d_xT[3][s]])
                norm_mod(P, C, x3, BLK, C.gain[1][1], C.shiftv[1][1], s, h2)
                P.dma("pool", C.d_h2T[1][s][:, :, tsl], h2[:], reads=[h2], accs=[C.d_h2T[1][s]])

class _ModView:
    def __init__(self, buf, j):
        self.buf = buf
        self.j = j

    @property
    def w(self):
        return self.buf.w

    @property
    def r(self):
        return self.buf.r

    @property
    def a(self):
        return self.buf.a

    def __getitem__(self, idx):
        p, k, s = idx
        return self.buf[p, self.j * 8 + k, s]


def mod_shift(C, l, which):
    return C.shiftv[l][which]


def host_consts():
    c = {}
    c["ident32"] = np.eye(128, dtype=np.float32)
    c["ones32"] = np.ones((128, 128), dtype=np.float32)
    bo = np.zeros((128, 128), np.float32)
    bo[0:64, 0:64] = 1.0
    bo[64:128, 64:128] = 1.0
    c["c_bones"] = bo
    ii = np.arange(64)[:, None]
    tt = np.arange(64)[None, :]
    sm = np.zeros((64, 2, 192), np.float32)
    sm[:, 0, 0:64] = (ii < tt)
    sm[:, 0, 64:128] = (ii <= tt)
    sm[:, 0, 128:192] = (tt < ii)
    sm[:, 1, 0:64] = (ii > tt)
    sm[:, 1, 64:128] = (ii >= tt)
    sm[:, 1, 128:192] = (tt > ii)
    c["c_scanmask"] = sm
    rm = np.ones((64, 16 * 256), np.float32)
    rm[:, ::64] = 0.0
    c["c_rmask"] = rm
    slopes = np.exp2(-8.0 * (np.arange(12, dtype=np.float32) + 1.0) / 12).astype(np.float32).reshape(3, 4)
    tab = np.zeros((128, 9, 4, 128), np.float32)
    kk = np.arange(128)[:, None]
    qq = np.arange(128)[None, :]
    NEG = -30000.0
    for g, d in enumerate((1, 4, 16)):
        for h in range(4):
            sl = slopes[g, h] * d
            lo = np.where(kk >= qq, -sl * np.abs(kk - qq - 64), NEG)
            up = np.where(kk <= qq, -sl * np.abs(kk - qq + 64), NEG)
            tab[:, 3 * g + 0, h, :] = lo
            tab[:, 3 * g + 1, h, :] = up
            tab[0:64, 3 * g + 2, h, :] = lo[64:128]
    c["abias"] = tab.astype(np.float32)
    n1 = np.arange(64, dtype=np.float64)[:, None]
    k1 = np.arange(128, dtype=np.float64)[None, :]
    ang = 2 * np.pi * n1 * k1 / 128.0
    c["c_w128"] = np.concatenate([np.cos(ang), -np.sin(ang)], 1).astype(np.float32)
    n2 = np.tile(np.arange(64, dtype=np.float64), 2)[:, None]
    ang = 2 * np.pi * n2 * k1 / 8192.0
    c["c_tw"] = np.stack([np.cos(ang), -np.sin(ang)], 1).astype(np.float32)
    a64 = 2 * np.pi * np.outer(np.arange(64), np.arange(64)) / 64.0
    def bdiag(m):
        z = np.zeros((128, 128))
        z[0:64, 0:64] = m
        z[64:128, 64:128] = m
        return z
    bdr, bdi = bdiag(np.cos(a64)), bdiag(-np.sin(a64))
    c["c_bd"] = np.stack([bdr, bdi, -bdi], 1).astype(np.float32)
    cr, ci = bdiag(np.cos(a64)), bdiag(np.sin(a64))
    c["c_bdc"] = np.stack([np.concatenate([cr, ci], 1), np.concatenate([-ci, cr], 1)], 1).astype(np.float32)
    kk1 = np.arange(128, dtype=np.float64)[:, None]
    nn2 = np.tile(np.arange(64, dtype=np.float64), 2)[None, :]
    ang = 2 * np.pi * kk1 * nn2 / 8192.0
    c["c_twi"] = np.stack([np.cos(ang), np.sin(ang)], 1).astype(np.float32)
    ang = 2 * np.pi * np.outer(np.arange(128), np.arange(64)) / 128.0
    c["c_vinv"] = np.stack([np.cos(ang) / 8192.0, -np.sin(ang) / 8192.0], 1).astype(np.float32)
    t = np.linspace(0.0, 1.0, L, dtype=np.float32)[:, None]
    w = (2.0 * np.float32(math.pi) * np.arange(L, dtype=np.float32)[:, None] / np.float32(L)).astype(np.float32)
    f = np.linspace(1e-4, 15, 16, dtype=np.float32)[None, :]
    z = (f * w).astype(np.float32)
    pos = np.concatenate([t, np.cos(z), -np.sin(z)], -1).astype(np.float32)
    c["c_posT"] = np.ascontiguousarray(pos.T)
    min_decay = math.log(1e-2) / 1.5
    max_decay = math.log(1e-2) / 0.3
    deltas = np.linspace(min_decay, max_decay, 256, dtype=np.float32)[None, :]
    win = np.exp(-t * np.abs(deltas)).astype(np.float32)
    c["c_winT"] = np.ascontiguousarray(win.T.reshape(2, 128, L).transpose(1, 0, 2))
    return c


def build_program(stages, dbg=()):
    nc = bass.Bass("TRN2", target_bir_lowering=False)
    P = Prog(nc)
    C = Ctx()
    C.dbg = {}
    din = lambda name, shape, dt=F32: P.dram(name, shape, dt, kind="ExternalInput")
    C.d_x = din("x_in", [NS, L, D])
    C.d_cT = din("cT", [128, NK, NS])
    C.d_adaw = [din("ada_w%d" % l, [128, NK, 6 * D]) for l in range(2)]
    C.d_adab = din("ada_b", [128, 2, 48])
    C.d_normw = din("normw", [128, 5, NK])
    C.d_w_in = din("w_in", [128, NK, 3072])
    C.d_abias = din("abias", [128, 9, 4, 128])
    C.d_w128 = din("c_w128", [64, 256])
    C.d_tw = din("c_tw", [128, 2, 128])
    C.d_bd = din("c_bd", [128, 3, 128])
    C.d_bdc = din("c_bdc", [128, 2, 256])
    C.d_twi = din("c_twi", [128, 2, 128])
    C.d_vinv = din("c_vinv", [128, 2, 64])
    C.d_posT = din("c_posT", [33, L])
    C.d_winT = din("c_winT", [128, 2, L])
    C.d_hcol = din("hcol", [64, 4])
    C.d_fw1 = din("fw1", [33, 64])
    C.d_fw23 = din("fw23", [64, 2, 64])
    C.d_fw4 = din("fw4", [64, 512])
    C.d_fbias = din("fbias", [128, 2])
    C.d_shortw = din("shortw", [128, 6, 4])
    C.d_w_out = din("w_out", [128, 4, D])
    C.d_ffn_up = [din("ffn_up%d" % l, [128, NK, 2 * DFF]) for l in range(2)]
    C.d_ffn_dn = [din("ffn_dn%d" % l, [128, NFC, D]) for l in range(2)]
    C.d_ffn_cw = [din("ffn_cw%d" % l, [128, NFC, 4]) for l in range(2)]
    C.d_w_r = din("w_r", [128, NK, D])
    C.d_w_k = din("w_k", [128, NK, D])
    C.d_w_v = din("w_v", [128, NK, D])
    C.d_w_o = din("w_o", [128, NK, D])
    C.d_g1 = din("rw_g1", [128, NK, 256])
    C.d_g2 = din("rw_g2", [128, 2, D])
    C.d_lora1 = din("rw_lora1", [128, NK, 256])
    C.d_lora2 = din("rw_lora2", [128, 4, D])
    C.d_rwcols = din("rw_cols", [128, 14, NK])
    C.d_bones = din("c_bones", [128, 128])
    C.d_lnb = din("rw_lnb", [128, NK])
    C.d_scanmask = din("c_scanmask", [64, 2, 192])
    C.d_rmask = din("c_rmask", [64, 16 * 256])
    C.d_ident32 = din("ident32", [128, 128])
    C.d_ones32 = din("ones32", [128, 128])
    C.d_y = P.dram("y_out", [NS, L, D], F32, kind="ExternalOutput")
    outs = []

    def scratch(name, shape, dt):
        if name in dbg:
            b = P.dram(name, shape, dt, kind="ExternalOutput")
            outs.append(b)
            return b
        return P.dram(name, shape, dt)

    C.d_xT = [[scratch("xT%d_%d" % (i, s), [128, NK, L], F32) for s in range(NS)] for i in range(5)]
    C.d_qkT = [scratch("qkT_%d" % s, [128, 12, L], BF16) for s in range(NS)]
    C.d_vaug = [scratch("vaug_%d" % s, [L, 12, 65], BF16) for s in range(NS)]
    C.d_hyT = [scratch("hyT_%d" % s, [128, 6, L], F32) for s in range(NS)]
    C.d_oacc = [scratch("oacc_%d" % s, [3, L, 4, 65], F32) for s in range(NS)]
    C.d_h2T = [[scratch("h2T%d_%d" % (l, s), [128, NK, L], BF16) for s in range(NS)] for l in range(2)]
    C.d_h1T = [scratch("h1T_%d" % s, [128, NK, L], F32) for s in range(NS)]
    C.d_vtok = [scratch("vtok_%d" % s, [L, D], BF16) for s in range(NS)]
    C.d_rT = [scratch("rT_%d" % s, [128, NK, L], BF16) for s in range(NS)]
    C.d_vT = [scratch("vT_%d" % s, [128, NK, L], BF16) for s in range(NS)]
    C.d_gT = [scratch("gT_%d" % s, [128, NK, L], BF16) for s in range(NS)]
    C.d_kkT = [scratch("kkT_%d" % s, [128, NK, L], BF16) for s in range(NS)]
    C.d_lwT = [[scratch("lwT%d_%d" % (dd, s), [128, NK, L], F32) for s in range(NS)] for dd in range(2)]
    C.d_bT = [[scratch("bT%d_%d" % (dd, s), [128, NK, L], BF16) for s in range(NS)] for dd in range(2)]
    C.d_kdT = [[scratch("kdT%d_%d" % (dd, s), [128, NK, L], BF16) for s in range(NS)] for dd in range(2)]
    C.d_ytok = [[scratch("ytok%d_%d" % (dd, s), [L, D], F32) for s in range(NS)] for dd in range(2)]
    C.d_filtT = scratch("filtT", [128, 4, L], F32)
    C.d_F = scratch("Fspec", [128, 128, 2, 128], F32)
    C.d_zT = [scratch("zT_%d" % s, [128, 2, L], F32) for s in range(NS)]
    C.d_x0T = [scratch("x0T_%d" % s, [128, 2, L], F32) for s in range(NS)]
    C.d_hyoT = [scratch("hyoT_%d" % s, [128, 2, L], F32) for s in range(NS)]
    C.ident32 = P.sbuf([128, 128], F32, "ident32")
    C.ones32 = P.sbuf([128, 128], F32, "ones32")
    C.eps_col = P.sbuf([128, 1], F32, "eps")
    P.dma("sp", C.ident32[:], C.d_ident32[:], reads=[C.d_ident32], writes=[C.ident32])
    P.dma("sp", C.ones32[:], C.d_ones32[:], reads=[C.d_ones32], writes=[C.ones32])
    P.op("pool", lambda e: e.memset(C.eps_col[:], RMS_EPS), writes=[C.eps_col])
    C.modT = [P.sbuf([128, 48, NS], F32, "modT%d" % l) for l in range(2)]
    C.gain = [[P.sbuf([128, NK, NS], F32, "gain%d%d" % (l, w)) for w in range(2)] for l in range(2)]
    C.gain_fin = P.sbuf([128, NK, NS], F32, "gainf")
    C.shiftv = [[_ModView(C.modT[l], 0), _ModView(C.modT[l], 3)] for l in range(2)]
    C.gatev = [[_ModView(C.modT[l], 2), _ModView(C.modT[l], 5)] for l in range(2)]

    def dbg_out(name, src_buf, src_ap, shape, dt=F32):
        if name in dbg:
            o = P.dram("dbg_" + name, shape, dt, kind="ExternalOutput")
            P.dma("sp", o[:], src_ap, reads=[src_buf], writes=[o])
            outs.append(o)

    C.dbg_out = dbg_out
    if "adaln" in stages:
        stage_adaln(P, C)
        dbg_out("modT0", C.modT[0], C.modT[0][:], [128, 48, NS])
        dbg_out("gain00", C.gain[0][0], C.gain[0][0][:], [128, NK, NS])
    if "l0_inproj" in stages:
        stage_l0_inproj(P, C)
    if "attn" in stages:
        stage_attention(P, C)
    if "hyfilt" in stages:
        stage_hyena_filter(P, C)
    if "hyena" in stages:
        stage_hyena(P, C)
    if "l0_outproj" in stages:
        stage_l0_outproj(P, C)
    if "ffn0" in stages:
        stage_ffn(P, C, 0, 1, 2)
    if "rw_norm" in stages:
        stage_rwkv_norm(P, C)
    if "rw_proj" in stages:
        stage_rwkv_proj(P, C)
    if "rw_scan0" in stages:
        stage_rwkv_scan(P, C, 0)
    if "rw_scan1" in stages:
        stage_rwkv_scan(P, C, 1)
    if "rw_post" in stages:
        stage_rwkv_post(P, C)
    if "ffn1" in stages:
        stage_ffn(P, C, 1, 3, 4)
    if "final" in stages:
        stage_final(P, C, FINAL_SRC)
    outs.append(C.d_y)
    final = [o for o in outs if o is not None]
    C.final_bufs = final
    return nc, P, C


def finish_program(P, C, extra=()):
    bufs = list(C.final_bufs) + list(extra)
    P.wait_all("sp", bufs)
    P.close()


def arrange_w(w, nk=None):
    K, N = w.shape
    nk = K // 128
    return np.ascontiguousarray(w.reshape(nk, 128, N).transpose(1, 0, 2))


def col_layout(v):
    return np.ascontiguousarray(v.reshape(-1, 128).T)


def prep_shared(inp):
    m = {}
    m["ada_w0"] = arrange_w(inp["l0_ada_w"])
    m["ada_w1"] = arrange_w(inp["l1_ada_w"])
    m["ada_b"] = np.ascontiguousarray(np.stack([col_layout(inp["l0_ada_b"]), col_layout(inp["l1_ada_b"])], 1))
    m["normw"] = np.ascontiguousarray(np.stack([col_layout(inp[k]) for k in
                                               ("l0_norm1", "l0_norm2", "l1_norm1", "l1_norm2", "final_norm")], 1))
    m["w_in"] = arrange_w(inp["l0_w_in"])
    m["w_out"] = arrange_w(inp["l0_w_out"])
    ffn = [(inp["l0_ffn_up"], inp["l0_ffn_down"], inp["l0_ffn_conv_w"], inp["l0_ffn_conv_b"]),
           (inp["l1_ffn_up"], inp["l1_ffn_down"], inp["l1_ffn_conv_w"], inp["l1_ffn_conv_b"])]
    for l in range(2):
        up, dn, cw_, cb_ = ffn[l]
        m["ffn_up%d" % l] = arrange_w(up)
        m["ffn_dn%d" % l] = arrange_w(dn)
        cwb = np.concatenate([cw_, cb_[None, :]], 0)
        m["ffn_cw%d" % l] = np.ascontiguousarray(cwb.T.reshape(NFC, 128, 4).transpose(1, 0, 2))
    for nm in ("w_r", "w_k", "w_v", "w_o"):
        m[nm] = arrange_w(inp["l1_" + nm])
    g1p = np.zeros((D, 256), np.float32)
    g1p[:, :160] = inp["l1_g1"]
    m["rw_g1"] = arrange_w(g1p)
    g2 = np.zeros((256, D), np.float32)
    g2[:160] = inp["l1_g2"]
    m["rw_g2"] = arrange_w(g2)
    m["rw_lora1"] = arrange_w(np.concatenate([inp["l1_w1"][0], inp["l1_w1"][1], inp["l1_a1"][0], inp["l1_a1"][1]], 1))
    l2 = np.zeros((128, 4, D), np.float32)
    l2[:64, 0] = inp["l1_w2"][0]
    l2[:64, 1] = inp["l1_w2"][1]
    l2[:64, 2] = inp["l1_a2"][0]
    l2[:64, 3] = inp["l1_a2"][1]
    m["rw_lora2"] = l2
    cl = [col_layout(inp["l1_mu"][i]) for i in range(6)]
    cl += [col_layout(inp["l1_w0"][0]), col_layout(inp["l1_w0"][1]), col_layout(inp["l1_a0"][0]),
           col_layout(inp["l1_a0"][1]), col_layout(inp["l1_k_k"]), col_layout(inp["l1_k_a"]),
           col_layout(inp["l1_r_k"].reshape(-1)), col_layout(inp["l1_ln_w"])]
    m["rw_cols"] = np.ascontiguousarray(np.stack(cl, 1))
    m["rw_lnb"] = col_layout(inp["l1_ln_b"])
    m["hcol"] = np.ascontiguousarray(np.stack([inp["l0_filt_b1"], inp["l0_filt_b2"], inp["l0_filt_b3"],
                                               inp["l0_filt_freq"]], 1))
    m["fw1"] = np.ascontiguousarray(inp["l0_filt_w1"])
    m["fw23"] = np.ascontiguousarray(np.stack([inp["l0_filt_w2"], inp["l0_filt_w3"]], 1))
    m["fw4"] = np.ascontiguousarray(inp["l0_filt_w4"])
    m["fbias"] = col_layout(inp["l0_filt_bias"])
    sw = np.concatenate([inp["l0_short_w"], inp["l0_short_b"][None, :]], 0)
    m["shortw"] = np.ascontiguousarray(sw.T.reshape(6, 128, 4).transpose(1, 0, 2))
    m.update(host_consts())
    return m


def prep_core(xs, cs):
    m = {}
    m["x_in"] = np.ascontiguousarray(np.stack(xs, 0))
    c = np.stack(cs, 0)
    m["cT"] = np.ascontiguousarray(c.reshape(len(xs), NK, 128).transpose(2, 1, 0))
    return m


ALL_STAGES = ("adaln", "l0_inproj", "attn", "hyfilt", "hyena", "l0_outproj", "ffn0", "rw_norm", "rw_proj",
              "rw_scan0", "rw_scan1", "rw_post", "ffn1", "final")


def kernel(**inputs):
    inp = {k: np.asarray(v) for k, v in inputs.items()}
    xs = [inp["x_prompt"][i] for i in range(inp["x_prompt"].shape[0])] + \
         [inp["x_sample"][i] for i in range(inp["x_sample"].shape[0])]
    cs = [inp["c_prompt"][i] for i in range(inp["c_prompt"].shape[0])] + \
         [inp["c_sample"][i] for i in range(inp["c_sample"].shape[0])]
    nseq = len(xs)
    nb = inp["x_prompt"].shape[0]
    assign = []
    for core in range(NCORES):
        ids = []
        for slot in range(NS):
            sid = core + NCORES * slot
            ids.append(sid if sid < nseq else core)
        assign.append(ids)
    nc, P, C = build_program(ALL_STAGES, ())
    finish_program(P, C)
    shared = prep_shared(inp)
    in_maps = []
    for core in range(NCORES):
        m = dict(shared)
        m.update(prep_core([xs[i] for i in assign[core]], [cs[i] for i in assign[core]]))
        in_maps.append(m)
    res = run_bass_kernel_spmd(nc, in_maps, core_ids=list(range(NCORES)))
    outs = [None] * nseq
    for core in range(NCORES):
        y = np.asarray(res.results[core]["y_out"])
        for slot in range(NS):
            sid = core + NCORES * slot
            if sid < nseq:
                outs[sid] = y[slot]
    y_prompt = np.stack(outs[:nb], 0).astype(np.float32)
    y_sample = np.stack(outs[nb:], 0).astype(np.float32)
    return (y_prompt, y_sample)
```
